# Optimizing a Trainium2 kernel written in Bass

```python
import math
import jax, jax.numpy as jnp
from jax import lax
import numpy as np

D_MODEL = 2048
BATCH = 1
SEQ = 8192
DEPTH = 2
DEC_BATCH = 128
DEC_SEQ = 4
PAST_LEN = 8192
PAGE_SIZE = 128

N_MIXERS = 2
N_HEADS = 32
N_KV_HEADS = 8
HEAD_DIM = D_MODEL // N_HEADS
GROUP = N_HEADS // N_KV_HEADS
WINDOW = 128
HG_EXPAND = 128
HG_HEADS = D_MODEL // HG_EXPAND
HG_DK = HG_EXPAND
HG_DV = D_MODEL // HG_HEADS
HG_CHUNK = 64
D_FF = 5632
CONV_W = 3
N_MOD = 6
EPS = 1e-6

kernel_name = 'hybrid_swa_sink_hgrn2_convffn_adaln_step'


def rmsnorm(x, g):
    xf = x.astype(jnp.float32)
    y = xf * lax.rsqrt(jnp.mean(xf * xf, axis=-1, keepdims=True) + EPS) * g.astype(jnp.float32)
    return y.astype(x.dtype)


def adaln(c, w, b):
    mod = jax.nn.silu(c) @ w + b
    return [m[:, None, :] for m in jnp.split(mod, N_MOD, axis=-1)]


def modulate(x, g, shift, scale):
    return rmsnorm(x, g) * (1 + scale) + shift


def alibi_slopes():
    m = jnp.exp2(-8.0 * jnp.arange(1, N_HEADS + 1, dtype=jnp.float32) / N_HEADS)
    return m.reshape(N_KV_HEADS, GROUP)


def swa_project(h, w_qkv):
    B, L, _ = h.shape
    q, k, v = jnp.split(h @ w_qkv, [N_HEADS * HEAD_DIM, (N_HEADS + N_KV_HEADS) * HEAD_DIM], axis=-1)
    return (q.reshape(B, L, N_KV_HEADS, GROUP, HEAD_DIM),
            k.reshape(B, L, N_KV_HEADS, HEAD_DIM),
            v.reshape(B, L, N_KV_HEADS, HEAD_DIM))


def swa_core(q, k, v, q_pos, k_pos, sinks):
    s = jnp.einsum('bnqkgd,bnskd->bnkgqs', q, k).astype(jnp.float32) * (HEAD_DIM ** -0.5)
    dist = q_pos[:, :, None] - k_pos[:, None, :]
    allowed = (dist >= 0) & (dist < WINDOW) & (k_pos[:, None, :] >= 0)
    s = s - alibi_slopes()[:, :, None, None] * dist[:, None, None].astype(jnp.float32)
    s = jnp.where(allowed[:, None, None], s, -jnp.inf)
    sink = jnp.broadcast_to(sinks.astype(jnp.float32).reshape(N_KV_HEADS, GROUP, 1, 1), s.shape[:-1] + (1,))
    p = jax.nn.softmax(jnp.concatenate([s, sink], axis=-1), axis=-1)[..., :-1]
    return jnp.einsum('bnkgqs,bnskd->bnqkgd', p.astype(v.dtype), v)


def swa_prompt(h, w_qkv, w_o, sinks):
    B, L, _ = h.shape
    q, k, v = swa_project(h, w_qkv)
    nb = L // WINDOW
    qb = q.reshape(B, nb, WINDOW, N_KV_HEADS, GROUP, HEAD_DIM)

    def band(a):
        ap = jnp.pad(a, ((0, 0), (WINDOW, 0), (0, 0), (0, 0))).reshape(B, nb + 1, WINDOW, N_KV_HEADS, HEAD_DIM)
        return jnp.concatenate([ap[:, :-1], ap[:, 1:]], axis=2)

    kp = jnp.arange(-WINDOW, L, dtype=jnp.int32).reshape(nb + 1, WINDOW)
    kpos = jnp.concatenate([kp[:-1], kp[1:]], axis=1)
    qpos = jnp.arange(L, dtype=jnp.int32).reshape(nb, WINDOW)
    o = swa_core(qb, band(k), band(v), qpos, kpos, sinks).reshape(B, L, N_HEADS * HEAD_DIM)
    keep = min(WINDOW, L)
    return o @ w_o, k[:, L - keep:], v[:, L - keep:]


def swa_sample(h, cache_k, cache_v, w_qkv, w_o, sinks):
    B, L, _ = h.shape
    q, k, v = swa_project(h, w_qkv)
    cr = cache_k.shape[1]
    kk = jnp.concatenate([cache_k.astype(k.dtype), k], axis=1)[:, None]
    vv = jnp.concatenate([cache_v.astype(v.dtype), v], axis=1)[:, None]
    qpos = (PAST_LEN + jnp.arange(L, dtype=jnp.int32))[None]
    kpos = jnp.concatenate([PAST_LEN - cr + jnp.arange(cr, dtype=jnp.int32), qpos[0]])[None]
    o = swa_core(q[:, None], kk, vv, qpos, kpos, sinks).reshape(B, L, N_HEADS * HEAD_DIM)
    return o @ w_o, k, v


def hgrn_lower_bound(p, layer):
    cs = jnp.cumsum(jax.nn.softmax(p.astype(jnp.float32), axis=0), axis=0)
    return cs[layer] - cs[0]


def hgrn2_chunkwise(q, k, v, logf, s0, chunk):
    B, L, H, DK = q.shape
    DV = v.shape[-1]
    n = L // chunk
    tri = jnp.tril(jnp.ones((chunk, chunk), dtype=bool))

    def to_chunks(a):
        return jnp.moveaxis(a.reshape(B, n, chunk, H, a.shape[-1]), 1, 0)

    def step(s, inp):
        qc, kc, vc, fc = inp
        b = jnp.cumsum(fc, axis=1)
        o_inter = jnp.einsum('bthk,bhkv->bthv', qc * jnp.exp(b), s)
        rel = jnp.where(tri[None, :, :, None, None], b[:, :, None] - b[:, None, :], -jnp.inf)
        a = jnp.einsum('bthk,bshk,btshk->bhts', qc, kc, jnp.exp(rel))
        o_intra = jnp.einsum('bhts,bshv->bthv', a, vc)
        b_end = b[:, -1]
        s_new = jnp.exp(b_end)[..., None] * s + jnp.einsum('bshk,bshv->bhkv', kc * jnp.exp(b_end[:, None] - b), vc)
        return s_new, o_inter + o_intra

    s_fin, o = lax.scan(step, s0, (to_chunks(q), to_chunks(k), to_chunks(v), to_chunks(logf)))
    return jnp.moveaxis(o, 0, 1).reshape(B, L, H, DV), s_fin


def hgrn2_mixer(h, w_in, lb, norm_g, w_o, s0):
    B, L, _ = h.shape
    fk = HG_HEADS * HG_DK
    qr, fr, ir, gr = jnp.split(h @ w_in, [fk, 2 * fk, 2 * fk + HG_HEADS * HG_DV], axis=-1)
    q = jax.nn.silu(qr.astype(jnp.float32)).reshape(B, L, HG_HEADS, HG_DK)
    fr = fr.astype(jnp.float32).reshape(B, L, HG_HEADS, HG_DK)
    lb = lb.reshape(HG_HEADS, HG_DK)
    logf = jnp.logaddexp(jnp.log(lb), jnp.log1p(-lb) + jax.nn.log_sigmoid(fr))
    k = (1.0 - lb) * jax.nn.sigmoid(-fr)
    v = ir.astype(jnp.float32).reshape(B, L, HG_HEADS, HG_DV)
    chunk = math.gcd(L, HG_CHUNK)
    o, s_fin = hgrn2_chunkwise(q, k, v, logf, s0.astype(jnp.float32), chunk)
    o = o * lax.rsqrt(jnp.mean(o * o, axis=-1, keepdims=True) + EPS) * norm_g.astype(jnp.float32).reshape(HG_HEADS, HG_DV)
    o = (o.reshape(B, L, HG_HEADS * HG_DV) * jax.nn.silu(gr.astype(jnp.float32))).astype(h.dtype)
    return o @ w_o, s_fin


def conv_ffn(h, w_in, conv_w, conv_b, w_out, buf):
    u, g = jnp.split(h @ w_in, 2, axis=-1)
    up = jnp.concatenate([buf.astype(u.dtype), u], axis=1)
    a = lax.conv_general_dilated(up, conv_w[:, None, :].astype(u.dtype), window_strides=(1,), padding='VALID',
                                 dimension_numbers=('NWC', 'WIO', 'NWC'), feature_group_count=D_FF) + conv_b
    y = (jax.nn.gelu(a, approximate=False) * g) @ w_out
    return y, up[:, up.shape[1] - (CONV_W - 1):]


def setup_inputs(seed: int = 0) -> dict:
    key = jax.random.key(seed)
    ks = jax.random.split(key, 24)
    D = D_MODEL

    def nrm(k, shape, scale):
        return jax.random.normal(k, shape, jnp.float32) * scale

    qkv = (N_HEADS + 2 * N_KV_HEADS) * HEAD_DIM
    hg_in = 2 * HG_HEADS * HG_DK + 2 * HG_HEADS * HG_DV
    cache_rows = min(WINDOW, PAST_LEN)
    return {
        'x_prompt': nrm(ks[0], (BATCH, SEQ, D), 1.0),
        'x_sample': nrm(ks[1], (DEC_BATCH, DEC_SEQ, D), 1.0),
        'cache_swa_k': nrm(ks[2], (DEC_BATCH, cache_rows, N_KV_HEADS, HEAD_DIM), 1.0),
        'cache_swa_v': nrm(ks[3], (DEC_BATCH, cache_rows, N_KV_HEADS, HEAD_DIM), 1.0),
        'state_hgrn': nrm(ks[4], (DEC_BATCH, HG_HEADS, HG_DK, HG_DV), 0.5),
        'state_ffn_conv': nrm(ks[5], (DEPTH, DEC_BATCH, CONV_W - 1, D_FF), 1.0),
        'c_prompt': nrm(ks[6], (BATCH, D), 1.0),
        'c_sample': nrm(ks[7], (DEC_BATCH, D), 1.0),
        'norm1_g': 1.0 + nrm(ks[8], (DEPTH, D), 0.02),
        'norm2_g': 1.0 + nrm(ks[9], (DEPTH, D), 0.02),
        'w_ada': nrm(ks[10], (DEPTH, D, N_MOD * D), 0.5 * D ** -0.5),
        'b_ada': nrm(ks[11], (DEPTH, N_MOD * D), 0.02),
        'attn_w_qkv': nrm(ks[12], (D, qkv), D ** -0.5),
        'attn_w_o': nrm(ks[13], (N_HEADS * HEAD_DIM, D), (N_HEADS * HEAD_DIM) ** -0.5),
        'attn_sinks': nrm(ks[14], (N_HEADS,), 1.0),
        'hgrn_w_in': nrm(ks[15], (D, hg_in), D ** -0.5),
        'hgrn_lower_bounds': nrm(ks[16], (DEPTH, HG_HEADS * HG_DK), 0.5),
        'hgrn_norm_g': 1.0 + nrm(ks[17], (HG_HEADS * HG_DV,), 0.02),
        'hgrn_w_o': nrm(ks[18], (HG_HEADS * HG_DV, D), (HG_HEADS * HG_DV) ** -0.5),
        'ffn_w_in': nrm(ks[19], (DEPTH, D, 2 * D_FF), D ** -0.5),
        'ffn_conv_w': nrm(ks[20], (DEPTH, CONV_W, D_FF), CONV_W ** -0.5),
        'ffn_conv_b': nrm(ks[21], (DEPTH, D_FF), 0.01),
        'ffn_w_out': nrm(ks[22], (DEPTH, D_FF, D), D_FF ** -0.5),
        'final_norm_g': 1.0 + nrm(ks[23], (D,), 0.02),
    }


def reference(x_prompt, x_sample, cache_swa_k, cache_swa_v, state_hgrn, state_ffn_conv, c_prompt, c_sample,
              norm1_g, norm2_g, w_ada, b_ada, attn_w_qkv, attn_w_o, attn_sinks,
              hgrn_w_in, hgrn_lower_bounds, hgrn_norm_g, hgrn_w_o,
              ffn_w_in, ffn_conv_w, ffn_conv_b, ffn_w_out, final_norm_g):
    xp, xs = x_prompt, x_sample
    bp_n, bs_n = xp.shape[0], xs.shape[0]
    conv_p, conv_s = [], []
    for l in range(DEPTH):
        sh1p, sc1p, g1p, sh2p, sc2p, g2p = adaln(c_prompt, w_ada[l], b_ada[l])
        sh1s, sc1s, g1s, sh2s, sc2s, g2s = adaln(c_sample, w_ada[l], b_ada[l])
        hp = modulate(xp, norm1_g[l], sh1p, sc1p)
        hs = modulate(xs, norm1_g[l], sh1s, sc1s)
        if l % N_MIXERS == 0:
            mp, swa_k_prompt, swa_v_prompt = swa_prompt(hp, attn_w_qkv, attn_w_o, attn_sinks)
            ms, swa_k_sample, swa_v_sample = swa_sample(hs, cache_swa_k, cache_swa_v, attn_w_qkv, attn_w_o, attn_sinks)
        else:
            lb = hgrn_lower_bound(hgrn_lower_bounds, l)
            s0p = jnp.zeros((bp_n, HG_HEADS, HG_DK, HG_DV), jnp.float32)
            mp, hgrn_state_prompt = hgrn2_mixer(hp, hgrn_w_in, lb, hgrn_norm_g, hgrn_w_o, s0p)
            ms, hgrn_state_sample = hgrn2_mixer(hs, hgrn_w_in, lb, hgrn_norm_g, hgrn_w_o, state_hgrn)
        xp = xp + g1p * mp
        xs = xs + g1s * ms
        hp = modulate(xp, norm2_g[l], sh2p, sc2p)
        hs = modulate(xs, norm2_g[l], sh2s, sc2s)
        fp, bufp = conv_ffn(hp, ffn_w_in[l], ffn_conv_w[l], ffn_conv_b[l], ffn_w_out[l],
                            jnp.zeros((bp_n, CONV_W - 1, D_FF), hp.dtype))
        fs, bufs = conv_ffn(hs, ffn_w_in[l], ffn_conv_w[l], ffn_conv_b[l], ffn_w_out[l], state_ffn_conv[l])
        xp = xp + g2p * fp
        xs = xs + g2s * fs
        conv_p.append(bufp)
        conv_s.append(bufs)
    y_prompt = rmsnorm(xp, final_norm_g)
    y_sample = rmsnorm(xs, final_norm_g)
    ffn_conv_prompt = jnp.stack(conv_p)
    ffn_conv_sample = jnp.stack(conv_s)
    return (y_prompt, y_sample, swa_k_prompt, swa_v_prompt, swa_k_sample, swa_v_sample,
            hgrn_state_prompt, hgrn_state_sample, ffn_conv_prompt, ffn_conv_sample)
```

```python
from concourse.bass_utils import run_bass_kernel_spmd

import contextlib
import numpy as np
import concourse.bass as bass
import concourse.mybir as mybir

F32 = mybir.dt.float32
BF16 = mybir.dt.bfloat16
I32 = mybir.dt.int32
AF = mybir.ActivationFunctionType
ALU = mybir.AluOpType
AX = mybir.AxisListType

ENGS = ["sp", "act", "pool", "dve", "pe"]


class Res:
    __slots__ = ("name", "w", "r")

    def __init__(self, name=""):
        self.name = name
        self.w = None
        self.r = []


class DSem:
    def __init__(self, handle, name):
        self.h = handle
        self.name = name
        self.cnt = 0


class _Rec:
    def __getattr__(self, name):
        return lambda *a, **k: (name, a, k)


_REC = _Rec()


class Builder:
    def __init__(self, nc, n_dsem=12):
        self.nc = nc
        self.es = contextlib.ExitStack()
        self.q = {e: [] for e in ENGS}
        self.cnt = {e: 0 for e in ENGS}
        self.waited = {e: {} for e in ENGS}
        self.esem = {e: self.es.enter_context(nc.semaphore("s_" + e)) for e in ENGS}
        self.dsems = [DSem(self.es.enter_context(nc.semaphore("d%d" % i)), "d%d" % i)
                      for i in range(n_dsem)]
        self.pending = {e: [] for e in ENGS}
        self.n_inst = 0

    def sb(self, name, shape, dt):
        return self.es.enter_context(self.nc.sbuf_tensor(name, list(shape), dt))

    def ps(self, name, shape, dt=F32):
        return self.es.enter_context(self.nc.psum_tensor(name, list(shape), dt))

    def _deps(self, eng, reads, writes):
        need = {}

        def add(t):
            if t is None:
                return
            k, v = t
            if need.get(k, 0) < v:
                need[k] = v
        for r in reads:
            add(r.w)
        for w in writes:
            add(w.w)
            for t in w.r:
                add(t)
        waits = []
        for k, v in need.items():
            if k == "pe" and eng == "pe":
                continue
            if self.waited[eng].get(k, 0) >= v:
                continue
            self.waited[eng][k] = v
            waits.append((k, v))
        return waits

    def _semh(self, k):
        return self.esem[k] if isinstance(k, str) else k.h

    def op(self, eng, fn, reads=(), writes=(), sig=True):
        reads = [r for r in reads if r is not None]
        writes = [w for w in writes if w is not None]
        waits = self._deps(eng, reads, writes)
        for k, v in waits:
            cur = self.cnt[k] if isinstance(k, str) else k.cnt
            assert v <= cur, "forward wait %s %d > %d" % (k, v, cur)
        ticket = None
        if sig:
            self.cnt[eng] += 1
            ticket = (eng, self.cnt[eng])
            pend = self.pending[eng]
            self.pending[eng] = []
            for pr, pw in pend:
                self._commit(pr, pw, ticket)
            self._commit(reads, writes, ticket)
        else:
            t = (eng, self.cnt[eng] + 1)
            self._commit(reads, writes, t)
        self.q[eng].append((waits, fn(_REC), ticket, None))
        self.n_inst += 1
        return ticket

    def _commit(self, reads, writes, ticket):
        for r in reads:
            r.r.append(ticket)
        for w in writes:
            w.w = ticket
            w.r = []

    def dma(self, eng, out, in_, dsem, reads=(), writes=(), **kw):
        reads = [r for r in reads if r is not None]
        writes = [w for w in writes if w is not None]
        waits = self._deps(eng, reads, writes)
        for k, v in waits:
            cur = self.cnt[k] if isinstance(k, str) else k.cnt
            assert v <= cur, "forward wait %s %d > %d" % (k, v, cur)
        dsem.cnt += 16
        ticket = (dsem, dsem.cnt)
        self._commit(reads, writes, ticket)
        kw2 = dict(kw); kw2["out"] = out; kw2["in_"] = in_
        self.q[eng].append((waits, ("dma_start", (), kw2), None, (dsem, 16)))
        self.n_inst += 1
        return ticket

    def wait_all(self, eng, tickets):
        waits = []
        for t in tickets:
            if t is None:
                continue
            k, v = t
            if self.waited[eng].get(k, 0) >= v:
                continue
            self.waited[eng][k] = v
            waits.append((k, v))
        self.q[eng].append((waits, None, None, None))

    def emit(self):
        nc = self.nc
        handles = {"sp": "sync", "act": "scalar", "pool": "gpsimd", "dve": "vector", "pe": "tensor"}
        with nc.Block() as block:
            for eng in ENGS:
                items = self.q[eng]
                if not items:
                    continue

                def body(e, items=items, eng=eng):
                    for waits, fn, ticket, dinc in items:
                        for k, v in waits:
                            e.wait_ge(self._semh(k), v)
                        if fn is None:
                            continue
                        name, a, k = fn
                        ins = getattr(e, name)(*a, **k)
                        if ticket is not None:
                            ins.then_inc(self.esem[eng], 1)
                        if dinc is not None:
                            ins.then_inc(dinc[0].h, dinc[1])
                getattr(block, handles[eng])(body)

    def close(self):
        self.es.close()


D = 2048
KC = 16
DFF = 5632
FC = 44
NQ = 4
FQ = FC // NQ
NP = 1024
NS = 64
EPS = 1e-6


class PsumRot:
    def __init__(self, b, n=8):
        self.tiles = [b.ps("psb%d" % i, [128, 512]) for i in range(n)]
        self.res = [Res("psb%d" % i) for i in range(n)]
        self.i = 0

    def next(self):
        t, r = self.tiles[self.i], self.res[self.i]
        self.i = (self.i + 1) % len(self.tiles)
        return t, r


def ntiles(n, step=512):
    return [(s, min(step, n - s)) for s in range(0, n, step)]


def emit_norm_mod(b, pr, x, r_x, h, r_h, ncol, np_cols, vec, iv_g, iv_sh, iv_sc, mods, im_sh, im_sc,
                  ones, r_const, tmp):
    xsq, r_xsq, rstd, r_rstd, gm, r_gm, gms, r_gms, t32, r_t32 = tmp
    tiles = ntiles(ncol)
    banks = [pr.next() for _ in tiles]
    for k in range(KC):
        i2 = k % 2
        b.op("act", lambda e, k=k, i2=i2: e.activation(xsq[i2][:, 0:ncol], x[:, k, 0:ncol], AF.Square),
             reads=[r_x], writes=[r_xsq[i2]])
        for ti, (s, n) in enumerate(tiles):
            pt, rp = banks[ti]
            b.op("pe", lambda e, pt=pt, s=s, n=n, i2=i2, k=k: e.matmul(
                pt[:, 0:n], ones[:], xsq[i2][:, s:s + n], start=(k == 0), stop=(k == KC - 1)),
                reads=[r_const, r_xsq[i2]], writes=[rp], sig=(k == KC - 1) or ti == len(tiles) - 1)
    for ti, (s, n) in enumerate(tiles):
        pt, rp = banks[ti]
        b.op("act", lambda e, pt=pt, s=s, n=n: e.activation(rstd[:, s:s + n], pt[:, 0:n], AF.Sqrt,
                                                            bias=EPS, scale=1.0 / D),
             reads=[rp], writes=[r_rstd])
    b.op("dve", lambda e: e.reciprocal(rstd[:, 0:ncol], rstd[:, 0:ncol]), reads=[r_rstd], writes=[r_rstd])
    b.op("dve", lambda e: e.scalar_tensor_tensor(gm[:], vec[:, iv_sc, :], 1.0, vec[:, iv_g, :], ALU.add, ALU.mult),
         reads=[r_const], writes=[r_gm])
    ns = ncol - np_cols
    if ns:
        b.op("dve", lambda e: e.scalar_tensor_tensor(
            gms[:], mods[:, im_sc, :, :], 1.0, vec[:, iv_g, :].unsqueeze(2).broadcast_to([128, KC, 16]),
            ALU.add, ALU.mult), reads=[r_const], writes=[r_gms])
    for k in range(KC):
        i2 = k % 2
        b.op("dve", lambda e, k=k, i2=i2: e.scalar_tensor_tensor(
            t32[i2][:, 0:np_cols], x[:, k, 0:np_cols], gm[:, k:k + 1], rstd[:, 0:np_cols], ALU.mult, ALU.mult),
            reads=[r_x, r_gm, r_rstd], writes=[r_t32[i2]])
        if ns:
            b.op("dve", lambda e, k=k, i2=i2: e.tensor_tensor(
                t32[i2][:, np_cols:ncol].rearrange("p (s t) -> p s t", t=4),
                x[:, k, np_cols:ncol].rearrange("p (s t) -> p s t", t=4),
                gms[:, k, :].unsqueeze(2).broadcast_to([128, 16, 4]), ALU.mult),
                reads=[r_x, r_gms], writes=[r_t32[i2]])
            b.op("dve", lambda e, k=k, i2=i2: e.tensor_tensor(
                t32[i2][:, np_cols:ncol], t32[i2][:, np_cols:ncol], rstd[:, np_cols:ncol], ALU.mult),
                reads=[r_t32[i2], r_rstd], writes=[r_t32[i2]])
            b.op("dve", lambda e, k=k, i2=i2: e.tensor_tensor(
                t32[i2][:, np_cols:ncol].rearrange("p (s t) -> p s t", t=4),
                t32[i2][:, np_cols:ncol].rearrange("p (s t) -> p s t", t=4),
                mods[:, im_sh, k, :].unsqueeze(2).broadcast_to([128, 16, 4]), ALU.add),
                reads=[r_t32[i2], r_const], writes=[r_t32[i2]])
            b.op("act", lambda e, k=k, i2=i2: e.activation(h[:, k, np_cols:ncol], t32[i2][:, np_cols:ncol], AF.Copy),
                 reads=[r_t32[i2]], writes=[r_h])
        b.op("act", lambda e, k=k, i2=i2: e.activation(
            h[:, k, 0:np_cols], t32[i2][:, 0:np_cols], AF.Identity, bias=vec[:, iv_sh, k:k + 1], scale=1.0),
            reads=[r_t32[i2], r_const], writes=[r_h])


def build_ffn(last):
    nc = bass.Bass("TRN2", target_bir_lowering=False)
    NCOL = 2 + NP + NS
    UW = 2 + NP + 16 * 6
    AW = UW - 2
    dram = lambda name, shape, kind="ExternalInput": nc.dram_tensor(name, list(shape), F32, kind=kind).ap()
    xT = dram("xT", [128, KC, NCOL])
    vecd = dram("vec", [128, 5, KC])
    modsd = dram("mods", [128, 3, KC, 16])
    w_in = dram("w_in", [FC, 128, 2, KC, 128])
    w_out = dram("w_out", [NQ, KC, 128, FQ, 128])
    convd = dram("convw", [128, FC, 4])
    cstd = dram("cstate", [128, FC, 16, 2])
    flagd = dram("flag", [128, 1])
    xo = dram("xo", [128, KC, NP + NS], "ExternalOutput")
    cbp = dram("cbp", [128, FC, 2], "ExternalOutput")
    cbs = dram("cbs", [128, FC, 16, 2], "ExternalOutput")

    b = Builder(nc, n_dsem=14)
    d = b.dsems
    pr = PsumRot(b)
    x = b.sb("x", [128, KC, NCOL], F32)
    h = b.sb("h", [128, KC, NCOL], BF16)
    act = b.sb("act", [128, FQ, NP + NS], BF16)
    U = [b.sb("U%d" % i, [128, UW], F32) for i in range(2)]
    G = [b.sb("G%d" % i, [128, NP + NS], F32) for i in range(2)]
    A = b.sb("A", [128, UW], F32)
    rstd = b.sb("rstd", [128, NCOL], F32)
    xsq = [b.sb("xsq%d" % i, [128, NCOL], BF16) for i in range(2)]
    t32 = [b.sb("t32%d" % i, [128, NCOL], F32) for i in range(2)]
    win = [b.sb("win%d" % i, [128, 2, KC, 128], BF16) for i in range(2)]
    wout = [b.sb("wout%d" % i, [128, FQ, 128], BF16) for i in range(2)]
    vec = b.sb("vecs", [128, 5, KC], F32)
    mods = b.sb("modss", [128, 3, KC, 16], F32)
    convw = b.sb("convws", [128, FC, 4], F32)
    cst = b.sb("csts", [128, FC, 16, 2], F32)
    cbps = b.sb("cbps", [128, FC, 2], F32)
    cbss = b.sb("cbss", [128, FC, 16, 2], F32)
    flag = b.sb("flags", [128, 1], F32)
    ones = b.sb("ones", [128, 128], BF16)
    gm = b.sb("gm", [128, KC], F32)
    gms = b.sb("gms", [128, KC, 16], F32)
    tmps = b.sb("tmps", [128, NS], F32)

    R = lambda n: Res(n)
    r_x, r_h, r_act, r_A, r_rstd, r_const, r_gm, r_gms, r_cb, r_tmps = [R(n) for n in
        "x h act A rstd const gm gms cb tmps".split()]
    r_U = [R("U0"), R("U1")]; r_G = [R("G0"), R("G1")]
    r_xsq = [R("xsq0"), R("xsq1")]; r_t32 = [R("t0"), R("t1")]
    r_win = [R("win0"), R("win1")]; r_wout = [R("wo0"), R("wo1")]

    b.dma("sp", x[:], xT, d[0], writes=[r_x])
    b.dma("sp", vec[:], vecd, d[1], writes=[r_const])
    b.dma("sp", mods[:], modsd, d[1], writes=[r_const])
    b.dma("sp", convw[:], convd, d[1], writes=[r_const])
    b.dma("sp", cst[:], cstd, d[1], writes=[r_const])
    b.dma("sp", flag[:], flagd, d[1], writes=[r_const])
    b.op("pool", lambda e: e.memset(ones[:], 1.0), writes=[r_const])

    win_sem = [d[2], d[3]]
    wout_sem = [d[4], d[5]]

    def load_win(j):
        b.dma("pool", win[j % 2][:], w_in[j], win_sem[j % 2], writes=[r_win[j % 2]])

    def load_wout(q, i):
        n = q * KC + i
        b.dma("pool", wout[n % 2][:], w_out[q, i], wout_sem[n % 2], writes=[r_wout[n % 2]])

    load_win(0)
    emit_norm_mod(b, pr, x, r_x, h, r_h, NCOL, 2 + NP, vec, 0, 1, 2, mods, 0, 1, ones, r_const,
                  (xsq, r_xsq, rstd, r_rstd, gm, r_gm, gms, r_gms, t32, r_t32))

    tiles = ntiles(NCOL)
    for q in range(NQ):
        for jj in range(FQ):
            j = q * FQ + jj
            if j + 1 < FC:
                load_win(j + 1)
            w = win[j % 2]; rw = r_win[j % 2]
            Uj, rU = U[j % 2], r_U[j % 2]
            Gj, rG = G[j % 2], r_G[j % 2]
            for which in range(2):
                for (s, n) in tiles:
                    pt, rp = pr.next()
                    for k in range(KC):
                        b.op("pe", lambda e, pt=pt, w=w, which=which, k=k, s=s, n=n: e.matmul(
                            pt[:, 0:n], w[:, which, k, :], h[:, k, s:s + n], start=(k == 0), stop=(k == KC - 1)),
                            reads=[rw, r_h], writes=[rp], sig=(k == KC - 1))
                    if which == 0:
                        if s + n <= 2 + NP:
                            b.op("act", lambda e, pt=pt, s=s, n=n, Uj=Uj: e.activation(Uj[:, s:s + n], pt[:, 0:n], AF.Copy),
                                 reads=[rp], writes=[rU])
                        else:
                            npart = 2 + NP - s
                            b.op("act", lambda e, pt=pt, s=s, npart=npart, Uj=Uj: e.activation(
                                Uj[:, s:s + npart], pt[:, 0:npart], AF.Copy), reads=[rp], writes=[rU])
                            b.op("act", lambda e, pt=pt, npart=npart, Uj=Uj: e.activation(
                                Uj[:, 2 + NP:UW].rearrange("p (s c) -> p s c", c=6)[:, :, 2:6],
                                pt[:, npart:npart + NS].rearrange("p (s t) -> p s t", t=4), AF.Copy),
                                reads=[rp], writes=[rU])
                    else:
                        if s == 0:
                            b.op("act", lambda e, pt=pt, n=n, Gj=Gj: e.activation(Gj[:, 0:n - 2], pt[:, 2:n], AF.Copy),
                                 reads=[rp], writes=[rG])
                        else:
                            b.op("act", lambda e, pt=pt, s=s, n=n, Gj=Gj: e.activation(Gj[:, s - 2:s - 2 + n], pt[:, 0:n], AF.Copy),
                                 reads=[rp], writes=[rG])
            b.op("pool", lambda e, Uj=Uj: e.tensor_scalar(Uj[:, 0:2], Uj[:, 0:2], flag[:, 0:1], None, ALU.mult),
                 reads=[rU, r_const], writes=[rU])
            b.op("pool", lambda e, Uj=Uj, j=j: e.tensor_copy(
                Uj[:, 2 + NP:UW].rearrange("p (s c) -> p s c", c=6)[:, :, 0:2], cst[:, j, :, :]),
                reads=[r_const], writes=[rU])
            b.op("dve", lambda e, Uj=Uj, j=j: e.tensor_scalar(A[:, 0:AW], Uj[:, 2:UW], convw[:, j, 2:3], None, ALU.mult),
                 reads=[rU, r_const], writes=[r_A])
            b.op("dve", lambda e, Uj=Uj, j=j: e.scalar_tensor_tensor(A[:, 0:AW], Uj[:, 1:UW - 1], convw[:, j, 1:2], A[:, 0:AW],
                                                                 ALU.mult, ALU.add), reads=[rU, r_A, r_const], writes=[r_A])
            b.op("dve", lambda e, Uj=Uj, j=j: e.scalar_tensor_tensor(A[:, 0:AW], Uj[:, 0:AW], convw[:, j, 0:1], A[:, 0:AW],
                                                                 ALU.mult, ALU.add), reads=[rU, r_A, r_const], writes=[r_A])
            b.op("act", lambda e, j=j: e.activation(A[:, 0:AW], A[:, 0:AW], AF.Gelu, bias=convw[:, j, 3:4], scale=1.0),
                 reads=[r_A, r_const], writes=[r_A])
            b.op("dve", lambda e, jj=jj, Gj=Gj: e.tensor_tensor(act[:, jj, 0:NP], A[:, 0:NP], Gj[:, 0:NP], ALU.mult),
                 reads=[r_A, rG], writes=[r_act])
            b.op("dve", lambda e, jj=jj, Gj=Gj: e.tensor_tensor(
                act[:, jj, NP:NP + NS].rearrange("p (s t) -> p s t", t=4),
                A[:, NP + 2:UW].rearrange("p (s c) -> p s c", c=6)[:, :, 0:4],
                Gj[:, NP:NP + NS].rearrange("p (s t) -> p s t", t=4), ALU.mult),
                reads=[r_A, rG], writes=[r_act])
            b.op("pool", lambda e, Uj=Uj, j=j: e.tensor_copy(cbps[:, j, :], Uj[:, NP:NP + 2]), reads=[rU], writes=[r_cb])
            b.op("pool", lambda e, Uj=Uj, j=j: e.tensor_copy(
                cbss[:, j, :, :], Uj[:, 2 + NP:UW].rearrange("p (s c) -> p s c", c=6)[:, :, 4:6]), reads=[rU], writes=[r_cb])
        load_wout(q, 0)
        for i in range(KC):
            if i + 1 < KC:
                load_wout(q, i + 1)
            n_ = q * KC + i
            w = wout[n_ % 2]; rw = r_wout[n_ % 2]
            for (s, n) in ntiles(NP + NS):
                pt, rp = pr.next()
                for jj in range(FQ):
                    b.op("pe", lambda e, pt=pt, w=w, jj=jj, s=s, n=n: e.matmul(
                        pt[:, 0:n], w[:, jj, :], act[:, jj, s:s + n], start=(jj == 0), stop=(jj == FQ - 1)),
                        reads=[rw, r_act], writes=[rp], sig=(jj == FQ - 1))
                if s + n <= NP:
                    b.op("dve", lambda e, pt=pt, i=i, s=s, n=n: e.scalar_tensor_tensor(
                        x[:, i, 2 + s:2 + s + n], pt[:, 0:n], vec[:, 3, i:i + 1], x[:, i, 2 + s:2 + s + n], ALU.mult, ALU.add),
                        reads=[rp, r_const, r_x], writes=[r_x])
                else:
                    assert s == NP and n == NS
                    b.op("dve", lambda e, pt=pt, i=i: e.tensor_tensor(
                        tmps[:].rearrange("p (s t) -> p s t", t=4), pt[:, 0:NS].rearrange("p (s t) -> p s t", t=4),
                        mods[:, 2, i, :].unsqueeze(2).broadcast_to([128, 16, 4]), ALU.mult),
                        reads=[rp, r_const], writes=[r_tmps])
                    b.op("dve", lambda e, i=i: e.tensor_tensor(x[:, i, 2 + NP:NCOL], x[:, i, 2 + NP:NCOL], tmps[:], ALU.add),
                         reads=[r_tmps, r_x], writes=[r_x])
    outs = []
    if last:
        tl = ntiles(NCOL)
        banks = [pr.next() for _ in tl]
        for k in range(KC):
            i2 = k % 2
            b.op("act", lambda e, k=k, i2=i2: e.activation(xsq[i2][:, 0:NCOL], x[:, k, 0:NCOL], AF.Square),
                 reads=[r_x], writes=[r_xsq[i2]])
            for ti, (s, n) in enumerate(tl):
                pt, rp = banks[ti]
                b.op("pe", lambda e, pt=pt, s=s, n=n, i2=i2, k=k: e.matmul(
                    pt[:, 0:n], ones[:], xsq[i2][:, s:s + n], start=(k == 0), stop=(k == KC - 1)),
                    reads=[r_const, r_xsq[i2]], writes=[rp], sig=True)
        for ti, (s, n) in enumerate(tl):
            pt, rp = banks[ti]
            b.op("act", lambda e, pt=pt, s=s, n=n: e.activation(rstd[:, s:s + n], pt[:, 0:n], AF.Sqrt, bias=EPS, scale=1.0 / D),
                 reads=[rp], writes=[r_rstd])
        b.op("dve", lambda e: e.reciprocal(rstd[:, 0:NCOL], rstd[:, 0:NCOL]), reads=[r_rstd], writes=[r_rstd])
        for k in range(KC):
            b.op("dve", lambda e, k=k: e.scalar_tensor_tensor(
                x[:, k, :], x[:, k, :], vec[:, 4, k:k + 1], rstd[:, 0:NCOL], ALU.mult, ALU.mult),
                reads=[r_x, r_rstd, r_const], writes=[r_x])
    outs.append(b.dma("sp", xo, x[:, :, 2:NCOL], d[6], reads=[r_x]))
    outs.append(b.dma("sp", cbp, cbps[:], d[7], reads=[r_cb]))
    outs.append(b.dma("sp", cbs, cbss[:], d[8], reads=[r_cb]))
    b.wait_all("sp", outs)
    b.emit()
    b.close()
    return nc


NKV = 8
SCALE = 64 ** -0.5
NEG = -30000.0


def alibi_slope(h):
    return float(2.0 ** (-8.0 * (h + 1) / 32))


def build_attn2():
    nc = bass.Bass("TRN2", target_bir_lowering=False)
    NH = 128
    NCOL = NH + NP + NS
    NQC = NP + NS
    NB = 8
    dram = lambda name, shape, kind="ExternalInput": nc.dram_tensor(name, list(shape), F32, kind=kind).ap()
    xT = dram("xT", [128, KC, NCOL])
    vecd = dram("vec", [128, 4, KC])
    modsd = dram("mods", [128, 3, KC, 16])
    wqkv = dram("wqkv", [NKV, 128, 4, KC, 128])
    wo = dram("wo", [KC, 128, KC, 128])
    kcT = dram("kcT", [NKV, 64, 16, 128])
    vc = dram("vc", [NKV, 128, 16, 64])
    ndd = dram("nd", [128, 2, 128]); mkd = dram("mk", [128, 2, 128])
    ndcd = dram("ndc", [128, 4]); mkcd = dram("mkc", [128, 4])
    ndnd = dram("ndn", [64, 64]); mknd = dram("mkn", [64, 64])
    sinkd = dram("sinks", [128, 32])
    hbd = dram("hb", [128, 1])
    xo = dram("xo", [128, KC, NQC], "ExternalOutput")
    kout = dram("kout", [128, NKV // 2, 192], "ExternalOutput")
    vout = dram("vout", [128, 2, NKV, 64], "ExternalOutput")

    b = Builder(nc, n_dsem=16)
    d = b.dsems
    pr = PsumRot(b)
    xbuf = b.sb("xbuf", [128, KC, NCOL], F32)
    x = xbuf
    OT = xbuf[:].rearrange("p k n -> p (k n)").bitcast(BF16)[:, 0:KC * NQC].rearrange("p (k n) -> p k n", k=KC)
    h = b.sb("h", [128, KC, NCOL], BF16)
    rstd = b.sb("rstd", [128, NCOL], F32)
    xsq0 = b.sb("xsq0", [128, NCOL], BF16); xsq = [xsq0, xsq0]
    t32a = b.sb("t32a", [128, NCOL], F32); t32 = [t32a, t32a]
    NWB = 4
    wring = [b.sb("wr%d" % i, [128, KC, 128], BF16) for i in range(NWB)]
    Qgs = [b.sb("Qg%d" % i, [128, 2, NQC], BF16) for i in range(2)]
    Klos = [b.sb("Klo%d" % i, [128, NCOL], BF16) for i in range(2)]
    Khis = [b.sb("Khi%d" % i, [128, NCOL], BF16) for i in range(2)]
    Vds = [b.sb("Vd%d" % i, [128, 10, 64], BF16) for i in range(2)]
    Kclo = b.sb("Kclo", [128, 16, 128], BF16); Kchi = b.sb("Kchi", [128, 16, 128], BF16)
    Vcd = b.sb("Vcd", [128, 16, 64], BF16)
    sc = [b.sb("sc%d" % i, [128, 512], F32) for i in range(2)]
    P = [b.sb("P%d" % i, [128, 512], BF16) for i in range(4)]
    rden = b.sb("rden", [128, 512], F32)
    biasg = b.sb("biasg", [128, 4, 2, 128], F32)
    biasc = b.sb("biasc", [128, 4, 4], F32)
    biasn = b.sb("biasn", [64, 4, 64], F32)
    Pc = b.sb("Pc", [128, 16, 16], BF16)
    Pn = b.sb("Pn", [64, 16, 4, 4], BF16)
    scc = b.sb("scc", [128, 16, 16], F32)
    scn = b.sb("scn", [64, 4, 64], F32)
    vec = b.sb("vecs", [128, 4, KC], F32)
    mods = b.sb("modss", [128, 3, KC, 16], F32)
    nd = b.sb("nds", [128, 2, 128], F32); mk = b.sb("mks", [128, 2, 128], F32)
    ndc = b.sb("ndcs", [128, 4], F32); mkc = b.sb("mkcs", [128, 4], F32)
    ndn = b.sb("ndns", [64, 64], F32); mkn = b.sb("mkns", [64, 64], F32)
    esink = b.sb("esink", [128, 32], F32)
    hb = b.sb("hbs", [128, 1], F32)
    esg = b.sb("esg", [128, 4], F32)
    ones = b.sb("ones", [128, 128], BF16)
    gm = b.sb("gm", [128, KC], F32)
    gms = b.sb("gms", [128, KC, 16], F32)
    koutS = b.sb("koutS", [128, NKV // 2, 192], F32)
    voutS = b.sb("voutS", [128, 2, NKV, 64], F32)
    xr = [t32a[:, 0:NQC], rstd[:, 0:NQC]]
    tmps = b.sb("tmps", [128, NS], F32)

    R = Res
    r_x, r_h, r_rstd, r_const, r_gm, r_gms, r_Q, r_K, r_V, r_Kc, r_Vc, r_rden, r_bias, r_OT = [R(n) for n in
        "x h rstd const gm gms Q K V Kc Vc rden bias OT".split()]
    r_Pc, r_Pn, r_scc, r_scn, r_ko, r_vo, r_tmps = [R(n) for n in "Pc Pn scc scn ko vo tmps".split()]
    r_xsq0 = R("xsq"); r_xsq = [r_xsq0, r_xsq0]; r_t32a = R("t32"); r_t32 = [r_t32a, r_t32a]
    r_wr = [R("wr%d" % i) for i in range(NWB)]
    r_sc = [R("a"), R("b")]; r_P = [R("a") for _ in range(4)]
    r_xr = [r_t32a, r_rstd]
    r_OT = r_x

    b.dma("sp", x[:], xT, d[0], writes=[r_x])
    for i, (dst, src) in enumerate([(vec, vecd), (mods, modsd), (nd, ndd), (mk, mkd), (ndc, ndcd), (mkc, mkcd),
                                    (ndn, ndnd), (mkn, mknd), (esink, sinkd), (hb, hbd)]):
        b.dma("sp", dst[:], src, d[1], writes=[r_const])
    b.op("pool", lambda e: e.memset(ones[:], 1.0), writes=[r_const])
    b.op("pool", lambda e: e.memset(voutS[:], 0.0), writes=[r_vo])
    r_Qs = [Res("Q0"), Res("Q1")]; r_Ks = [Res("K0"), Res("K1")]; r_Vs = [Res("V0"), Res("V1")]
    for i_ in range(2):
        b.op("pool", lambda e, i_=i_: e.memset(Klos[i_][:], 0.0), writes=[r_Ks[i_]])
        b.op("pool", lambda e, i_=i_: e.memset(Khis[i_][:], 0.0), writes=[r_Ks[i_]])
    b.op("pool", lambda e: e.memset(Kclo[:], 0.0), writes=[r_Kc])
    b.op("pool", lambda e: e.memset(Kchi[:], 0.0), writes=[r_Kc])
    b.op("act", lambda e: e.activation(esink[:], esink[:], AF.Exp), reads=[r_const], writes=[r_const])

    wsem = [d[2], d[3], d[8], d[9]]
    NWT = NKV * 4 + KC

    def load_w(n):
        if n >= NWT:
            return
        src = wqkv[n // 4][:, n % 4, :, :] if n < NKV * 4 else wo[n - NKV * 4]
        b.dma("pool", wring[n % NWB][:], src, wsem[n % NWB], writes=[r_wr[n % NWB]])

    for n in range(4):
        load_w(n)
    emit_norm_mod(b, pr, x, r_x, h, r_h, NCOL, NH + NP, vec, 0, 1, 2, mods, 0, 1, ones, r_const,
                  (xsq, r_xsq, rstd, r_rstd, gm, r_gm, gms, r_gms, t32, r_t32))

    def proj(g):
        Qg = Qgs[g % 2]; Klo = Klos[g % 2]; Khi = Khis[g % 2]; Vd = Vds[g % 2]
        r_Q = r_Qs[g % 2]; r_K = r_Ks[g % 2]; r_V = r_Vs[g % 2]
        for which in range(2):
            wn = g * 4 + which; w = wring[wn % NWB]; rw = r_wr[wn % NWB]
            for (s, n) in ntiles(NQC):
                yield
                pt, rp = pr.next()
                for k in range(KC):
                    b.op("pe", lambda e, pt=pt, w=w, which=which, k=k, s=s, n=n: e.matmul(
                        pt[:, 0:n], w[:, k, :], h[:, k, NH + s:NH + s + n], start=(k == 0), stop=(k == KC - 1)),
                        reads=[rw, r_h], writes=[rp], sig=(k == KC - 1))
                b.op("act", lambda e, pt=pt, which=which, s=s, n=n: e.activation(Qg[:, which, s:s + n], pt[:, 0:n], AF.Copy),
                     reads=[rp], writes=[r_Q])
            load_w(wn + 4)
        wn = g * 4 + 2; w = wring[wn % NWB]; rw = r_wr[wn % NWB]
        for (s, n) in ntiles(NCOL):
            yield
            pt, rp = pr.next()
            for k in range(KC):
                b.op("pe", lambda e, pt=pt, w=w, k=k, s=s, n=n: e.matmul(
                    pt[:, 0:n], w[:, k, :], h[:, k, s:s + n], start=(k == 0), stop=(k == KC - 1)),
                    reads=[rw, r_h], writes=[rp], sig=(k == KC - 1))
            b.op("act", lambda e, pt=pt, s=s, n=n: e.activation(Klo[0:64, s:s + n], pt[0:64, 0:n], AF.Copy),
                 reads=[rp], writes=[r_K])
            b.op("act", lambda e, pt=pt, s=s, n=n: e.activation(Khi[64:128, s:s + n], pt[64:128, 0:n], AF.Copy),
                 reads=[rp], writes=[r_K])
            if s == 1024:
                lo = 64 * (g % 2)
                if lo == 0:
                    b.op("act", lambda e, pt=pt, g=g, lo=lo: e.activation(koutS[lo:lo + 64, g // 2, :], pt[lo:lo + 64, 0:192], AF.Copy),
                         reads=[rp], writes=[r_ko])
                else:
                    b.op("act", lambda e, pt=pt, g=g, lo=lo: e.activation(koutS[lo:lo + 64, g // 2, :], pt[lo:lo + 64, 0:192], AF.Copy),
                         reads=[rp], writes=[r_ko])
        load_w(wn + 4)
        wn = g * 4 + 3; w = wring[wn % NWB]; rw = r_wr[wn % NWB]
        for blk in range(10):
            m = 128 if blk < 9 else NS
            c0 = blk * 128
            yield
            pt, rp = pr.next()
            for k in range(KC):
                b.op("pe", lambda e, pt=pt, w=w, k=k, c0=c0, m=m: e.matmul(
                    pt[0:m, 0:64], h[:, k, c0:c0 + m], w[:, k, 0:64], start=(k == 0), stop=(k == KC - 1)),
                    reads=[rw, r_h], writes=[rp], sig=(k == KC - 1))
            b.op("dve", lambda e, pt=pt, blk=blk, m=m: e.tensor_copy(Vd[0:m, blk, :], pt[0:m, 0:64]),
                 reads=[rp], writes=[r_V])
            if blk >= 8:
                b.op("dve", lambda e, pt=pt, blk=blk, m=m, g=g: e.tensor_copy(voutS[0:m, blk - 8, g, :], pt[0:m, 0:64]),
                     reads=[rp], writes=[r_vo])
        load_w(wn + 4)
        yield

    def attn(g):
        Qg = Qgs[g % 2]; Klo = Klos[g % 2]; Khi = Khis[g % 2]; Vd = Vds[g % 2]
        r_Q = r_Qs[g % 2]; r_K = r_Ks[g % 2]; r_V = r_Vs[g % 2]
        b.dma("pool", Kclo[0:64, :, :], kcT[g], d[4], writes=[r_Kc])
        b.dma("pool", Kchi[64:128, :, :], kcT[g], d[5], writes=[r_Kc])
        b.dma("pool", Vcd[:], vc[g], d[6], writes=[r_Vc])
        b.op("dve", lambda e, g=g: e.tensor_copy(esg[:], esink[:, 4 * g:4 * g + 4]), reads=[r_const], writes=[r_bias])
        for hq in range(4):
            sl = alibi_slope(4 * g + hq)
            b.op("dve", lambda e, hq=hq, sl=sl: e.scalar_tensor_tensor(biasg[:, hq, :, :], nd[:], sl, mk[:], ALU.mult, ALU.add),
                 reads=[r_const], writes=[r_bias])
            b.op("dve", lambda e, hq=hq, sl=sl: e.scalar_tensor_tensor(biasc[:, hq, :], ndc[:], sl, mkc[:], ALU.mult, ALU.add),
                 reads=[r_const], writes=[r_bias])
            b.op("dve", lambda e, hq=hq, sl=sl: e.scalar_tensor_tensor(biasn[:, hq, :], ndn[:], sl, mkn[:], ALU.mult, ALU.add),
                 reads=[r_const], writes=[r_bias])
        for i in range(1, NB + 1):
            yield
            qc = (i - 1) * 128
            ptd, rpd = pr.next()
            ptv, rpv = pr.next()
            Pp = []
            for pair in range(2):
                pts, rps = pr.next()
                for hh in range(2):
                    hq = pair * 2 + hh
                    Kx = Klo if hh == 0 else Khi
                    for j in range(2):
                        kc0 = (i - 1 + j) * 128
                        b.op("pe", lambda e, pts=pts, hh=hh, j=j, Kx=Kx, kc0=kc0, pair=pair, qc=qc: e.matmul(
                            pts[:, (hh * 2 + j) * 128:(hh * 2 + j + 1) * 128], Kx[:, kc0:kc0 + 128], Qg[:, pair, qc:qc + 128],
                            start=True, stop=True), reads=[r_K, r_Q], writes=[rps], sig=(hh == 1 and j == 1))
                si = pair
                b.op("dve", lambda e, pts=pts, si=si, pair=pair: e.scalar_tensor_tensor(
                    sc[si][:], pts[:, 0:512], SCALE, biasg[:, pair * 2:pair * 2 + 2, :, :].rearrange("p a b c -> p (a b c)"),
                    ALU.mult, ALU.add), reads=[rps, r_bias], writes=[r_sc[si]])
                if i == 1:
                    b.op("dve", lambda e, si=si: e.tensor_scalar(
                        sc[si][:].rearrange("p (a b c) -> p a b c", a=2, b=2)[:, :, 0, :],
                        sc[si][:].rearrange("p (a b c) -> p a b c", a=2, b=2)[:, :, 0, :], hb[:, 0:1], None, ALU.add),
                        reads=[r_sc[si], r_const], writes=[r_sc[si]])
                pi = (i % 2) * 2 + pair
                b.op("act", lambda e, pi=pi, si=si: e.activation(P[pi][:], sc[si][:], AF.Exp),
                     reads=[r_sc[si]], writes=[r_P[pi]])
                Pp.append(pi)
            for pair in range(2):
                pi = Pp[pair]
                for hh in range(2):
                    hq = pair * 2 + hh
                    for j in range(2):
                        b.op("pe", lambda e, pi=pi, hh=hh, j=j, hq=hq: e.matmul(
                            ptd[:, hq * 128:(hq + 1) * 128], ones[:], P[pi][:, (hh * 2 + j) * 128:(hh * 2 + j + 1) * 128],
                            start=(j == 0), stop=(j == 1)), reads=[r_P[pi], r_const], writes=[rpd],
                            sig=(pair == 1 and hh == 1 and j == 1))
            for pair in range(2):
                pi = Pp[pair]
                for hh in range(2):
                    hq = pair * 2 + hh
                    for j in range(2):
                        blk = i - 1 + j
                        b.op("pe", lambda e, pi=pi, hh=hh, j=j, hq=hq, blk=blk: e.matmul(
                            ptv[64 * hh:64 * hh + 64, hq * 128:(hq + 1) * 128], Vd[:, blk, :], P[pi][:, (hh * 2 + j) * 128:(hh * 2 + j + 1) * 128],
                            start=(j == 0), stop=(j == 1)), reads=[r_P[pi], r_V], writes=[rpv],
                            sig=(pair == 1 and hh == 1 and j == 1))
            b.op("dve", lambda e, ptd=ptd, g=g: e.tensor_tensor(
                rden[:].rearrange("p (a q) -> p a q", a=4), ptd[:, 0:512].rearrange("p (a q) -> p a q", a=4),
                esg[:].unsqueeze(2).broadcast_to([128, 4, 128]), ALU.add),
                reads=[rpd, r_bias], writes=[r_rden])
            b.op("act", lambda e: e.activation(rden[:], rden[:], AF.Ln), reads=[r_rden], writes=[r_rden])
            b.op("act", lambda e: e.activation(rden[:], rden[:], AF.Exp, scale=-1.0), reads=[r_rden], writes=[r_rden])
            for hq in range(4):
                lo = 0 if hq % 2 == 0 else 64
                ch = (2 * g + hq // 2)
                b.op("dve", lambda e, ptv=ptv, hq=hq, lo=lo, ch=ch, qc=qc: e.tensor_tensor(
                    OT[lo:lo + 64, ch, qc:qc + 128], ptv[lo:lo + 64, hq * 128:(hq + 1) * 128],
                    rden[lo:lo + 64, hq * 128:(hq + 1) * 128], ALU.mult),
                    reads=[rpv, r_rden, r_x], writes=[r_OT])
        yield
        ptc, rpc = pr.next()
        ptn, rpn = pr.next()
        for sq in range(16):
            for hq in range(4):
                Kx = Kclo if hq % 2 == 0 else Kchi
                b.op("pe", lambda e, sq=sq, hq=hq, Kx=Kx: e.matmul(
                    ptc[:, sq * 16 + hq * 4:sq * 16 + hq * 4 + 4], Kx[:, sq, :], Qg[:, hq // 2, NP + sq * 4:NP + sq * 4 + 4],
                    start=True, stop=True), reads=[r_Kc, r_Q], writes=[rpc], sig=(sq == 15 and hq == 3))
        for hq in range(4):
            Kx = Klo if hq % 2 == 0 else Khi
            b.op("pe", lambda e, hq=hq, Kx=Kx: e.matmul(
                ptn[0:64, hq * 64:(hq + 1) * 64], Kx[:, NH + NP:NCOL], Qg[:, hq // 2, NP:NQC],
                start=True, stop=True), reads=[r_K, r_Q], writes=[rpn], sig=(hq == 3))
        b.op("dve", lambda e: e.scalar_tensor_tensor(
            scc[:].rearrange("p s (a t) -> p s a t", a=4), ptc[:, 0:256].rearrange("p (s a t) -> p s a t", s=16, a=4),
            SCALE, biasc[:].unsqueeze(1).broadcast_to([128, 16, 4, 4]), ALU.mult, ALU.add),
            reads=[rpc, r_bias], writes=[r_scc])
        b.op("act", lambda e: e.activation(Pc[:], scc[:], AF.Exp), reads=[r_scc], writes=[r_Pc])
        b.op("dve", lambda e: e.scalar_tensor_tensor(
            scn[:].rearrange("p a n -> p (a n)"), ptn[0:64, 0:256], SCALE, biasn[:].rearrange("p a n -> p (a n)"),
            ALU.mult, ALU.add), reads=[rpn, r_bias], writes=[r_scn])
        b.op("act", lambda e: e.activation(
            Pn[:].rearrange("p s a t -> p a s t"), scn[:].rearrange("p a (s t) -> p a s t", t=4), AF.Exp),
            reads=[r_scn], writes=[r_Pn])
        ptd, rpd = pr.next()
        ptv, rpv = pr.next()
        b.op("pe", lambda e: e.matmul(ptd[:, 0:256], ones[:], Pc[:].rearrange("p s c -> p (s c)"), start=True, stop=False),
             reads=[r_Pc, r_const], writes=[rpd], sig=False)
        b.op("pe", lambda e: e.matmul(ptd[:, 0:256], ones[0:64, :], Pn[:].rearrange("p s a t -> p (s a t)"), start=False, stop=True),
             reads=[r_Pn, r_const], writes=[rpd])
        for sq in range(16):
            for lo in (0, 64):
                b.op("pe", lambda e, sq=sq, lo=lo: e.matmul(ptv[lo:lo + 64, sq * 16:(sq + 1) * 16], Vcd[:, sq, :], Pc[:, sq, :], start=True, stop=False),
                     reads=[r_Pc, r_Vc], writes=[rpv], sig=False)
                b.op("pe", lambda e, sq=sq, lo=lo: e.matmul(ptv[lo:lo + 64, sq * 16:(sq + 1) * 16], Vd[0:64, 9, :],
                                                     Pn[:, sq, :, :].rearrange("p a t -> p (a t)"), start=False, stop=True),
                     reads=[r_Pn, r_V], writes=[rpv], sig=(sq == 15 and lo == 64))
        b.op("dve", lambda e, g=g: e.tensor_tensor(
            rden[:, 0:256].rearrange("p (s a t) -> p s a t", s=16, a=4), ptd[:, 0:256].rearrange("p (s a t) -> p s a t", s=16, a=4),
            esg[:].unsqueeze(1).unsqueeze(3).broadcast_to([128, 16, 4, 4]), ALU.add),
            reads=[rpd, r_bias], writes=[r_rden])
        b.op("act", lambda e: e.activation(rden[:, 0:256], rden[:, 0:256], AF.Ln), reads=[r_rden], writes=[r_rden])
        b.op("act", lambda e: e.activation(rden[:, 0:256], rden[:, 0:256], AF.Exp, scale=-1.0), reads=[r_rden], writes=[r_rden])
        for hq in range(4):
            lo = 0 if hq % 2 == 0 else 64
            ch = (2 * g + hq // 2)
            b.op("dve", lambda e, hq=hq, lo=lo, ch=ch: e.tensor_tensor(
                OT[lo:lo + 64, ch, NP:NQC].rearrange("p (s t) -> p s t", t=4),
                ptv[lo:lo + 64, 0:256].rearrange("p (s a t) -> p s a t", s=16, a=4)[:, :, hq, :],
                rden[lo:lo + 64, 0:256].rearrange("p (s a t) -> p s a t", s=16, a=4)[:, :, hq, :], ALU.mult),
                reads=[rpv, r_rden, r_x], writes=[r_OT])

        yield

    def drain(gen):
        for _ in gen:
            pass

    drain(proj(0))
    for g in range(NKV):
        crit = attn(g)
        fill = proj(g + 1) if g + 1 < NKV else iter(())
        done_c = done_f = False
        while not (done_c and done_f):
            if not done_c:
                try:
                    next(crit)
                except StopIteration:
                    done_c = True
            for _ in range(2):
                if not done_f:
                    try:
                        next(fill)
                    except StopIteration:
                        done_f = True
    xsem = [d[11], d[12]]
    osem = [d[13], d[14]]
    outs = []
    for i in range(KC):
        wn = NKV * 4 + i; w = wring[wn % NWB]; rw = r_wr[wn % NWB]
        xi = xr[i % 2]; rxi = r_xr[i % 2]
        b.dma("sp", xi, xT[:, i, NH:NCOL], xsem[i % 2], writes=[rxi])
        for (s, n) in ntiles(NQC):
            pt, rp = pr.next()
            for k in range(KC):
                b.op("pe", lambda e, pt=pt, w=w, k=k, s=s, n=n: e.matmul(
                    pt[:, 0:n], w[:, k, :], OT[:, k, s:s + n], start=(k == 0), stop=(k == KC - 1)),
                    reads=[rw, r_OT], writes=[rp], sig=(k == KC - 1))
            if s + n <= NP:
                b.op("dve", lambda e, pt=pt, i=i, s=s, n=n, xi=xi: e.scalar_tensor_tensor(
                    xi[:, s:s + n], pt[:, 0:n], vec[:, 3, i:i + 1], xi[:, s:s + n], ALU.mult, ALU.add),
                    reads=[rp, r_const, rxi], writes=[rxi])
            else:
                b.op("dve", lambda e, pt=pt, i=i: e.tensor_tensor(
                    tmps[:].rearrange("p (s t) -> p s t", t=4), pt[:, 0:NS].rearrange("p (s t) -> p s t", t=4),
                    mods[:, 2, i, :].unsqueeze(2).broadcast_to([128, 16, 4]), ALU.mult),
                    reads=[rp, r_const], writes=[r_tmps])
                b.op("dve", lambda e, xi=xi: e.tensor_tensor(xi[:, NP:NQC], xi[:, NP:NQC], tmps[:], ALU.add),
                     reads=[r_tmps, rxi], writes=[rxi])
        outs.append(b.dma("sp", xo[:, i, :], xi, osem[i % 2], reads=[rxi]))
        load_w(wn + 4)
    outs.append(b.dma("sp", kout, koutS[:], d[15], reads=[r_ko]))
    outs.append(b.dma("sp", vout, voutS[:], d[7], reads=[r_vo]))
    b.wait_all("sp", outs)
    b.emit()
    b.close()
    return nc


def build_adaln():
    nc = bass.Bass("TRN2", target_bir_lowering=False)
    NSEQ = 129
    NCH = 24
    dram = lambda name, shape, kind="ExternalInput": nc.dram_tensor(name, list(shape), F32, kind=kind).ap()
    cT = dram("cT", [128, KC, NSEQ])
    wada = dram("wada", [NCH, 128, KC, 128])
    bada = dram("bada", [128, NCH])
    modT = dram("modT", [128, NCH, NSEQ], "ExternalOutput")
    b = Builder(nc, n_dsem=8)
    d = b.dsems
    pr = PsumRot(b)
    cs = b.sb("cs", [128, KC, NSEQ], F32)
    sc = b.sb("sc", [128, KC, NSEQ], BF16)
    bs = b.sb("bs", [128, NCH], F32)
    outT = b.sb("outT", [128, NCH, NSEQ], F32)
    NWB = 4
    wr = [b.sb("wr%d" % i, [128, KC, 128], BF16) for i in range(NWB)]
    r_c, r_sc, r_b, r_out = Res(), Res(), Res(), Res()
    r_wr = [Res() for _ in range(NWB)]
    b.dma("sp", cs[:], cT, d[0], writes=[r_c])
    b.dma("sp", bs[:], bada, d[1], writes=[r_b])

    def load_w(n):
        if n < NCH:
            b.dma("pool", wr[n % NWB][:], wada[n], d[2 + n % NWB], writes=[r_wr[n % NWB]])
    for n in range(NWB - 1):
        load_w(n)
    b.op("act", lambda e: e.activation(sc[:], cs[:], AF.Silu), reads=[r_c], writes=[r_sc])
    for n in range(NCH):
        load_w(n + NWB - 1)
        w, rw = wr[n % NWB], r_wr[n % NWB]
        pt, rp = pr.next()
        for k in range(KC):
            b.op("pe", lambda e, pt=pt, w=w, k=k: e.matmul(pt[:, 0:NSEQ], w[:, k, :], sc[:, k, :], start=(k == 0), stop=(k == KC - 1)),
                 reads=[rw, r_sc], writes=[rp], sig=(k == KC - 1))
        b.op("act", lambda e, pt=pt, n=n: e.activation(outT[:, n, :], pt[:, 0:NSEQ], AF.Identity, bias=bs[:, n:n + 1], scale=1.0),
             reads=[rp, r_b], writes=[r_out])
    t = b.dma("sp", modT, outT[:], d[6], reads=[r_out])
    b.wait_all("sp", [t])
    b.emit(); b.close()
    return nc


NHH = 16
CH = 32
NRK = 7


def cumsum_chunks(b, engs, bufA, rA, bufB, rB, ncol0, ncols, clen):
    src, rs, dst, rd = bufA, rA, bufB, rB
    s = 1
    i = 0
    while s < clen:
        sv = src[:, ncol0:ncol0 + ncols].rearrange("p (c t) -> p c t", t=clen)
        dv = dst[:, ncol0:ncol0 + ncols].rearrange("p (c t) -> p c t", t=clen)
        eng = engs[i % len(engs)]
        b.op(eng, lambda e, dv=dv, sv=sv, s=s: e.tensor_tensor(dv[:, :, s:clen], sv[:, :, s:clen], sv[:, :, 0:clen - s], ALU.add),
             reads=[rs], writes=[rd])
        b.op(eng, lambda e, dv=dv, sv=sv, s=s: e.tensor_copy(dv[:, :, 0:s], sv[:, :, 0:s]), reads=[rs], writes=[rd])
        src, rs, dst, rd = dst, rd, src, rs
        s *= 2
        i += 1
    return src, rs


def build_hgrn(pass1, modeA=False):
    nc = bass.Bass("TRN2", target_bir_lowering=False)
    NT = NP if pass1 else NP + NS
    NBLK = NP // 128
    NCK = NP // CH
    dram = lambda name, shape, kind="ExternalInput": nc.dram_tensor(name, list(shape), F32, kind=kind).ap()
    xT = dram("xT", [128, KC, NT])
    vecd = dram("vec", [128, 4, KC])
    modsd = dram("mods", [128, 3, KC, 16])
    whg = dram("whg", [NHH, 128, 4, KC, 128])
    lbpd = dram("lbp", [128, 2, NHH])
    m01d = dram("m01", [128, 128]); cmd = dram("cm", [128, 4, 128])
    identd = dram("ident", [128, 128])
    if not pass1:
        if not modeA:
            wo = dram("wo", [KC, 128, KC, 128])
            ngd = dram("ng", [128, NHH])
            srd = dram("sr", [NHH, 128, NRK, 128])
            drd = dram("dr", [128, NHH, NRK])
            xo = dram("xo", [128, KC, NT], "ExternalOutput")
        else:
            oloc = dram("oloc", [NHH, 128, NT], "ExternalOutput")
            qbo = dram("qbo", [NHH, 128, NT], "ExternalOutput")
            sgo = dram("sgo", [NHH, 128, NT], "ExternalOutput")
        s0d = dram("s0", [NHH, 128, 16, 128])
        msd = dram("ms", [64, 64]); cmsd = dram("cms", [128, 16, 64])
        snew = dram("snew", [NHH, 128, 16, 128], "ExternalOutput")
    send = dram("send", [128, NHH, 128], "ExternalOutput")
    dout = dram("dout", [128, NHH], "ExternalOutput")

    b = Builder(nc, n_dsem=18)
    d = b.dsems
    pr = PsumRot(b)
    xbuf = b.sb("xbuf", [128, KC, NT], F32)
    x = xbuf
    O2T = xbuf[:].rearrange("p k n -> p (k n)").bitcast(BF16)[:, 0:KC * NT].rearrange("p (k n) -> p k n", k=KC)
    h = b.sb("h", [128, KC, NT], BF16)
    rstd = b.sb("rstd", [128, NT], F32)
    xsq0 = b.sb("xsq0", [128, NT], BF16)
    t32a = b.sb("t32a", [128, NT], F32)
    NWB = 5
    wring = [b.sb("wr%d" % i, [128, KC, 128], BF16) for i in range(NWB)]
    qf = b.sb("qf", [128, NT], F32); kf = b.sb("kf", [128, NT], F32)
    bA = b.sb("bA", [128, NT], F32); bB = b.sb("bB", [128, NT], F32)
    sg = b.sb("sg", [128, NT], F32); Oraw = b.sb("Oraw", [128, NT], F32)
    qt = b.sb("qt", [128, NT], BF16); kt = b.sb("kt", [128, NT], BF16); kh = b.sb("kh", [128, NT], BF16)
    dec = b.sb("dec", [128, NCK + 16], F32)
    Vt = b.sb("Vt", [128, NBLK + 1, 128], BF16)
    At = b.sb("At", [128, 128], BF16)
    Khm = b.sb("Khm", [128, 4, 128], BF16)
    KhmT = b.sb("KhmT", [128, 4, 128], BF16)
    S = b.sb("S", [128, 128], F32); Sbf = b.sb("Sbf", [128, 128], BF16)
    Dall = b.sb("Dall", [128, NHH], F32)
    btot = b.sb("btot", [128, 1], F32)
    vec = b.sb("vecs", [128, 4, KC], F32)
    mods = b.sb("modss", [128, 3, KC, 16], F32)
    lbp = b.sb("lbps", [128, 2, NHH], F32)
    oml = b.sb("oml", [128, NHH], F32)
    m01 = b.sb("m01s", [128, 128], F32); cm = b.sb("cms_", [128, 4, 128], F32)
    ident = b.sb("idents", [128, 128], BF16); identf = b.sb("identf", [128, 128], F32)
    ones = b.sb("ones", [128, 128], BF16)
    gm = b.sb("gm", [128, KC], F32); gms = b.sb("gms", [128, KC, 16], F32)
    if not pass1:
        if not modeA:
            ng = b.sb("ngs", [128, NHH], F32)
            sr = b.sb("srs", [128, NRK, 128], F32)
            dr = b.sb("drs", [128, NHH, NRK], F32)
        else:
            QBf = b.sb("QBf", [128, NT], F32)
            pbA = b.sb("pbA", [128, NCK], F32); pbB = b.sb("pbB", [128, NCK], F32)
            r_QBf, r_pbA, r_pbB = Res("QBf"), Res("pbA"), Res("pbB")
        S0 = b.sb("S0", [128, 16, 128], F32); S0b = b.sb("S0b", [128, 16, 128], BF16)
        ms = b.sb("mss", [64, 64], F32); cms = b.sb("cmss", [128, 16, 64], F32)
        Ats = b.sb("Ats", [64, 64], BF16)
        Khms = b.sb("Khms", [128, 16, 64], BF16)
        KhmTs = b.sb("KhmTs", [64, 16, 128], BF16)
        tmps = b.sb("tmps", [128, NS], F32)
    R = Res
    r_x, r_h, r_rstd, r_const, r_gm, r_gms = [R(n) for n in "x h rstd const gm gms".split()]
    r_xsq0, r_t32a = R("xsq"), R("t32")
    r_wr = [R("wr%d" % i) for i in range(NWB)]
    r_qf, r_kf, r_bA, r_bB, r_sg, r_Oraw, r_qt, r_kt, r_kh, r_dec, r_Vt, r_At, r_Khm, r_KhmT, r_S, r_Sbf, r_Sall, r_bt = [
        R(n) for n in "qf kf bA bB sg Oraw qt kt kh dec Vt At Khm KhmT S Sbf Sall bt".split()]
    r_sr, r_S0, r_S0b, r_Ats, r_Khms, r_KhmTs, r_tmps = [R(n) for n in "sr S0 S0b Ats Khms KhmTs tmps".split()]
    r_O2T = r_x

    b.dma("sp", x[:], xT, d[0], writes=[r_x])
    cl = [(vec, vecd), (mods, modsd), (lbp, lbpd), (m01, m01d), (cm, cmd), (identf, identd)]
    if not pass1:
        cl += [(ms, msd), (cms, cmsd)] + ([] if modeA else [(ng, ngd), (dr, drd)])
    for dst, src in cl:
        b.dma("sp", dst[:], src, d[1], writes=[r_const])
    b.op("pool", lambda e: e.memset(ones[:], 1.0), writes=[r_const])
    b.op("act", lambda e: e.activation(ident[:], identf[:], AF.Copy), reads=[r_const], writes=[r_const])
    b.op("dve", lambda e: e.tensor_tensor(oml[:], lbp[:, 0, :], lbp[:, 1, :], ALU.subtract), reads=[r_const], writes=[r_const])
    b.op("act", lambda e: e.activation(oml[:], oml[:], AF.Sigmoid), reads=[r_const], writes=[r_const])

    wsem = [d[2], d[3], d[4], d[5], d[6]]
    NWT = NHH * 4 + (0 if (pass1 or modeA) else KC)
    used = [1, 2] if pass1 else [0, 1, 2, 3]
    wlist = [(hh, wh) for hh in range(NHH) for wh in used] + ([("o", i) for i in range(KC)] if not (pass1 or modeA) else [])

    def load_w(n):
        if n >= len(wlist):
            return
        a, c = wlist[n]
        src = wo[c] if a == "o" else whg[a][:, c, :, :]
        b.dma("pool", wring[n % NWB][:], src, wsem[n % NWB], writes=[r_wr[n % NWB]])
    for n in range(4):
        load_w(n)
    wcount = [0]

    def next_w():
        n = wcount[0]
        wcount[0] += 1
        return n, wring[n % NWB], r_wr[n % NWB]

    emit_norm_mod(b, pr, x, r_x, h, r_h, NT, NP, vec, 0, 1, 2, mods, 0, 1, ones, r_const,
                  ([xsq0, xsq0], [r_xsq0, r_xsq0], rstd, r_rstd, gm, r_gm, gms, r_gms, [t32a, t32a], [r_t32a, r_t32a]))

    def proj_fm(dst_fn):
        n_, w, rw = next_w()
        for (s, n) in ntiles(NT):
            pt, rp = pr.next()
            for k in range(KC):
                b.op("pe", lambda e, pt=pt, w=w, k=k, s=s, n=n: e.matmul(
                    pt[:, 0:n], w[:, k, :], h[:, k, s:s + n], start=(k == 0), stop=(k == KC - 1)),
                    reads=[rw, r_h], writes=[rp], sig=(k == KC - 1))
            dst_fn(pt, rp, s, n)
        load_w(n_ + 4)

    outs = []
    for hh in range(NHH):
        if not pass1:
            if not modeA:
                b.dma("sp", sr[:], srd[hh], d[7], writes=[r_sr])
            b.dma("sp", S0[:], s0d[hh], d[8], writes=[r_S0])
            b.dma("pool", S0b[:], s0d[hh], d[9], writes=[r_S0b])
            proj_fm(lambda pt, rp, s, n: b.op("act", lambda e: e.activation(qf[:, s:s + n], pt[:, 0:n], AF.Silu),
                                              reads=[rp], writes=[r_qf]))
        proj_fm(lambda pt, rp, s, n: b.op("act", lambda e: e.activation(kf[:, s:s + n], pt[:, 0:n], AF.Sigmoid, scale=-1.0),
                                          reads=[rp], writes=[r_kf]))
        b.op("dve", lambda e, hh=hh: e.tensor_scalar(kf[:], kf[:], oml[:, hh:hh + 1], None, ALU.mult),
             reads=[r_kf, r_const], writes=[r_kf])
        b.op("act", lambda e: e.activation(bA[:], kf[:], AF.Ln, bias=1.0, scale=-1.0), reads=[r_kf], writes=[r_bA])
        n_, w, rw = next_w()
        for blk in range(NBLK + (0 if pass1 else 1)):
            m = 128 if blk < NBLK else NS
            c0 = blk * 128
            pt, rp = pr.next()
            for k in range(KC):
                b.op("pe", lambda e, pt=pt, w=w, k=k, c0=c0, m=m: e.matmul(
                    pt[0:m, 0:128], h[:, k, c0:c0 + m], w[:, k, :], start=(k == 0), stop=(k == KC - 1)),
                    reads=[rw, r_h], writes=[rp], sig=(k == KC - 1))
            b.op("act", lambda e, pt=pt, blk=blk, m=m: e.activation(Vt[0:m, blk, :], pt[0:m, 0:128], AF.Copy),
                 reads=[rp], writes=[r_Vt])
        load_w(n_ + 4)
        if not pass1:
            proj_fm(lambda pt, rp, s, n: b.op("act", lambda e: e.activation(sg[:, s:s + n], pt[:, 0:n], AF.Silu),
                                              reads=[rp], writes=[r_sg]))
        bb, rbb = cumsum_chunks(b, ["dve", "pool"], bA, r_bA, bB, r_bB, 0, NP, CH)
        other, rother = (bB, r_bB) if bb is bA else (bA, r_bA)
        if not pass1:
            if bb is not bA:
                b.op("pool", lambda e: e.tensor_copy(bB[:, NP:NT], bA[:, NP:NT]), reads=[r_bA], writes=[r_bB])
            sb_, rsb = cumsum_chunks(b, ["pool"], bb, rbb, other, rother, NP, NS, 4)
            if sb_ is not bb:
                b.op("pool", lambda e, sb_=sb_, bb=bb: e.tensor_copy(bb[:, NP:NT], sb_[:, NP:NT]), reads=[rsb], writes=[rbb])
        bv = bb[:, 0:NP].rearrange("p (c t) -> p c t", t=CH)
        ov = other[:, 0:NP].rearrange("p (c t) -> p c t", t=CH)
        b.op("act", lambda e, bv=bv: e.activation(dec[:, 0:NCK].unsqueeze(2), bv[:, :, CH - 1:CH], AF.Exp), reads=[rbb], writes=[r_dec])
        b.op("dve", lambda e, bv=bv, ov=ov: e.tensor_tensor(ov, bv[:, :, CH - 1:CH].broadcast_to([128, NCK, CH]), bv, ALU.subtract),
             reads=[rbb], writes=[rother])
        if not pass1:
            bs_ = bb[:, NP:NT].rearrange("p (c t) -> p c t", t=4)
            os_ = other[:, NP:NT].rearrange("p (c t) -> p c t", t=4)
            b.op("act", lambda e, bs_=bs_: e.activation(dec[:, NCK:NCK + 16].unsqueeze(2), bs_[:, :, 3:4], AF.Exp), reads=[rbb], writes=[r_dec])
            b.op("dve", lambda e, bs_=bs_, os_=os_: e.tensor_tensor(os_, bs_[:, :, 3:4].broadcast_to([128, 16, 4]), bs_, ALU.subtract),
                 reads=[rbb], writes=[rother])
        b.op("act", lambda e, other=other: e.activation(other[:, 0:NT], other[:, 0:NT], AF.Exp), reads=[rother], writes=[rother])
        b.op("dve", lambda e, other=other: e.tensor_tensor(kh[:], kf[:], other[:, 0:NT], ALU.mult), reads=[rother, r_kf], writes=[r_kh])
        if pass1:
            b.op("dve", lambda e, bv=bv: e.tensor_reduce(btot[:], bv[:, :, CH - 1], AX.X, ALU.add), reads=[rbb], writes=[r_bt])
            b.op("act", lambda e, hh=hh: e.activation(Dall[:, hh:hh + 1], btot[:], AF.Exp), reads=[r_bt], writes=[r_Sall])
        else:
            b.op("act", lambda e, other=other, bb=bb: e.activation(other[:, 0:NT], bb[:, 0:NT], AF.Exp), reads=[rbb, r_kh], writes=[rother])
            b.op("dve", lambda e, other=other: e.tensor_tensor(qt[:], qf[:], other[:, 0:NT], ALU.mult), reads=[rother, r_qf], writes=[r_qt])
            b.op("act", lambda e, other=other, bb=bb: e.activation(other[:, 0:NT], bb[:, 0:NT], AF.Exp, scale=-1.0), reads=[rbb, r_qt], writes=[rother])
            b.op("dve", lambda e, other=other: e.tensor_tensor(kt[:], kf[:], other[:, 0:NT], ALU.mult), reads=[rother, r_kf], writes=[r_kt])
            if modeA:
                b.op("pool", lambda e, bv=bv: e.tensor_copy(pbA[:].unsqueeze(2), bv[:, :, CH - 1:CH]), reads=[rbb], writes=[r_pbA])
                pin, rpin = cumsum_chunks(b, ["pool"], pbA, r_pbA, pbB, r_pbB, 0, NCK, NCK)
                pex, rpex = (pbB, r_pbB) if pin is pbA else (pbA, r_pbA)
                b.op("act", lambda e, hh=hh, pin=pin: e.activation(Dall[:, hh:hh + 1], pin[:, NCK - 1:NCK], AF.Exp), reads=[rpin], writes=[r_Sall])
                b.op("pool", lambda e, pin=pin, pex=pex, bv=bv: e.tensor_tensor(pex[:].unsqueeze(2), pin[:].unsqueeze(2), bv[:, :, CH - 1:CH], ALU.subtract),
                     reads=[rpin, rbb], writes=[rpex])
                b.op("act", lambda e, pex=pex: e.activation(pex[:], pex[:], AF.Exp), reads=[rpex], writes=[rpex])
                b.op("pool", lambda e, pex=pex: e.tensor_tensor(
                    QBf[:, 0:NP].rearrange("p (c t) -> p c t", t=CH), qt[:, 0:NP].rearrange("p (c t) -> p c t", t=CH),
                    pex[:].unsqueeze(2).broadcast_to([128, NCK, CH]), ALU.mult), reads=[rpex, r_qt], writes=[r_QBf])
                b.op("pool", lambda e: e.memset(QBf[:, NP:NT], 0.0), writes=[r_QBf])
                outs.append(b.dma("sp", qbo[hh], QBf[:], d[7], reads=[r_QBf]))
                outs.append(b.dma("sp", sgo[hh], sg[:], d[13], reads=[r_sg]))
        if pass1 or modeA:
            b.op("pool", lambda e: e.memset(S[:], 0.0), writes=[r_S])
            if modeA:
                b.op("pool", lambda e: e.memset(Sbf[:], 0.0), writes=[r_Sbf])
        else:
            b.op("dve", lambda e, hh=hh: e.tensor_scalar(S[:], sr[:, 0, :], 1.0, None, ALU.mult), reads=[r_sr], writes=[r_S])
            for r in range(1, NRK):
                b.op("dve", lambda e, hh=hh, r=r: e.scalar_tensor_tensor(S[:], S[:], dr[:, hh, r:r + 1], sr[:, r, :], ALU.mult, ALU.add),
                     reads=[r_S, r_sr, r_const], writes=[r_S])
            b.op("act", lambda e: e.activation(Sbf[:], S[:], AF.Copy), reads=[r_S], writes=[r_Sbf])
        for blk in range(NBLK):
            c0 = blk * 128
            b.op("dve", lambda e, c0=c0: e.tensor_tensor(Khm[:], kh[:, c0:c0 + 128].unsqueeze(1).broadcast_to([128, 4, 128]), cm[:], ALU.mult),
                 reads=[r_kh, r_const], writes=[r_Khm])
            ptT, rpT = pr.next()
            ptTb = ptT[:, :].bitcast(BF16)
            for c in range(4):
                b.op("pe", lambda e, ptTb=ptTb, c=c: e.transpose(ptTb[:, c * 128:(c + 1) * 128], Khm[:, c, :], ident[:]),
                     reads=[r_Khm, r_const], writes=[rpT], sig=(c == 3))
            b.op("act", lambda e, ptTb=ptTb: e.activation(KhmT[:].rearrange("p c k -> p (c k)"), ptTb[:, 0:512], AF.Copy),
                 reads=[rpT], writes=[r_KhmT])
            if not pass1:
                pa, rpa = pr.next()
                b.op("pe", lambda e, pa=pa, c0=c0: e.matmul(pa[:, 0:128], kt[:, c0:c0 + 128], qt[:, c0:c0 + 128], start=True, stop=True),
                     reads=[r_kt, r_qt], writes=[rpa])
                b.op("dve", lambda e, pa=pa: e.tensor_tensor(At[:], pa[:, 0:128], m01[:], ALU.mult), reads=[rpa, r_const], writes=[r_At])
                po, rpo = pr.next()
                b.op("pe", lambda e, po=po, blk=blk: e.matmul(po[:, 0:128], Vt[:, blk, :], At[:], start=True, stop=False),
                     reads=[r_Vt, r_At], writes=[rpo], sig=False)
            for c in range(4):
                ck = blk * 4 + c
                if not pass1:
                    b.op("pe", lambda e, po=po, c=c, c0=c0: e.matmul(po[:, c * CH:(c + 1) * CH], Sbf[:], qt[:, c0 + c * CH:c0 + (c + 1) * CH],
                                                                   start=False, stop=(c == 3)),
                         reads=[r_Sbf, r_qt], writes=[rpo], sig=True)
                pu, rpu = pr.next()
                b.op("pe", lambda e, pu=pu, c=c, blk=blk: e.matmul(pu[:, 0:128], KhmT[:, c, :], Vt[:, blk, :], start=True, stop=True),
                     reads=[r_KhmT, r_Vt], writes=[rpu])
                b.op("dve", lambda e, pu=pu, ck=ck: e.scalar_tensor_tensor(S[:], S[:], dec[:, ck:ck + 1], pu[:, 0:128], ALU.mult, ALU.add),
                     reads=[rpu, r_S, r_dec], writes=[r_S])
                if not pass1:
                    b.op("act", lambda e: e.activation(Sbf[:], S[:], AF.Copy), reads=[r_S], writes=[r_Sbf])
            if not pass1:
                b.op("act", lambda e, po=po, c0=c0: e.activation(Oraw[:, c0:c0 + 128], po[:, 0:128], AF.Copy), reads=[rpo], writes=[r_Oraw])
        outs.append(b.dma("sp", send[:, hh, :], S[:], d[11], reads=[r_S]))
        if pass1:
            continue
        b.op("dve", lambda e: e.tensor_tensor(Khms[:], kh[:, NP:NT].unsqueeze(1).broadcast_to([128, 16, 64]), cms[:], ALU.mult),
             reads=[r_kh, r_const], writes=[r_Khms])
        for half in range(2):
            ptT, rpT = pr.next()
            ptTb = ptT[:, :].bitcast(BF16)
            for s8 in range(8):
                sq = half * 8 + s8
                b.op("pe", lambda e, ptTb=ptTb, s8=s8, sq=sq: e.transpose(ptTb[0:64, s8 * 128:(s8 + 1) * 128], Khms[:, sq, :], ident[:]),
                     reads=[r_Khms, r_const], writes=[rpT], sig=(s8 == 7))
            b.op("act", lambda e, ptTb=ptTb, half=half: e.activation(
                KhmTs[:, half * 8:half * 8 + 8, :].rearrange("p c k -> p (c k)"), ptTb[0:64, 0:1024], AF.Copy),
                reads=[rpT], writes=[r_KhmTs])
        pa, rpa = pr.next()
        b.op("pe", lambda e, pa=pa: e.matmul(pa[0:64, 0:64], kt[:, NP:NT], qt[:, NP:NT], start=True, stop=True),
             reads=[r_kt, r_qt], writes=[rpa])
        b.op("dve", lambda e, pa=pa: e.tensor_tensor(Ats[:], pa[0:64, 0:64], ms[:], ALU.mult), reads=[rpa, r_const], writes=[r_Ats])
        po, rpo = pr.next()
        b.op("pe", lambda e, po=po: e.matmul(po[:, 0:64], Vt[0:64, NBLK, :], Ats[:], start=True, stop=False),
             reads=[r_Vt, r_Ats], writes=[rpo], sig=False)
        for sq in range(16):
            b.op("pe", lambda e, po=po, sq=sq: e.matmul(po[:, sq * 4:sq * 4 + 4], S0b[:, sq, :], qt[:, NP + sq * 4:NP + sq * 4 + 4],
                                                       start=False, stop=(sq == 15)),
                 reads=[r_S0b, r_qt], writes=[rpo], sig=(sq == 15))
        b.op("act", lambda e, po=po: e.activation(Oraw[:, NP:NT], po[:, 0:64], AF.Copy), reads=[rpo], writes=[r_Oraw])
        for q4 in range(4):
            pu, rpu = pr.next()
            for s4 in range(4):
                sq = q4 * 4 + s4
                b.op("pe", lambda e, pu=pu, s4=s4, sq=sq: e.matmul(pu[:, s4 * 128:(s4 + 1) * 128], KhmTs[:, sq, :], Vt[0:64, NBLK, :],
                                                                 start=True, stop=True),
                     reads=[r_KhmTs, r_Vt], writes=[rpu], sig=(s4 == 3))
            for s4 in range(4):
                sq = q4 * 4 + s4
                b.op("dve", lambda e, pu=pu, s4=s4, sq=sq: e.scalar_tensor_tensor(
                    S0[:, sq, :], S0[:, sq, :], dec[:, NCK + sq:NCK + sq + 1], pu[:, s4 * 128:(s4 + 1) * 128], ALU.mult, ALU.add),
                    reads=[rpu, r_S0, r_dec], writes=[r_S0])
        outs.append(b.dma("sp", snew[hh], S0[:], d[10], reads=[r_S0]))
        if modeA:
            outs.append(b.dma("sp", oloc[hh], Oraw[:], d[14], reads=[r_Oraw]))
            continue
        b.op("act", lambda e: e.activation(xsq0[:], Oraw[:], AF.Square), reads=[r_Oraw], writes=[r_xsq0])
        tl = ntiles(NT)
        bk = [pr.next() for _ in tl]
        for ti, (s, n) in enumerate(tl):
            pt, rp = bk[ti]
            b.op("pe", lambda e, pt=pt, s=s, n=n: e.matmul(pt[:, 0:n], ones[:], xsq0[:, s:s + n], start=True, stop=True),
                 reads=[r_xsq0, r_const], writes=[rp])
            b.op("act", lambda e, pt=pt, s=s, n=n: e.activation(rstd[:, s:s + n], pt[:, 0:n], AF.Sqrt, bias=EPS, scale=1.0 / 128),
                 reads=[rp], writes=[r_rstd])
        b.op("dve", lambda e: e.reciprocal(rstd[:], rstd[:]), reads=[r_rstd], writes=[r_rstd])
        b.op("dve", lambda e, hh=hh: e.scalar_tensor_tensor(Oraw[:], Oraw[:], ng[:, hh:hh + 1], rstd[:], ALU.mult, ALU.mult),
             reads=[r_Oraw, r_rstd, r_const], writes=[r_Oraw])
        b.op("dve", lambda e, hh=hh: e.tensor_tensor(O2T[:, hh, :], Oraw[:], sg[:], ALU.mult), reads=[r_Oraw, r_sg, r_x], writes=[r_O2T])

    if pass1 or modeA:
        outs.append(b.dma("sp", dout, Dall[:], d[12], reads=[r_Sall]))
    else:
        b.op("pool", lambda e: e.memset(Dall[:], 0.0), writes=[r_Sall])
        outs.append(b.dma("sp", dout, Dall[:], d[12], reads=[r_Sall]))
        xr = [t32a, rstd]; r_xr = [r_t32a, r_rstd]
        xsem = [d[13], d[14]]; osem = [d[15], d[16]]
        for i in range(KC):
            n_, w, rw = next_w()
            xi = xr[i % 2]; rxi = r_xr[i % 2]
            b.dma("sp", xi[:], xT[:, i, :], xsem[i % 2], writes=[rxi])
            for (s, n) in ntiles(NT):
                pt, rp = pr.next()
                for k in range(KC):
                    b.op("pe", lambda e, pt=pt, w=w, k=k, s=s, n=n: e.matmul(
                        pt[:, 0:n], w[:, k, :], O2T[:, k, s:s + n], start=(k == 0), stop=(k == KC - 1)),
                        reads=[rw, r_O2T], writes=[rp], sig=(k == KC - 1))
                if s + n <= NP:
                    b.op("dve", lambda e, pt=pt, i=i, s=s, n=n, xi=xi: e.scalar_tensor_tensor(
                        xi[:, s:s + n], pt[:, 0:n], vec[:, 3, i:i + 1], xi[:, s:s + n], ALU.mult, ALU.add),
                        reads=[rp, r_const, rxi], writes=[rxi])
                else:
                    b.op("dve", lambda e, pt=pt, i=i: e.tensor_tensor(
                        tmps[:].rearrange("p (s t) -> p s t", t=4), pt[:, 0:NS].rearrange("p (s t) -> p s t", t=4),
                        mods[:, 2, i, :].unsqueeze(2).broadcast_to([128, 16, 4]), ALU.mult),
                        reads=[rp, r_const], writes=[r_tmps])
                    b.op("dve", lambda e, xi=xi: e.tensor_tensor(xi[:, NP:NT], xi[:, NP:NT], tmps[:], ALU.add),
                         reads=[r_tmps, rxi], writes=[rxi])
            outs.append(b.dma("sp", xo[:, i, :], xi[:], osem[i % 2], reads=[rxi]))
            load_w(n_ + 4)
    b.wait_all("sp", outs)
    b.emit(); b.close()
    return nc


def build_hgrnb():
    nc = bass.Bass("TRN2", target_bir_lowering=False)
    NT = NP + NS
    dram = lambda name, shape, kind="ExternalInput": nc.dram_tensor(name, list(shape), F32, kind=kind).ap()
    xT = dram("xT", [128, KC, NT])
    vecd = dram("vec", [128, 4, KC])
    modsd = dram("mods", [128, 3, KC, 16])
    olocd = dram("oloc", [NHH, 128, NT]); qbd = dram("qb", [NHH, 128, NT]); sgd = dram("sg", [NHH, 128, NT])
    srd = dram("sr", [NHH, 128, NRK, 128]); drd = dram("dr", [128, NHH, NRK])
    slocd = dram("sloc", [128, NHH, 128]); dld = dram("dl", [128, NHH])
    ngd = dram("ng", [128, NHH])
    wo = dram("wo", [KC, 128, KC, 128])
    xo = dram("xo", [128, KC, NT], "ExternalOutput")
    send = dram("send", [128, NHH, 128], "ExternalOutput")

    b = Builder(nc, n_dsem=18)
    d = b.dsems
    pr = PsumRot(b)
    O2T = b.sb("O2T", [128, KC, NT], BF16)
    ol = [b.sb("ol%d" % i, [128, NT], F32) for i in range(2)]
    qbb = [b.sb("qbb%d" % i, [128, NT], BF16) for i in range(2)]
    sgl = [b.sb("sgl%d" % i, [128, NT], F32) for i in range(2)]
    sr = [b.sb("sr%d" % i, [128, NRK, 128], F32) for i in range(2)]
    Oraw = b.sb("Oraw", [128, NT], F32)
    xsq0 = b.sb("xsq0", [128, NT], BF16)
    rstd = b.sb("rstd", [128, NT], F32)
    S = b.sb("S", [128, 128], F32); Sbf = b.sb("Sbf", [128, 128], BF16); Se = b.sb("Se", [128, NHH, 128], F32)
    sloc = b.sb("slocs", [128, NHH, 128], F32)
    dr = b.sb("drs", [128, NHH, NRK], F32); dl = b.sb("dls", [128, NHH], F32); ng = b.sb("ngs", [128, NHH], F32)
    vec = b.sb("vecs", [128, 4, KC], F32); mods = b.sb("modss", [128, 3, KC, 16], F32)
    ones = b.sb("ones", [128, 128], BF16)
    NWB = 4
    wring = [b.sb("wr%d" % i, [128, KC, 128], BF16) for i in range(NWB)]
    xr = [b.sb("xr%d" % i, [128, NT], F32) for i in range(2)]
    tmps = b.sb("tmps", [128, NS], F32)
    R = Res
    r_O2T, r_Oraw, r_xsq0, r_rstd, r_S, r_Sbf, r_Se, r_const, r_tmps = [R(n) for n in "O2T Oraw xsq rstd S Sbf Se const tmps".split()]
    r_ol = [R("a"), R("b")]; r_qbb = [R("a"), R("b")]; r_sgl = [R("a"), R("b")]; r_sr = [R("a"), R("b")]
    r_wr = [R("w%d" % i) for i in range(NWB)]; r_xr = [R("a"), R("b")]
    for dst, src in [(vec, vecd), (mods, modsd), (sloc, slocd), (dr, drd), (dl, dld), (ng, ngd)]:
        b.dma("sp", dst[:], src, d[0], writes=[r_const])
    b.op("pool", lambda e: e.memset(ones[:], 1.0), writes=[r_const])

    def load_w(i):
        if i < KC:
            b.dma("pool", wring[i % NWB][:], wo[i], d[1 + i % NWB], writes=[r_wr[i % NWB]])

    def load_head(hh):
        if hh >= NHH:
            return
        i = hh % 2
        b.dma("sp", ol[i][:], olocd[hh], d[5 + i], writes=[r_ol[i]])
        b.dma("pool", qbb[i][:], qbd[hh], d[7 + i], writes=[r_qbb[i]])
        b.dma("sp", sgl[i][:], sgd[hh], d[9 + i], writes=[r_sgl[i]])
        b.dma("sp", sr[i][:], srd[hh], d[11 + i], writes=[r_sr[i]])
    load_head(0)
    for i in range(NWB - 1):
        load_w(i)
    outs = []
    for hh in range(NHH):
        load_head(hh + 1)
        i2 = hh % 2
        b.op("dve", lambda e, i2=i2: e.tensor_scalar(S[:], sr[i2][:, 0, :], 1.0, None, ALU.mult), reads=[r_sr[i2]], writes=[r_S])
        for r in range(1, NRK):
            b.op("dve", lambda e, hh=hh, r=r, i2=i2: e.scalar_tensor_tensor(S[:], S[:], dr[:, hh, r:r + 1], sr[i2][:, r, :], ALU.mult, ALU.add),
                 reads=[r_S, r_sr[i2], r_const], writes=[r_S])
        b.op("act", lambda e: e.activation(Sbf[:], S[:], AF.Copy), reads=[r_S], writes=[r_Sbf])
        b.op("dve", lambda e, hh=hh: e.scalar_tensor_tensor(Se[:, hh, :], S[:], dl[:, hh:hh + 1], sloc[:, hh, :], ALU.mult, ALU.add),
             reads=[r_S, r_const], writes=[r_Se])
        for (s, n) in ntiles(NT):
            pt, rp = pr.next()
            b.op("pe", lambda e, pt=pt, s=s, n=n, i2=i2: e.matmul(pt[:, 0:n], Sbf[:], qbb[i2][:, s:s + n], start=True, stop=True),
                 reads=[r_Sbf, r_qbb[i2]], writes=[rp])
            b.op("dve", lambda e, pt=pt, s=s, n=n, i2=i2: e.tensor_tensor(Oraw[:, s:s + n], pt[:, 0:n], ol[i2][:, s:s + n], ALU.add),
                 reads=[rp, r_ol[i2]], writes=[r_Oraw])
        b.op("act", lambda e: e.activation(xsq0[:], Oraw[:], AF.Square), reads=[r_Oraw], writes=[r_xsq0])
        for (s, n) in ntiles(NT):
            pt, rp = pr.next()
            b.op("pe", lambda e, pt=pt, s=s, n=n: e.matmul(pt[:, 0:n], ones[:], xsq0[:, s:s + n], start=True, stop=True),
                 reads=[r_xsq0, r_const], writes=[rp])
            b.op("act", lambda e, pt=pt, s=s, n=n: e.activation(rstd[:, s:s + n], pt[:, 0:n], AF.Ln, bias=EPS, scale=1.0 / 128),
                 reads=[rp], writes=[r_rstd])
        b.op("act", lambda e: e.activation(rstd[:], rstd[:], AF.Exp, scale=-0.5), reads=[r_rstd], writes=[r_rstd])
        b.op("dve", lambda e, hh=hh: e.scalar_tensor_tensor(Oraw[:], Oraw[:], ng[:, hh:hh + 1], rstd[:], ALU.mult, ALU.mult),
             reads=[r_Oraw, r_rstd, r_const], writes=[r_Oraw])
        b.op("pool", lambda e, hh=hh, i2=i2: e.tensor_tensor(O2T[:, hh, :], Oraw[:], sgl[i2][:], ALU.mult),
             reads=[r_Oraw, r_sgl[i2]], writes=[r_O2T])
    outs.append(b.dma("sp", send, Se[:], d[13], reads=[r_Se]))
    for i in range(KC):
        load_w(i + NWB - 1)
        w, rw = wring[i % NWB], r_wr[i % NWB]
        xi, rxi = xr[i % 2], r_xr[i % 2]
        b.dma("sp", xi[:], xT[:, i, :], d[14 + i % 2], writes=[rxi])
        for (s, n) in ntiles(NT):
            pt, rp = pr.next()
            for k in range(KC):
                b.op("pe", lambda e, pt=pt, w=w, k=k, s=s, n=n: e.matmul(
                    pt[:, 0:n], w[:, k, :], O2T[:, k, s:s + n], start=(k == 0), stop=(k == KC - 1)),
                    reads=[rw, r_O2T], writes=[rp], sig=(k == KC - 1))
            if s + n <= NP:
                b.op("dve", lambda e, pt=pt, i=i, s=s, n=n, xi=xi: e.scalar_tensor_tensor(
                    xi[:, s:s + n], pt[:, 0:n], vec[:, 3, i:i + 1], xi[:, s:s + n], ALU.mult, ALU.add),
                    reads=[rp, r_const, rxi], writes=[rxi])
            else:
                b.op("dve", lambda e, pt=pt, i=i: e.tensor_tensor(
                    tmps[:].rearrange("p (s t) -> p s t", t=4), pt[:, 0:NS].rearrange("p (s t) -> p s t", t=4),
                    mods[:, 2, i, :].unsqueeze(2).broadcast_to([128, 16, 4]), ALU.mult),
                    reads=[rp, r_const], writes=[r_tmps])
                b.op("dve", lambda e, xi=xi: e.tensor_tensor(xi[:, NP:NT], xi[:, NP:NT], tmps[:], ALU.add),
                     reads=[r_tmps, rxi], writes=[rxi])
        outs.append(b.dma("sp", xo[:, i, :], xi[:], d[16 + i % 2], reads=[rxi]))
    b.wait_all("sp", outs)
    b.emit(); b.close()
    return nc


class _HSet:
    pass


def build_hgrna():
    nc = bass.Bass("TRN2", target_bir_lowering=False)
    NT = NP + NS
    NBLK = NP // 128
    NCK = NP // CH
    dram = lambda name, shape, kind="ExternalInput": nc.dram_tensor(name, list(shape), F32, kind=kind).ap()
    xT = dram("xT", [128, KC, NT])
    vecd = dram("vec", [128, 4, KC]); modsd = dram("mods", [128, 3, KC, 16])
    whg = dram("whg", [NHH, 128, 4, KC, 128])
    lbpd = dram("lbp", [128, 2, NHH])
    m01d = dram("m01", [128, 128]); cmd = dram("cm", [128, 4, 128]); identd = dram("ident", [128, 128])
    s0d = dram("s0", [NHH, 128, 16, 128]); msd = dram("ms", [64, 64]); cmsd = dram("cms", [128, 16, 64])
    smd = dram("smask", [128, NT])
    oloc = dram("oloc", [NHH, 128, NT], "ExternalOutput")
    qbo = dram("qbo", [NHH, 128, NT], "ExternalOutput")
    sgo = dram("sgo", [NHH, 128, NT], "ExternalOutput")
    snew = dram("snew", [NHH, 128, 16, 128], "ExternalOutput")
    send = dram("send", [128, NHH, 128], "ExternalOutput")
    dout = dram("dout", [128, NHH], "ExternalOutput")

    b = Builder(nc, n_dsem=22)
    d = b.dsems
    pr = PsumRot(b, 4)
    po_banks = [(b.ps("pob%d" % i, [128, 512]), Res("pob%d" % i)) for i in range(2)]
    pu_banks = [(b.ps("pub%d" % i, [128, 512]), Res("pub%d" % i)) for i in range(2)]
    xbuf = b.sb("xbuf", [128, KC * NT], F32)
    x = xbuf[:].rearrange("p (k n) -> p k n", k=KC)
    h = b.sb("h", [128, KC, NT], BF16)
    rstd = b.sb("rstd", [128, NT], F32)
    xsq0 = b.sb("xsq0", [128, NT], BF16)
    t32a = b.sb("t32a", [128, NT], F32)
    NWB = 5
    wring = [b.sb("wr%d" % i, [128, KC, 128], BF16) for i in range(NWB)]
    Dall = b.sb("Dall", [128, NHH], F32)
    vec = b.sb("vecs", [128, 4, KC], F32); mods = b.sb("modss", [128, 3, KC, 16], F32)
    lbp = b.sb("lbps", [128, 2, NHH], F32); oml = b.sb("oml", [128, NHH], F32)
    m01 = b.sb("m01s", [128, 128], F32); cm = b.sb("cms_", [128, 4, 128], F32)
    ident = b.sb("idents", [128, 128], BF16); identf = b.sb("identf", [128, 128], F32)
    ones = b.sb("ones", [128, 128], BF16)
    gm = b.sb("gm", [128, KC], F32); gms = b.sb("gms", [128, KC, 16], F32)
    ms = b.sb("mss", [64, 64], F32); cms = b.sb("cmss", [128, 16, 64], F32)
    smask = b.sb("smasks", [128, NT], F32); onesf = b.sb("onesf", [128, NCK], F32)
    R = Res
    r_x, r_h, r_rstd, r_const, r_gm, r_gms, r_xsq0, r_t32a, r_D = [R(n) for n in "x h rstd const gm gms xsq t32 D".split()]
    r_wr = [R("wr%d" % i) for i in range(NWB)]

    f32_names = ["qf", "kf", "bA", "bB", "sg", "Oraw", "QBf"]
    bf_names = ["qt", "kt", "kh"]
    sets = []
    for si in range(2):
        B = _HSet()
        if si == 0:
            for nm in f32_names:
                if nm == "QBf":
                    B.QBf = t32a[:]
                elif nm == "Oraw":
                    B.Oraw = rstd[:]
                else:
                    setattr(B, nm, b.sb(nm + "0", [128, NT], F32)[:])
            for nm in bf_names:
                setattr(B, nm, b.sb(nm + "0", [128, NT], BF16)[:])
            B.Vt = b.sb("Vt0", [128, NBLK + 1, 128], BF16)[:]
            B.S0 = b.sb("S00", [128, 16, 128], F32)[:]; B.S0b = b.sb("S0b0", [128, 16, 128], BF16)[:]
            B.Khms = b.sb("Khms0", [128, 16, 64], BF16)[:]; B.KhmTs = b.sb("KhmTs0", [64, 16, 128], BF16)[:]
            B.At = b.sb("At0", [128, 128], BF16)[:]; B.Khm = b.sb("Khm0", [128, 4, 128], BF16)[:]
            B.KhmT = b.sb("KhmT0", [128, 4, 128], BF16)[:]
            B.At_b = b.sb("At0b", [128, 128], BF16)[:]; B.Khm_b = b.sb("Khm0b", [128, 4, 128], BF16)[:]
            B.KhmT_b = b.sb("KhmT0b", [128, 4, 128], BF16)[:]
            B.S = b.sb("S_0", [128, 128], F32)[:]; B.Sbf = b.sb("Sbf0", [128, 128], BF16)[:]
            B.S2 = b.sb("S2_0", [128, 128], F32)[:]; B.Sbf2 = b.sb("Sbf2_0", [128, 128], BF16)[:]
            B.dec = b.sb("dec0", [128, NCK + 16], F32)[:]
            B.pbA = b.sb("pbA0", [128, NCK], F32)[:]; B.pbB = b.sb("pbB0", [128, NCK], F32)[:]
            B.Ats = b.sb("Ats0", [64, 64], BF16)[:]
        else:
            off = [0]

            def carve(n_f32, dt, shape):
                v = xbuf[:, off[0]:off[0] + n_f32]
                off[0] += n_f32
                if dt is BF16:
                    v = v.bitcast(BF16)
                if len(shape) == 2:
                    return v[:, 0:shape[1]] if shape[0] == 128 else v[0:shape[0], 0:shape[1]]
                if len(shape) == 3:
                    vv = v[:, 0:shape[1] * shape[2]].rearrange("p (a c) -> p a c", a=shape[1])
                    return vv if shape[0] == 128 else vv[0:shape[0]]
            for nm in f32_names:
                setattr(B, nm, carve(NT, F32, [128, NT]))
            for nm in bf_names:
                setattr(B, nm, carve(NT // 2, BF16, [128, NT]))
            B.Vt = carve((NBLK + 1) * 64, BF16, [128, NBLK + 1, 128])
            B.S0 = carve(2048, F32, [128, 16, 128]); B.S0b = carve(1024, BF16, [128, 16, 128])
            B.Khms = carve(512, BF16, [128, 16, 64]); B.KhmTs = carve(1024, BF16, [64, 16, 128])
            B.At = carve(64, BF16, [128, 128]); B.Khm = carve(256, BF16, [128, 4, 128]); B.KhmT = carve(256, BF16, [128, 4, 128])
            B.At_b = carve(64, BF16, [128, 128]); B.Khm_b = carve(256, BF16, [128, 4, 128]); B.KhmT_b = carve(256, BF16, [128, 4, 128])
            B.S = carve(128, F32, [128, 128]); B.Sbf = carve(64, BF16, [128, 128])
            B.S2 = carve(128, F32, [128, 128]); B.Sbf2 = carve(64, BF16, [128, 128])
            B.dec = carve(NCK + 16, F32, [128, NCK + 16])
            B.pbA = carve(NCK, F32, [128, NCK]); B.pbB = carve(NCK, F32, [128, NCK])
            B.Ats = carve(32, BF16, [64, 64])
            assert off[0] <= KC * NT, off[0]
        for nm in f32_names + bf_names + ["Vt", "S0", "S0b", "Khms", "KhmTs", "At", "Khm", "KhmT", "At_b", "Khm_b", "KhmT_b", "S", "Sbf", "S2", "Sbf2", "dec", "pbA", "pbB", "Ats"]:
            setattr(B, "r_" + nm, Res(nm + str(si)))
        if si == 0:
            B.r_QBf = r_t32a
            B.r_Oraw = r_rstd
        sets.append(B)

    b.dma("sp", xbuf[:], xT.rearrange("p k n -> p (k n)"), d[0], writes=[r_x])
    for dst, src in [(vec, vecd), (mods, modsd), (lbp, lbpd), (m01, m01d), (cm, cmd), (identf, identd), (ms, msd), (cms, cmsd), (smask, smd)]:
        b.dma("sp", dst[:], src, d[1], writes=[r_const])
    b.op("pool", lambda e: e.memset(ones[:], 1.0), writes=[r_const])
    b.op("pool", lambda e: e.memset(Dall[:], 0.0), writes=[r_D])
    b.op("pool", lambda e: e.memset(onesf[:], 1.0), writes=[r_const])
    b.op("act", lambda e: e.activation(ident[:], identf[:], AF.Copy), reads=[r_const], writes=[r_const])
    b.op("dve", lambda e: e.tensor_tensor(oml[:], lbp[:, 0, :], lbp[:, 1, :], ALU.subtract), reads=[r_const], writes=[r_const])
    b.op("act", lambda e: e.activation(oml[:], oml[:], AF.Sigmoid), reads=[r_const], writes=[r_const])

    wsem = [d[2], d[3], d[4], d[5], d[6]]
    wlist = [(hh, wh) for hh in range(NHH) for wh in range(4)]

    def load_w(n):
        if n >= len(wlist):
            return
        a, c = wlist[n]
        b.dma("pool", wring[n % NWB][:], whg[a][:, c, :, :], wsem[n % NWB], writes=[r_wr[n % NWB]])
    for n in range(4):
        load_w(n)
    wcount = [0]

    def next_w():
        n = wcount[0]
        wcount[0] += 1
        return n, wring[n % NWB], r_wr[n % NWB]

    emit_norm_mod(b, pr, x, r_x, h, r_h, NT, NP, vec, 0, 1, 2, mods, 0, 1, ones, r_const,
                  ([xsq0[:], xsq0[:]], [r_xsq0, r_xsq0], rstd[:], r_rstd, gm, r_gm, gms, r_gms, [t32a[:], t32a[:]], [r_t32a, r_t32a]))
    B1 = sets[1]
    for nm in f32_names + bf_names + ["Vt", "S0", "S0b", "Khms", "KhmTs", "At", "Khm", "KhmT", "At_b", "Khm_b", "KhmT_b", "S", "Sbf", "S2", "Sbf2", "dec", "pbA", "pbB", "Ats"]:
        rr = getattr(B1, "r_" + nm)
        rr.r = list(r_x.r)
        rr.w = r_x.w

    outs = []
    dsem_set = [dict(s0=d[7], s0b=d[8], qb=d[9], sg=d[10], ol=d[11], sn=d[12], se=d[13]),
                dict(s0=d[14], s0b=d[15], qb=d[16], sg=d[17], ol=d[18], sn=d[19], se=d[20])]

    def proj_fm(dst_fn):
        n_, w, rw = next_w()
        for (s, n) in ntiles(NT):
            pt, rp = pr.next()
            for k in range(KC):
                b.op("pe", lambda e, pt=pt, w=w, k=k, s=s, n=n: e.matmul(
                    pt[:, 0:n], w[:, k, :], h[:, k, s:s + n], start=(k == 0), stop=(k == KC - 1)),
                    reads=[rw, r_h], writes=[rp], sig=(k == KC - 1))
            dst_fn(pt, rp, s, n)
            yield
        load_w(n_ + 4)

    def prep(hh):
        B = sets[hh % 2]; ds = dsem_set[hh % 2]
        b.dma("sp", B.S0, s0d[hh], ds["s0"], writes=[B.r_S0])
        b.dma("pool", B.S0b, s0d[hh], ds["s0b"], writes=[B.r_S0b])
        yield from proj_fm(lambda pt, rp, s, n: b.op("act", lambda e: e.activation(B.qf[:, s:s + n], pt[:, 0:n], AF.Silu),
                                                     reads=[rp], writes=[B.r_qf]))
        yield from proj_fm(lambda pt, rp, s, n: b.op("act", lambda e: e.activation(B.kf[:, s:s + n], pt[:, 0:n], AF.Sigmoid, scale=-1.0),
                                                     reads=[rp], writes=[B.r_kf]))
        b.op("dve", lambda e: e.tensor_scalar(B.kf, B.kf, oml[:, hh:hh + 1], None, ALU.mult), reads=[B.r_kf, r_const], writes=[B.r_kf])
        b.op("act", lambda e: e.activation(B.bA, B.kf, AF.Ln, bias=1.0, scale=-1.0), reads=[B.r_kf], writes=[B.r_bA])
        yield
        n_, w, rw = next_w()
        for blk in range(NBLK + 1):
            m = 128 if blk < NBLK else NS
            c0 = blk * 128
            pt, rp = pr.next()
            for k in range(KC):
                b.op("pe", lambda e, pt=pt, w=w, k=k, c0=c0, m=m: e.matmul(
                    pt[0:m, 0:128], h[:, k, c0:c0 + m], w[:, k, :], start=(k == 0), stop=(k == KC - 1)),
                    reads=[rw, r_h], writes=[rp], sig=(k == KC - 1))
            b.op("act", lambda e, pt=pt, blk=blk, m=m: e.activation(B.Vt[0:m, blk, :], pt[0:m, 0:128], AF.Copy),
                 reads=[rp], writes=[B.r_Vt])
            yield
        load_w(n_ + 4)
        yield from proj_fm(lambda pt, rp, s, n: b.op("act", lambda e: e.activation(B.sg[:, s:s + n], pt[:, 0:n], AF.Silu),
                                                     reads=[rp], writes=[B.r_sg]))
        outs.append(b.dma("sp", sgo[hh], B.sg, ds["sg"], reads=[B.r_sg]))
        b.op("dve", lambda e: e.tensor_tensor_scan(B.bB, smask[:], B.bA, 0.0, ALU.mult, ALU.add),
             reads=[B.r_bA, r_const], writes=[B.r_bB])
        bb, rbb = B.bB, B.r_bB
        other, rother = B.bA, B.r_bA
        yield
        bv = bb[:, 0:NP].rearrange("p (c t) -> p c t", t=CH)
        ov = other[:, 0:NP].rearrange("p (c t) -> p c t", t=CH)
        b.op("act", lambda e: e.activation(B.dec[:, 0:NCK].unsqueeze(2), bv[:, :, CH - 1:CH], AF.Exp), reads=[rbb], writes=[B.r_dec])
        b.op("dve", lambda e: e.tensor_tensor(ov, bv[:, :, CH - 1:CH].broadcast_to([128, NCK, CH]), bv, ALU.subtract),
             reads=[rbb], writes=[rother])
        bs_ = bb[:, NP:NT].rearrange("p (c t) -> p c t", t=4)
        os_ = other[:, NP:NT].rearrange("p (c t) -> p c t", t=4)
        b.op("act", lambda e: e.activation(B.dec[:, NCK:NCK + 16].unsqueeze(2), bs_[:, :, 3:4], AF.Exp), reads=[rbb], writes=[B.r_dec])
        b.op("dve", lambda e: e.tensor_tensor(os_, bs_[:, :, 3:4].broadcast_to([128, 16, 4]), bs_, ALU.subtract),
             reads=[rbb], writes=[rother])
        b.op("act", lambda e: e.activation(other[:, 0:NT], other[:, 0:NT], AF.Exp), reads=[rother], writes=[rother])
        b.op("dve", lambda e: e.tensor_tensor(B.kh, B.kf, other[:, 0:NT], ALU.mult), reads=[rother, B.r_kf], writes=[B.r_kh])
        yield
        b.op("act", lambda e: e.activation(other[:, 0:NT], bb[:, 0:NT], AF.Exp), reads=[rbb, B.r_kh], writes=[rother])
        b.op("dve", lambda e: e.tensor_tensor(B.qt, B.qf, other[:, 0:NT], ALU.mult), reads=[rother, B.r_qf], writes=[B.r_qt])
        b.op("act", lambda e: e.activation(other[:, 0:NT], bb[:, 0:NT], AF.Exp, scale=-1.0), reads=[rbb, B.r_qt], writes=[rother])
        b.op("dve", lambda e: e.tensor_tensor(B.kt, B.kf, other[:, 0:NT], ALU.mult), reads=[rother, B.r_kf], writes=[B.r_kt])
        yield
        b.op("pool", lambda e: e.tensor_copy(B.pbA.unsqueeze(2), bv[:, :, CH - 1:CH]), reads=[rbb], writes=[B.r_pbA])
        b.op("dve", lambda e: e.tensor_tensor_scan(B.pbB, onesf[:], B.pbA, 0.0, ALU.mult, ALU.add),
             reads=[B.r_pbA, r_const], writes=[B.r_pbB])
        pin, rpin = B.pbB, B.r_pbB
        pex, rpex = B.pbA, B.r_pbA
        b.op("act", lambda e: e.activation(Dall[:, hh:hh + 1], pin[:, NCK - 1:NCK], AF.Exp), reads=[rpin], writes=[r_D])
        b.op("pool", lambda e: e.tensor_tensor(pex.unsqueeze(2), pin.unsqueeze(2), bv[:, :, CH - 1:CH], ALU.subtract),
             reads=[rpin, rbb], writes=[rpex])
        b.op("act", lambda e: e.activation(pex, pex, AF.Exp), reads=[rpex], writes=[rpex])
        b.op("pool", lambda e: e.tensor_tensor(
            B.QBf[:, 0:NP].rearrange("p (c t) -> p c t", t=CH), B.qt[:, 0:NP].rearrange("p (c t) -> p c t", t=CH),
            pex.unsqueeze(2).broadcast_to([128, NCK, CH]), ALU.mult), reads=[rpex, B.r_qt], writes=[B.r_QBf])
        b.op("pool", lambda e: e.memset(B.QBf[:, NP:NT], 0.0), writes=[B.r_QBf])
        outs.append(b.dma("sp", qbo[hh], B.QBf, ds["qb"], reads=[B.r_QBf]))
        b.op("pool", lambda e: e.memset(B.S, 0.0), writes=[B.r_S])
        b.op("pool", lambda e: e.memset(B.Sbf, 0.0), writes=[B.r_Sbf])
        yield

    def loop(hh):
        B = sets[hh % 2]; ds = dsem_set[hh % 2]
        def front(blk):
            c0 = blk * 128
            Khm, rKhm, KhmT, rKhmT, At, rAt = ((B.Khm, B.r_Khm, B.KhmT, B.r_KhmT, B.At, B.r_At) if blk % 2 == 0 else
                                               (B.Khm_b, B.r_Khm_b, B.KhmT_b, B.r_KhmT_b, B.At_b, B.r_At_b))
            b.op("dve", lambda e: e.tensor_tensor(Khm, B.kh[:, c0:c0 + 128].unsqueeze(1).broadcast_to([128, 4, 128]), cm[:], ALU.mult),
                 reads=[B.r_kh, r_const], writes=[rKhm])
            ptT, rpT = pr.next()
            ptTb = ptT[:, :].bitcast(BF16)
            for c in range(4):
                b.op("pe", lambda e, c=c: e.transpose(ptTb[:, c * 128:(c + 1) * 128], Khm[:, c, :], ident[:]),
                     reads=[rKhm, r_const], writes=[rpT], sig=(c == 3))
            b.op("act", lambda e: e.activation(KhmT.rearrange("p c k -> p (c k)"), ptTb[:, 0:512], AF.Copy),
                 reads=[rpT], writes=[rKhmT])
            pa, rpa = pr.next()
            b.op("pe", lambda e: e.matmul(pa[:, 0:128], B.kt[:, c0:c0 + 128], B.qt[:, c0:c0 + 128], start=True, stop=True),
                 reads=[B.r_kt, B.r_qt], writes=[rpa])
            b.op("dve", lambda e: e.tensor_tensor(At, pa[:, 0:128], m01[:], ALU.mult), reads=[rpa, r_const], writes=[rAt])
            po, rpo = po_banks[blk % 2]
            b.op("pe", lambda e: e.matmul(po[:, 0:128], B.Vt[:, blk, :], At, start=True, stop=False),
                 reads=[B.r_Vt, rAt], writes=[rpo], sig=False)
            pu, rpu = pu_banks[blk % 2]
            for c in range(4):
                b.op("pe", lambda e, c=c: e.matmul(pu[:, c * 128:(c + 1) * 128], KhmT[:, c, :], B.Vt[:, blk, :], start=True, stop=True),
                     reads=[rKhmT, B.r_Vt], writes=[rpu], sig=(c == 3))

        front(0)
        for blk in range(NBLK):
            c0 = blk * 128
            if blk + 1 < NBLK:
                front(blk + 1)
            yield
            po, rpo = po_banks[blk % 2]
            pu, rpu = pu_banks[blk % 2]
            for c in range(4):
                ck = blk * 4 + c
                Sc, rSc, Sn, rSn = (B.S, B.r_S, B.S2, B.r_S2) if ck % 2 == 0 else (B.S2, B.r_S2, B.S, B.r_S)
                Sbc, rSbc, Sbn, rSbn = (B.Sbf, B.r_Sbf, B.Sbf2, B.r_Sbf2) if ck % 2 == 0 else (B.Sbf2, B.r_Sbf2, B.Sbf, B.r_Sbf)
                b.op("pe", lambda e, c=c: e.matmul(po[:, c * CH:(c + 1) * CH], Sbc, B.qt[:, c0 + c * CH:c0 + (c + 1) * CH],
                                                   start=False, stop=(c == 3)),
                     reads=[rSbc, B.r_qt], writes=[rpo], sig=True)
                b.op("dve", lambda e, c=c, ck=ck: e.scalar_tensor_tensor(Sn, Sc, B.dec[:, ck:ck + 1], pu[:, c * 128:(c + 1) * 128], ALU.mult, ALU.add),
                     reads=[rpu, rSc, B.r_dec], writes=[rSn])
                b.op("pool", lambda e: e.tensor_copy(Sbn, Sn), reads=[rSn], writes=[rSbn])
                yield
            b.op("act", lambda e: e.activation(B.Oraw[:, c0:c0 + 128], po[:, 0:128], AF.Copy), reads=[rpo], writes=[B.r_Oraw])
        outs.append(b.dma("sp", send[:, hh, :], B.S, ds["se"], reads=[B.r_S]))
        b.op("dve", lambda e: e.tensor_tensor(B.Khms, B.kh[:, NP:NT].unsqueeze(1).broadcast_to([128, 16, 64]), cms[:], ALU.mult),
             reads=[B.r_kh, r_const], writes=[B.r_Khms])
        for half in range(2):
            ptT, rpT = pr.next()
            ptTb = ptT[:, :].bitcast(BF16)
            for s8 in range(8):
                sq = half * 8 + s8
                b.op("pe", lambda e, ptTb=ptTb, s8=s8, sq=sq: e.transpose(ptTb[0:64, s8 * 128:(s8 + 1) * 128], B.Khms[:, sq, :], ident[:]),
                     reads=[B.r_Khms, r_const], writes=[rpT], sig=(s8 == 7))
            b.op("act", lambda e, ptTb=ptTb, half=half: e.activation(
                B.KhmTs[:, half * 8:half * 8 + 8, :].rearrange("p c k -> p (c k)"), ptTb[0:64, 0:1024], AF.Copy),
                reads=[rpT], writes=[B.r_KhmTs])
            yield
        pa, rpa = pr.next()
        b.op("pe", lambda e, pa=pa: e.matmul(pa[0:64, 0:64], B.kt[:, NP:NT], B.qt[:, NP:NT], start=True, stop=True),
             reads=[B.r_kt, B.r_qt], writes=[rpa])
        b.op("dve", lambda e, pa=pa: e.tensor_tensor(B.Ats, pa[0:64, 0:64], ms[:], ALU.mult), reads=[rpa, r_const], writes=[B.r_Ats])
        po, rpo = pr.next()
        b.op("pe", lambda e, po=po: e.matmul(po[:, 0:64], B.Vt[0:64, NBLK, :], B.Ats, start=True, stop=False),
             reads=[B.r_Vt, B.r_Ats], writes=[rpo], sig=False)
        for sq in range(16):
            b.op("pe", lambda e, po=po, sq=sq: e.matmul(po[:, sq * 4:sq * 4 + 4], B.S0b[:, sq, :], B.qt[:, NP + sq * 4:NP + sq * 4 + 4],
                                                       start=False, stop=(sq == 15)),
                 reads=[B.r_S0b, B.r_qt], writes=[rpo], sig=(sq == 15))
        b.op("act", lambda e, po=po: e.activation(B.Oraw[:, NP:NT], po[:, 0:64], AF.Copy), reads=[rpo], writes=[B.r_Oraw])
        outs.append(b.dma("sp", oloc[hh], B.Oraw, ds["ol"], reads=[B.r_Oraw]))
        yield
        for q4 in range(4):
            pu, rpu = pr.next()
            for s4 in range(4):
                sq = q4 * 4 + s4
                b.op("pe", lambda e, pu=pu, s4=s4, sq=sq: e.matmul(pu[:, s4 * 128:(s4 + 1) * 128], B.KhmTs[:, sq, :], B.Vt[0:64, NBLK, :],
                                                                 start=True, stop=True),
                     reads=[B.r_KhmTs, B.r_Vt], writes=[rpu], sig=(s4 == 3))
            for s4 in range(4):
                sq = q4 * 4 + s4
                b.op("dve", lambda e, pu=pu, s4=s4, sq=sq: e.scalar_tensor_tensor(
                    B.S0[:, sq, :], B.S0[:, sq, :], B.dec[:, NCK + sq:NCK + sq + 1], pu[:, s4 * 128:(s4 + 1) * 128], ALU.mult, ALU.add),
                    reads=[rpu, B.r_S0, B.r_dec], writes=[B.r_S0])
            yield
        outs.append(b.dma("sp", snew[hh], B.S0, ds["sn"], reads=[B.r_S0]))

    def drain(g):
        for _ in g:
            pass

    drain(prep(0))
    for hh in range(NHH):
        crit = loop(hh)
        fill = prep(hh + 1) if hh + 1 < NHH else iter(())
        done_c = done_f = False
        while not (done_c and done_f):
            if not done_c:
                try:
                    next(crit)
                except StopIteration:
                    done_c = True
            if not done_f:
                try:
                    next(fill)
                except StopIteration:
                    done_f = True
    outs.append(b.dma("sp", dout, Dall[:], d[21], reads=[r_D]))
    b.wait_all("sp", outs)
    b.emit(); b.close()
    return nc


def fm(a):
    n = a.shape[0]
    return np.ascontiguousarray(a.T.reshape(16, 128, n).transpose(1, 0, 2))
def unfm(t):
    return np.ascontiguousarray(t.transpose(2, 1, 0).reshape(t.shape[2], D))
def vfm(v):
    return v.reshape(-1, 128).T
def mods_fm(m3):
    return np.ascontiguousarray(m3.reshape(3, 16, 16, 128).transpose(3, 0, 2, 1))
def tile_w_in(w):
    return np.ascontiguousarray(w.reshape(16, 128, 2, 44, 128).transpose(3, 1, 2, 0, 4))
def tile_w_out(w):
    return np.ascontiguousarray(w.reshape(4, 11, 128, 16, 128).transpose(0, 3, 2, 1, 4))
def tile_cols(w):
    return w.reshape(16, 128, w.shape[1]).transpose(1, 0, 2)
def tile_wqkv(w):
    out = np.empty((8, 128, 4, 16, 128), np.float32)
    for g in range(8):
        out[g, :, 0] = tile_cols(w[:, 256 * g:256 * g + 128])
        out[g, :, 1] = tile_cols(w[:, 256 * g + 128:256 * g + 256])
        kk = w[:, 2048 + 64 * g:2048 + 64 * g + 64]; vv = w[:, 2560 + 64 * g:2560 + 64 * g + 64]
        out[g, :, 2] = tile_cols(np.concatenate([kk, kk], 1))
        out[g, :, 3] = tile_cols(np.concatenate([vv, vv], 1))
    return out
def tile_sq(w):
    return np.ascontiguousarray(w.reshape(16, 128, 16, 128).transpose(2, 1, 0, 3))
NEG = -30000.0
def attn_consts():
    s = np.arange(128)[:, None]; q = np.arange(128)[None, :]
    nd = np.zeros((128, 2, 128), np.float32); mk = np.zeros((128, 2, 128), np.float32)
    nd[:, 0] = -(q - s + 128); mk[:, 0] = np.where(s > q, 0, NEG)
    nd[:, 1] = -(q - s); mk[:, 1] = np.where(s <= q, 0, NEG)
    nd = np.where(mk < 0, 0, nd).astype(np.float32)
    t = np.arange(4)[None, :]
    ndc = -(128 + t - s).astype(np.float32); mkc = np.where(s > t, 0, NEG).astype(np.float32)
    ndc = np.where(mkc < 0, 0, ndc).astype(np.float32)
    a = np.arange(64)
    same = (a[:, None] // 4) == (a[None, :] // 4)
    tp = a[:, None] % 4; tq = a[None, :] % 4
    ok = same & (tp <= tq)
    ndn = np.where(ok, -(tq - tp), 0).astype(np.float32); mkn = np.where(ok, 0, NEG).astype(np.float32)
    return dict(nd=nd, mk=mk, ndc=ndc, mkc=mkc, ndn=ndn, mkn=mkn)
def tile_whg(w):
    out = np.empty((16, 128, 4, 16, 128), np.float32)
    for hh in range(16):
        for wh in range(4):
            out[hh, :, wh] = tile_cols(w[:, wh * 2048 + hh * 128: wh * 2048 + hh * 128 + 128])
    return out
def hgrn_consts():
    a = np.arange(128)
    m01 = ((a[:, None] // 32 == a[None, :] // 32) & (a[:, None] <= a[None, :])).astype(np.float32)
    cm = np.broadcast_to((a[None, :] // 32 == np.arange(4)[:, None]).astype(np.float32)[None], (128, 4, 128)).copy()
    s = np.arange(64)
    ms = ((s[:, None] // 4 == s[None, :] // 4) & (s[:, None] <= s[None, :])).astype(np.float32)
    cms = np.broadcast_to((s[None, :] // 4 == np.arange(16)[:, None]).astype(np.float32)[None], (128, 16, 64)).copy()
    t = np.arange(1088)
    sm = np.where(t < 1024, (t % 32) != 0, ((t - 1024) % 4) != 0).astype(np.float32)
    smask = np.ascontiguousarray(np.broadcast_to(sm[None], (128, 1088)))
    return dict(m01=m01, cm=cm, ms=ms, cms=cms, ident=np.eye(128, dtype=np.float32), smask=smask)

NCORE = 8
_PROGS = {}


def _prog(name, fn):
    if name not in _PROGS:
        _PROGS[name] = fn()
    return _PROGS[name]


def _run(name, fn, in_maps):
    nc = _prog(name, fn)
    res = run_bass_kernel_spmd(nc, in_maps, core_ids=list(range(NCORE)))
    return res.results


def _f32(a):
    return np.ascontiguousarray(np.asarray(a, dtype=np.float32))


def kernel(x_prompt, x_sample, cache_swa_k, cache_swa_v, state_hgrn, state_ffn_conv, c_prompt, c_sample,
           norm1_g, norm2_g, w_ada, b_ada, attn_w_qkv, attn_w_o, attn_sinks,
           hgrn_w_in, hgrn_lower_bounds, hgrn_norm_g, hgrn_w_o,
           ffn_w_in, ffn_conv_w, ffn_conv_b, ffn_w_out, final_norm_g):
    (x_prompt, x_sample, cache_swa_k, cache_swa_v, state_hgrn, state_ffn_conv, c_prompt, c_sample,
     norm1_g, norm2_g, w_ada, b_ada, attn_w_qkv, attn_w_o, attn_sinks,
     hgrn_w_in, hgrn_lower_bounds, hgrn_norm_g, hgrn_w_o,
     ffn_w_in, ffn_conv_w, ffn_conv_b, ffn_w_out, final_norm_g) = [_f32(a) for a in (
        x_prompt, x_sample, cache_swa_k, cache_swa_v, state_hgrn, state_ffn_conv, c_prompt, c_sample,
        norm1_g, norm2_g, w_ada, b_ada, attn_w_qkv, attn_w_o, attn_sinks,
        hgrn_w_in, hgrn_lower_bounds, hgrn_norm_g, hgrn_w_o,
        ffn_w_in, ffn_conv_w, ffn_conv_b, ffn_w_out, final_norm_g)]
    Dm = 2048
    xp = x_prompt[0]
    xs = x_sample.reshape(128 * 4, Dm)

    c_all = np.concatenate([c_prompt, c_sample], 0)
    cT = fm(c_all)
    maps = []
    for c in range(NCORE):
        wt = np.empty((24, 128, 16, 128), np.float32)
        bt = np.empty((128, 24), np.float32)
        for n in range(24):
            l, ch = n // 12, 12 * c + n % 12
            wt[n] = tile_cols(w_ada[l][:, ch * 128:(ch + 1) * 128])
            bt[:, n] = b_ada[l][ch * 128:(ch + 1) * 128]
        maps.append({"cT": cT, "wada": wt, "bada": bt})
    res = _run("adaln", build_adaln, maps)
    mod = np.empty((2, 129, 6 * Dm), np.float32)
    for c in range(NCORE):
        mt = res[c]["modT"]
        for n in range(24):
            l, ch = n // 12, 12 * c + n % 12
            mod[l][:, ch * 128:(ch + 1) * 128] = mt[:, n, :].T
    mod = mod.reshape(2, 129, 6, Dm)

    def vec_for(l, g_vec, i0, extra=None):
        rows = [vfm(g_vec), vfm(mod[l, 0, i0]), vfm(mod[l, 0, i0 + 1]), vfm(mod[l, 0, i0 + 2])]
        if extra is not None:
            rows.append(vfm(extra))
        return np.ascontiguousarray(np.stack(rows, 1))

    def mods_for(l, c, i0):
        sl = slice(1 + 16 * c, 1 + 16 * c + 16)
        return mods_fm(np.stack([mod[l, sl, i0], mod[l, sl, i0 + 1], mod[l, sl, i0 + 2]], 0))

    aconst = attn_consts()
    wq_t = tile_wqkv(attn_w_qkv)
    wo_t = tile_sq(attn_w_o)
    sinks_b = np.ascontiguousarray(np.broadcast_to(attn_sinks[None], (128, 32)))
    vec0 = vec_for(0, norm1_g[0], 0)
    maps = []
    for c in range(NCORE):
        halo = xp[1024 * c - 128:1024 * c] if c > 0 else np.zeros((128, Dm), np.float32)
        xc = np.concatenate([halo, xp[1024 * c:1024 * (c + 1)], xs[64 * c:64 * (c + 1)]], 0)
        m = {"xT": fm(xc), "vec": vec0, "mods": mods_for(0, c, 0), "wqkv": wq_t, "wo": wo_t,
             "kcT": np.ascontiguousarray(cache_swa_k[16 * c:16 * c + 16].transpose(2, 3, 0, 1)),
             "vc": np.ascontiguousarray(cache_swa_v[16 * c:16 * c + 16].transpose(2, 1, 0, 3)),
             "sinks": sinks_b, "hb": np.full((128, 1), NEG if c == 0 else 0.0, np.float32)}
        m.update(aconst)
        maps.append(m)
    res = _run("attn", build_attn2, maps)
    x1 = [unfm(res[c]["xo"]) for c in range(NCORE)]
    ko = res[NCORE - 1]["kout"].reshape(2, 64, 4, 192).transpose(1, 2, 0, 3).reshape(64, 8, 192)
    swa_k_prompt = np.ascontiguousarray(ko[:, :, :128].transpose(2, 1, 0))[None]
    swa_v_prompt = np.ascontiguousarray(res[NCORE - 1]["vout"][:, 0])[None]
    swa_k_sample = np.empty((128, 4, 8, 64), np.float32)
    swa_v_sample = np.empty((128, 4, 8, 64), np.float32)
    for c in range(NCORE):
        ko = res[c]["kout"].reshape(2, 64, 4, 192).transpose(1, 2, 0, 3).reshape(64, 8, 192)
        swa_k_sample[16 * c:16 * c + 16] = ko[:, :, 128:].transpose(2, 1, 0).reshape(16, 4, 8, 64)
        swa_v_sample[16 * c:16 * c + 16] = res[c]["vout"][:64, 1].reshape(16, 4, 8, 64)

    def run_ffn(l, xin, last):
        w_in_t = tile_w_in(ffn_w_in[l])
        w_out_t = tile_w_out(ffn_w_out[l])
        convw = np.ascontiguousarray(np.concatenate([ffn_conv_w[l], ffn_conv_b[l][None]], 0).reshape(4, 44, 128).transpose(2, 1, 0))
        vec = vec_for(l, norm2_g[l], 3, extra=final_norm_g)
        maps = []
        for c in range(NCORE):
            halo = xin[c - 1][1022:1024] if c > 0 else np.zeros((2, Dm), np.float32)
            xc = np.concatenate([halo, xin[c]], 0)
            maps.append({"xT": fm(xc), "vec": vec, "mods": mods_for(l, c, 3), "w_in": w_in_t, "w_out": w_out_t,
                         "convw": convw,
                         "cstate": np.ascontiguousarray(state_ffn_conv[l, 16 * c:16 * c + 16].reshape(16, 2, 44, 128).transpose(3, 2, 0, 1)),
                         "flag": np.full((128, 1), 0.0 if c == 0 else 1.0, np.float32)})
        res = _run("ffn_last" if last else "ffn", (lambda: build_ffn(True)) if last else (lambda: build_ffn(False)), maps)
        xout = [unfm(res[c]["xo"]) for c in range(NCORE)]
        cbp = np.ascontiguousarray(res[NCORE - 1]["cbp"].transpose(2, 1, 0).reshape(2, 5632))[None]
        cbs = np.concatenate([res[c]["cbs"].transpose(2, 3, 1, 0).reshape(16, 2, 5632) for c in range(NCORE)], 0)
        return xout, cbp, cbs

    x2, cbp0, cbs0 = run_ffn(0, x1, False)

    hconst = hgrn_consts()
    whg_t = tile_whg(hgrn_w_in)
    lbp = np.ascontiguousarray(hgrn_lower_bounds.reshape(2, 16, 128).transpose(2, 0, 1))
    vec1 = vec_for(1, norm1_g[1], 0)
    maps = []
    for c in range(NCORE):
        m = {"xT": fm(x2[c]), "vec": vec1, "mods": mods_for(1, c, 0), "whg": whg_t, "lbp": lbp,
             "s0": np.ascontiguousarray(state_hgrn[16 * c:16 * c + 16].transpose(1, 2, 0, 3))}
        m.update(hconst)
        maps.append(m)
    resA = _run("hgrnA", build_hgrna, maps)
    s_loc = [resA[c]["send"] for c in range(NCORE)]
    d_loc = [resA[c]["dout"] for c in range(NCORE)]
    hgrn_state_sample = np.concatenate([resA[c]["snew"].transpose(2, 0, 1, 3) for c in range(NCORE)], 0)

    wo2_t = tile_sq(hgrn_w_o)
    ng = np.ascontiguousarray(hgrn_norm_g.reshape(16, 128).T)
    maps = []
    for c in range(NCORE):
        sr = np.zeros((16, 128, NRK, 128), np.float32)
        dr = np.ones((128, 16, NRK), np.float32)
        for r in range(c):
            sr[:, :, r, :] = s_loc[r].transpose(1, 0, 2)
            dr[:, :, r] = d_loc[r]
        maps.append({"xT": fm(x2[c]), "vec": vec1, "mods": mods_for(1, c, 0), "oloc": resA[c]["oloc"], "qb": resA[c]["qbo"],
                     "sg": resA[c]["sgo"], "sr": sr, "dr": dr, "sloc": s_loc[c], "dl": d_loc[c], "ng": ng, "wo": wo2_t})
    res = _run("hgrnB", build_hgrnb, maps)
    x3 = [unfm(res[c]["xo"]) for c in range(NCORE)]
    hgrn_state_prompt = np.ascontiguousarray(res[NCORE - 1]["send"].transpose(1, 0, 2))[None]

    y, cbp1, cbs1 = run_ffn(1, x3, True)
    y_prompt = np.concatenate([y[c][:1024] for c in range(NCORE)], 0)[None]
    y_sample = np.concatenate([y[c][1024:] for c in range(NCORE)], 0).reshape(128, 4, Dm)
    ffn_conv_prompt = np.stack([cbp0, cbp1], 0)
    ffn_conv_sample = np.stack([cbs0, cbs1], 0)
    outs = (y_prompt, y_sample, swa_k_prompt, swa_v_prompt, swa_k_sample, swa_v_sample,
            hgrn_state_prompt, hgrn_state_sample, ffn_conv_prompt, ffn_conv_sample)
    return tuple(np.ascontiguousarray(o, dtype=np.float32) for o in outs)
```

```python
from concourse.bass_utils import run_bass_kernel_spmd

import contextlib
import numpy as np
import concourse.bass as bass
import concourse.mybir as mybir

F32 = mybir.dt.float32
BF16 = mybir.dt.bfloat16
I32 = mybir.dt.int32
AF = mybir.ActivationFunctionType
ALU = mybir.AluOpType
AX = mybir.AxisListType

ENGS = ["sp", "act", "pool", "dve", "pe"]


class Res:
    __slots__ = ("name", "w", "r")

    def __init__(self, name=""):
        self.name = name
        self.w = None
        self.r = []


class DSem:
    def __init__(self, handle, name):
        self.h = handle
        self.name = name
        self.cnt = 0


class _Rec:
    def __getattr__(self, name):
        return lambda *a, **k: (name, a, k)


_REC = _Rec()


class Builder:
    def __init__(self, nc, n_dsem=12):
        self.nc = nc
        self.es = contextlib.ExitStack()
        self.q = {e: [] for e in ENGS}
        self.cnt = {e: 0 for e in ENGS}
        self.waited = {e: {} for e in ENGS}
        self.esem = {e: self.es.enter_context(nc.semaphore("s_" + e)) for e in ENGS}
        self.dsems = [DSem(self.es.enter_context(nc.semaphore("d%d" % i)), "d%d" % i)
                      for i in range(n_dsem)]
        self.pending = {e: [] for e in ENGS}
        self.n_inst = 0

    def sb(self, name, shape, dt):
        return self.es.enter_context(self.nc.sbuf_tensor(name, list(shape), dt))

    def ps(self, name, shape, dt=F32):
        return self.es.enter_context(self.nc.psum_tensor(name, list(shape), dt))

    def _deps(self, eng, reads, writes):
        need = {}

        def add(t):
            if t is None:
                return
            k, v = t
            if need.get(k, 0) < v:
                need[k] = v
        for r in reads:
            add(r.w)
        for w in writes:
            add(w.w)
            for t in w.r:
                add(t)
        waits = []
        for k, v in need.items():
            if k == "pe" and eng == "pe":
                continue
            if self.waited[eng].get(k, 0) >= v:
                continue
            self.waited[eng][k] = v
            waits.append((k, v))
        return waits

    def _semh(self, k):
        return self.esem[k] if isinstance(k, str) else k.h

    def op(self, eng, fn, reads=(), writes=(), sig=True):
        reads = [r for r in reads if r is not None]
        writes = [w for w in writes if w is not None]
        waits = self._deps(eng, reads, writes)
        for k, v in waits:
            cur = self.cnt[k] if isinstance(k, str) else k.cnt
            assert v <= cur, "forward wait %s %d > %d" % (k, v, cur)
        ticket = None
        if sig:
            self.cnt[eng] += 1
            ticket = (eng, self.cnt[eng])
            pend = self.pending[eng]
            self.pending[eng] = []
            for pr, pw in pend:
                self._commit(pr, pw, ticket)
            self._commit(reads, writes, ticket)
        else:
            t = (eng, self.cnt[eng] + 1)
            self._commit(reads, writes, t)
        self.q[eng].append((waits, fn(_REC), ticket, None))
        self.n_inst += 1
        return ticket

    def _commit(self, reads, writes, ticket):
        for r in reads:
            r.r.append(ticket)
        for w in writes:
            w.w = ticket
            w.r = []

    def dma(self, eng, out, in_, dsem, reads=(), writes=(), **kw):
        reads = [r for r in reads if r is not None]
        writes = [w for w in writes if w is not None]
        waits = self._deps(eng, reads, writes)
        for k, v in waits:
            cur = self.cnt[k] if isinstance(k, str) else k.cnt
            assert v <= cur, "forward wait %s %d > %d" % (k, v, cur)
        dsem.cnt += 16
        ticket = (dsem, dsem.cnt)
        self._commit(reads, writes, ticket)
        kw2 = dict(kw); kw2["out"] = out; kw2["in_"] = in_
        self.q[eng].append((waits, ("dma_start", (), kw2), None, (dsem, 16)))
        self.n_inst += 1
        return ticket

    def wait_all(self, eng, tickets):
        waits = []
        for t in tickets:
            if t is None:
                continue
            k, v = t
            if self.waited[eng].get(k, 0) >= v:
                continue
            self.waited[eng][k] = v
            waits.append((k, v))
        self.q[eng].append((waits, None, None, None))

    def emit(self):
        nc = self.nc
        handles = {"sp": "sync", "act": "scalar", "pool": "gpsimd", "dve": "vector", "pe": "tensor"}
        with nc.Block() as block:
            for eng in ENGS:
                items = self.q[eng]
                if not items:
                    continue

                def body(e, items=items, eng=eng):
                    for waits, fn, ticket, dinc in items:
                        for k, v in waits:
                            e.wait_ge(self._semh(k), v)
                        if fn is None:
                            continue
                        name, a, k = fn
                        ins = getattr(e, name)(*a, **k)
                        if ticket is not None:
                            ins.then_inc(self.esem[eng], 1)
                        if dinc is not None:
                            ins.then_inc(dinc[0].h, dinc[1])
                getattr(block, handles[eng])(body)

    def close(self):
        self.es.close()


D = 2048
KC = 16
DFF = 5632
FC = 44
NQ = 4
FQ = FC // NQ
NP = 1024
NS = 64
EPS = 1e-6


class PsumRot:
    def __init__(self, b, n=8):
        self.tiles = [b.ps("psb%d" % i, [128, 512]) for i in range(n)]
        self.res = [Res("psb%d" % i) for i in range(n)]
        self.i = 0

    def next(self):
        t, r = self.tiles[self.i], self.res[self.i]
        self.i = (self.i + 1) % len(self.tiles)
        return t, r


def ntiles(n, step=512):
    return [(s, min(step, n - s)) for s in range(0, n, step)]


def emit_norm_mod(b, pr, x, r_x, h, r_h, ncol, np_cols, vec, iv_g, iv_sh, iv_sc, mods, im_sh, im_sc,
                  ones, r_const, tmp):
    xsq, r_xsq, rstd, r_rstd, gm, r_gm, gms, r_gms, t32, r_t32 = tmp
    tiles = ntiles(ncol)
    banks = [pr.next() for _ in tiles]
    for k in range(KC):
        i2 = k % 2
        b.op("act", lambda e, k=k, i2=i2: e.activation(xsq[i2][:, 0:ncol], x[:, k, 0:ncol], AF.Square),
             reads=[r_x], writes=[r_xsq[i2]])
        for ti, (s, n) in enumerate(tiles):
            pt, rp = banks[ti]
            b.op("pe", lambda e, pt=pt, s=s, n=n, i2=i2, k=k: e.matmul(
                pt[:, 0:n], ones[:], xsq[i2][:, s:s + n], start=(k == 0), stop=(k == KC - 1)),
                reads=[r_const, r_xsq[i2]], writes=[rp], sig=(k == KC - 1) or ti == len(tiles) - 1)
    for ti, (s, n) in enumerate(tiles):
        pt, rp = banks[ti]
        b.op("act", lambda e, pt=pt, s=s, n=n: e.activation(rstd[:, s:s + n], pt[:, 0:n], AF.Sqrt,
                                                            bias=EPS, scale=1.0 / D),
             reads=[rp], writes=[r_rstd])
    b.op("dve", lambda e: e.reciprocal(rstd[:, 0:ncol], rstd[:, 0:ncol]), reads=[r_rstd], writes=[r_rstd])
    b.op("dve", lambda e: e.scalar_tensor_tensor(gm[:], vec[:, iv_sc, :], 1.0, vec[:, iv_g, :], ALU.add, ALU.mult),
         reads=[r_const], writes=[r_gm])
    ns = ncol - np_cols
    if ns:
        b.op("dve", lambda e: e.scalar_tensor_tensor(
            gms[:], mods[:, im_sc, :, :], 1.0, vec[:, iv_g, :].unsqueeze(2).broadcast_to([128, KC, 16]),
            ALU.add, ALU.mult), reads=[r_const], writes=[r_gms])
    for k in range(KC):
        i2 = k % 2
        b.op("dve", lambda e, k=k, i2=i2: e.scalar_tensor_tensor(
            t32[i2][:, 0:np_cols], x[:, k, 0:np_cols], gm[:, k:k + 1], rstd[:, 0:np_cols], ALU.mult, ALU.mult),
            reads=[r_x, r_gm, r_rstd], writes=[r_t32[i2]])
        if ns:
            b.op("dve", lambda e, k=k, i2=i2: e.tensor_tensor(
                t32[i2][:, np_cols:ncol].rearrange("p (s t) -> p s t", t=4),
                x[:, k, np_cols:ncol].rearrange("p (s t) -> p s t", t=4),
                gms[:, k, :].unsqueeze(2).broadcast_to([128, 16, 4]), ALU.mult),
                reads=[r_x, r_gms], writes=[r_t32[i2]])
            b.op("dve", lambda e, k=k, i2=i2: e.tensor_tensor(
                t32[i2][:, np_cols:ncol], t32[i2][:, np_cols:ncol], rstd[:, np_cols:ncol], ALU.mult),
                reads=[r_t32[i2], r_rstd], writes=[r_t32[i2]])
            b.op("dve", lambda e, k=k, i2=i2: e.tensor_tensor(
                t32[i2][:, np_cols:ncol].rearrange("p (s t) -> p s t", t=4),
                t32[i2][:, np_cols:ncol].rearrange("p (s t) -> p s t", t=4),
                mods[:, im_sh, k, :].unsqueeze(2).broadcast_to([128, 16, 4]), ALU.add),
                reads=[r_t32[i2], r_const], writes=[r_t32[i2]])
            b.op("act", lambda e, k=k, i2=i2: e.activation(h[:, k, np_cols:ncol], t32[i2][:, np_cols:ncol], AF.Copy),
                 reads=[r_t32[i2]], writes=[r_h])
        b.op("act", lambda e, k=k, i2=i2: e.activation(
            h[:, k, 0:np_cols], t32[i2][:, 0:np_cols], AF.Identity, bias=vec[:, iv_sh, k:k + 1], scale=1.0),
            reads=[r_t32[i2], r_const], writes=[r_h])


def build_ffn(last):
    nc = bass.Bass("TRN2", target_bir_lowering=False)
    NCOL = 2 + NP + NS
    UW = 2 + NP + 16 * 6
    AW = UW - 2
    dram = lambda name, shape, kind="ExternalInput": nc.dram_tensor(name, list(shape), F32, kind=kind).ap()
    xT = dram("xT", [128, KC, NCOL])
    vecd = dram("vec", [128, 5, KC])
    modsd = dram("mods", [128, 3, KC, 16])
    w_in = dram("w_in", [FC, 128, 2, KC, 128])
    w_out = dram("w_out", [NQ, KC, 128, FQ, 128])
    convd = dram("convw", [128, FC, 4])
    cstd = dram("cstate", [128, FC, 16, 2])
    flagd = dram("flag", [128, 1])
    xo = dram("xo", [128, KC, NP + NS], "ExternalOutput")
    cbp = dram("cbp", [128, FC, 2], "ExternalOutput")
    cbs = dram("cbs", [128, FC, 16, 2], "ExternalOutput")

    b = Builder(nc, n_dsem=14)
    d = b.dsems
    pr = PsumRot(b)
    x = b.sb("x", [128, KC, NCOL], F32)
    h = b.sb("h", [128, KC, NCOL], BF16)
    act = b.sb("act", [128, FQ, NP + NS], BF16)
    U = [b.sb("U%d" % i, [128, UW], F32) for i in range(2)]
    G = [b.sb("G%d" % i, [128, NP + NS], F32) for i in range(2)]
    A = b.sb("A", [128, UW], F32)
    rstd = b.sb("rstd", [128, NCOL], F32)
    xsq = [b.sb("xsq%d" % i, [128, NCOL], BF16) for i in range(2)]
    t32 = [b.sb("t32%d" % i, [128, NCOL], F32) for i in range(2)]
    win = [b.sb("win%d" % i, [128, 2, KC, 128], BF16) for i in range(2)]
    wout = [b.sb("wout%d" % i, [128, FQ, 128], BF16) for i in range(2)]
    vec = b.sb("vecs", [128, 5, KC], F32)
    mods = b.sb("modss", [128, 3, KC, 16], F32)
    convw = b.sb("convws", [128, FC, 4], F32)
    cst = b.sb("csts", [128, FC, 16, 2], F32)
    cbps = b.sb("cbps", [128, FC, 2], F32)
    cbss = b.sb("cbss", [128, FC, 16, 2], F32)
    flag = b.sb("flags", [128, 1], F32)
    ones = b.sb("ones", [128, 128], BF16)
    gm = b.sb("gm", [128, KC], F32)
    gms = b.sb("gms", [128, KC, 16], F32)
    tmps = b.sb("tmps", [128, NS], F32)

    R = lambda n: Res(n)
    r_x, r_h, r_act, r_A, r_rstd, r_const, r_gm, r_gms, r_cb, r_tmps = [R(n) for n in
        "x h act A rstd const gm gms cb tmps".split()]
    r_U = [R("U0"), R("U1")]; r_G = [R("G0"), R("G1")]
    r_xsq = [R("xsq0"), R("xsq1")]; r_t32 = [R("t0"), R("t1")]
    r_win = [R("win0"), R("win1")]; r_wout = [R("wo0"), R("wo1")]

    b.dma("sp", x[:], xT, d[0], writes=[r_x])
    b.dma("sp", vec[:], vecd, d[1], writes=[r_const])
    b.dma("sp", mods[:], modsd, d[1], writes=[r_const])
    b.dma("sp", convw[:], convd, d[1], writes=[r_const])
    b.dma("sp", cst[:], cstd, d[1], writes=[r_const])
    b.dma("sp", flag[:], flagd, d[1], writes=[r_const])
    b.op("pool", lambda e: e.memset(ones[:], 1.0), writes=[r_const])

    win_sem = [d[2], d[3]]
    wout_sem = [d[4], d[5]]

    def load_win(j):
        b.dma("pool", win[j % 2][:], w_in[j], win_sem[j % 2], writes=[r_win[j % 2]])

    def load_wout(q, i):
        n = q * KC + i
        b.dma("pool", wout[n % 2][:], w_out[q, i], wout_sem[n % 2], writes=[r_wout[n % 2]])

    load_win(0)
    emit_norm_mod(b, pr, x, r_x, h, r_h, NCOL, 2 + NP, vec, 0, 1, 2, mods, 0, 1, ones, r_const,
                  (xsq, r_xsq, rstd, r_rstd, gm, r_gm, gms, r_gms, t32, r_t32))

    tiles = ntiles(NCOL)
    for q in range(NQ):
        for jj in range(FQ):
            j = q * FQ + jj
            if j + 1 < FC:
                load_win(j + 1)
            w = win[j % 2]; rw = r_win[j % 2]
            Uj, rU = U[j % 2], r_U[j % 2]
            Gj, rG = G[j % 2], r_G[j % 2]
            for which in range(2):
                for (s, n) in tiles:
                    pt, rp = pr.next()
                    for k in range(KC):
                        b.op("pe", lambda e, pt=pt, w=w, which=which, k=k, s=s, n=n: e.matmul(
                            pt[:, 0:n], w[:, which, k, :], h[:, k, s:s + n], start=(k == 0), stop=(k == KC - 1)),
                            reads=[rw, r_h], writes=[rp], sig=(k == KC - 1))
                    if which == 0:
                        if s + n <= 2 + NP:
                            b.op("act", lambda e, pt=pt, s=s, n=n, Uj=Uj: e.activation(Uj[:, s:s + n], pt[:, 0:n], AF.Copy),
                                 reads=[rp], writes=[rU])
                        else:
                            npart = 2 + NP - s
                            b.op("act", lambda e, pt=pt, s=s, npart=npart, Uj=Uj: e.activation(
                                Uj[:, s:s + npart], pt[:, 0:npart], AF.Copy), reads=[rp], writes=[rU])
                            b.op("act", lambda e, pt=pt, npart=npart, Uj=Uj: e.activation(
                                Uj[:, 2 + NP:UW].rearrange("p (s c) -> p s c", c=6)[:, :, 2:6],
                                pt[:, npart:npart + NS].rearrange("p (s t) -> p s t", t=4), AF.Copy),
                                reads=[rp], writes=[rU])
                    else:
                        if s == 0:
                            b.op("act", lambda e, pt=pt, n=n, Gj=Gj: e.activation(Gj[:, 0:n - 2], pt[:, 2:n], AF.Copy),
                                 reads=[rp], writes=[rG])
                        else:
                            b.op("act", lambda e, pt=pt, s=s, n=n, Gj=Gj: e.activation(Gj[:, s - 2:s - 2 + n], pt[:, 0:n], AF.Copy),
                                 reads=[rp], writes=[rG])
            b.op("pool", lambda e, Uj=Uj: e.tensor_scalar(Uj[:, 0:2], Uj[:, 0:2], flag[:, 0:1], None, ALU.mult),
                 reads=[rU, r_const], writes=[rU])
            b.op("pool", lambda e, Uj=Uj, j=j: e.tensor_copy(
                Uj[:, 2 + NP:UW].rearrange("p (s c) -> p s c", c=6)[:, :, 0:2], cst[:, j, :, :]),
                reads=[r_const], writes=[rU])
            b.op("dve", lambda e, Uj=Uj, j=j: e.tensor_scalar(A[:, 0:AW], Uj[:, 2:UW], convw[:, j, 2:3], None, ALU.mult),
                 reads=[rU, r_const], writes=[r_A])
            b.op("dve", lambda e, Uj=Uj, j=j: e.scalar_tensor_tensor(A[:, 0:AW], Uj[:, 1:UW - 1], convw[:, j, 1:2], A[:, 0:AW],
                                                                 ALU.mult, ALU.add), reads=[rU, r_A, r_const], writes=[r_A])
            b.op("dve", lambda e, Uj=Uj, j=j: e.scalar_tensor_tensor(A[:, 0:AW], Uj[:, 0:AW], convw[:, j, 0:1], A[:, 0:AW],
                                                                 ALU.mult, ALU.add), reads=[rU, r_A, r_const], writes=[r_A])
            b.op("act", lambda e, j=j: e.activation(A[:, 0:AW], A[:, 0:AW], AF.Gelu, bias=convw[:, j, 3:4], scale=1.0),
                 reads=[r_A, r_const], writes=[r_A])
            b.op("dve", lambda e, jj=jj, Gj=Gj: e.tensor_tensor(act[:, jj, 0:NP], A[:, 0:NP], Gj[:, 0:NP], ALU.mult),
                 reads=[r_A, rG], writes=[r_act])
            b.op("dve", lambda e, jj=jj, Gj=Gj: e.tensor_tensor(
                act[:, jj, NP:NP + NS].rearrange("p (s t) -> p s t", t=4),
                A[:, NP + 2:UW].rearrange("p (s c) -> p s c", c=6)[:, :, 0:4],
                Gj[:, NP:NP + NS].rearrange("p (s t) -> p s t", t=4), ALU.mult),
                reads=[r_A, rG], writes=[r_act])
            b.op("pool", lambda e, Uj=Uj, j=j: e.tensor_copy(cbps[:, j, :], Uj[:, NP:NP + 2]), reads=[rU], writes=[r_cb])
            b.op("pool", lambda e, Uj=Uj, j=j: e.tensor_copy(
                cbss[:, j, :, :], Uj[:, 2 + NP:UW].rearrange("p (s c) -> p s c", c=6)[:, :, 4:6]), reads=[rU], writes=[r_cb])
        load_wout(q, 0)
        for i in range(KC):
            if i + 1 < KC:
                load_wout(q, i + 1)
            n_ = q * KC + i
            w = wout[n_ % 2]; rw = r_wout[n_ % 2]
            for (s, n) in ntiles(NP + NS):
                pt, rp = pr.next()
                for jj in range(FQ):
                    b.op("pe", lambda e, pt=pt, w=w, jj=jj, s=s, n=n: e.matmul(
                        pt[:, 0:n], w[:, jj, :], act[:, jj, s:s + n], start=(jj == 0), stop=(jj == FQ - 1)),
                        reads=[rw, r_act], writes=[rp], sig=(jj == FQ - 1))
                if s + n <= NP:
                    b.op("dve", lambda e, pt=pt, i=i, s=s, n=n: e.scalar_tensor_tensor(
                        x[:, i, 2 + s:2 + s + n], pt[:, 0:n], vec[:, 3, i:i + 1], x[:, i, 2 + s:2 + s + n], ALU.mult, ALU.add),
                        reads=[rp, r_const, r_x], writes=[r_x])
                else:
                    assert s == NP and n == NS
                    b.op("dve", lambda e, pt=pt, i=i: e.tensor_tensor(
                        tmps[:].rearrange("p (s t) -> p s t", t=4), pt[:, 0:NS].rearrange("p (s t) -> p s t", t=4),
                        mods[:, 2, i, :].unsqueeze(2).broadcast_to([128, 16, 4]), ALU.mult),
                        reads=[rp, r_const], writes=[r_tmps])
                    b.op("dve", lambda e, i=i: e.tensor_tensor(x[:, i, 2 + NP:NCOL], x[:, i, 2 + NP:NCOL], tmps[:], ALU.add),
                         reads=[r_tmps, r_x], writes=[r_x])
    outs = []
    if last:
        tl = ntiles(NCOL)
        banks = [pr.next() for _ in tl]
        for k in range(KC):
            i2 = k % 2
            b.op("act", lambda e, k=k, i2=i2: e.activation(xsq[i2][:, 0:NCOL], x[:, k, 0:NCOL], AF.Square),
                 reads=[r_x], writes=[r_xsq[i2]])
            for ti, (s, n) in enumerate(tl):
                pt, rp = banks[ti]
                b.op("pe", lambda e, pt=pt, s=s, n=n, i2=i2, k=k: e.matmul(
                    pt[:, 0:n], ones[:], xsq[i2][:, s:s + n], start=(k == 0), stop=(k == KC - 1)),
                    reads=[r_const, r_xsq[i2]], writes=[rp], sig=True)
        for ti, (s, n) in enumerate(tl):
            pt, rp = banks[ti]
            b.op("act", lambda e, pt=pt, s=s, n=n: e.activation(rstd[:, s:s + n], pt[:, 0:n], AF.Sqrt, bias=EPS, scale=1.0 / D),
                 reads=[rp], writes=[r_rstd])
        b.op("dve", lambda e: e.reciprocal(rstd[:, 0:NCOL], rstd[:, 0:NCOL]), reads=[r_rstd], writes=[r_rstd])
        for k in range(KC):
            b.op("dve", lambda e, k=k: e.scalar_tensor_tensor(
                x[:, k, :], x[:, k, :], vec[:, 4, k:k + 1], rstd[:, 0:NCOL], ALU.mult, ALU.mult),
                reads=[r_x, r_rstd, r_const], writes=[r_x])
    outs.append(b.dma("sp", xo, x[:, :, 2:NCOL], d[6], reads=[r_x]))
    outs.append(b.dma("sp", cbp, cbps[:], d[7], reads=[r_cb]))
    outs.append(b.dma("sp", cbs, cbss[:], d[8], reads=[r_cb]))
    b.wait_all("sp", outs)
    b.emit()
    b.close()
    return nc


NKV = 8
SCALE = 64 ** -0.5
NEG = -30000.0


def alibi_slope(h):
    return float(2.0 ** (-8.0 * (h + 1) / 32))


def build_attn2():
    nc = bass.Bass("TRN2", target_bir_lowering=False)
    NH = 128
    NCOL = NH + NP + NS
    NQC = NP + NS
    NB = 8
    dram = lambda name, shape, kind="ExternalInput": nc.dram_tensor(name, list(shape), F32, kind=kind).ap()
    xT = dram("xT", [128, KC, NCOL])
    vecd = dram("vec", [128, 4, KC])
    modsd = dram("mods", [128, 3, KC, 16])
    wqkv = dram("wqkv", [NKV, 128, 4, KC, 128])
    wo = dram("wo", [KC, 128, KC, 128])
    kcT = dram("kcT", [NKV, 64, 16, 128])
    vc = dram("vc", [NKV, 128, 16, 64])
    ndd = dram("nd", [128, 2, 128]); mkd = dram("mk", [128, 2, 128])
    ndcd = dram("ndc", [128, 4]); mkcd = dram("mkc", [128, 4])
    ndnd = dram("ndn", [64, 64]); mknd = dram("mkn", [64, 64])
    sinkd = dram("sinks", [128, 32])
    hbd = dram("hb", [128, 1])
    xo = dram("xo", [128, KC, NQC], "ExternalOutput")
    kout = dram("kout", [128, NKV // 2, 192], "ExternalOutput")
    vout = dram("vout", [128, 2, NKV, 64], "ExternalOutput")

    b = Builder(nc, n_dsem=16)
    d = b.dsems
    pr = PsumRot(b)
    xbuf = b.sb("xbuf", [128, KC, NCOL], F32)
    x = xbuf
    OT = xbuf[:].rearrange("p k n -> p (k n)").bitcast(BF16)[:, 0:KC * NQC].rearrange("p (k n) -> p k n", k=KC)
    h = b.sb("h", [128, KC, NCOL], BF16)
    rstd = b.sb("rstd", [128, NCOL], F32)
    xsq0 = b.sb("xsq0", [128, NCOL], BF16); xsq = [xsq0, xsq0]
    t32a = b.sb("t32a", [128, NCOL], F32); t32 = [t32a, t32a]
    NWB = 4
    wring = [b.sb("wr%d" % i, [128, KC, 128], BF16) for i in range(NWB)]
    Qgs = [b.sb("Qg%d" % i, [128, 2, NQC], BF16) for i in range(2)]
    Klos = [b.sb("Klo%d" % i, [128, NCOL], BF16) for i in range(2)]
    Khis = [b.sb("Khi%d" % i, [128, NCOL], BF16) for i in range(2)]
    Vds = [b.sb("Vd%d" % i, [128, 10, 64], BF16) for i in range(2)]
    Kclo = b.sb("Kclo", [128, 16, 128], BF16); Kchi = b.sb("Kchi", [128, 16, 128], BF16)
    Vcd = b.sb("Vcd", [128, 16, 64], BF16)
    sc = [b.sb("sc%d" % i, [128, 512], F32) for i in range(2)]
    P = [b.sb("P%d" % i, [128, 512], BF16) for i in range(4)]
    rden = b.sb("rden", [128, 512], F32)
    biasg = b.sb("biasg", [128, 4, 2, 128], F32)
    biasc = b.sb("biasc", [128, 4, 4], F32)
    biasn = b.sb("biasn", [64, 4, 64], F32)
    Pc = b.sb("Pc", [128, 16, 16], BF16)
    Pn = b.sb("Pn", [64, 16, 4, 4], BF16)
    scc = b.sb("scc", [128, 16, 16], F32)
    scn = b.sb("scn", [64, 4, 64], F32)
    vec = b.sb("vecs", [128, 4, KC], F32)
    mods = b.sb("modss", [128, 3, KC, 16], F32)
    nd = b.sb("nds", [128, 2, 128], F32); mk = b.sb("mks", [128, 2, 128], F32)
    ndc = b.sb("ndcs", [128, 4], F32); mkc = b.sb("mkcs", [128, 4], F32)
    ndn = b.sb("ndns", [64, 64], F32); mkn = b.sb("mkns", [64, 64], F32)
    esink = b.sb("esink", [128, 32], F32)
    hb = b.sb("hbs", [128, 1], F32)
    esg = b.sb("esg", [128, 4], F32)
    ones = b.sb("ones", [128, 128], BF16)
    gm = b.sb("gm", [128, KC], F32)
    gms = b.sb("gms", [128, KC, 16], F32)
    koutS = b.sb("koutS", [128, NKV // 2, 192], F32)
    voutS = b.sb("voutS", [128, 2, NKV, 64], F32)
    xr = [t32a[:, 0:NQC], rstd[:, 0:NQC]]
    tmps = b.sb("tmps", [128, NS], F32)

    R = Res
    r_x, r_h, r_rstd, r_const, r_gm, r_gms, r_Q, r_K, r_V, r_Kc, r_Vc, r_rden, r_bias, r_OT = [R(n) for n in
        "x h rstd const gm gms Q K V Kc Vc rden bias OT".split()]
    r_Pc, r_Pn, r_scc, r_scn, r_ko, r_vo, r_tmps = [R(n) for n in "Pc Pn scc scn ko vo tmps".split()]
    r_xsq0 = R("xsq"); r_xsq = [r_xsq0, r_xsq0]; r_t32a = R("t32"); r_t32 = [r_t32a, r_t32a]
    r_wr = [R("wr%d" % i) for i in range(NWB)]
    r_sc = [R("a"), R("b")]; r_P = [R("a") for _ in range(4)]
    r_xr = [r_t32a, r_rstd]
    r_OT = r_x

    b.dma("sp", x[:], xT, d[0], writes=[r_x])
    for i, (dst, src) in enumerate([(vec, vecd), (mods, modsd), (nd, ndd), (mk, mkd), (ndc, ndcd), (mkc, mkcd),
                                    (ndn, ndnd), (mkn, mknd), (esink, sinkd), (hb, hbd)]):
        b.dma("sp", dst[:], src, d[1], writes=[r_const])
    b.op("pool", lambda e: e.memset(ones[:], 1.0), writes=[r_const])
    b.op("pool", lambda e: e.memset(voutS[:], 0.0), writes=[r_vo])
    r_Qs = [Res("Q0"), Res("Q1")]; r_Ks = [Res("K0"), Res("K1")]; r_Vs = [Res("V0"), Res("V1")]
    for i_ in range(2):
        b.op("pool", lambda e, i_=i_: e.memset(Klos[i_][:], 0.0), writes=[r_Ks[i_]])
        b.op("pool", lambda e, i_=i_: e.memset(Khis[i_][:], 0.0), writes=[r_Ks[i_]])
    b.op("pool", lambda e: e.memset(Kclo[:], 0.0), writes=[r_Kc])
    b.op("pool", lambda e: e.memset(Kchi[:], 0.0), writes=[r_Kc])
    b.op("act", lambda e: e.activation(esink[:], esink[:], AF.Exp), reads=[r_const], writes=[r_const])

    wsem = [d[2], d[3], d[8], d[9]]
    NWT = NKV * 4 + KC

    def load_w(n):
        if n >= NWT:
            return
        src = wqkv[n // 4][:, n % 4, :, :] if n < NKV * 4 else wo[n - NKV * 4]
        b.dma("pool", wring[n % NWB][:], src, wsem[n % NWB], writes=[r_wr[n % NWB]])

    for n in range(4):
        load_w(n)
    emit_norm_mod(b, pr, x, r_x, h, r_h, NCOL, NH + NP, vec, 0, 1, 2, mods, 0, 1, ones, r_const,
                  (xsq, r_xsq, rstd, r_rstd, gm, r_gm, gms, r_gms, t32, r_t32))

    def proj(g):
        Qg = Qgs[g % 2]; Klo = Klos[g % 2]; Khi = Khis[g % 2]; Vd = Vds[g % 2]
        r_Q = r_Qs[g % 2]; r_K = r_Ks[g % 2]; r_V = r_Vs[g % 2]
        for which in range(2):
            wn = g * 4 + which; w = wring[wn % NWB]; rw = r_wr[wn % NWB]
            for (s, n) in ntiles(NQC):
                yield
                pt, rp = pr.next()
                for k in range(KC):
                    b.op("pe", lambda e, pt=pt, w=w, which=which, k=k, s=s, n=n: e.matmul(
                        pt[:, 0:n], w[:, k, :], h[:, k, NH + s:NH + s + n], start=(k == 0), stop=(k == KC - 1)),
                        reads=[rw, r_h], writes=[rp], sig=(k == KC - 1))
                b.op("act", lambda e, pt=pt, which=which, s=s, n=n: e.activation(Qg[:, which, s:s + n], pt[:, 0:n], AF.Copy),
                     reads=[rp], writes=[r_Q])
            load_w(wn + 4)
        wn = g * 4 + 2; w = wring[wn % NWB]; rw = r_wr[wn % NWB]
        for (s, n) in ntiles(NCOL):
            yield
            pt, rp = pr.next()
            for k in range(KC):
                b.op("pe", lambda e, pt=pt, w=w, k=k, s=s, n=n: e.matmul(
                    pt[:, 0:n], w[:, k, :], h[:, k, s:s + n], start=(k == 0), stop=(k == KC - 1)),
                    reads=[rw, r_h], writes=[rp], sig=(k == KC - 1))
            b.op("act", lambda e, pt=pt, s=s, n=n: e.activation(Klo[0:64, s:s + n], pt[0:64, 0:n], AF.Copy),
                 reads=[rp], writes=[r_K])
            b.op("act", lambda e, pt=pt, s=s, n=n: e.activation(Khi[64:128, s:s + n], pt[64:128, 0:n], AF.Copy),
                 reads=[rp], writes=[r_K])
            if s == 1024:
                lo = 64 * (g % 2)
                if lo == 0:
                    b.op("act", lambda e, pt=pt, g=g, lo=lo: e.activation(koutS[lo:lo + 64, g // 2, :], pt[lo:lo + 64, 0:192], AF.Copy),
                         reads=[rp], writes=[r_ko])
                else:
                    b.op("act", lambda e, pt=pt, g=g, lo=lo: e.activation(koutS[lo:lo + 64, g // 2, :], pt[lo:lo + 64, 0:192], AF.Copy),
                         reads=[rp], writes=[r_ko])
        load_w(wn + 4)
        wn = g * 4 + 3; w = wring[wn % NWB]; rw = r_wr[wn % NWB]
        for blk in range(10):
            m = 128 if blk < 9 else NS
            c0 = blk * 128
            yield
            pt, rp = pr.next()
            for k in range(KC):
                b.op("pe", lambda e, pt=pt, w=w, k=k, c0=c0, m=m: e.matmul(
                    pt[0:m, 0:64], h[:, k, c0:c0 + m], w[:, k, 0:64], start=(k == 0), stop=(k == KC - 1)),
                    reads=[rw, r_h], writes=[rp], sig=(k == KC - 1))
            b.op("dve", lambda e, pt=pt, blk=blk, m=m: e.tensor_copy(Vd[0:m, blk, :], pt[0:m, 0:64]),
                 reads=[rp], writes=[r_V])
            if blk >= 8:
                b.op("dve", lambda e, pt=pt, blk=blk, m=m, g=g: e.tensor_copy(voutS[0:m, blk - 8, g, :], pt[0:m, 0:64]),
                     reads=[rp], writes=[r_vo])
        load_w(wn + 4)
        yield

    def attn(g):
        Qg = Qgs[g % 2]; Klo = Klos[g % 2]; Khi = Khis[g % 2]; Vd = Vds[g % 2]
        r_Q = r_Qs[g % 2]; r_K = r_Ks[g % 2]; r_V = r_Vs[g % 2]
        b.dma("pool", Kclo[0:64, :, :], kcT[g], d[4], writes=[r_Kc])
        b.dma("pool", Kchi[64:128, :, :], kcT[g], d[5], writes=[r_Kc])
        b.dma("pool", Vcd[:], vc[g], d[6], writes=[r_Vc])
        b.op("dve", lambda e, g=g: e.tensor_copy(esg[:], esink[:, 4 * g:4 * g + 4]), reads=[r_const], writes=[r_bias])
        for hq in range(4):
            sl = alibi_slope(4 * g + hq)
            b.op("dve", lambda e, hq=hq, sl=sl: e.scalar_tensor_tensor(biasg[:, hq, :, :], nd[:], sl, mk[:], ALU.mult, ALU.add),
                 reads=[r_const], writes=[r_bias])
            b.op("dve", lambda e, hq=hq, sl=sl: e.scalar_tensor_tensor(biasc[:, hq, :], ndc[:], sl, mkc[:], ALU.mult, ALU.add),
                 reads=[r_const], writes=[r_bias])
            b.op("dve", lambda e, hq=hq, sl=sl: e.scalar_tensor_tensor(biasn[:, hq, :], ndn[:], sl, mkn[:], ALU.mult, ALU.add),
                 reads=[r_const], writes=[r_bias])
        for i in range(1, NB + 1):
            yield
            qc = (i - 1) * 128
            ptd, rpd = pr.next()
            ptv, rpv = pr.next()
            Pp = []
            for pair in range(2):
                pts, rps = pr.next()
                for hh in range(2):
                    hq = pair * 2 + hh
                    Kx = Klo if hh == 0 else Khi
                    for j in range(2):
                        kc0 = (i - 1 + j) * 128
                        b.op("pe", lambda e, pts=pts, hh=hh, j=j, Kx=Kx, kc0=kc0, pair=pair, qc=qc: e.matmul(
                            pts[:, (hh * 2 + j) * 128:(hh * 2 + j + 1) * 128], Kx[:, kc0:kc0 + 128], Qg[:, pair, qc:qc + 128],
                            start=True, stop=True), reads=[r_K, r_Q], writes=[rps], sig=(hh == 1 and j == 1))
                si = pair
                b.op("dve", lambda e, pts=pts, si=si, pair=pair: e.scalar_tensor_tensor(
                    sc[si][:], pts[:, 0:512], SCALE, biasg[:, pair * 2:pair * 2 + 2, :, :].rearrange("p a b c -> p (a b c)"),
                    ALU.mult, ALU.add), reads=[rps, r_bias], writes=[r_sc[si]])
                if i == 1:
                    b.op("dve", lambda e, si=si: e.tensor_scalar(
                        sc[si][:].rearrange("p (a b c) -> p a b c", a=2, b=2)[:, :, 0, :],
                        sc[si][:].rearrange("p (a b c) -> p a b c", a=2, b=2)[:, :, 0, :], hb[:, 0:1], None, ALU.add),
                        reads=[r_sc[si], r_const], writes=[r_sc[si]])
                pi = (i % 2) * 2 + pair
                b.op("act", lambda e, pi=pi, si=si: e.activation(P[pi][:], sc[si][:], AF.Exp),
                     reads=[r_sc[si]], writes=[r_P[pi]])
                Pp.append(pi)
            for pair in range(2):
                pi = Pp[pair]
                for hh in range(2):
                    hq = pair * 2 + hh
                    for j in range(2):
                        b.op("pe", lambda e, pi=pi, hh=hh, j=j, hq=hq: e.matmul(
                            ptd[:, hq * 128:(hq + 1) * 128], ones[:], P[pi][:, (hh * 2 + j) * 128:(hh * 2 + j + 1) * 128],
                            start=(j == 0), stop=(j == 1)), reads=[r_P[pi], r_const], writes=[rpd],
                            sig=(pair == 1 and hh == 1 and j == 1))
            for pair in range(2):
                pi = Pp[pair]
                for hh in range(2):
                    hq = pair * 2 + hh
                    for j in range(2):
                        blk = i - 1 + j
                        b.op("pe", lambda e, pi=pi, hh=hh, j=j, hq=hq, blk=blk: e.matmul(
                            ptv[64 * hh:64 * hh + 64, hq * 128:(hq + 1) * 128], Vd[:, blk, :], P[pi][:, (hh * 2 + j) * 128:(hh * 2 + j + 1) * 128],
                            start=(j == 0), stop=(j == 1)), reads=[r_P[pi], r_V], writes=[rpv],
                            sig=(pair == 1 and hh == 1 and j == 1))
            b.op("dve", lambda e, ptd=ptd, g=g: e.tensor_tensor(
                rden[:].rearrange("p (a q) -> p a q", a=4), ptd[:, 0:512].rearrange("p (a q) -> p a q", a=4),
                esg[:].unsqueeze(2).broadcast_to([128, 4, 128]), ALU.add),
                reads=[rpd, r_bias], writes=[r_rden])
            b.op("act", lambda e: e.activation(rden[:], rden[:], AF.Ln), reads=[r_rden], writes=[r_rden])
            b.op("act", lambda e: e.activation(rden[:], rden[:], AF.Exp, scale=-1.0), reads=[r_rden], writes=[r_rden])
            for hq in range(4):
                lo = 0 if hq % 2 == 0 else 64
                ch = (2 * g + hq // 2)
                b.op("dve", lambda e, ptv=ptv, hq=hq, lo=lo, ch=ch, qc=qc: e.tensor_tensor(
                    OT[lo:lo + 64, ch, qc:qc + 128], ptv[lo:lo + 64, hq * 128:(hq + 1) * 128],
                    rden[lo:lo + 64, hq * 128:(hq + 1) * 128], ALU.mult),
                    reads=[rpv, r_rden, r_x], writes=[r_OT])
        yield
        ptc, rpc = pr.next()
        ptn, rpn = pr.next()
        for sq in range(16):
            for hq in range(4):
                Kx = Kclo if hq % 2 == 0 else Kchi
                b.op("pe", lambda e, sq=sq, hq=hq, Kx=Kx: e.matmul(
                    ptc[:, sq * 16 + hq * 4:sq * 16 + hq * 4 + 4], Kx[:, sq, :], Qg[:, hq // 2, NP + sq * 4:NP + sq * 4 + 4],
                    start=True, stop=True), reads=[r_Kc, r_Q], writes=[rpc], sig=(sq == 15 and hq == 3))
        for hq in range(4):
            Kx = Klo if hq % 2 == 0 else Khi
            b.op("pe", lambda e, hq=hq, Kx=Kx: e.matmul(
                ptn[0:64, hq * 64:(hq + 1) * 64], Kx[:, NH + NP:NCOL], Qg[:, hq // 2, NP:NQC],
                start=True, stop=True), reads=[r_K, r_Q], writes=[rpn], sig=(hq == 3))
        b.op("dve", lambda e: e.scalar_tensor_tensor(
            scc[:].rearrange("p s (a t) -> p s a t", a=4), ptc[:, 0:256].rearrange("p (s a t) -> p s a t", s=16, a=4),
            SCALE, biasc[:].unsqueeze(1).broadcast_to([128, 16, 4, 4]), ALU.mult, ALU.add),
            reads=[rpc, r_bias], writes=[r_scc])
        b.op("act", lambda e: e.activation(Pc[:], scc[:], AF.Exp), reads=[r_scc], writes=[r_Pc])
        b.op("dve", lambda e: e.scalar_tensor_tensor(
            scn[:].rearrange("p a n -> p (a n)"), ptn[0:64, 0:256], SCALE, biasn[:].rearrange("p a n -> p (a n)"),
            ALU.mult, ALU.add), reads=[rpn, r_bias], writes=[r_scn])
        b.op("act", lambda e: e.activation(
            Pn[:].rearrange("p s a t -> p a s t"), scn[:].rearrange("p a (s t) -> p a s t", t=4), AF.Exp),
            reads=[r_scn], writes=[r_Pn])
        ptd, rpd = pr.next()
        ptv, rpv = pr.next()
        b.op("pe", lambda e: e.matmul(ptd[:, 0:256], ones[:], Pc[:].rearrange("p s c -> p (s c)"), start=True, stop=False),
             reads=[r_Pc, r_const], writes=[rpd], sig=False)
        b.op("pe", lambda e: e.matmul(ptd[:, 0:256], ones[0:64, :], Pn[:].rearrange("p s a t -> p (s a t)"), start=False, stop=True),
             reads=[r_Pn, r_const], writes=[rpd])
        for sq in range(16):
            for lo in (0, 64):
                b.op("pe", lambda e, sq=sq, lo=lo: e.matmul(ptv[lo:lo + 64, sq * 16:(sq + 1) * 16], Vcd[:, sq, :], Pc[:, sq, :], start=True, stop=False),
                     reads=[r_Pc, r_Vc], writes=[rpv], sig=False)
                b.op("pe", lambda e, sq=sq, lo=lo: e.matmul(ptv[lo:lo + 64, sq * 16:(sq + 1) * 16], Vd[0:64, 9, :],
                                                     Pn[:, sq, :, :].rearrange("p a t -> p (a t)"), start=False, stop=True),
                     reads=[r_Pn, r_V], writes=[rpv], sig=(sq == 15 and lo == 64))
        b.op("dve", lambda e, g=g: e.tensor_tensor(
            rden[:, 0:256].rearrange("p (s a t) -> p s a t", s=16, a=4), ptd[:, 0:256].rearrange("p (s a t) -> p s a t", s=16, a=4),
            esg[:].unsqueeze(1).unsqueeze(3).broadcast_to([128, 16, 4, 4]), ALU.add),
            reads=[rpd, r_bias], writes=[r_rden])
        b.op("act", lambda e: e.activation(rden[:, 0:256], rden[:, 0:256], AF.Ln), reads=[r_rden], writes=[r_rden])
        b.op("act", lambda e: e.activation(rden[:, 0:256], rden[:, 0:256], AF.Exp, scale=-1.0), reads=[r_rden], writes=[r_rden])
        for hq in range(4):
            lo = 0 if hq % 2 == 0 else 64
            ch = (2 * g + hq // 2)
            b.op("dve", lambda e, hq=hq, lo=lo, ch=ch: e.tensor_tensor(
                OT[lo:lo + 64, ch, NP:NQC].rearrange("p (s t) -> p s t", t=4),
                ptv[lo:lo + 64, 0:256].rearrange("p (s a t) -> p s a t", s=16, a=4)[:, :, hq, :],
                rden[lo:lo + 64, 0:256].rearrange("p (s a t) -> p s a t", s=16, a=4)[:, :, hq, :], ALU.mult),
                reads=[rpv, r_rden, r_x], writes=[r_OT])

        yield

    def drain(gen):
        for _ in gen:
            pass

    drain(proj(0))
    for g in range(NKV):
        crit = attn(g)
        fill = proj(g + 1) if g + 1 < NKV else iter(())
        done_c = done_f = False
        while not (done_c and done_f):
            if not done_c:
                try:
                    next(crit)
                except StopIteration:
                    done_c = True
            for _ in range(2):
                if not done_f:
                    try:
                        next(fill)
                    except StopIteration:
                        done_f = True
    xsem = [d[11], d[12]]
    osem = [d[13], d[14]]
    outs = []
    for i in range(KC):
        wn = NKV * 4 + i; w = wring[wn % NWB]; rw = r_wr[wn % NWB]
        xi = xr[i % 2]; rxi = r_xr[i % 2]
        b.dma("sp", xi, xT[:, i, NH:NCOL], xsem[i % 2], writes=[rxi])
        for (s, n) in ntiles(NQC):
            pt, rp = pr.next()
            for k in range(KC):
                b.op("pe", lambda e, pt=pt, w=w, k=k, s=s, n=n: e.matmul(
                    pt[:, 0:n], w[:, k, :], OT[:, k, s:s + n], start=(k == 0), stop=(k == KC - 1)),
                    reads=[rw, r_OT], writes=[rp], sig=(k == KC - 1))
            if s + n <= NP:
                b.op("dve", lambda e, pt=pt, i=i, s=s, n=n, xi=xi: e.scalar_tensor_tensor(
                    xi[:, s:s + n], pt[:, 0:n], vec[:, 3, i:i + 1], xi[:, s:s + n], ALU.mult, ALU.add),
                    reads=[rp, r_const, rxi], writes=[rxi])
            else:
                b.op("dve", lambda e, pt=pt, i=i: e.tensor_tensor(
                    tmps[:].rearrange("p (s t) -> p s t", t=4), pt[:, 0:NS].rearrange("p (s t) -> p s t", t=4),
                    mods[:, 2, i, :].unsqueeze(2).broadcast_to([128, 16, 4]), ALU.mult),
                    reads=[rp, r_const], writes=[r_tmps])
                b.op("dve", lambda e, xi=xi: e.tensor_tensor(xi[:, NP:NQC], xi[:, NP:NQC], tmps[:], ALU.add),
                     reads=[r_tmps, rxi], writes=[rxi])
        outs.append(b.dma("sp", xo[:, i, :], xi, osem[i % 2], reads=[rxi]))
        load_w(wn + 4)
    outs.append(b.dma("sp", kout, koutS[:], d[15], reads=[r_ko]))
    outs.append(b.dma("sp", vout, voutS[:], d[7], reads=[r_vo]))
    b.wait_all("sp", outs)
    b.emit()
    b.close()
    return nc


def build_adaln():
    nc = bass.Bass("TRN2", target_bir_lowering=False)
    NSEQ = 129
    NCH = 24
    dram = lambda name, shape, kind="ExternalInput": nc.dram_tensor(name, list(shape), F32, kind=kind).ap()
    cT = dram("cT", [128, KC, NSEQ])
    wada = dram("wada", [NCH, 128, KC, 128])
    bada = dram("bada", [128, NCH])
    modT = dram("modT", [128, NCH, NSEQ], "ExternalOutput")
    b = Builder(nc, n_dsem=8)
    d = b.dsems
    pr = PsumRot(b)
    cs = b.sb("cs", [128, KC, NSEQ], F32)
    sc = b.sb("sc", [128, KC, NSEQ], BF16)
    bs = b.sb("bs", [128, NCH], F32)
    outT = b.sb("outT", [128, NCH, NSEQ], F32)
    NWB = 4
    wr = [b.sb("wr%d" % i, [128, KC, 128], BF16) for i in range(NWB)]
    r_c, r_sc, r_b, r_out = Res(), Res(), Res(), Res()
    r_wr = [Res() for _ in range(NWB)]
    b.dma("sp", cs[:], cT, d[0], writes=[r_c])
    b.dma("sp", bs[:], bada, d[1], writes=[r_b])

    def load_w(n):
        if n < NCH:
            b.dma("pool", wr[n % NWB][:], wada[n], d[2 + n % NWB], writes=[r_wr[n % NWB]])
    for n in range(NWB - 1):
        load_w(n)
    b.op("act", lambda e: e.activation(sc[:], cs[:], AF.Silu), reads=[r_c], writes=[r_sc])
    for n in range(NCH):
        load_w(n + NWB - 1)
        w, rw = wr[n % NWB], r_wr[n % NWB]
        pt, rp = pr.next()
        for k in range(KC):
            b.op("pe", lambda e, pt=pt, w=w, k=k: e.matmul(pt[:, 0:NSEQ], w[:, k, :], sc[:, k, :], start=(k == 0), stop=(k == KC - 1)),
                 reads=[rw, r_sc], writes=[rp], sig=(k == KC - 1))
        b.op("act", lambda e, pt=pt, n=n: e.activation(outT[:, n, :], pt[:, 0:NSEQ], AF.Identity, bias=bs[:, n:n + 1], scale=1.0),
             reads=[rp, r_b], writes=[r_out])
    t = b.dma("sp", modT, outT[:], d[6], reads=[r_out])
    b.wait_all("sp", [t])
    b.emit(); b.close()
    return nc


NHH = 16
CH = 32
NRK = 7


def cumsum_chunks(b, engs, bufA, rA, bufB, rB, ncol0, ncols, clen):
    src, rs, dst, rd = bufA, rA, bufB, rB
    s = 1
    i = 0
    while s < clen:
        sv = src[:, ncol0:ncol0 + ncols].rearrange("p (c t) -> p c t", t=clen)
        dv = dst[:, ncol0:ncol0 + ncols].rearrange("p (c t) -> p c t", t=clen)
        eng = engs[i % len(engs)]
        b.op(eng, lambda e, dv=dv, sv=sv, s=s: e.tensor_tensor(dv[:, :, s:clen], sv[:, :, s:clen], sv[:, :, 0:clen - s], ALU.add),
             reads=[rs], writes=[rd])
        b.op(eng, lambda e, dv=dv, sv=sv, s=s: e.tensor_copy(dv[:, :, 0:s], sv[:, :, 0:s]), reads=[rs], writes=[rd])
        src, rs, dst, rd = dst, rd, src, rs
        s *= 2
        i += 1
    return src, rs


def build_hgrn(pass1, modeA=False):
    nc = bass.Bass("TRN2", target_bir_lowering=False)
    NT = NP if pass1 else NP + NS
    NBLK = NP // 128
    NCK = NP // CH
    dram = lambda name, shape, kind="ExternalInput": nc.dram_tensor(name, list(shape), F32, kind=kind).ap()
    xT = dram("xT", [128, KC, NT])
    vecd = dram("vec", [128, 4, KC])
    modsd = dram("mods", [128, 3, KC, 16])
    whg = dram("whg", [NHH, 128, 4, KC, 128])
    lbpd = dram("lbp", [128, 2, NHH])
    m01d = dram("m01", [128, 128]); cmd = dram("cm", [128, 4, 128])
    identd = dram("ident", [128, 128])
    if not pass1:
        if not modeA:
            wo = dram("wo", [KC, 128, KC, 128])
            ngd = dram("ng", [128, NHH])
            srd = dram("sr", [NHH, 128, NRK, 128])
            drd = dram("dr", [128, NHH, NRK])
            xo = dram("xo", [128, KC, NT], "ExternalOutput")
        else:
            oloc = dram("oloc", [NHH, 128, NT], "ExternalOutput")
            qbo = dram("qbo", [NHH, 128, NT], "ExternalOutput")
            sgo = dram("sgo", [NHH, 128, NT], "ExternalOutput")
        s0d = dram("s0", [NHH, 128, 16, 128])
        msd = dram("ms", [64, 64]); cmsd = dram("cms", [128, 16, 64])
        snew = dram("snew", [NHH, 128, 16, 128], "ExternalOutput")
    send = dram("send", [128, NHH, 128], "ExternalOutput")
    dout = dram("dout", [128, NHH], "ExternalOutput")

    b = Builder(nc, n_dsem=18)
    d = b.dsems
    pr = PsumRot(b)
    xbuf = b.sb("xbuf", [128, KC, NT], F32)
    x = xbuf
    O2T = xbuf[:].rearrange("p k n -> p (k n)").bitcast(BF16)[:, 0:KC * NT].rearrange("p (k n) -> p k n", k=KC)
    h = b.sb("h", [128, KC, NT], BF16)
    rstd = b.sb("rstd", [128, NT], F32)
    xsq0 = b.sb("xsq0", [128, NT], BF16)
    t32a = b.sb("t32a", [128, NT], F32)
    NWB = 5
    wring = [b.sb("wr%d" % i, [128, KC, 128], BF16) for i in range(NWB)]
    qf = b.sb("qf", [128, NT], F32); kf = b.sb("kf", [128, NT], F32)
    bA = b.sb("bA", [128, NT], F32); bB = b.sb("bB", [128, NT], F32)
    sg = b.sb("sg", [128, NT], F32); Oraw = b.sb("Oraw", [128, NT], F32)
    qt = b.sb("qt", [128, NT], BF16); kt = b.sb("kt", [128, NT], BF16); kh = b.sb("kh", [128, NT], BF16)
    dec = b.sb("dec", [128, NCK + 16], F32)
    Vt = b.sb("Vt", [128, NBLK + 1, 128], BF16)
    At = b.sb("At", [128, 128], BF16)
    Khm = b.sb("Khm", [128, 4, 128], BF16)
    KhmT = b.sb("KhmT", [128, 4, 128], BF16)
    S = b.sb("S", [128, 128], F32); Sbf = b.sb("Sbf", [128, 128], BF16)
    Dall = b.sb("Dall", [128, NHH], F32)
    btot = b.sb("btot", [128, 1], F32)
    vec = b.sb("vecs", [128, 4, KC], F32)
    mods = b.sb("modss", [128, 3, KC, 16], F32)
    lbp = b.sb("lbps", [128, 2, NHH], F32)
    oml = b.sb("oml", [128, NHH], F32)
    m01 = b.sb("m01s", [128, 128], F32); cm = b.sb("cms_", [128, 4, 128], F32)
    ident = b.sb("idents", [128, 128], BF16); identf = b.sb("identf", [128, 128], F32)
    ones = b.sb("ones", [128, 128], BF16)
    gm = b.sb("gm", [128, KC], F32); gms = b.sb("gms", [128, KC, 16], F32)
    if not pass1:
        if not modeA:
            ng = b.sb("ngs", [128, NHH], F32)
            sr = b.sb("srs", [128, NRK, 128], F32)
            dr = b.sb("drs", [128, NHH, NRK], F32)
        else:
            QBf = b.sb("QBf", [128, NT], F32)
            pbA = b.sb("pbA", [128, NCK], F32); pbB = b.sb("pbB", [128, NCK], F32)
            r_QBf, r_pbA, r_pbB = Res("QBf"), Res("pbA"), Res("pbB")
        S0 = b.sb("S0", [128, 16, 128], F32); S0b = b.sb("S0b", [128, 16, 128], BF16)
        ms = b.sb("mss", [64, 64], F32); cms = b.sb("cmss", [128, 16, 64], F32)
        Ats = b.sb("Ats", [64, 64], BF16)
        Khms = b.sb("Khms", [128, 16, 64], BF16)
        KhmTs = b.sb("KhmTs", [64, 16, 128], BF16)
        tmps = b.sb("tmps", [128, NS], F32)
    R = Res
    r_x, r_h, r_rstd, r_const, r_gm, r_gms = [R(n) for n in "x h rstd const gm gms".split()]
    r_xsq0, r_t32a = R("xsq"), R("t32")
    r_wr = [R("wr%d" % i) for i in range(NWB)]
    r_qf, r_kf, r_bA, r_bB, r_sg, r_Oraw, r_qt, r_kt, r_kh, r_dec, r_Vt, r_At, r_Khm, r_KhmT, r_S, r_Sbf, r_Sall, r_bt = [
        R(n) for n in "qf kf bA bB sg Oraw qt kt kh dec Vt At Khm KhmT S Sbf Sall bt".split()]
    r_sr, r_S0, r_S0b, r_Ats, r_Khms, r_KhmTs, r_tmps = [R(n) for n in "sr S0 S0b Ats Khms KhmTs tmps".split()]
    r_O2T = r_x

    b.dma("sp", x[:], xT, d[0], writes=[r_x])
    cl = [(vec, vecd), (mods, modsd), (lbp, lbpd), (m01, m01d), (cm, cmd), (identf, identd)]
    if not pass1:
        cl += [(ms, msd), (cms, cmsd)] + ([] if modeA else [(ng, ngd), (dr, drd)])
    for dst, src in cl:
        b.dma("sp", dst[:], src, d[1], writes=[r_const])
    b.op("pool", lambda e: e.memset(ones[:], 1.0), writes=[r_const])
    b.op("act", lambda e: e.activation(ident[:], identf[:], AF.Copy), reads=[r_const], writes=[r_const])
    b.op("dve", lambda e: e.tensor_tensor(oml[:], lbp[:, 0, :], lbp[:, 1, :], ALU.subtract), reads=[r_const], writes=[r_const])
    b.op("act", lambda e: e.activation(oml[:], oml[:], AF.Sigmoid), reads=[r_const], writes=[r_const])

    wsem = [d[2], d[3], d[4], d[5], d[6]]
    NWT = NHH * 4 + (0 if (pass1 or modeA) else KC)
    used = [1, 2] if pass1 else [0, 1, 2, 3]
    wlist = [(hh, wh) for hh in range(NHH) for wh in used] + ([("o", i) for i in range(KC)] if not (pass1 or modeA) else [])

    def load_w(n):
        if n >= len(wlist):
            return
        a, c = wlist[n]
        src = wo[c] if a == "o" else whg[a][:, c, :, :]
        b.dma("pool", wring[n % NWB][:], src, wsem[n % NWB], writes=[r_wr[n % NWB]])
    for n in range(4):
        load_w(n)
    wcount = [0]

    def next_w():
        n = wcount[0]
        wcount[0] += 1
        return n, wring[n % NWB], r_wr[n % NWB]

    emit_norm_mod(b, pr, x, r_x, h, r_h, NT, NP, vec, 0, 1, 2, mods, 0, 1, ones, r_const,
                  ([xsq0, xsq0], [r_xsq0, r_xsq0], rstd, r_rstd, gm, r_gm, gms, r_gms, [t32a, t32a], [r_t32a, r_t32a]))

    def proj_fm(dst_fn):
        n_, w, rw = next_w()
        for (s, n) in ntiles(NT):
            pt, rp = pr.next()
            for k in range(KC):
                b.op("pe", lambda e, pt=pt, w=w, k=k, s=s, n=n: e.matmul(
                    pt[:, 0:n], w[:, k, :], h[:, k, s:s + n], start=(k == 0), stop=(k == KC - 1)),
                    reads=[rw, r_h], writes=[rp], sig=(k == KC - 1))
            dst_fn(pt, rp, s, n)
        load_w(n_ + 4)

    outs = []
    for hh in range(NHH):
        if not pass1:
            if not modeA:
                b.dma("sp", sr[:], srd[hh], d[7], writes=[r_sr])
            b.dma("sp", S0[:], s0d[hh], d[8], writes=[r_S0])
            b.dma("pool", S0b[:], s0d[hh], d[9], writes=[r_S0b])
            proj_fm(lambda pt, rp, s, n: b.op("act", lambda e: e.activation(qf[:, s:s + n], pt[:, 0:n], AF.Silu),
                                              reads=[rp], writes=[r_qf]))
        proj_fm(lambda pt, rp, s, n: b.op("act", lambda e: e.activation(kf[:, s:s + n], pt[:, 0:n], AF.Sigmoid, scale=-1.0),
                                          reads=[rp], writes=[r_kf]))
        b.op("dve", lambda e, hh=hh: e.tensor_scalar(kf[:], kf[:], oml[:, hh:hh + 1], None, ALU.mult),
             reads=[r_kf, r_const], writes=[r_kf])
        b.op("act", lambda e: e.activation(bA[:], kf[:], AF.Ln, bias=1.0, scale=-1.0), reads=[r_kf], writes=[r_bA])
        n_, w, rw = next_w()
        for blk in range(NBLK + (0 if pass1 else 1)):
            m = 128 if blk < NBLK else NS
            c0 = blk * 128
            pt, rp = pr.next()
            for k in range(KC):
                b.op("pe", lambda e, pt=pt, w=w, k=k, c0=c0, m=m: e.matmul(
                    pt[0:m, 0:128], h[:, k, c0:c0 + m], w[:, k, :], start=(k == 0), stop=(k == KC - 1)),
                    reads=[rw, r_h], writes=[rp], sig=(k == KC - 1))
            b.op("act", lambda e, pt=pt, blk=blk, m=m: e.activation(Vt[0:m, blk, :], pt[0:m, 0:128], AF.Copy),
                 reads=[rp], writes=[r_Vt])
        load_w(n_ + 4)
        if not pass1:
            proj_fm(lambda pt, rp, s, n: b.op("act", lambda e: e.activation(sg[:, s:s + n], pt[:, 0:n], AF.Silu),
                                              reads=[rp], writes=[r_sg]))
        bb, rbb = cumsum_chunks(b, ["dve", "pool"], bA, r_bA, bB, r_bB, 0, NP, CH)
        other, rother = (bB, r_bB) if bb is bA else (bA, r_bA)
        if not pass1:
            if bb is not bA:
                b.op("pool", lambda e: e.tensor_copy(bB[:, NP:NT], bA[:, NP:NT]), reads=[r_bA], writes=[r_bB])
            sb_, rsb = cumsum_chunks(b, ["pool"], bb, rbb, other, rother, NP, NS, 4)
            if sb_ is not bb:
                b.op("pool", lambda e, sb_=sb_, bb=bb: e.tensor_copy(bb[:, NP:NT], sb_[:, NP:NT]), reads=[rsb], writes=[rbb])
        bv = bb[:, 0:NP].rearrange("p (c t) -> p c t", t=CH)
        ov = other[:, 0:NP].rearrange("p (c t) -> p c t", t=CH)
        b.op("act", lambda e, bv=bv: e.activation(dec[:, 0:NCK].unsqueeze(2), bv[:, :, CH - 1:CH], AF.Exp), reads=[rbb], writes=[r_dec])
        b.op("dve", lambda e, bv=bv, ov=ov: e.tensor_tensor(ov, bv[:, :, CH - 1:CH].broadcast_to([128, NCK, CH]), bv, ALU.subtract),
             reads=[rbb], writes=[rother])
        if not pass1:
            bs_ = bb[:, NP:NT].rearrange("p (c t) -> p c t", t=4)
            os_ = other[:, NP:NT].rearrange("p (c t) -> p c t", t=4)
            b.op("act", lambda e, bs_=bs_: e.activation(dec[:, NCK:NCK + 16].unsqueeze(2), bs_[:, :, 3:4], AF.Exp), reads=[rbb], writes=[r_dec])
            b.op("dve", lambda e, bs_=bs_, os_=os_: e.tensor_tensor(os_, bs_[:, :, 3:4].broadcast_to([128, 16, 4]), bs_, ALU.subtract),
                 reads=[rbb], writes=[rother])
        b.op("act", lambda e, other=other: e.activation(other[:, 0:NT], other[:, 0:NT], AF.Exp), reads=[rother], writes=[rother])
        b.op("dve", lambda e, other=other: e.tensor_tensor(kh[:], kf[:], other[:, 0:NT], ALU.mult), reads=[rother, r_kf], writes=[r_kh])
        if pass1:
            b.op("dve", lambda e, bv=bv: e.tensor_reduce(btot[:], bv[:, :, CH - 1], AX.X, ALU.add), reads=[rbb], writes=[r_bt])
            b.op("act", lambda e, hh=hh: e.activation(Dall[:, hh:hh + 1], btot[:], AF.Exp), reads=[r_bt], writes=[r_Sall])
        else:
            b.op("act", lambda e, other=other, bb=bb: e.activation(other[:, 0:NT], bb[:, 0:NT], AF.Exp), reads=[rbb, r_kh], writes=[rother])
            b.op("dve", lambda e, other=other: e.tensor_tensor(qt[:], qf[:], other[:, 0:NT], ALU.mult), reads=[rother, r_qf], writes=[r_qt])
            b.op("act", lambda e, other=other, bb=bb: e.activation(other[:, 0:NT], bb[:, 0:NT], AF.Exp, scale=-1.0), reads=[rbb, r_qt], writes=[rother])
            b.op("dve", lambda e, other=other: e.tensor_tensor(kt[:], kf[:], other[:, 0:NT], ALU.mult), reads=[rother, r_kf], writes=[r_kt])
            if modeA:
                b.op("pool", lambda e, bv=bv: e.tensor_copy(pbA[:].unsqueeze(2), bv[:, :, CH - 1:CH]), reads=[rbb], writes=[r_pbA])
                pin, rpin = cumsum_chunks(b, ["pool"], pbA, r_pbA, pbB, r_pbB, 0, NCK, NCK)
                pex, rpex = (pbB, r_pbB) if pin is pbA else (pbA, r_pbA)
                b.op("act", lambda e, hh=hh, pin=pin: e.activation(Dall[:, hh:hh + 1], pin[:, NCK - 1:NCK], AF.Exp), reads=[rpin], writes=[r_Sall])
                b.op("pool", lambda e, pin=pin, pex=pex, bv=bv: e.tensor_tensor(pex[:].unsqueeze(2), pin[:].unsqueeze(2), bv[:, :, CH - 1:CH], ALU.subtract),
                     reads=[rpin, rbb], writes=[rpex])
                b.op("act", lambda e, pex=pex: e.activation(pex[:], pex[:], AF.Exp), reads=[rpex], writes=[rpex])
                b.op("pool", lambda e, pex=pex: e.tensor_tensor(
                    QBf[:, 0:NP].rearrange("p (c t) -> p c t", t=CH), qt[:, 0:NP].rearrange("p (c t) -> p c t", t=CH),
                    pex[:].unsqueeze(2).broadcast_to([128, NCK, CH]), ALU.mult), reads=[rpex, r_qt], writes=[r_QBf])
                b.op("pool", lambda e: e.memset(QBf[:, NP:NT], 0.0), writes=[r_QBf])
                outs.append(b.dma("sp", qbo[hh], QBf[:], d[7], reads=[r_QBf]))
                outs.append(b.dma("sp", sgo[hh], sg[:], d[13], reads=[r_sg]))
        if pass1 or modeA:
            b.op("pool", lambda e: e.memset(S[:], 0.0), writes=[r_S])
            if modeA:
                b.op("pool", lambda e: e.memset(Sbf[:], 0.0), writes=[r_Sbf])
        else:
            b.op("dve", lambda e, hh=hh: e.tensor_scalar(S[:], sr[:, 0, :], 1.0, None, ALU.mult), reads=[r_sr], writes=[r_S])
            for r in range(1, NRK):
                b.op("dve", lambda e, hh=hh, r=r: e.scalar_tensor_tensor(S[:], S[:], dr[:, hh, r:r + 1], sr[:, r, :], ALU.mult, ALU.add),
                     reads=[r_S, r_sr, r_const], writes=[r_S])
            b.op("act", lambda e: e.activation(Sbf[:], S[:], AF.Copy), reads=[r_S], writes=[r_Sbf])
        for blk in range(NBLK):
            c0 = blk * 128
            b.op("dve", lambda e, c0=c0: e.tensor_tensor(Khm[:], kh[:, c0:c0 + 128].unsqueeze(1).broadcast_to([128, 4, 128]), cm[:], ALU.mult),
                 reads=[r_kh, r_const], writes=[r_Khm])
            ptT, rpT = pr.next()
            ptTb = ptT[:, :].bitcast(BF16)
            for c in range(4):
                b.op("pe", lambda e, ptTb=ptTb, c=c: e.transpose(ptTb[:, c * 128:(c + 1) * 128], Khm[:, c, :], ident[:]),
                     reads=[r_Khm, r_const], writes=[rpT], sig=(c == 3))
            b.op("act", lambda e, ptTb=ptTb: e.activation(KhmT[:].rearrange("p c k -> p (c k)"), ptTb[:, 0:512], AF.Copy),
                 reads=[rpT], writes=[r_KhmT])
            if not pass1:
                pa, rpa = pr.next()
                b.op("pe", lambda e, pa=pa, c0=c0: e.matmul(pa[:, 0:128], kt[:, c0:c0 + 128], qt[:, c0:c0 + 128], start=True, stop=True),
                     reads=[r_kt, r_qt], writes=[rpa])
                b.op("dve", lambda e, pa=pa: e.tensor_tensor(At[:], pa[:, 0:128], m01[:], ALU.mult), reads=[rpa, r_const], writes=[r_At])
                po, rpo = pr.next()
                b.op("pe", lambda e, po=po, blk=blk: e.matmul(po[:, 0:128], Vt[:, blk, :], At[:], start=True, stop=False),
                     reads=[r_Vt, r_At], writes=[rpo], sig=False)
            for c in range(4):
                ck = blk * 4 + c
                if not pass1:
                    b.op("pe", lambda e, po=po, c=c, c0=c0: e.matmul(po[:, c * CH:(c + 1) * CH], Sbf[:], qt[:, c0 + c * CH:c0 + (c + 1) * CH],
                                                                   start=False, stop=(c == 3)),
                         reads=[r_Sbf, r_qt], writes=[rpo], sig=True)
                pu, rpu = pr.next()
                b.op("pe", lambda e, pu=pu, c=c, blk=blk: e.matmul(pu[:, 0:128], KhmT[:, c, :], Vt[:, blk, :], start=True, stop=True),
                     reads=[r_KhmT, r_Vt], writes=[rpu])
                b.op("dve", lambda e, pu=pu, ck=ck: e.scalar_tensor_tensor(S[:], S[:], dec[:, ck:ck + 1], pu[:, 0:128], ALU.mult, ALU.add),
                     reads=[rpu, r_S, r_dec], writes=[r_S])
                if not pass1:
                    b.op("act", lambda e: e.activation(Sbf[:], S[:], AF.Copy), reads=[r_S], writes=[r_Sbf])
            if not pass1:
                b.op("act", lambda e, po=po, c0=c0: e.activation(Oraw[:, c0:c0 + 128], po[:, 0:128], AF.Copy), reads=[rpo], writes=[r_Oraw])
        outs.append(b.dma("sp", send[:, hh, :], S[:], d[11], reads=[r_S]))
        if pass1:
            continue
        b.op("dve", lambda e: e.tensor_tensor(Khms[:], kh[:, NP:NT].unsqueeze(1).broadcast_to([128, 16, 64]), cms[:], ALU.mult),
             reads=[r_kh, r_const], writes=[r_Khms])
        for half in range(2):
            ptT, rpT = pr.next()
            ptTb = ptT[:, :].bitcast(BF16)
            for s8 in range(8):
                sq = half * 8 + s8
                b.op("pe", lambda e, ptTb=ptTb, s8=s8, sq=sq: e.transpose(ptTb[0:64, s8 * 128:(s8 + 1) * 128], Khms[:, sq, :], ident[:]),
                     reads=[r_Khms, r_const], writes=[rpT], sig=(s8 == 7))
            b.op("act", lambda e, ptTb=ptTb, half=half: e.activation(
                KhmTs[:, half * 8:half * 8 + 8, :].rearrange("p c k -> p (c k)"), ptTb[0:64, 0:1024], AF.Copy),
                reads=[rpT], writes=[r_KhmTs])
        pa, rpa = pr.next()
        b.op("pe", lambda e, pa=pa: e.matmul(pa[0:64, 0:64], kt[:, NP:NT], qt[:, NP:NT], start=True, stop=True),
             reads=[r_kt, r_qt], writes=[rpa])
        b.op("dve", lambda e, pa=pa: e.tensor_tensor(Ats[:], pa[0:64, 0:64], ms[:], ALU.mult), reads=[rpa, r_const], writes=[r_Ats])
        po, rpo = pr.next()
        b.op("pe", lambda e, po=po: e.matmul(po[:, 0:64], Vt[0:64, NBLK, :], Ats[:], start=True, stop=False),
             reads=[r_Vt, r_Ats], writes=[rpo], sig=False)
        for sq in range(16):
            b.op("pe", lambda e, po=po, sq=sq: e.matmul(po[:, sq * 4:sq * 4 + 4], S0b[:, sq, :], qt[:, NP + sq * 4:NP + sq * 4 + 4],
                                                       start=False, stop=(sq == 15)),
                 reads=[r_S0b, r_qt], writes=[rpo], sig=(sq == 15))
        b.op("act", lambda e, po=po: e.activation(Oraw[:, NP:NT], po[:, 0:64], AF.Copy), reads=[rpo], writes=[r_Oraw])
        for q4 in range(4):
            pu, rpu = pr.next()
            for s4 in range(4):
                sq = q4 * 4 + s4
                b.op("pe", lambda e, pu=pu, s4=s4, sq=sq: e.matmul(pu[:, s4 * 128:(s4 + 1) * 128], KhmTs[:, sq, :], Vt[0:64, NBLK, :],
                                                                 start=True, stop=True),
                     reads=[r_KhmTs, r_Vt], writes=[rpu], sig=(s4 == 3))
            for s4 in range(4):
                sq = q4 * 4 + s4
                b.op("dve", lambda e, pu=pu, s4=s4, sq=sq: e.scalar_tensor_tensor(
                    S0[:, sq, :], S0[:, sq, :], dec[:, NCK + sq:NCK + sq + 1], pu[:, s4 * 128:(s4 + 1) * 128], ALU.mult, ALU.add),
                    reads=[rpu, r_S0, r_dec], writes=[r_S0])
        outs.append(b.dma("sp", snew[hh], S0[:], d[10], reads=[r_S0]))
        if modeA:
            outs.append(b.dma("sp", oloc[hh], Oraw[:], d[14], reads=[r_Oraw]))
            continue
        b.op("act", lambda e: e.activation(xsq0[:], Oraw[:], AF.Square), reads=[r_Oraw], writes=[r_xsq0])
        tl = ntiles(NT)
        bk = [pr.next() for _ in tl]
        for ti, (s, n) in enumerate(tl):
            pt, rp = bk[ti]
            b.op("pe", lambda e, pt=pt, s=s, n=n: e.matmul(pt[:, 0:n], ones[:], xsq0[:, s:s + n], start=True, stop=True),
                 reads=[r_xsq0, r_const], writes=[rp])
            b.op("act", lambda e, pt=pt, s=s, n=n: e.activation(rstd[:, s:s + n], pt[:, 0:n], AF.Sqrt, bias=EPS, scale=1.0 / 128),
                 reads=[rp], writes=[r_rstd])
        b.op("dve", lambda e: e.reciprocal(rstd[:], rstd[:]), reads=[r_rstd], writes=[r_rstd])
        b.op("dve", lambda e, hh=hh: e.scalar_tensor_tensor(Oraw[:], Oraw[:], ng[:, hh:hh + 1], rstd[:], ALU.mult, ALU.mult),
             reads=[r_Oraw, r_rstd, r_const], writes=[r_Oraw])
        b.op("dve", lambda e, hh=hh: e.tensor_tensor(O2T[:, hh, :], Oraw[:], sg[:], ALU.mult), reads=[r_Oraw, r_sg, r_x], writes=[r_O2T])

    if pass1 or modeA:
        outs.append(b.dma("sp", dout, Dall[:], d[12], reads=[r_Sall]))
    else:
        b.op("pool", lambda e: e.memset(Dall[:], 0.0), writes=[r_Sall])
        outs.append(b.dma("sp", dout, Dall[:], d[12], reads=[r_Sall]))
        xr = [t32a, rstd]; r_xr = [r_t32a, r_rstd]
        xsem = [d[13], d[14]]; osem = [d[15], d[16]]
        for i in range(KC):
            n_, w, rw = next_w()
            xi = xr[i % 2]; rxi = r_xr[i % 2]
            b.dma("sp", xi[:], xT[:, i, :], xsem[i % 2], writes=[rxi])
            for (s, n) in ntiles(NT):
                pt, rp = pr.next()
                for k in range(KC):
                    b.op("pe", lambda e, pt=pt, w=w, k=k, s=s, n=n: e.matmul(
                        pt[:, 0:n], w[:, k, :], O2T[:, k, s:s + n], start=(k == 0), stop=(k == KC - 1)),
                        reads=[rw, r_O2T], writes=[rp], sig=(k == KC - 1))
                if s + n <= NP:
                    b.op("dve", lambda e, pt=pt, i=i, s=s, n=n, xi=xi: e.scalar_tensor_tensor(
                        xi[:, s:s + n], pt[:, 0:n], vec[:, 3, i:i + 1], xi[:, s:s + n], ALU.mult, ALU.add),
                        reads=[rp, r_const, rxi], writes=[rxi])
                else:
                    b.op("dve", lambda e, pt=pt, i=i: e.tensor_tensor(
                        tmps[:].rearrange("p (s t) -> p s t", t=4), pt[:, 0:NS].rearrange("p (s t) -> p s t", t=4),
                        mods[:, 2, i, :].unsqueeze(2).broadcast_to([128, 16, 4]), ALU.mult),
                        reads=[rp, r_const], writes=[r_tmps])
                    b.op("dve", lambda e, xi=xi: e.tensor_tensor(xi[:, NP:NT], xi[:, NP:NT], tmps[:], ALU.add),
                         reads=[r_tmps, rxi], writes=[rxi])
            outs.append(b.dma("sp", xo[:, i, :], xi[:], osem[i % 2], reads=[rxi]))
            load_w(n_ + 4)
    b.wait_all("sp", outs)
    b.emit(); b.close()
    return nc


def build_hgrnb():
    nc = bass.Bass("TRN2", target_bir_lowering=False)
    NT = NP + NS
    dram = lambda name, shape, kind="ExternalInput": nc.dram_tensor(name, list(shape), F32, kind=kind).ap()
    xT = dram("xT", [128, KC, NT])
    vecd = dram("vec", [128, 4, KC])
    modsd = dram("mods", [128, 3, KC, 16])
    olocd = dram("oloc", [NHH, 128, NT]); qbd = dram("qb", [NHH, 128, NT]); sgd = dram("sg", [NHH, 128, NT])
    srd = dram("sr", [NHH, 128, NRK, 128]); drd = dram("dr", [128, NHH, NRK])
    slocd = dram("sloc", [128, NHH, 128]); dld = dram("dl", [128, NHH])
    ngd = dram("ng", [128, NHH])
    wo = dram("wo", [KC, 128, KC, 128])
    xo = dram("xo", [128, KC, NT], "ExternalOutput")
    send = dram("send", [128, NHH, 128], "ExternalOutput")

    b = Builder(nc, n_dsem=18)
    d = b.dsems
    pr = PsumRot(b)
    O2T = b.sb("O2T", [128, KC, NT], BF16)
    ol = [b.sb("ol%d" % i, [128, NT], F32) for i in range(2)]
    qbb = [b.sb("qbb%d" % i, [128, NT], BF16) for i in range(2)]
    sgl = [b.sb("sgl%d" % i, [128, NT], F32) for i in range(2)]
    sr = [b.sb("sr%d" % i, [128, NRK, 128], F32) for i in range(2)]
    Oraws = [b.sb("Oraw%d" % i, [128, NT], F32) for i in range(2)]
    xsqs = [b.sb("xsq%d" % i, [128, NT], BF16) for i in range(2)]
    rstds = [b.sb("rstd%d" % i, [128, NT], F32) for i in range(2)]
    Ss = [b.sb("S%d" % i, [128, 128], F32) for i in range(2)]; Sbfs = [b.sb("Sbf%d" % i, [128, 128], BF16) for i in range(2)]
    Se = b.sb("Se", [128, NHH, 128], F32)
    sloc = b.sb("slocs", [128, NHH, 128], F32)
    dr = b.sb("drs", [128, NHH, NRK], F32); dl = b.sb("dls", [128, NHH], F32); ng = b.sb("ngs", [128, NHH], F32)
    vec = b.sb("vecs", [128, 4, KC], F32); mods = b.sb("modss", [128, 3, KC, 16], F32)
    ones = b.sb("ones", [128, 128], BF16)
    NWB = 4
    wring = [b.sb("wr%d" % i, [128, KC, 128], BF16) for i in range(NWB)]
    xr = [b.sb("xr%d" % i, [128, NT], F32) for i in range(2)]
    tmps = b.sb("tmps", [128, NS], F32)
    R = Res
    r_O2T, r_Se, r_const, r_tmps = [R(n) for n in "O2T Se const tmps".split()]
    r_Oraws = [R("a"), R("b")]; r_xsqs = [R("a"), R("b")]; r_rstds = [R("a"), R("b")]; r_Ss = [R("a"), R("b")]; r_Sbfs = [R("a"), R("b")]
    r_ol = [R("a"), R("b")]; r_qbb = [R("a"), R("b")]; r_sgl = [R("a"), R("b")]; r_sr = [R("a"), R("b")]
    r_wr = [R("w%d" % i) for i in range(NWB)]; r_xr = [R("a"), R("b")]
    for dst, src in [(vec, vecd), (mods, modsd), (sloc, slocd), (dr, drd), (dl, dld), (ng, ngd)]:
        b.dma("sp", dst[:], src, d[0], writes=[r_const])
    b.op("pool", lambda e: e.memset(ones[:], 1.0), writes=[r_const])

    def load_w(i):
        if i < KC:
            b.dma("pool", wring[i % NWB][:], wo[i], d[1 + i % NWB], writes=[r_wr[i % NWB]])

    def load_head(hh):
        if hh >= NHH:
            return
        i = hh % 2
        b.dma("sp", ol[i][:], olocd[hh], d[5 + i], writes=[r_ol[i]])
        b.dma("pool", qbb[i][:], qbd[hh], d[7 + i], writes=[r_qbb[i]])
        b.dma("sp", sgl[i][:], sgd[hh], d[9 + i], writes=[r_sgl[i]])
        b.dma("sp", sr[i][:], srd[hh], d[11 + i], writes=[r_sr[i]])
    load_head(0)
    for i in range(NWB - 1):
        load_w(i)
    outs = []
    for hh in range(NHH):
        load_head(hh + 1)
        i2 = hh % 2
        Oraw, xsq0, rstd, S, Sbf = Oraws[i2], xsqs[i2], rstds[i2], Ss[i2], Sbfs[i2]
        r_Oraw, r_xsq0, r_rstd, r_S, r_Sbf = r_Oraws[i2], r_xsqs[i2], r_rstds[i2], r_Ss[i2], r_Sbfs[i2]
        b.op("dve", lambda e, i2=i2: e.tensor_scalar(S[:], sr[i2][:, 0, :], 1.0, None, ALU.mult), reads=[r_sr[i2]], writes=[r_S])
        for r in range(1, NRK):
            b.op("dve", lambda e, hh=hh, r=r, i2=i2: e.scalar_tensor_tensor(S[:], S[:], dr[:, hh, r:r + 1], sr[i2][:, r, :], ALU.mult, ALU.add),
                 reads=[r_S, r_sr[i2], r_const], writes=[r_S])
        b.op("act", lambda e: e.activation(Sbf[:], S[:], AF.Copy), reads=[r_S], writes=[r_Sbf])
        b.op("dve", lambda e, hh=hh: e.scalar_tensor_tensor(Se[:, hh, :], S[:], dl[:, hh:hh + 1], sloc[:, hh, :], ALU.mult, ALU.add),
             reads=[r_S, r_const], writes=[r_Se])
        for (s, n) in ntiles(NT):
            pt, rp = pr.next()
            b.op("pe", lambda e, pt=pt, s=s, n=n, i2=i2: e.matmul(pt[:, 0:n], Sbf[:], qbb[i2][:, s:s + n], start=True, stop=True),
                 reads=[r_Sbf, r_qbb[i2]], writes=[rp])
            b.op("dve", lambda e, pt=pt, s=s, n=n, i2=i2: e.tensor_tensor(Oraw[:, s:s + n], pt[:, 0:n], ol[i2][:, s:s + n], ALU.add),
                 reads=[rp, r_ol[i2]], writes=[r_Oraw])
        b.op("act", lambda e: e.activation(xsq0[:], Oraw[:], AF.Square), reads=[r_Oraw], writes=[r_xsq0])
        for (s, n) in ntiles(NT):
            pt, rp = pr.next()
            b.op("pe", lambda e, pt=pt, s=s, n=n: e.matmul(pt[:, 0:n], ones[:], xsq0[:, s:s + n], start=True, stop=True),
                 reads=[r_xsq0, r_const], writes=[rp])
            b.op("act", lambda e, pt=pt, s=s, n=n: e.activation(rstd[:, s:s + n], pt[:, 0:n], AF.Ln, bias=EPS, scale=1.0 / 128),
                 reads=[rp], writes=[r_rstd])
        b.op("act", lambda e: e.activation(rstd[:], rstd[:], AF.Exp, scale=-0.5), reads=[r_rstd], writes=[r_rstd])
        b.op("dve", lambda e, hh=hh: e.scalar_tensor_tensor(Oraw[:], Oraw[:], ng[:, hh:hh + 1], rstd[:], ALU.mult, ALU.mult),
             reads=[r_Oraw, r_rstd, r_const], writes=[r_Oraw])
        b.op("pool", lambda e, hh=hh, i2=i2: e.tensor_tensor(O2T[:, hh, :], Oraw[:], sgl[i2][:], ALU.mult),
             reads=[r_Oraw, r_sgl[i2]], writes=[r_O2T])
    outs.append(b.dma("sp", send, Se[:], d[13], reads=[r_Se]))
    for i in range(KC):
        load_w(i + NWB - 1)
        w, rw = wring[i % NWB], r_wr[i % NWB]
        xi, rxi = xr[i % 2], r_xr[i % 2]
        b.dma("sp", xi[:], xT[:, i, :], d[14 + i % 2], writes=[rxi])
        for (s, n) in ntiles(NT):
            pt, rp = pr.next()
            for k in range(KC):
                b.op("pe", lambda e, pt=pt, w=w, k=k, s=s, n=n: e.matmul(
                    pt[:, 0:n], w[:, k, :], O2T[:, k, s:s + n], start=(k == 0), stop=(k == KC - 1)),
                    reads=[rw, r_O2T], writes=[rp], sig=(k == KC - 1))
            if s + n <= NP:
                b.op("dve", lambda e, pt=pt, i=i, s=s, n=n, xi=xi: e.scalar_tensor_tensor(
                    xi[:, s:s + n], pt[:, 0:n], vec[:, 3, i:i + 1], xi[:, s:s + n], ALU.mult, ALU.add),
                    reads=[rp, r_const, rxi], writes=[rxi])
            else:
                b.op("dve", lambda e, pt=pt, i=i: e.tensor_tensor(
                    tmps[:].rearrange("p (s t) -> p s t", t=4), pt[:, 0:NS].rearrange("p (s t) -> p s t", t=4),
                    mods[:, 2, i, :].unsqueeze(2).broadcast_to([128, 16, 4]), ALU.mult),
                    reads=[rp, r_const], writes=[r_tmps])
                b.op("dve", lambda e, xi=xi: e.tensor_tensor(xi[:, NP:NT], xi[:, NP:NT], tmps[:], ALU.add),
                     reads=[r_tmps, rxi], writes=[rxi])
        outs.append(b.dma("sp", xo[:, i, :], xi[:], d[16 + i % 2], reads=[rxi]))
    b.wait_all("sp", outs)
    b.emit(); b.close()
    return nc


class _HSet:
    pass


FILL_RATIO = 1
CRIT_RATIO = 2


def build_hgrna():
    nc = bass.Bass("TRN2", target_bir_lowering=False)
    NT = NP + NS
    NBLK = NP // 128
    NCK = NP // CH
    dram = lambda name, shape, kind="ExternalInput": nc.dram_tensor(name, list(shape), F32, kind=kind).ap()
    xT = dram("xT", [128, KC, NT])
    vecd = dram("vec", [128, 4, KC]); modsd = dram("mods", [128, 3, KC, 16])
    whg = dram("whg", [NHH, 128, 4, KC, 128])
    lbpd = dram("lbp", [128, 2, NHH])
    m01d = dram("m01", [128, 128]); cmd = dram("cm", [128, 4, 128]); identd = dram("ident", [128, 128])
    s0d = dram("s0", [NHH, 128, 16, 128]); msd = dram("ms", [64, 64]); cmsd = dram("cms", [128, 16, 64])
    smd = dram("smask", [128, NT])
    oloc = dram("oloc", [NHH, 128, NT], "ExternalOutput")
    qbo = dram("qbo", [NHH, 128, NT], "ExternalOutput")
    sgo = dram("sgo", [NHH, 128, NT], "ExternalOutput")
    snew = dram("snew", [NHH, 128, 16, 128], "ExternalOutput")
    send = dram("send", [128, NHH, 128], "ExternalOutput")
    dout = dram("dout", [128, NHH], "ExternalOutput")

    b = Builder(nc, n_dsem=22)
    d = b.dsems
    pr = PsumRot(b, 4)
    po_banks = [(b.ps("pob%d" % i, [128, 512]), Res("pob%d" % i)) for i in range(2)]
    pu_banks = [(b.ps("pub%d" % i, [128, 512]), Res("pub%d" % i)) for i in range(2)]
    xbuf = b.sb("xbuf", [128, KC * NT], F32)
    x = xbuf[:].rearrange("p (k n) -> p k n", k=KC)
    h = b.sb("h", [128, KC, NT], BF16)
    rstd = b.sb("rstd", [128, NT], F32)
    xsq0 = b.sb("xsq0", [128, NT], BF16)
    t32a = b.sb("t32a", [128, NT], F32)
    NWB = 5
    wring = [b.sb("wr%d" % i, [128, KC, 128], BF16) for i in range(NWB)]
    Dall = b.sb("Dall", [128, NHH], F32)
    vec = b.sb("vecs", [128, 4, KC], F32); mods = b.sb("modss", [128, 3, KC, 16], F32)
    lbp = b.sb("lbps", [128, 2, NHH], F32); oml = b.sb("oml", [128, NHH], F32)
    m01 = b.sb("m01s", [128, 128], F32); cm = b.sb("cms_", [128, 4, 128], F32)
    ident = b.sb("idents", [128, 128], BF16); identf = b.sb("identf", [128, 128], F32)
    ones = b.sb("ones", [128, 128], BF16)
    gm = b.sb("gm", [128, KC], F32); gms = b.sb("gms", [128, KC, 16], F32)
    ms = b.sb("mss", [64, 64], F32); cms = b.sb("cmss", [128, 16, 64], F32)
    smask = b.sb("smasks", [128, NT], F32); onesf = b.sb("onesf", [128, NCK], F32)
    R = Res
    r_x, r_h, r_rstd, r_const, r_gm, r_gms, r_xsq0, r_t32a, r_D = [R(n) for n in "x h rstd const gm gms xsq t32 D".split()]
    r_wr = [R("wr%d" % i) for i in range(NWB)]

    f32_names = ["qf", "kf", "bA", "bB", "sg", "Oraw", "QBf"]
    bf_names = ["qt", "kt", "kh"]
    sets = []
    for si in range(2):
        B = _HSet()
        if si == 0:
            for nm in f32_names:
                if nm == "QBf":
                    B.QBf = t32a[:]
                elif nm == "Oraw":
                    B.Oraw = rstd[:]
                else:
                    setattr(B, nm, b.sb(nm + "0", [128, NT], F32)[:])
            for nm in bf_names:
                setattr(B, nm, b.sb(nm + "0", [128, NT], BF16)[:])
            B.Vt = b.sb("Vt0", [128, NBLK + 1, 128], BF16)[:]
            B.S0 = b.sb("S00", [128, 16, 128], F32)[:]; B.S0b = b.sb("S0b0", [128, 16, 128], BF16)[:]
            B.Khms = b.sb("Khms0", [128, 16, 64], BF16)[:]; B.KhmTs = b.sb("KhmTs0", [64, 16, 128], BF16)[:]
            B.At = b.sb("At0", [128, 128], BF16)[:]; B.Khm = b.sb("Khm0", [128, 4, 128], BF16)[:]
            B.KhmT = b.sb("KhmT0", [128, 4, 128], BF16)[:]
            B.At_b = b.sb("At0b", [128, 128], BF16)[:]; B.Khm_b = b.sb("Khm0b", [128, 4, 128], BF16)[:]
            B.KhmT_b = b.sb("KhmT0b", [128, 4, 128], BF16)[:]
            B.S = b.sb("S_0", [128, 128], F32)[:]; B.Sbf = b.sb("Sbf0", [128, 128], BF16)[:]
            B.S2 = b.sb("S2_0", [128, 128], F32)[:]; B.Sbf2 = b.sb("Sbf2_0", [128, 128], BF16)[:]
            B.dec = b.sb("dec0", [128, NCK + 16], F32)[:]
            B.pbA = b.sb("pbA0", [128, NCK], F32)[:]; B.pbB = b.sb("pbB0", [128, NCK], F32)[:]
            B.Ats = b.sb("Ats0", [64, 64], BF16)[:]
        else:
            off = [0]

            def carve(n_f32, dt, shape):
                v = xbuf[:, off[0]:off[0] + n_f32]
                off[0] += n_f32
                if dt is BF16:
                    v = v.bitcast(BF16)
                if len(shape) == 2:
                    return v[:, 0:shape[1]] if shape[0] == 128 else v[0:shape[0], 0:shape[1]]
                if len(shape) == 3:
                    vv = v[:, 0:shape[1] * shape[2]].rearrange("p (a c) -> p a c", a=shape[1])
                    return vv if shape[0] == 128 else vv[0:shape[0]]
            for nm in f32_names:
                setattr(B, nm, carve(NT, F32, [128, NT]))
            for nm in bf_names:
                setattr(B, nm, carve(NT // 2, BF16, [128, NT]))
            B.Vt = carve((NBLK + 1) * 64, BF16, [128, NBLK + 1, 128])
            B.S0 = carve(2048, F32, [128, 16, 128]); B.S0b = carve(1024, BF16, [128, 16, 128])
            B.Khms = carve(512, BF16, [128, 16, 64]); B.KhmTs = carve(1024, BF16, [64, 16, 128])
            B.At = carve(64, BF16, [128, 128]); B.Khm = carve(256, BF16, [128, 4, 128]); B.KhmT = carve(256, BF16, [128, 4, 128])
            B.At_b = carve(64, BF16, [128, 128]); B.Khm_b = carve(256, BF16, [128, 4, 128]); B.KhmT_b = carve(256, BF16, [128, 4, 128])
            B.S = carve(128, F32, [128, 128]); B.Sbf = carve(64, BF16, [128, 128])
            B.S2 = carve(128, F32, [128, 128]); B.Sbf2 = carve(64, BF16, [128, 128])
            B.dec = carve(NCK + 16, F32, [128, NCK + 16])
            B.pbA = carve(NCK, F32, [128, NCK]); B.pbB = carve(NCK, F32, [128, NCK])
            B.Ats = carve(32, BF16, [64, 64])
            assert off[0] <= KC * NT, off[0]
        for nm in f32_names + bf_names + ["Vt", "S0", "S0b", "Khms", "KhmTs", "At", "Khm", "KhmT", "At_b", "Khm_b", "KhmT_b", "S", "Sbf", "S2", "Sbf2", "dec", "pbA", "pbB", "Ats"]:
            setattr(B, "r_" + nm, Res(nm + str(si)))
        if si == 0:
            B.r_QBf = r_t32a
            B.r_Oraw = r_rstd
        sets.append(B)

    b.dma("sp", xbuf[:], xT.rearrange("p k n -> p (k n)"), d[0], writes=[r_x])
    for dst, src in [(vec, vecd), (mods, modsd), (lbp, lbpd), (m01, m01d), (cm, cmd), (identf, identd), (ms, msd), (cms, cmsd), (smask, smd)]:
        b.dma("sp", dst[:], src, d[1], writes=[r_const])
    b.op("pool", lambda e: e.memset(ones[:], 1.0), writes=[r_const])
    b.op("pool", lambda e: e.memset(Dall[:], 0.0), writes=[r_D])
    b.op("pool", lambda e: e.memset(onesf[:], 1.0), writes=[r_const])
    b.op("act", lambda e: e.activation(ident[:], identf[:], AF.Copy), reads=[r_const], writes=[r_const])
    b.op("dve", lambda e: e.tensor_tensor(oml[:], lbp[:, 0, :], lbp[:, 1, :], ALU.subtract), reads=[r_const], writes=[r_const])
    b.op("act", lambda e: e.activation(oml[:], oml[:], AF.Sigmoid), reads=[r_const], writes=[r_const])

    wsem = [d[2], d[3], d[4], d[5], d[6]]
    wlist = [(hh, wh) for hh in range(NHH) for wh in range(4)]

    def load_w(n):
        if n >= len(wlist):
            return
        a, c = wlist[n]
        b.dma("pool", wring[n % NWB][:], whg[a][:, c, :, :], wsem[n % NWB], writes=[r_wr[n % NWB]])
    for n in range(4):
        load_w(n)
    wcount = [0]

    def next_w():
        n = wcount[0]
        wcount[0] += 1
        return n, wring[n % NWB], r_wr[n % NWB]

    emit_norm_mod(b, pr, x, r_x, h, r_h, NT, NP, vec, 0, 1, 2, mods, 0, 1, ones, r_const,
                  ([xsq0[:], xsq0[:]], [r_xsq0, r_xsq0], rstd[:], r_rstd, gm, r_gm, gms, r_gms, [t32a[:], t32a[:]], [r_t32a, r_t32a]))
    B1 = sets[1]
    for nm in f32_names + bf_names + ["Vt", "S0", "S0b", "Khms", "KhmTs", "At", "Khm", "KhmT", "At_b", "Khm_b", "KhmT_b", "S", "Sbf", "S2", "Sbf2", "dec", "pbA", "pbB", "Ats"]:
        rr = getattr(B1, "r_" + nm)
        rr.r = list(r_x.r)
        rr.w = r_x.w

    outs = []
    dsem_set = [dict(s0=d[7], s0b=d[8], qb=d[9], sg=d[10], ol=d[11], sn=d[12], se=d[13]),
                dict(s0=d[14], s0b=d[15], qb=d[16], sg=d[17], ol=d[18], sn=d[19], se=d[20])]

    def proj_fm(dst_fn):
        n_, w, rw = next_w()
        for (s, n) in ntiles(NT):
            pt, rp = pr.next()
            for k in range(KC):
                b.op("pe", lambda e, pt=pt, w=w, k=k, s=s, n=n: e.matmul(
                    pt[:, 0:n], w[:, k, :], h[:, k, s:s + n], start=(k == 0), stop=(k == KC - 1)),
                    reads=[rw, r_h], writes=[rp], sig=(k == KC - 1))
            dst_fn(pt, rp, s, n)
            yield
        load_w(n_ + 4)

    def prep(hh):
        B = sets[hh % 2]; ds = dsem_set[hh % 2]
        b.dma("sp", B.S0, s0d[hh], ds["s0"], writes=[B.r_S0])
        b.dma("pool", B.S0b, s0d[hh], ds["s0b"], writes=[B.r_S0b])
        yield from proj_fm(lambda pt, rp, s, n: b.op("act", lambda e: e.activation(B.qf[:, s:s + n], pt[:, 0:n], AF.Silu),
                                                     reads=[rp], writes=[B.r_qf]))
        yield from proj_fm(lambda pt, rp, s, n: b.op("act", lambda e: e.activation(B.kf[:, s:s + n], pt[:, 0:n], AF.Sigmoid, scale=-1.0),
                                                     reads=[rp], writes=[B.r_kf]))
        b.op("dve", lambda e: e.tensor_scalar(B.kf, B.kf, oml[:, hh:hh + 1], None, ALU.mult), reads=[B.r_kf, r_const], writes=[B.r_kf])
        b.op("act", lambda e: e.activation(B.bA, B.kf, AF.Ln, bias=1.0, scale=-1.0), reads=[B.r_kf], writes=[B.r_bA])
        yield
        n_, w, rw = next_w()
        for blk in range(NBLK + 1):
            m = 128 if blk < NBLK else NS
            c0 = blk * 128
            pt, rp = pr.next()
            for k in range(KC):
                b.op("pe", lambda e, pt=pt, w=w, k=k, c0=c0, m=m: e.matmul(
                    pt[0:m, 0:128], h[:, k, c0:c0 + m], w[:, k, :], start=(k == 0), stop=(k == KC - 1)),
                    reads=[rw, r_h], writes=[rp], sig=(k == KC - 1))
            b.op("act", lambda e, pt=pt, blk=blk, m=m: e.activation(B.Vt[0:m, blk, :], pt[0:m, 0:128], AF.Copy),
                 reads=[rp], writes=[B.r_Vt])
            yield
        load_w(n_ + 4)
        yield from proj_fm(lambda pt, rp, s, n: b.op("act", lambda e: e.activation(B.sg[:, s:s + n], pt[:, 0:n], AF.Silu),
                                                     reads=[rp], writes=[B.r_sg]))
        outs.append(b.dma("sp", sgo[hh], B.sg, ds["sg"], reads=[B.r_sg]))
        b.op("dve", lambda e: e.tensor_tensor_scan(B.bB, smask[:], B.bA, 0.0, ALU.mult, ALU.add),
             reads=[B.r_bA, r_const], writes=[B.r_bB])
        bb, rbb = B.bB, B.r_bB
        other, rother = B.bA, B.r_bA
        yield
        bv = bb[:, 0:NP].rearrange("p (c t) -> p c t", t=CH)
        ov = other[:, 0:NP].rearrange("p (c t) -> p c t", t=CH)
        b.op("act", lambda e: e.activation(B.dec[:, 0:NCK].unsqueeze(2), bv[:, :, CH - 1:CH], AF.Exp), reads=[rbb], writes=[B.r_dec])
        b.op("dve", lambda e: e.tensor_tensor(ov, bv[:, :, CH - 1:CH].broadcast_to([128, NCK, CH]), bv, ALU.subtract),
             reads=[rbb], writes=[rother])
        bs_ = bb[:, NP:NT].rearrange("p (c t) -> p c t", t=4)
        os_ = other[:, NP:NT].rearrange("p (c t) -> p c t", t=4)
        b.op("act", lambda e: e.activation(B.dec[:, NCK:NCK + 16].unsqueeze(2), bs_[:, :, 3:4], AF.Exp), reads=[rbb], writes=[B.r_dec])
        b.op("dve", lambda e: e.tensor_tensor(os_, bs_[:, :, 3:4].broadcast_to([128, 16, 4]), bs_, ALU.subtract),
             reads=[rbb], writes=[rother])
        b.op("act", lambda e: e.activation(other[:, 0:NT], other[:, 0:NT], AF.Exp), reads=[rother], writes=[rother])
        b.op("dve", lambda e: e.tensor_tensor(B.kh, B.kf, other[:, 0:NT], ALU.mult), reads=[rother, B.r_kf], writes=[B.r_kh])
        yield
        b.op("act", lambda e: e.activation(other[:, 0:NT], bb[:, 0:NT], AF.Exp), reads=[rbb, B.r_kh], writes=[rother])
        b.op("dve", lambda e: e.tensor_tensor(B.qt, B.qf, other[:, 0:NT], ALU.mult), reads=[rother, B.r_qf], writes=[B.r_qt])
        b.op("act", lambda e: e.activation(other[:, 0:NT], bb[:, 0:NT], AF.Exp, scale=-1.0), reads=[rbb, B.r_qt], writes=[rother])
        b.op("dve", lambda e: e.tensor_tensor(B.kt, B.kf, other[:, 0:NT], ALU.mult), reads=[rother, B.r_kf], writes=[B.r_kt])
        yield
        b.op("pool", lambda e: e.tensor_copy(B.pbA.unsqueeze(2), bv[:, :, CH - 1:CH]), reads=[rbb], writes=[B.r_pbA])
        b.op("dve", lambda e: e.tensor_tensor_scan(B.pbB, onesf[:], B.pbA, 0.0, ALU.mult, ALU.add),
             reads=[B.r_pbA, r_const], writes=[B.r_pbB])
        pin, rpin = B.pbB, B.r_pbB
        pex, rpex = B.pbA, B.r_pbA
        b.op("act", lambda e: e.activation(Dall[:, hh:hh + 1], pin[:, NCK - 1:NCK], AF.Exp), reads=[rpin], writes=[r_D])
        b.op("pool", lambda e: e.tensor_tensor(pex.unsqueeze(2), pin.unsqueeze(2), bv[:, :, CH - 1:CH], ALU.subtract),
             reads=[rpin, rbb], writes=[rpex])
        b.op("act", lambda e: e.activation(pex, pex, AF.Exp), reads=[rpex], writes=[rpex])
        b.op("pool", lambda e: e.tensor_tensor(
            B.QBf[:, 0:NP].rearrange("p (c t) -> p c t", t=CH), B.qt[:, 0:NP].rearrange("p (c t) -> p c t", t=CH),
            pex.unsqueeze(2).broadcast_to([128, NCK, CH]), ALU.mult), reads=[rpex, B.r_qt], writes=[B.r_QBf])
        b.op("pool", lambda e: e.memset(B.QBf[:, NP:NT], 0.0), writes=[B.r_QBf])
        outs.append(b.dma("sp", qbo[hh], B.QBf, ds["qb"], reads=[B.r_QBf]))
        b.op("pool", lambda e: e.memset(B.S, 0.0), writes=[B.r_S])
        b.op("pool", lambda e: e.memset(B.Sbf, 0.0), writes=[B.r_Sbf])
        yield

    def loop(hh):
        B = sets[hh % 2]; ds = dsem_set[hh % 2]
        def front(blk):
            c0 = blk * 128
            Khm, rKhm, KhmT, rKhmT, At, rAt = ((B.Khm, B.r_Khm, B.KhmT, B.r_KhmT, B.At, B.r_At) if blk % 2 == 0 else
                                               (B.Khm_b, B.r_Khm_b, B.KhmT_b, B.r_KhmT_b, B.At_b, B.r_At_b))
            b.op("dve", lambda e: e.tensor_tensor(Khm, B.kh[:, c0:c0 + 128].unsqueeze(1).broadcast_to([128, 4, 128]), cm[:], ALU.mult),
                 reads=[B.r_kh, r_const], writes=[rKhm])
            ptT, rpT = pr.next()
            ptTb = ptT[:, :].bitcast(BF16)
            for c in range(4):
                b.op("pe", lambda e, c=c: e.transpose(ptTb[:, c * 128:(c + 1) * 128], Khm[:, c, :], ident[:]),
                     reads=[rKhm, r_const], writes=[rpT], sig=(c == 3))
            b.op("act", lambda e: e.activation(KhmT.rearrange("p c k -> p (c k)"), ptTb[:, 0:512], AF.Copy),
                 reads=[rpT], writes=[rKhmT])
            pa, rpa = pr.next()
            b.op("pe", lambda e: e.matmul(pa[:, 0:128], B.kt[:, c0:c0 + 128], B.qt[:, c0:c0 + 128], start=True, stop=True),
                 reads=[B.r_kt, B.r_qt], writes=[rpa])
            b.op("dve", lambda e: e.tensor_tensor(At, pa[:, 0:128], m01[:], ALU.mult), reads=[rpa, r_const], writes=[rAt])
            po, rpo = po_banks[blk % 2]
            b.op("pe", lambda e: e.matmul(po[:, 0:128], B.Vt[:, blk, :], At, start=True, stop=False),
                 reads=[B.r_Vt, rAt], writes=[rpo], sig=False)
            pu, rpu = pu_banks[blk % 2]
            for c in range(4):
                b.op("pe", lambda e, c=c: e.matmul(pu[:, c * 128:(c + 1) * 128], KhmT[:, c, :], B.Vt[:, blk, :], start=True, stop=True),
                     reads=[rKhmT, B.r_Vt], writes=[rpu], sig=(c == 3))

        front(0)
        for blk in range(NBLK):
            c0 = blk * 128
            if blk + 1 < NBLK:
                front(blk + 1)
            yield
            po, rpo = po_banks[blk % 2]
            pu, rpu = pu_banks[blk % 2]
            for c in range(4):
                ck = blk * 4 + c
                Sc, rSc, Sn, rSn = (B.S, B.r_S, B.S2, B.r_S2) if ck % 2 == 0 else (B.S2, B.r_S2, B.S, B.r_S)
                Sbc, rSbc, Sbn, rSbn = (B.Sbf, B.r_Sbf, B.Sbf2, B.r_Sbf2) if ck % 2 == 0 else (B.Sbf2, B.r_Sbf2, B.Sbf, B.r_Sbf)
                b.op("pe", lambda e, c=c: e.matmul(po[:, c * CH:(c + 1) * CH], Sbc, B.qt[:, c0 + c * CH:c0 + (c + 1) * CH],
                                                   start=False, stop=(c == 3)),
                     reads=[rSbc, B.r_qt], writes=[rpo], sig=True)
                b.op("dve", lambda e, c=c, ck=ck: e.scalar_tensor_tensor(Sn, Sc, B.dec[:, ck:ck + 1], pu[:, c * 128:(c + 1) * 128], ALU.mult, ALU.add),
                     reads=[rpu, rSc, B.r_dec], writes=[rSn])
                b.op("pool", lambda e: e.tensor_copy(Sbn, Sn), reads=[rSn], writes=[rSbn])
                yield
            b.op("act", lambda e: e.activation(B.Oraw[:, c0:c0 + 128], po[:, 0:128], AF.Copy), reads=[rpo], writes=[B.r_Oraw])
        outs.append(b.dma("sp", send[:, hh, :], B.S, ds["se"], reads=[B.r_S]))
        b.op("dve", lambda e: e.tensor_tensor(B.Khms, B.kh[:, NP:NT].unsqueeze(1).broadcast_to([128, 16, 64]), cms[:], ALU.mult),
             reads=[B.r_kh, r_const], writes=[B.r_Khms])
        for half in range(2):
            ptT, rpT = pr.next()
            ptTb = ptT[:, :].bitcast(BF16)
            for s8 in range(8):
                sq = half * 8 + s8
                b.op("pe", lambda e, ptTb=ptTb, s8=s8, sq=sq: e.transpose(ptTb[0:64, s8 * 128:(s8 + 1) * 128], B.Khms[:, sq, :], ident[:]),
                     reads=[B.r_Khms, r_const], writes=[rpT], sig=(s8 == 7))
            b.op("act", lambda e, ptTb=ptTb, half=half: e.activation(
                B.KhmTs[:, half * 8:half * 8 + 8, :].rearrange("p c k -> p (c k)"), ptTb[0:64, 0:1024], AF.Copy),
                reads=[rpT], writes=[B.r_KhmTs])
            yield
        pa, rpa = pr.next()
        b.op("pe", lambda e, pa=pa: e.matmul(pa[0:64, 0:64], B.kt[:, NP:NT], B.qt[:, NP:NT], start=True, stop=True),
             reads=[B.r_kt, B.r_qt], writes=[rpa])
        b.op("dve", lambda e, pa=pa: e.tensor_tensor(B.Ats, pa[0:64, 0:64], ms[:], ALU.mult), reads=[rpa, r_const], writes=[B.r_Ats])
        po, rpo = pr.next()
        b.op("pe", lambda e, po=po: e.matmul(po[:, 0:64], B.Vt[0:64, NBLK, :], B.Ats, start=True, stop=False),
             reads=[B.r_Vt, B.r_Ats], writes=[rpo], sig=False)
        for sq in range(16):
            b.op("pe", lambda e, po=po, sq=sq: e.matmul(po[:, sq * 4:sq * 4 + 4], B.S0b[:, sq, :], B.qt[:, NP + sq * 4:NP + sq * 4 + 4],
                                                       start=False, stop=(sq == 15)),
                 reads=[B.r_S0b, B.r_qt], writes=[rpo], sig=(sq == 15))
        b.op("act", lambda e, po=po: e.activation(B.Oraw[:, NP:NT], po[:, 0:64], AF.Copy), reads=[rpo], writes=[B.r_Oraw])
        outs.append(b.dma("sp", oloc[hh], B.Oraw, ds["ol"], reads=[B.r_Oraw]))
        yield
        for q4 in range(4):
            pu, rpu = pr.next()
            for s4 in range(4):
                sq = q4 * 4 + s4
                b.op("pe", lambda e, pu=pu, s4=s4, sq=sq: e.matmul(pu[:, s4 * 128:(s4 + 1) * 128], B.KhmTs[:, sq, :], B.Vt[0:64, NBLK, :],
                                                                 start=True, stop=True),
                     reads=[B.r_KhmTs, B.r_Vt], writes=[rpu], sig=(s4 == 3))
            for s4 in range(4):
                sq = q4 * 4 + s4
                b.op("dve", lambda e, pu=pu, s4=s4, sq=sq: e.scalar_tensor_tensor(
                    B.S0[:, sq, :], B.S0[:, sq, :], B.dec[:, NCK + sq:NCK + sq + 1], pu[:, s4 * 128:(s4 + 1) * 128], ALU.mult, ALU.add),
                    reads=[rpu, B.r_S0, B.r_dec], writes=[B.r_S0])
            yield
        outs.append(b.dma("sp", snew[hh], B.S0, ds["sn"], reads=[B.r_S0]))

    def drain(g):
        for _ in g:
            pass

    drain(prep(0))
    for hh in range(NHH):
        crit = loop(hh)
        fill = prep(hh + 1) if hh + 1 < NHH else iter(())
        done_c = done_f = False
        while not (done_c and done_f):
            for _ in range(CRIT_RATIO):
                if not done_c:
                    try:
                        next(crit)
                    except StopIteration:
                        done_c = True
            for _ in range(FILL_RATIO):
                if not done_f:
                    try:
                        next(fill)
                    except StopIteration:
                        done_f = True
    outs.append(b.dma("sp", dout, Dall[:], d[21], reads=[r_D]))
    b.wait_all("sp", outs)
    b.emit(); b.close()
    return nc


def fm(a):
    n = a.shape[0]
    return np.ascontiguousarray(a.T.reshape(16, 128, n).transpose(1, 0, 2))
def unfm(t):
    return np.ascontiguousarray(t.transpose(2, 1, 0).reshape(t.shape[2], D))
def vfm(v):
    return v.reshape(-1, 128).T
def mods_fm(m3):
    return np.ascontiguousarray(m3.reshape(3, 16, 16, 128).transpose(3, 0, 2, 1))
def tile_w_in(w):
    return np.ascontiguousarray(w.reshape(16, 128, 2, 44, 128).transpose(3, 1, 2, 0, 4))
def tile_w_out(w):
    return np.ascontiguousarray(w.reshape(4, 11, 128, 16, 128).transpose(0, 3, 2, 1, 4))
def tile_cols(w):
    return w.reshape(16, 128, w.shape[1]).transpose(1, 0, 2)
def tile_wqkv(w):
    out = np.empty((8, 128, 4, 16, 128), np.float32)
    for g in range(8):
        out[g, :, 0] = tile_cols(w[:, 256 * g:256 * g + 128])
        out[g, :, 1] = tile_cols(w[:, 256 * g + 128:256 * g + 256])
        kk = w[:, 2048 + 64 * g:2048 + 64 * g + 64]; vv = w[:, 2560 + 64 * g:2560 + 64 * g + 64]
        out[g, :, 2] = tile_cols(np.concatenate([kk, kk], 1))
        out[g, :, 3] = tile_cols(np.concatenate([vv, vv], 1))
    return out
def tile_sq(w):
    return np.ascontiguousarray(w.reshape(16, 128, 16, 128).transpose(2, 1, 0, 3))
NEG = -30000.0
def attn_consts():
    s = np.arange(128)[:, None]; q = np.arange(128)[None, :]
    nd = np.zeros((128, 2, 128), np.float32); mk = np.zeros((128, 2, 128), np.float32)
    nd[:, 0] = -(q - s + 128); mk[:, 0] = np.where(s > q, 0, NEG)
    nd[:, 1] = -(q - s); mk[:, 1] = np.where(s <= q, 0, NEG)
    nd = np.where(mk < 0, 0, nd).astype(np.float32)
    t = np.arange(4)[None, :]
    ndc = -(128 + t - s).astype(np.float32); mkc = np.where(s > t, 0, NEG).astype(np.float32)
    ndc = np.where(mkc < 0, 0, ndc).astype(np.float32)
    a = np.arange(64)
    same = (a[:, None] // 4) == (a[None, :] // 4)
    tp = a[:, None] % 4; tq = a[None, :] % 4
    ok = same & (tp <= tq)
    ndn = np.where(ok, -(tq - tp), 0).astype(np.float32); mkn = np.where(ok, 0, NEG).astype(np.float32)
    return dict(nd=nd, mk=mk, ndc=ndc, mkc=mkc, ndn=ndn, mkn=mkn)
def tile_whg(w):
    out = np.empty((16, 128, 4, 16, 128), np.float32)
    for hh in range(16):
        for wh in range(4):
            out[hh, :, wh] = tile_cols(w[:, wh * 2048 + hh * 128: wh * 2048 + hh * 128 + 128])
    return out
def hgrn_consts():
    a = np.arange(128)
    m01 = ((a[:, None] // 32 == a[None, :] // 32) & (a[:, None] <= a[None, :])).astype(np.float32)
    cm = np.broadcast_to((a[None, :] // 32 == np.arange(4)[:, None]).astype(np.float32)[None], (128, 4, 128)).copy()
    s = np.arange(64)
    ms = ((s[:, None] // 4 == s[None, :] // 4) & (s[:, None] <= s[None, :])).astype(np.float32)
    cms = np.broadcast_to((s[None, :] // 4 == np.arange(16)[:, None]).astype(np.float32)[None], (128, 16, 64)).copy()
    t = np.arange(1088)
    sm = np.where(t < 1024, (t % 32) != 0, ((t - 1024) % 4) != 0).astype(np.float32)
    smask = np.ascontiguousarray(np.broadcast_to(sm[None], (128, 1088)))
    return dict(m01=m01, cm=cm, ms=ms, cms=cms, ident=np.eye(128, dtype=np.float32), smask=smask)

NCORE = 8
_PROGS = {}


def _prog(name, fn):
    if name not in _PROGS:
        _PROGS[name] = fn()
    return _PROGS[name]


def _run(name, fn, in_maps):
    nc = _prog(name, fn)
    res = run_bass_kernel_spmd(nc, in_maps, core_ids=list(range(NCORE)))
    return res.results


def _f32(a):
    return np.ascontiguousarray(np.asarray(a, dtype=np.float32))


def kernel(x_prompt, x_sample, cache_swa_k, cache_swa_v, state_hgrn, state_ffn_conv, c_prompt, c_sample,
           norm1_g, norm2_g, w_ada, b_ada, attn_w_qkv, attn_w_o, attn_sinks,
           hgrn_w_in, hgrn_lower_bounds, hgrn_norm_g, hgrn_w_o,
           ffn_w_in, ffn_conv_w, ffn_conv_b, ffn_w_out, final_norm_g):
    (x_prompt, x_sample, cache_swa_k, cache_swa_v, state_hgrn, state_ffn_conv, c_prompt, c_sample,
     norm1_g, norm2_g, w_ada, b_ada, attn_w_qkv, attn_w_o, attn_sinks,
     hgrn_w_in, hgrn_lower_bounds, hgrn_norm_g, hgrn_w_o,
     ffn_w_in, ffn_conv_w, ffn_conv_b, ffn_w_out, final_norm_g) = [_f32(a) for a in (
        x_prompt, x_sample, cache_swa_k, cache_swa_v, state_hgrn, state_ffn_conv, c_prompt, c_sample,
        norm1_g, norm2_g, w_ada, b_ada, attn_w_qkv, attn_w_o, attn_sinks,
        hgrn_w_in, hgrn_lower_bounds, hgrn_norm_g, hgrn_w_o,
        ffn_w_in, ffn_conv_w, ffn_conv_b, ffn_w_out, final_norm_g)]
    Dm = 2048
    xp = x_prompt[0]
    xs = x_sample.reshape(128 * 4, Dm)

    c_all = np.concatenate([c_prompt, c_sample], 0)
    cT = fm(c_all)
    maps = []
    for c in range(NCORE):
        wt = np.empty((24, 128, 16, 128), np.float32)
        bt = np.empty((128, 24), np.float32)
        for n in range(24):
            l, ch = n // 12, 12 * c + n % 12
            wt[n] = tile_cols(w_ada[l][:, ch * 128:(ch + 1) * 128])
            bt[:, n] = b_ada[l][ch * 128:(ch + 1) * 128]
        maps.append({"cT": cT, "wada": wt, "bada": bt})
    res = _run("adaln", build_adaln, maps)
    mod = np.empty((2, 129, 6 * Dm), np.float32)
    for c in range(NCORE):
        mt = res[c]["modT"]
        for n in range(24):
            l, ch = n // 12, 12 * c + n % 12
            mod[l][:, ch * 128:(ch + 1) * 128] = mt[:, n, :].T
    mod = mod.reshape(2, 129, 6, Dm)

    def vec_for(l, g_vec, i0, extra=None):
        rows = [vfm(g_vec), vfm(mod[l, 0, i0]), vfm(mod[l, 0, i0 + 1]), vfm(mod[l, 0, i0 + 2])]
        if extra is not None:
            rows.append(vfm(extra))
        return np.ascontiguousarray(np.stack(rows, 1))

    def mods_for(l, c, i0):
        sl = slice(1 + 16 * c, 1 + 16 * c + 16)
        return mods_fm(np.stack([mod[l, sl, i0], mod[l, sl, i0 + 1], mod[l, sl, i0 + 2]], 0))

    aconst = attn_consts()
    wq_t = tile_wqkv(attn_w_qkv)
    wo_t = tile_sq(attn_w_o)
    sinks_b = np.ascontiguousarray(np.broadcast_to(attn_sinks[None], (128, 32)))
    vec0 = vec_for(0, norm1_g[0], 0)
    maps = []
    for c in range(NCORE):
        halo = xp[1024 * c - 128:1024 * c] if c > 0 else np.zeros((128, Dm), np.float32)
        xc = np.concatenate([halo, xp[1024 * c:1024 * (c + 1)], xs[64 * c:64 * (c + 1)]], 0)
        m = {"xT": fm(xc), "vec": vec0, "mods": mods_for(0, c, 0), "wqkv": wq_t, "wo": wo_t,
             "kcT": np.ascontiguousarray(cache_swa_k[16 * c:16 * c + 16].transpose(2, 3, 0, 1)),
             "vc": np.ascontiguousarray(cache_swa_v[16 * c:16 * c + 16].transpose(2, 1, 0, 3)),
             "sinks": sinks_b, "hb": np.full((128, 1), NEG if c == 0 else 0.0, np.float32)}
        m.update(aconst)
        maps.append(m)
    res = _run("attn", build_attn2, maps)
    x1 = [unfm(res[c]["xo"]) for c in range(NCORE)]
    ko = res[NCORE - 1]["kout"].reshape(2, 64, 4, 192).transpose(1, 2, 0, 3).reshape(64, 8, 192)
    swa_k_prompt = np.ascontiguousarray(ko[:, :, :128].transpose(2, 1, 0))[None]
    swa_v_prompt = np.ascontiguousarray(res[NCORE - 1]["vout"][:, 0])[None]
    swa_k_sample = np.empty((128, 4, 8, 64), np.float32)
    swa_v_sample = np.empty((128, 4, 8, 64), np.float32)
    for c in range(NCORE):
        ko = res[c]["kout"].reshape(2, 64, 4, 192).transpose(1, 2, 0, 3).reshape(64, 8, 192)
        swa_k_sample[16 * c:16 * c + 16] = ko[:, :, 128:].transpose(2, 1, 0).reshape(16, 4, 8, 64)
        swa_v_sample[16 * c:16 * c + 16] = res[c]["vout"][:64, 1].reshape(16, 4, 8, 64)

    def run_ffn(l, xin, last):
        w_in_t = tile_w_in(ffn_w_in[l])
        w_out_t = tile_w_out(ffn_w_out[l])
        convw = np.ascontiguousarray(np.concatenate([ffn_conv_w[l], ffn_conv_b[l][None]], 0).reshape(4, 44, 128).transpose(2, 1, 0))
        vec = vec_for(l, norm2_g[l], 3, extra=final_norm_g)
        maps = []
        for c in range(NCORE):
            halo = xin[c - 1][1022:1024] if c > 0 else np.zeros((2, Dm), np.float32)
            xc = np.concatenate([halo, xin[c]], 0)
            maps.append({"xT": fm(xc), "vec": vec, "mods": mods_for(l, c, 3), "w_in": w_in_t, "w_out": w_out_t,
                         "convw": convw,
                         "cstate": np.ascontiguousarray(state_ffn_conv[l, 16 * c:16 * c + 16].reshape(16, 2, 44, 128).transpose(3, 2, 0, 1)),
                         "flag": np.full((128, 1), 0.0 if c == 0 else 1.0, np.float32)})
        res = _run("ffn_last" if last else "ffn", (lambda: build_ffn(True)) if last else (lambda: build_ffn(False)), maps)
        xout = [unfm(res[c]["xo"]) for c in range(NCORE)]
        cbp = np.ascontiguousarray(res[NCORE - 1]["cbp"].transpose(2, 1, 0).reshape(2, 5632))[None]
        cbs = np.concatenate([res[c]["cbs"].transpose(2, 3, 1, 0).reshape(16, 2, 5632) for c in range(NCORE)], 0)
        return xout, cbp, cbs

    x2, cbp0, cbs0 = run_ffn(0, x1, False)

    hconst = hgrn_consts()
    whg_t = tile_whg(hgrn_w_in)
    lbp = np.ascontiguousarray(hgrn_lower_bounds.reshape(2, 16, 128).transpose(2, 0, 1))
    vec1 = vec_for(1, norm1_g[1], 0)
    maps = []
    for c in range(NCORE):
        m = {"xT": fm(x2[c]), "vec": vec1, "mods": mods_for(1, c, 0), "whg": whg_t, "lbp": lbp,
             "s0": np.ascontiguousarray(state_hgrn[16 * c:16 * c + 16].transpose(1, 2, 0, 3))}
        m.update(hconst)
        maps.append(m)
    resA = _run("hgrnA", build_hgrna, maps)
    s_loc = [resA[c]["send"] for c in range(NCORE)]
    d_loc = [resA[c]["dout"] for c in range(NCORE)]
    hgrn_state_sample = np.concatenate([resA[c]["snew"].transpose(2, 0, 1, 3) for c in range(NCORE)], 0)

    wo2_t = tile_sq(hgrn_w_o)
    ng = np.ascontiguousarray(hgrn_norm_g.reshape(16, 128).T)
    maps = []
    for c in range(NCORE):
        sr = np.zeros((16, 128, NRK, 128), np.float32)
        dr = np.ones((128, 16, NRK), np.float32)
        for r in range(c):
            sr[:, :, r, :] = s_loc[r].transpose(1, 0, 2)
            dr[:, :, r] = d_loc[r]
        maps.append({"xT": fm(x2[c]), "vec": vec1, "mods": mods_for(1, c, 0), "oloc": resA[c]["oloc"], "qb": resA[c]["qbo"],
                     "sg": resA[c]["sgo"], "sr": sr, "dr": dr, "sloc": s_loc[c], "dl": d_loc[c], "ng": ng, "wo": wo2_t})
    res = _run("hgrnB", build_hgrnb, maps)
    x3 = [unfm(res[c]["xo"]) for c in range(NCORE)]
    hgrn_state_prompt = np.ascontiguousarray(res[NCORE - 1]["send"].transpose(1, 0, 2))[None]

    y, cbp1, cbs1 = run_ffn(1, x3, True)
    y_prompt = np.concatenate([y[c][:1024] for c in range(NCORE)], 0)[None]
    y_sample = np.concatenate([y[c][1024:] for c in range(NCORE)], 0).reshape(128, 4, Dm)
    ffn_conv_prompt = np.stack([cbp0, cbp1], 0)
    ffn_conv_sample = np.stack([cbs0, cbs1], 0)
    outs = (y_prompt, y_sample, swa_k_prompt, swa_v_prompt, swa_k_sample, swa_v_sample,
            hgrn_state_prompt, hgrn_state_sample, ffn_conv_prompt, ffn_conv_sample)
    return tuple(np.ascontiguousarray(o, dtype=np.float32) for o in outs)
```

```python
from concourse.bass_utils import run_bass_kernel_spmd

import contextlib
import numpy as np
import concourse.bass as bass
import concourse.mybir as mybir

F32 = mybir.dt.float32
BF16 = mybir.dt.bfloat16
I32 = mybir.dt.int32
AF = mybir.ActivationFunctionType
ALU = mybir.AluOpType
AX = mybir.AxisListType

ENGS = ["sp", "act", "pool", "dve", "pe"]


class Res:
    __slots__ = ("name", "w", "r")

    def __init__(self, name=""):
        self.name = name
        self.w = None
        self.r = []


class DSem:
    def __init__(self, handle, name):
        self.h = handle
        self.name = name
        self.cnt = 0


class _Rec:
    def __getattr__(self, name):
        return lambda *a, **k: (name, a, k)


_REC = _Rec()


class Builder:
    def __init__(self, nc, n_dsem=12):
        self.nc = nc
        self.es = contextlib.ExitStack()
        self.q = {e: [] for e in ENGS}
        self.cnt = {e: 0 for e in ENGS}
        self.waited = {e: {} for e in ENGS}
        self.esem = {e: self.es.enter_context(nc.semaphore("s_" + e)) for e in ENGS}
        self.dsems = [DSem(self.es.enter_context(nc.semaphore("d%d" % i)), "d%d" % i)
                      for i in range(n_dsem)]
        self.pending = {e: [] for e in ENGS}
        self.n_inst = 0

    def sb(self, name, shape, dt):
        return self.es.enter_context(self.nc.sbuf_tensor(name, list(shape), dt))

    def ps(self, name, shape, dt=F32):
        return self.es.enter_context(self.nc.psum_tensor(name, list(shape), dt))

    def _deps(self, eng, reads, writes):
        need = {}

        def add(t):
            if t is None:
                return
            k, v = t
            if need.get(k, 0) < v:
                need[k] = v
        for r in reads:
            add(r.w)
        for w in writes:
            add(w.w)
            for t in w.r:
                add(t)
        waits = []
        for k, v in need.items():
            if k == "pe" and eng == "pe":
                continue
            if self.waited[eng].get(k, 0) >= v:
                continue
            self.waited[eng][k] = v
            waits.append((k, v))
        return waits

    def _semh(self, k):
        return self.esem[k] if isinstance(k, str) else k.h

    def op(self, eng, fn, reads=(), writes=(), sig=True):
        reads = [r for r in reads if r is not None]
        writes = [w for w in writes if w is not None]
        waits = self._deps(eng, reads, writes)
        for k, v in waits:
            cur = self.cnt[k] if isinstance(k, str) else k.cnt
            assert v <= cur, "forward wait %s %d > %d" % (k, v, cur)
        ticket = None
        if sig:
            self.cnt[eng] += 1
            ticket = (eng, self.cnt[eng])
            pend = self.pending[eng]
            self.pending[eng] = []
            for pr, pw in pend:
                self._commit(pr, pw, ticket)
            self._commit(reads, writes, ticket)
        else:
            t = (eng, self.cnt[eng] + 1)
            self._commit(reads, writes, t)
        self.q[eng].append((waits, fn(_REC), ticket, None))
        self.n_inst += 1
        return ticket

    def _commit(self, reads, writes, ticket):
        for r in reads:
            r.r.append(ticket)
        for w in writes:
            w.w = ticket
            w.r = []

    def dma(self, eng, out, in_, dsem, reads=(), writes=(), **kw):
        reads = [r for r in reads if r is not None]
        writes = [w for w in writes if w is not None]
        waits = self._deps(eng, reads, writes)
        for k, v in waits:
            cur = self.cnt[k] if isinstance(k, str) else k.cnt
            assert v <= cur, "forward wait %s %d > %d" % (k, v, cur)
        dsem.cnt += 16
        ticket = (dsem, dsem.cnt)
        self._commit(reads, writes, ticket)
        kw2 = dict(kw); kw2["out"] = out; kw2["in_"] = in_
        self.q[eng].append((waits, ("dma_start", (), kw2), None, (dsem, 16)))
        self.n_inst += 1
        return ticket

    def wait_all(self, eng, tickets):
        waits = []
        for t in tickets:
            if t is None:
                continue
            k, v = t
            if self.waited[eng].get(k, 0) >= v:
                continue
            self.waited[eng][k] = v
            waits.append((k, v))
        self.q[eng].append((waits, None, None, None))

    def emit(self):
        nc = self.nc
        handles = {"sp": "sync", "act": "scalar", "pool": "gpsimd", "dve": "vector", "pe": "tensor"}
        with nc.Block() as block:
            for eng in ENGS:
                items = self.q[eng]
                if not items:
                    continue

                def body(e, items=items, eng=eng):
                    for waits, fn, ticket, dinc in items:
                        for k, v in waits:
                            e.wait_ge(self._semh(k), v)
                        if fn is None:
                            continue
                        name, a, k = fn
                        ins = getattr(e, name)(*a, **k)
                        if ticket is not None:
                            ins.then_inc(self.esem[eng], 1)
                        if dinc is not None:
                            ins.then_inc(dinc[0].h, dinc[1])
                getattr(block, handles[eng])(body)

    def close(self):
        self.es.close()


D = 2048
KC = 16
DFF = 5632
FC = 44
NQ = 4
FQ = FC // NQ
NP = 1024
NS = 64
EPS = 1e-6


class PsumRot:
    def __init__(self, b, n=8):
        self.tiles = [b.ps("psb%d" % i, [128, 512]) for i in range(n)]
        self.res = [Res("psb%d" % i) for i in range(n)]
        self.i = 0

    def next(self):
        t, r = self.tiles[self.i], self.res[self.i]
        self.i = (self.i + 1) % len(self.tiles)
        return t, r


def ntiles(n, step=512):
    return [(s, min(step, n - s)) for s in range(0, n, step)]


def emit_norm_mod(b, pr, x, r_x, h, r_h, ncol, np_cols, vec, iv_g, iv_sh, iv_sc, mods, im_sh, im_sc,
                  ones, r_const, tmp):
    xsq, r_xsq, rstd, r_rstd, gm, r_gm, gms, r_gms, t32, r_t32 = tmp
    tiles = ntiles(ncol)
    banks = [pr.next() for _ in tiles]
    for k in range(KC):
        i2 = k % 2
        b.op("act", lambda e, k=k, i2=i2: e.activation(xsq[i2][:, 0:ncol], x[:, k, 0:ncol], AF.Square),
             reads=[r_x], writes=[r_xsq[i2]])
        for ti, (s, n) in enumerate(tiles):
            pt, rp = banks[ti]
            b.op("pe", lambda e, pt=pt, s=s, n=n, i2=i2, k=k: e.matmul(
                pt[:, 0:n], ones[:], xsq[i2][:, s:s + n], start=(k == 0), stop=(k == KC - 1)),
                reads=[r_const, r_xsq[i2]], writes=[rp], sig=(k == KC - 1) or ti == len(tiles) - 1)
    for ti, (s, n) in enumerate(tiles):
        pt, rp = banks[ti]
        b.op("act", lambda e, pt=pt, s=s, n=n: e.activation(rstd[:, s:s + n], pt[:, 0:n], AF.Sqrt,
                                                            bias=EPS, scale=1.0 / D),
             reads=[rp], writes=[r_rstd])
    b.op("dve", lambda e: e.reciprocal(rstd[:, 0:ncol], rstd[:, 0:ncol]), reads=[r_rstd], writes=[r_rstd])
    b.op("dve", lambda e: e.scalar_tensor_tensor(gm[:], vec[:, iv_sc, :], 1.0, vec[:, iv_g, :], ALU.add, ALU.mult),
         reads=[r_const], writes=[r_gm])
    ns = ncol - np_cols
    if ns:
        b.op("dve", lambda e: e.scalar_tensor_tensor(
            gms[:], mods[:, im_sc, :, :], 1.0, vec[:, iv_g, :].unsqueeze(2).broadcast_to([128, KC, 16]),
            ALU.add, ALU.mult), reads=[r_const], writes=[r_gms])
    for k in range(KC):
        i2 = k % 2
        b.op("dve", lambda e, k=k, i2=i2: e.scalar_tensor_tensor(
            t32[i2][:, 0:np_cols], x[:, k, 0:np_cols], gm[:, k:k + 1], rstd[:, 0:np_cols], ALU.mult, ALU.mult),
            reads=[r_x, r_gm, r_rstd], writes=[r_t32[i2]])
        if ns:
            b.op("dve", lambda e, k=k, i2=i2: e.tensor_tensor(
                t32[i2][:, np_cols:ncol].rearrange("p (s t) -> p s t", t=4),
                x[:, k, np_cols:ncol].rearrange("p (s t) -> p s t", t=4),
                gms[:, k, :].unsqueeze(2).broadcast_to([128, 16, 4]), ALU.mult),
                reads=[r_x, r_gms], writes=[r_t32[i2]])
            b.op("dve", lambda e, k=k, i2=i2: e.tensor_tensor(
                t32[i2][:, np_cols:ncol], t32[i2][:, np_cols:ncol], rstd[:, np_cols:ncol], ALU.mult),
                reads=[r_t32[i2], r_rstd], writes=[r_t32[i2]])
            b.op("dve", lambda e, k=k, i2=i2: e.tensor_tensor(
                t32[i2][:, np_cols:ncol].rearrange("p (s t) -> p s t", t=4),
                t32[i2][:, np_cols:ncol].rearrange("p (s t) -> p s t", t=4),
                mods[:, im_sh, k, :].unsqueeze(2).broadcast_to([128, 16, 4]), ALU.add),
                reads=[r_t32[i2], r_const], writes=[r_t32[i2]])
            b.op("act", lambda e, k=k, i2=i2: e.activation(h[:, k, np_cols:ncol], t32[i2][:, np_cols:ncol], AF.Copy),
                 reads=[r_t32[i2]], writes=[r_h])
        b.op("act", lambda e, k=k, i2=i2: e.activation(
            h[:, k, 0:np_cols], t32[i2][:, 0:np_cols], AF.Identity, bias=vec[:, iv_sh, k:k + 1], scale=1.0),
            reads=[r_t32[i2], r_const], writes=[r_h])


def build_ffn(last):
    nc = bass.Bass("TRN2", target_bir_lowering=False)
    NCOL = 2 + NP + NS
    UW = 2 + NP + 16 * 6
    AW = UW - 2
    dram = lambda name, shape, kind="ExternalInput": nc.dram_tensor(name, list(shape), F32, kind=kind).ap()
    xT = dram("xT", [128, KC, NCOL])
    vecd = dram("vec", [128, 5, KC])
    modsd = dram("mods", [128, 3, KC, 16])
    w_in = dram("w_in", [FC, 128, 2, KC, 128])
    w_out = dram("w_out", [NQ, KC, 128, FQ, 128])
    convd = dram("convw", [128, FC, 4])
    cstd = dram("cstate", [128, FC, 16, 2])
    flagd = dram("flag", [128, 1])
    xo = dram("xo", [128, KC, NP + NS], "ExternalOutput")
    cbp = dram("cbp", [128, FC, 2], "ExternalOutput")
    cbs = dram("cbs", [128, FC, 16, 2], "ExternalOutput")

    b = Builder(nc, n_dsem=14)
    d = b.dsems
    pr = PsumRot(b)
    x = b.sb("x", [128, KC, NCOL], F32)
    h = b.sb("h", [128, KC, NCOL], BF16)
    act = b.sb("act", [128, FQ, NP + NS], BF16)
    U = [b.sb("U%d" % i, [128, UW], F32) for i in range(2)]
    G = [b.sb("G%d" % i, [128, NP + NS], F32) for i in range(2)]
    A = b.sb("A", [128, UW], F32)
    rstd = b.sb("rstd", [128, NCOL], F32)
    xsq = [b.sb("xsq%d" % i, [128, NCOL], BF16) for i in range(2)]
    t32 = [b.sb("t32%d" % i, [128, NCOL], F32) for i in range(2)]
    win = [b.sb("win%d" % i, [128, 2, KC, 128], BF16) for i in range(2)]
    wout = [b.sb("wout%d" % i, [128, FQ, 128], BF16) for i in range(2)]
    vec = b.sb("vecs", [128, 5, KC], F32)
    mods = b.sb("modss", [128, 3, KC, 16], F32)
    convw = b.sb("convws", [128, FC, 4], F32)
    cst = b.sb("csts", [128, FC, 16, 2], F32)
    cbps = b.sb("cbps", [128, FC, 2], F32)
    cbss = b.sb("cbss", [128, FC, 16, 2], F32)
    flag = b.sb("flags", [128, 1], F32)
    ones = b.sb("ones", [128, 128], BF16)
    gm = b.sb("gm", [128, KC], F32)
    gms = b.sb("gms", [128, KC, 16], F32)
    tmps = b.sb("tmps", [128, NS], F32)

    R = lambda n: Res(n)
    r_x, r_h, r_act, r_A, r_rstd, r_const, r_gm, r_gms, r_cb, r_tmps = [R(n) for n in
        "x h act A rstd const gm gms cb tmps".split()]
    r_U = [R("U0"), R("U1")]; r_G = [R("G0"), R("G1")]
    r_xsq = [R("xsq0"), R("xsq1")]; r_t32 = [R("t0"), R("t1")]
    r_win = [R("win0"), R("win1")]; r_wout = [R("wo0"), R("wo1")]

    b.dma("sp", x[:], xT, d[0], writes=[r_x])
    b.dma("sp", vec[:], vecd, d[1], writes=[r_const])
    b.dma("sp", mods[:], modsd, d[1], writes=[r_const])
    b.dma("sp", convw[:], convd, d[1], writes=[r_const])
    b.dma("sp", cst[:], cstd, d[1], writes=[r_const])
    b.dma("sp", flag[:], flagd, d[1], writes=[r_const])
    b.op("pool", lambda e: e.memset(ones[:], 1.0), writes=[r_const])

    win_sem = [d[2], d[3]]
    wout_sem = [d[4], d[5]]

    def load_win(j):
        b.dma("pool", win[j % 2][:], w_in[j], win_sem[j % 2], writes=[r_win[j % 2]])

    def load_wout(q, i):
        n = q * KC + i
        b.dma("pool", wout[n % 2][:], w_out[q, i], wout_sem[n % 2], writes=[r_wout[n % 2]])

    load_win(0)
    emit_norm_mod(b, pr, x, r_x, h, r_h, NCOL, 2 + NP, vec, 0, 1, 2, mods, 0, 1, ones, r_const,
                  (xsq, r_xsq, rstd, r_rstd, gm, r_gm, gms, r_gms, t32, r_t32))

    tiles = ntiles(NCOL)
    for q in range(NQ):
        for jj in range(FQ):
            j = q * FQ + jj
            if j + 1 < FC:
                load_win(j + 1)
            w = win[j % 2]; rw = r_win[j % 2]
            Uj, rU = U[j % 2], r_U[j % 2]
            Gj, rG = G[j % 2], r_G[j % 2]
            for which in range(2):
                for (s, n) in tiles:
                    pt, rp = pr.next()
                    for k in range(KC):
                        b.op("pe", lambda e, pt=pt, w=w, which=which, k=k, s=s, n=n: e.matmul(
                            pt[:, 0:n], w[:, which, k, :], h[:, k, s:s + n], start=(k == 0), stop=(k == KC - 1)),
                            reads=[rw, r_h], writes=[rp], sig=(k == KC - 1))
                    if which == 0:
                        if s + n <= 2 + NP:
                            b.op("act", lambda e, pt=pt, s=s, n=n, Uj=Uj: e.activation(Uj[:, s:s + n], pt[:, 0:n], AF.Copy),
                                 reads=[rp], writes=[rU])
                        else:
                            npart = 2 + NP - s
                            b.op("act", lambda e, pt=pt, s=s, npart=npart, Uj=Uj: e.activation(
                                Uj[:, s:s + npart], pt[:, 0:npart], AF.Copy), reads=[rp], writes=[rU])
                            b.op("act", lambda e, pt=pt, npart=npart, Uj=Uj: e.activation(
                                Uj[:, 2 + NP:UW].rearrange("p (s c) -> p s c", c=6)[:, :, 2:6],
                                pt[:, npart:npart + NS].rearrange("p (s t) -> p s t", t=4), AF.Copy),
                                reads=[rp], writes=[rU])
                    else:
                        if s == 0:
                            b.op("act", lambda e, pt=pt, n=n, Gj=Gj: e.activation(Gj[:, 0:n - 2], pt[:, 2:n], AF.Copy),
                                 reads=[rp], writes=[rG])
                        else:
                            b.op("act", lambda e, pt=pt, s=s, n=n, Gj=Gj: e.activation(Gj[:, s - 2:s - 2 + n], pt[:, 0:n], AF.Copy),
                                 reads=[rp], writes=[rG])
            b.op("pool", lambda e, Uj=Uj: e.tensor_scalar(Uj[:, 0:2], Uj[:, 0:2], flag[:, 0:1], None, ALU.mult),
                 reads=[rU, r_const], writes=[rU])
            b.op("pool", lambda e, Uj=Uj, j=j: e.tensor_copy(
                Uj[:, 2 + NP:UW].rearrange("p (s c) -> p s c", c=6)[:, :, 0:2], cst[:, j, :, :]),
                reads=[r_const], writes=[rU])
            b.op("dve", lambda e, Uj=Uj, j=j: e.tensor_scalar(A[:, 0:AW], Uj[:, 2:UW], convw[:, j, 2:3], None, ALU.mult),
                 reads=[rU, r_const], writes=[r_A])
            b.op("dve", lambda e, Uj=Uj, j=j: e.scalar_tensor_tensor(A[:, 0:AW], Uj[:, 1:UW - 1], convw[:, j, 1:2], A[:, 0:AW],
                                                                 ALU.mult, ALU.add), reads=[rU, r_A, r_const], writes=[r_A])
            b.op("dve", lambda e, Uj=Uj, j=j: e.scalar_tensor_tensor(A[:, 0:AW], Uj[:, 0:AW], convw[:, j, 0:1], A[:, 0:AW],
                                                                 ALU.mult, ALU.add), reads=[rU, r_A, r_const], writes=[r_A])
            b.op("act", lambda e, j=j: e.activation(A[:, 0:AW], A[:, 0:AW], AF.Gelu, bias=convw[:, j, 3:4], scale=1.0),
                 reads=[r_A, r_const], writes=[r_A])
            b.op("dve", lambda e, jj=jj, Gj=Gj: e.tensor_tensor(act[:, jj, 0:NP], A[:, 0:NP], Gj[:, 0:NP], ALU.mult),
                 reads=[r_A, rG], writes=[r_act])
            b.op("dve", lambda e, jj=jj, Gj=Gj: e.tensor_tensor(
                act[:, jj, NP:NP + NS].rearrange("p (s t) -> p s t", t=4),
                A[:, NP + 2:UW].rearrange("p (s c) -> p s c", c=6)[:, :, 0:4],
                Gj[:, NP:NP + NS].rearrange("p (s t) -> p s t", t=4), ALU.mult),
                reads=[r_A, rG], writes=[r_act])
            b.op("pool", lambda e, Uj=Uj, j=j: e.tensor_copy(cbps[:, j, :], Uj[:, NP:NP + 2]), reads=[rU], writes=[r_cb])
            b.op("pool", lambda e, Uj=Uj, j=j: e.tensor_copy(
                cbss[:, j, :, :], Uj[:, 2 + NP:UW].rearrange("p (s c) -> p s c", c=6)[:, :, 4:6]), reads=[rU], writes=[r_cb])
        load_wout(q, 0)
        for i in range(KC):
            if i + 1 < KC:
                load_wout(q, i + 1)
            n_ = q * KC + i
            w = wout[n_ % 2]; rw = r_wout[n_ % 2]
            for (s, n) in ntiles(NP + NS):
                pt, rp = pr.next()
                for jj in range(FQ):
                    b.op("pe", lambda e, pt=pt, w=w, jj=jj, s=s, n=n: e.matmul(
                        pt[:, 0:n], w[:, jj, :], act[:, jj, s:s + n], start=(jj == 0), stop=(jj == FQ - 1)),
                        reads=[rw, r_act], writes=[rp], sig=(jj == FQ - 1))
                if s + n <= NP:
                    b.op("dve", lambda e, pt=pt, i=i, s=s, n=n: e.scalar_tensor_tensor(
                        x[:, i, 2 + s:2 + s + n], pt[:, 0:n], vec[:, 3, i:i + 1], x[:, i, 2 + s:2 + s + n], ALU.mult, ALU.add),
                        reads=[rp, r_const, r_x], writes=[r_x])
                else:
                    assert s == NP and n == NS
                    b.op("dve", lambda e, pt=pt, i=i: e.tensor_tensor(
                        tmps[:].rearrange("p (s t) -> p s t", t=4), pt[:, 0:NS].rearrange("p (s t) -> p s t", t=4),
                        mods[:, 2, i, :].unsqueeze(2).broadcast_to([128, 16, 4]), ALU.mult),
                        reads=[rp, r_const], writes=[r_tmps])
                    b.op("dve", lambda e, i=i: e.tensor_tensor(x[:, i, 2 + NP:NCOL], x[:, i, 2 + NP:NCOL], tmps[:], ALU.add),
                         reads=[r_tmps, r_x], writes=[r_x])
    outs = []
    if last:
        tl = ntiles(NCOL)
        banks = [pr.next() for _ in tl]
        for k in range(KC):
            i2 = k % 2
            b.op("act", lambda e, k=k, i2=i2: e.activation(xsq[i2][:, 0:NCOL], x[:, k, 0:NCOL], AF.Square),
                 reads=[r_x], writes=[r_xsq[i2]])
            for ti, (s, n) in enumerate(tl):
                pt, rp = banks[ti]
                b.op("pe", lambda e, pt=pt, s=s, n=n, i2=i2, k=k: e.matmul(
                    pt[:, 0:n], ones[:], xsq[i2][:, s:s + n], start=(k == 0), stop=(k == KC - 1)),
                    reads=[r_const, r_xsq[i2]], writes=[rp], sig=True)
        for ti, (s, n) in enumerate(tl):
            pt, rp = banks[ti]
            b.op("act", lambda e, pt=pt, s=s, n=n: e.activation(rstd[:, s:s + n], pt[:, 0:n], AF.Sqrt, bias=EPS, scale=1.0 / D),
                 reads=[rp], writes=[r_rstd])
        b.op("dve", lambda e: e.reciprocal(rstd[:, 0:NCOL], rstd[:, 0:NCOL]), reads=[r_rstd], writes=[r_rstd])
        for k in range(KC):
            b.op("dve", lambda e, k=k: e.scalar_tensor_tensor(
                x[:, k, :], x[:, k, :], vec[:, 4, k:k + 1], rstd[:, 0:NCOL], ALU.mult, ALU.mult),
                reads=[r_x, r_rstd, r_const], writes=[r_x])
    outs.append(b.dma("sp", xo, x[:, :, 2:NCOL], d[6], reads=[r_x]))
    outs.append(b.dma("sp", cbp, cbps[:], d[7], reads=[r_cb]))
    outs.append(b.dma("sp", cbs, cbss[:], d[8], reads=[r_cb]))
    b.wait_all("sp", outs)
    b.emit()
    b.close()
    return nc


NKV = 8
SCALE = 64 ** -0.5
NEG = -30000.0


def alibi_slope(h):
    return float(2.0 ** (-8.0 * (h + 1) / 32))


def build_attn2():
    nc = bass.Bass("TRN2", target_bir_lowering=False)
    NH = 128
    NCOL = NH + NP + NS
    NQC = NP + NS
    NB = 8
    dram = lambda name, shape, kind="ExternalInput": nc.dram_tensor(name, list(shape), F32, kind=kind).ap()
    xT = dram("xT", [128, KC, NCOL])
    vecd = dram("vec", [128, 4, KC])
    modsd = dram("mods", [128, 3, KC, 16])
    wqkv = dram("wqkv", [NKV, 128, 4, KC, 128])
    wo = dram("wo", [KC, 128, KC, 128])
    kcT = dram("kcT", [NKV, 64, 16, 128])
    vc = dram("vc", [NKV, 128, 16, 64])
    ndd = dram("nd", [128, 2, 128]); mkd = dram("mk", [128, 2, 128])
    ndcd = dram("ndc", [128, 4]); mkcd = dram("mkc", [128, 4])
    ndnd = dram("ndn", [64, 64]); mknd = dram("mkn", [64, 64])
    sinkd = dram("sinks", [128, 32])
    hbd = dram("hb", [128, 1])
    xo = dram("xo", [128, KC, NQC], "ExternalOutput")
    kout = dram("kout", [128, NKV // 2, 192], "ExternalOutput")
    vout = dram("vout", [128, 2, NKV, 64], "ExternalOutput")

    b = Builder(nc, n_dsem=16)
    d = b.dsems
    pr = PsumRot(b)
    xbuf = b.sb("xbuf", [128, KC, NCOL], F32)
    x = xbuf
    OT = xbuf[:].rearrange("p k n -> p (k n)").bitcast(BF16)[:, 0:KC * NQC].rearrange("p (k n) -> p k n", k=KC)
    h = b.sb("h", [128, KC, NCOL], BF16)
    rstd = b.sb("rstd", [128, NCOL], F32)
    xsq0 = b.sb("xsq0", [128, NCOL], BF16); xsq = [xsq0, xsq0]
    t32a = b.sb("t32a", [128, NCOL], F32); t32 = [t32a, t32a]
    NWB = 4
    wring = [b.sb("wr%d" % i, [128, KC, 128], BF16) for i in range(NWB)]
    Qgs = [b.sb("Qg%d" % i, [128, 2, NQC], BF16) for i in range(2)]
    Klos = [b.sb("Klo%d" % i, [128, NCOL], BF16) for i in range(2)]
    Khis = [b.sb("Khi%d" % i, [128, NCOL], BF16) for i in range(2)]
    Vds = [b.sb("Vd%d" % i, [128, 10, 64], BF16) for i in range(2)]
    Kclo = b.sb("Kclo", [128, 16, 128], BF16); Kchi = b.sb("Kchi", [128, 16, 128], BF16)
    Vcd = b.sb("Vcd", [128, 16, 64], BF16)
    sc = [b.sb("sc%d" % i, [128, 512], F32) for i in range(2)]
    P = [b.sb("P%d" % i, [128, 512], BF16) for i in range(4)]
    rden = b.sb("rden", [128, 512], F32)
    biasg = b.sb("biasg", [128, 4, 2, 128], F32)
    biasc = b.sb("biasc", [128, 4, 4], F32)
    biasn = b.sb("biasn", [64, 4, 64], F32)
    Pc = b.sb("Pc", [128, 16, 16], BF16)
    Pn = b.sb("Pn", [64, 16, 4, 4], BF16)
    scc = b.sb("scc", [128, 16, 16], F32)
    scn = b.sb("scn", [64, 4, 64], F32)
    vec = b.sb("vecs", [128, 4, KC], F32)
    mods = b.sb("modss", [128, 3, KC, 16], F32)
    nd = b.sb("nds", [128, 2, 128], F32); mk = b.sb("mks", [128, 2, 128], F32)
    ndc = b.sb("ndcs", [128, 4], F32); mkc = b.sb("mkcs", [128, 4], F32)
    ndn = b.sb("ndns", [64, 64], F32); mkn = b.sb("mkns", [64, 64], F32)
    esink = b.sb("esink", [128, 32], F32)
    hb = b.sb("hbs", [128, 1], F32)
    esg = b.sb("esg", [128, 4], F32)
    ones = b.sb("ones", [128, 128], BF16)
    gm = b.sb("gm", [128, KC], F32)
    gms = b.sb("gms", [128, KC, 16], F32)
    koutS = b.sb("koutS", [128, NKV // 2, 192], F32)
    voutS = b.sb("voutS", [128, 2, NKV, 64], F32)
    xr = [t32a[:, 0:NQC], rstd[:, 0:NQC]]
    tmps = b.sb("tmps", [128, NS], F32)

    R = Res
    r_x, r_h, r_rstd, r_const, r_gm, r_gms, r_Q, r_K, r_V, r_Kc, r_Vc, r_rden, r_bias, r_OT = [R(n) for n in
        "x h rstd const gm gms Q K V Kc Vc rden bias OT".split()]
    r_Pc, r_Pn, r_scc, r_scn, r_ko, r_vo, r_tmps = [R(n) for n in "Pc Pn scc scn ko vo tmps".split()]
    r_xsq0 = R("xsq"); r_xsq = [r_xsq0, r_xsq0]; r_t32a = R("t32"); r_t32 = [r_t32a, r_t32a]
    r_wr = [R("wr%d" % i) for i in range(NWB)]
    r_sc = [R("a"), R("b")]; r_P = [R("a") for _ in range(4)]
    r_xr = [r_t32a, r_rstd]
    r_OT = r_x

    b.dma("sp", x[:], xT, d[0], writes=[r_x])
    for i, (dst, src) in enumerate([(vec, vecd), (mods, modsd), (nd, ndd), (mk, mkd), (ndc, ndcd), (mkc, mkcd),
                                    (ndn, ndnd), (mkn, mknd), (esink, sinkd), (hb, hbd)]):
        b.dma("sp", dst[:], src, d[1], writes=[r_const])
    b.op("pool", lambda e: e.memset(ones[:], 1.0), writes=[r_const])
    b.op("pool", lambda e: e.memset(voutS[:], 0.0), writes=[r_vo])
    r_Qs = [Res("Q0"), Res("Q1")]; r_Ks = [Res("K0"), Res("K1")]; r_Vs = [Res("V0"), Res("V1")]
    for i_ in range(2):
        b.op("pool", lambda e, i_=i_: e.memset(Klos[i_][:], 0.0), writes=[r_Ks[i_]])
        b.op("pool", lambda e, i_=i_: e.memset(Khis[i_][:], 0.0), writes=[r_Ks[i_]])
    b.op("pool", lambda e: e.memset(Kclo[:], 0.0), writes=[r_Kc])
    b.op("pool", lambda e: e.memset(Kchi[:], 0.0), writes=[r_Kc])
    b.op("act", lambda e: e.activation(esink[:], esink[:], AF.Exp), reads=[r_const], writes=[r_const])

    wsem = [d[2], d[3], d[8], d[9]]
    NWT = NKV * 4 + KC

    def load_w(n):
        if n >= NWT:
            return
        src = wqkv[n // 4][:, n % 4, :, :] if n < NKV * 4 else wo[n - NKV * 4]
        b.dma("pool", wring[n % NWB][:], src, wsem[n % NWB], writes=[r_wr[n % NWB]])

    for n in range(4):
        load_w(n)
    emit_norm_mod(b, pr, x, r_x, h, r_h, NCOL, NH + NP, vec, 0, 1, 2, mods, 0, 1, ones, r_const,
                  (xsq, r_xsq, rstd, r_rstd, gm, r_gm, gms, r_gms, t32, r_t32))

    def proj(g):
        Qg = Qgs[g % 2]; Klo = Klos[g % 2]; Khi = Khis[g % 2]; Vd = Vds[g % 2]
        r_Q = r_Qs[g % 2]; r_K = r_Ks[g % 2]; r_V = r_Vs[g % 2]
        for which in range(2):
            wn = g * 4 + which; w = wring[wn % NWB]; rw = r_wr[wn % NWB]
            for (s, n) in ntiles(NQC):
                yield
                pt, rp = pr.next()
                for k in range(KC):
                    b.op("pe", lambda e, pt=pt, w=w, which=which, k=k, s=s, n=n: e.matmul(
                        pt[:, 0:n], w[:, k, :], h[:, k, NH + s:NH + s + n], start=(k == 0), stop=(k == KC - 1)),
                        reads=[rw, r_h], writes=[rp], sig=(k == KC - 1))
                b.op("act", lambda e, pt=pt, which=which, s=s, n=n: e.activation(Qg[:, which, s:s + n], pt[:, 0:n], AF.Copy),
                     reads=[rp], writes=[r_Q])
            load_w(wn + 4)
        wn = g * 4 + 2; w = wring[wn % NWB]; rw = r_wr[wn % NWB]
        for (s, n) in ntiles(NCOL):
            yield
            pt, rp = pr.next()
            for k in range(KC):
                b.op("pe", lambda e, pt=pt, w=w, k=k, s=s, n=n: e.matmul(
                    pt[:, 0:n], w[:, k, :], h[:, k, s:s + n], start=(k == 0), stop=(k == KC - 1)),
                    reads=[rw, r_h], writes=[rp], sig=(k == KC - 1))
            b.op("act", lambda e, pt=pt, s=s, n=n: e.activation(Klo[0:64, s:s + n], pt[0:64, 0:n], AF.Copy),
                 reads=[rp], writes=[r_K])
            b.op("act", lambda e, pt=pt, s=s, n=n: e.activation(Khi[64:128, s:s + n], pt[64:128, 0:n], AF.Copy),
                 reads=[rp], writes=[r_K])
            if s == 1024:
                lo = 64 * (g % 2)
                if lo == 0:
                    b.op("act", lambda e, pt=pt, g=g, lo=lo: e.activation(koutS[lo:lo + 64, g // 2, :], pt[lo:lo + 64, 0:192], AF.Copy),
                         reads=[rp], writes=[r_ko])
                else:
                    b.op("act", lambda e, pt=pt, g=g, lo=lo: e.activation(koutS[lo:lo + 64, g // 2, :], pt[lo:lo + 64, 0:192], AF.Copy),
                         reads=[rp], writes=[r_ko])
        load_w(wn + 4)
        wn = g * 4 + 3; w = wring[wn % NWB]; rw = r_wr[wn % NWB]
        for blk in range(10):
            m = 128 if blk < 9 else NS
            c0 = blk * 128
            yield
            pt, rp = pr.next()
            for k in range(KC):
                b.op("pe", lambda e, pt=pt, w=w, k=k, c0=c0, m=m: e.matmul(
                    pt[0:m, 0:64], h[:, k, c0:c0 + m], w[:, k, 0:64], start=(k == 0), stop=(k == KC - 1)),
                    reads=[rw, r_h], writes=[rp], sig=(k == KC - 1))
            b.op("dve", lambda e, pt=pt, blk=blk, m=m: e.tensor_copy(Vd[0:m, blk, :], pt[0:m, 0:64]),
                 reads=[rp], writes=[r_V])
            if blk >= 8:
                b.op("dve", lambda e, pt=pt, blk=blk, m=m, g=g: e.tensor_copy(voutS[0:m, blk - 8, g, :], pt[0:m, 0:64]),
                     reads=[rp], writes=[r_vo])
        load_w(wn + 4)
        yield

    def attn(g):
        Qg = Qgs[g % 2]; Klo = Klos[g % 2]; Khi = Khis[g % 2]; Vd = Vds[g % 2]
        r_Q = r_Qs[g % 2]; r_K = r_Ks[g % 2]; r_V = r_Vs[g % 2]
        b.dma("pool", Kclo[0:64, :, :], kcT[g], d[4], writes=[r_Kc])
        b.dma("pool", Kchi[64:128, :, :], kcT[g], d[5], writes=[r_Kc])
        b.dma("pool", Vcd[:], vc[g], d[6], writes=[r_Vc])
        b.op("dve", lambda e, g=g: e.tensor_copy(esg[:], esink[:, 4 * g:4 * g + 4]), reads=[r_const], writes=[r_bias])
        for hq in range(4):
            sl = alibi_slope(4 * g + hq)
            b.op("dve", lambda e, hq=hq, sl=sl: e.scalar_tensor_tensor(biasg[:, hq, :, :], nd[:], sl, mk[:], ALU.mult, ALU.add),
                 reads=[r_const], writes=[r_bias])
            b.op("dve", lambda e, hq=hq, sl=sl: e.scalar_tensor_tensor(biasc[:, hq, :], ndc[:], sl, mkc[:], ALU.mult, ALU.add),
                 reads=[r_const], writes=[r_bias])
            b.op("dve", lambda e, hq=hq, sl=sl: e.scalar_tensor_tensor(biasn[:, hq, :], ndn[:], sl, mkn[:], ALU.mult, ALU.add),
                 reads=[r_const], writes=[r_bias])
        for i in range(1, NB + 1):
            yield
            qc = (i - 1) * 128
            ptd, rpd = pr.next()
            ptv, rpv = pr.next()
            Pp = []
            for pair in range(2):
                pts, rps = pr.next()
                for hh in range(2):
                    hq = pair * 2 + hh
                    Kx = Klo if hh == 0 else Khi
                    for j in range(2):
                        kc0 = (i - 1 + j) * 128
                        b.op("pe", lambda e, pts=pts, hh=hh, j=j, Kx=Kx, kc0=kc0, pair=pair, qc=qc: e.matmul(
                            pts[:, (hh * 2 + j) * 128:(hh * 2 + j + 1) * 128], Kx[:, kc0:kc0 + 128], Qg[:, pair, qc:qc + 128],
                            start=True, stop=True), reads=[r_K, r_Q], writes=[rps], sig=(hh == 1 and j == 1))
                si = pair
                b.op("dve", lambda e, pts=pts, si=si, pair=pair: e.scalar_tensor_tensor(
                    sc[si][:], pts[:, 0:512], SCALE, biasg[:, pair * 2:pair * 2 + 2, :, :].rearrange("p a b c -> p (a b c)"),
                    ALU.mult, ALU.add), reads=[rps, r_bias], writes=[r_sc[si]])
                if i == 1:
                    b.op("dve", lambda e, si=si: e.tensor_scalar(
                        sc[si][:].rearrange("p (a b c) -> p a b c", a=2, b=2)[:, :, 0, :],
                        sc[si][:].rearrange("p (a b c) -> p a b c", a=2, b=2)[:, :, 0, :], hb[:, 0:1], None, ALU.add),
                        reads=[r_sc[si], r_const], writes=[r_sc[si]])
                pi = (i % 2) * 2 + pair
                b.op("act", lambda e, pi=pi, si=si: e.activation(P[pi][:], sc[si][:], AF.Exp),
                     reads=[r_sc[si]], writes=[r_P[pi]])
                Pp.append(pi)
            for pair in range(2):
                pi = Pp[pair]
                for hh in range(2):
                    hq = pair * 2 + hh
                    for j in range(2):
                        b.op("pe", lambda e, pi=pi, hh=hh, j=j, hq=hq: e.matmul(
                            ptd[:, hq * 128:(hq + 1) * 128], ones[:], P[pi][:, (hh * 2 + j) * 128:(hh * 2 + j + 1) * 128],
                            start=(j == 0), stop=(j == 1)), reads=[r_P[pi], r_const], writes=[rpd],
                            sig=(pair == 1 and hh == 1 and j == 1))
            for pair in range(2):
                pi = Pp[pair]
                for hh in range(2):
                    hq = pair * 2 + hh
                    for j in range(2):
                        blk = i - 1 + j
                        b.op("pe", lambda e, pi=pi, hh=hh, j=j, hq=hq, blk=blk: e.matmul(
                            ptv[64 * hh:64 * hh + 64, hq * 128:(hq + 1) * 128], Vd[:, blk, :], P[pi][:, (hh * 2 + j) * 128:(hh * 2 + j + 1) * 128],
                            start=(j == 0), stop=(j == 1)), reads=[r_P[pi], r_V], writes=[rpv],
                            sig=(pair == 1 and hh == 1 and j == 1))
            b.op("dve", lambda e, ptd=ptd, g=g: e.tensor_tensor(
                rden[:].rearrange("p (a q) -> p a q", a=4), ptd[:, 0:512].rearrange("p (a q) -> p a q", a=4),
                esg[:].unsqueeze(2).broadcast_to([128, 4, 128]), ALU.add),
                reads=[rpd, r_bias], writes=[r_rden])
            b.op("act", lambda e: e.activation(rden[:], rden[:], AF.Ln), reads=[r_rden], writes=[r_rden])
            b.op("act", lambda e: e.activation(rden[:], rden[:], AF.Exp, scale=-1.0), reads=[r_rden], writes=[r_rden])
            for hq in range(4):
                lo = 0 if hq % 2 == 0 else 64
                ch = (2 * g + hq // 2)
                b.op("dve", lambda e, ptv=ptv, hq=hq, lo=lo, ch=ch, qc=qc: e.tensor_tensor(
                    OT[lo:lo + 64, ch, qc:qc + 128], ptv[lo:lo + 64, hq * 128:(hq + 1) * 128],
                    rden[lo:lo + 64, hq * 128:(hq + 1) * 128], ALU.mult),
                    reads=[rpv, r_rden, r_x], writes=[r_OT])
        yield
        ptc, rpc = pr.next()
        ptn, rpn = pr.next()
        for sq in range(16):
            for hq in range(4):
                Kx = Kclo if hq % 2 == 0 else Kchi
                b.op("pe", lambda e, sq=sq, hq=hq, Kx=Kx: e.matmul(
                    ptc[:, sq * 16 + hq * 4:sq * 16 + hq * 4 + 4], Kx[:, sq, :], Qg[:, hq // 2, NP + sq * 4:NP + sq * 4 + 4],
                    start=True, stop=True), reads=[r_Kc, r_Q], writes=[rpc], sig=(sq == 15 and hq == 3))
        for hq in range(4):
            Kx = Klo if hq % 2 == 0 else Khi
            b.op("pe", lambda e, hq=hq, Kx=Kx: e.matmul(
                ptn[0:64, hq * 64:(hq + 1) * 64], Kx[:, NH + NP:NCOL], Qg[:, hq // 2, NP:NQC],
                start=True, stop=True), reads=[r_K, r_Q], writes=[rpn], sig=(hq == 3))
        b.op("dve", lambda e: e.scalar_tensor_tensor(
            scc[:].rearrange("p s (a t) -> p s a t", a=4), ptc[:, 0:256].rearrange("p (s a t) -> p s a t", s=16, a=4),
            SCALE, biasc[:].unsqueeze(1).broadcast_to([128, 16, 4, 4]), ALU.mult, ALU.add),
            reads=[rpc, r_bias], writes=[r_scc])
        b.op("act", lambda e: e.activation(Pc[:], scc[:], AF.Exp), reads=[r_scc], writes=[r_Pc])
        b.op("dve", lambda e: e.scalar_tensor_tensor(
            scn[:].rearrange("p a n -> p (a n)"), ptn[0:64, 0:256], SCALE, biasn[:].rearrange("p a n -> p (a n)"),
            ALU.mult, ALU.add), reads=[rpn, r_bias], writes=[r_scn])
        b.op("act", lambda e: e.activation(
            Pn[:].rearrange("p s a t -> p a s t"), scn[:].rearrange("p a (s t) -> p a s t", t=4), AF.Exp),
            reads=[r_scn], writes=[r_Pn])
        ptd, rpd = pr.next()
        ptv, rpv = pr.next()
        b.op("pe", lambda e: e.matmul(ptd[:, 0:256], ones[:], Pc[:].rearrange("p s c -> p (s c)"), start=True, stop=False),
             reads=[r_Pc, r_const], writes=[rpd], sig=False)
        b.op("pe", lambda e: e.matmul(ptd[:, 0:256], ones[0:64, :], Pn[:].rearrange("p s a t -> p (s a t)"), start=False, stop=True),
             reads=[r_Pn, r_const], writes=[rpd])
        for sq in range(16):
            for lo in (0, 64):
                b.op("pe", lambda e, sq=sq, lo=lo: e.matmul(ptv[lo:lo + 64, sq * 16:(sq + 1) * 16], Vcd[:, sq, :], Pc[:, sq, :], start=True, stop=False),
                     reads=[r_Pc, r_Vc], writes=[rpv], sig=False)
                b.op("pe", lambda e, sq=sq, lo=lo: e.matmul(ptv[lo:lo + 64, sq * 16:(sq + 1) * 16], Vd[0:64, 9, :],
                                                     Pn[:, sq, :, :].rearrange("p a t -> p (a t)"), start=False, stop=True),
                     reads=[r_Pn, r_V], writes=[rpv], sig=(sq == 15 and lo == 64))
        b.op("dve", lambda e, g=g: e.tensor_tensor(
            rden[:, 0:256].rearrange("p (s a t) -> p s a t", s=16, a=4), ptd[:, 0:256].rearrange("p (s a t) -> p s a t", s=16, a=4),
            esg[:].unsqueeze(1).unsqueeze(3).broadcast_to([128, 16, 4, 4]), ALU.add),
            reads=[rpd, r_bias], writes=[r_rden])
        b.op("act", lambda e: e.activation(rden[:, 0:256], rden[:, 0:256], AF.Ln), reads=[r_rden], writes=[r_rden])
        b.op("act", lambda e: e.activation(rden[:, 0:256], rden[:, 0:256], AF.Exp, scale=-1.0), reads=[r_rden], writes=[r_rden])
        for hq in range(4):
            lo = 0 if hq % 2 == 0 else 64
            ch = (2 * g + hq // 2)
            b.op("dve", lambda e, hq=hq, lo=lo, ch=ch: e.tensor_tensor(
                OT[lo:lo + 64, ch, NP:NQC].rearrange("p (s t) -> p s t", t=4),
                ptv[lo:lo + 64, 0:256].rearrange("p (s a t) -> p s a t", s=16, a=4)[:, :, hq, :],
                rden[lo:lo + 64, 0:256].rearrange("p (s a t) -> p s a t", s=16, a=4)[:, :, hq, :], ALU.mult),
                reads=[rpv, r_rden, r_x], writes=[r_OT])

        yield

    def drain(gen):
        for _ in gen:
            pass

    drain(proj(0))
    for g in range(NKV):
        crit = attn(g)
        fill = proj(g + 1) if g + 1 < NKV else iter(())
        done_c = done_f = False
        while not (done_c and done_f):
            if not done_c:
                try:
                    next(crit)
                except StopIteration:
                    done_c = True
            for _ in range(2):
                if not done_f:
                    try:
                        next(fill)
                    except StopIteration:
                        done_f = True
    xsem = [d[11], d[12]]
    osem = [d[13], d[14]]
    outs = []
    for i in range(KC):
        wn = NKV * 4 + i; w = wring[wn % NWB]; rw = r_wr[wn % NWB]
        xi = xr[i % 2]; rxi = r_xr[i % 2]
        b.dma("sp", xi, xT[:, i, NH:NCOL], xsem[i % 2], writes=[rxi])
        for (s, n) in ntiles(NQC):
            pt, rp = pr.next()
            for k in range(KC):
                b.op("pe", lambda e, pt=pt, w=w, k=k, s=s, n=n: e.matmul(
                    pt[:, 0:n], w[:, k, :], OT[:, k, s:s + n], start=(k == 0), stop=(k == KC - 1)),
                    reads=[rw, r_OT], writes=[rp], sig=(k == KC - 1))
            if s + n <= NP:
                b.op("dve", lambda e, pt=pt, i=i, s=s, n=n, xi=xi: e.scalar_tensor_tensor(
                    xi[:, s:s + n], pt[:, 0:n], vec[:, 3, i:i + 1], xi[:, s:s + n], ALU.mult, ALU.add),
                    reads=[rp, r_const, rxi], writes=[rxi])
            else:
                b.op("dve", lambda e, pt=pt, i=i: e.tensor_tensor(
                    tmps[:].rearrange("p (s t) -> p s t", t=4), pt[:, 0:NS].rearrange("p (s t) -> p s t", t=4),
                    mods[:, 2, i, :].unsqueeze(2).broadcast_to([128, 16, 4]), ALU.mult),
                    reads=[rp, r_const], writes=[r_tmps])
                b.op("dve", lambda e, xi=xi: e.tensor_tensor(xi[:, NP:NQC], xi[:, NP:NQC], tmps[:], ALU.add),
                     reads=[r_tmps, rxi], writes=[rxi])
        outs.append(b.dma("sp", xo[:, i, :], xi, osem[i % 2], reads=[rxi]))
        load_w(wn + 4)
    outs.append(b.dma("sp", kout, koutS[:], d[15], reads=[r_ko]))
    outs.append(b.dma("sp", vout, voutS[:], d[7], reads=[r_vo]))
    b.wait_all("sp", outs)
    b.emit()
    b.close()
    return nc


def build_adaln():
    nc = bass.Bass("TRN2", target_bir_lowering=False)
    NSEQ = 129
    NCH = 24
    dram = lambda name, shape, kind="ExternalInput": nc.dram_tensor(name, list(shape), F32, kind=kind).ap()
    cT = dram("cT", [128, KC, NSEQ])
    wada = dram("wada", [NCH, 128, KC, 128])
    bada = dram("bada", [128, NCH])
    modT = dram("modT", [128, NCH, NSEQ], "ExternalOutput")
    b = Builder(nc, n_dsem=8)
    d = b.dsems
    pr = PsumRot(b)
    cs = b.sb("cs", [128, KC, NSEQ], F32)
    sc = b.sb("sc", [128, KC, NSEQ], BF16)
    bs = b.sb("bs", [128, NCH], F32)
    outT = b.sb("outT", [128, NCH, NSEQ], F32)
    NWB = 4
    wr = [b.sb("wr%d" % i, [128, KC, 128], BF16) for i in range(NWB)]
    r_c, r_sc, r_b, r_out = Res(), Res(), Res(), Res()
    r_wr = [Res() for _ in range(NWB)]
    b.dma("sp", cs[:], cT, d[0], writes=[r_c])
    b.dma("sp", bs[:], bada, d[1], writes=[r_b])

    def load_w(n):
        if n < NCH:
            b.dma("pool", wr[n % NWB][:], wada[n], d[2 + n % NWB], writes=[r_wr[n % NWB]])
    for n in range(NWB - 1):
        load_w(n)
    b.op("act", lambda e: e.activation(sc[:], cs[:], AF.Silu), reads=[r_c], writes=[r_sc])
    for n in range(NCH):
        load_w(n + NWB - 1)
        w, rw = wr[n % NWB], r_wr[n % NWB]
        pt, rp = pr.next()
        for k in range(KC):
            b.op("pe", lambda e, pt=pt, w=w, k=k: e.matmul(pt[:, 0:NSEQ], w[:, k, :], sc[:, k, :], start=(k == 0), stop=(k == KC - 1)),
                 reads=[rw, r_sc], writes=[rp], sig=(k == KC - 1))
        b.op("act", lambda e, pt=pt, n=n: e.activation(outT[:, n, :], pt[:, 0:NSEQ], AF.Identity, bias=bs[:, n:n + 1], scale=1.0),
             reads=[rp, r_b], writes=[r_out])
    t = b.dma("sp", modT, outT[:], d[6], reads=[r_out])
    b.wait_all("sp", [t])
    b.emit(); b.close()
    return nc


NHH = 16
CH = 32
NRK = 7


def cumsum_chunks(b, engs, bufA, rA, bufB, rB, ncol0, ncols, clen):
    src, rs, dst, rd = bufA, rA, bufB, rB
    s = 1
    i = 0
    while s < clen:
        sv = src[:, ncol0:ncol0 + ncols].rearrange("p (c t) -> p c t", t=clen)
        dv = dst[:, ncol0:ncol0 + ncols].rearrange("p (c t) -> p c t", t=clen)
        eng = engs[i % len(engs)]
        b.op(eng, lambda e, dv=dv, sv=sv, s=s: e.tensor_tensor(dv[:, :, s:clen], sv[:, :, s:clen], sv[:, :, 0:clen - s], ALU.add),
             reads=[rs], writes=[rd])
        b.op(eng, lambda e, dv=dv, sv=sv, s=s: e.tensor_copy(dv[:, :, 0:s], sv[:, :, 0:s]), reads=[rs], writes=[rd])
        src, rs, dst, rd = dst, rd, src, rs
        s *= 2
        i += 1
    return src, rs


def build_hgrn(pass1, modeA=False):
    nc = bass.Bass("TRN2", target_bir_lowering=False)
    NT = NP if pass1 else NP + NS
    NBLK = NP // 128
    NCK = NP // CH
    dram = lambda name, shape, kind="ExternalInput": nc.dram_tensor(name, list(shape), F32, kind=kind).ap()
    xT = dram("xT", [128, KC, NT])
    vecd = dram("vec", [128, 4, KC])
    modsd = dram("mods", [128, 3, KC, 16])
    whg = dram("whg", [NHH, 128, 4, KC, 128])
    lbpd = dram("lbp", [128, 2, NHH])
    m01d = dram("m01", [128, 128]); cmd = dram("cm", [128, 4, 128])
    identd = dram("ident", [128, 128])
    if not pass1:
        if not modeA:
            wo = dram("wo", [KC, 128, KC, 128])
            ngd = dram("ng", [128, NHH])
            srd = dram("sr", [NHH, 128, NRK, 128])
            drd = dram("dr", [128, NHH, NRK])
            xo = dram("xo", [128, KC, NT], "ExternalOutput")
        else:
            oloc = dram("oloc", [NHH, 128, NT], "ExternalOutput")
            qbo = dram("qbo", [NHH, 128, NT], "ExternalOutput")
            sgo = dram("sgo", [NHH, 128, NT], "ExternalOutput")
        s0d = dram("s0", [NHH, 128, 16, 128])
        msd = dram("ms", [64, 64]); cmsd = dram("cms", [128, 16, 64])
        snew = dram("snew", [NHH, 128, 16, 128], "ExternalOutput")
    send = dram("send", [128, NHH, 128], "ExternalOutput")
    dout = dram("dout", [128, NHH], "ExternalOutput")

    b = Builder(nc, n_dsem=18)
    d = b.dsems
    pr = PsumRot(b)
    xbuf = b.sb("xbuf", [128, KC, NT], F32)
    x = xbuf
    O2T = xbuf[:].rearrange("p k n -> p (k n)").bitcast(BF16)[:, 0:KC * NT].rearrange("p (k n) -> p k n", k=KC)
    h = b.sb("h", [128, KC, NT], BF16)
    rstd = b.sb("rstd", [128, NT], F32)
    xsq0 = b.sb("xsq0", [128, NT], BF16)
    t32a = b.sb("t32a", [128, NT], F32)
    NWB = 5
    wring = [b.sb("wr%d" % i, [128, KC, 128], BF16) for i in range(NWB)]
    qf = b.sb("qf", [128, NT], F32); kf = b.sb("kf", [128, NT], F32)
    bA = b.sb("bA", [128, NT], F32); bB = b.sb("bB", [128, NT], F32)
    sg = b.sb("sg", [128, NT], F32); Oraw = b.sb("Oraw", [128, NT], F32)
    qt = b.sb("qt", [128, NT], BF16); kt = b.sb("kt", [128, NT], BF16); kh = b.sb("kh", [128, NT], BF16)
    dec = b.sb("dec", [128, NCK + 16], F32)
    Vt = b.sb("Vt", [128, NBLK + 1, 128], BF16)
    At = b.sb("At", [128, 128], BF16)
    Khm = b.sb("Khm", [128, 4, 128], BF16)
    KhmT = b.sb("KhmT", [128, 4, 128], BF16)
    S = b.sb("S", [128, 128], F32); Sbf = b.sb("Sbf", [128, 128], BF16)
    Dall = b.sb("Dall", [128, NHH], F32)
    btot = b.sb("btot", [128, 1], F32)
    vec = b.sb("vecs", [128, 4, KC], F32)
    mods = b.sb("modss", [128, 3, KC, 16], F32)
    lbp = b.sb("lbps", [128, 2, NHH], F32)
    oml = b.sb("oml", [128, NHH], F32)
    m01 = b.sb("m01s", [128, 128], F32); cm = b.sb("cms_", [128, 4, 128], F32)
    ident = b.sb("idents", [128, 128], BF16); identf = b.sb("identf", [128, 128], F32)
    ones = b.sb("ones", [128, 128], BF16)
    gm = b.sb("gm", [128, KC], F32); gms = b.sb("gms", [128, KC, 16], F32)
    if not pass1:
        if not modeA:
            ng = b.sb("ngs", [128, NHH], F32)
            sr = b.sb("srs", [128, NRK, 128], F32)
            dr = b.sb("drs", [128, NHH, NRK], F32)
        else:
            QBf = b.sb("QBf", [128, NT], F32)
            pbA = b.sb("pbA", [128, NCK], F32); pbB = b.sb("pbB", [128, NCK], F32)
            r_QBf, r_pbA, r_pbB = Res("QBf"), Res("pbA"), Res("pbB")
        S0 = b.sb("S0", [128, 16, 128], F32); S0b = b.sb("S0b", [128, 16, 128], BF16)
        ms = b.sb("mss", [64, 64], F32); cms = b.sb("cmss", [128, 16, 64], F32)
        Ats = b.sb("Ats", [64, 64], BF16)
        Khms = b.sb("Khms", [128, 16, 64], BF16)
        KhmTs = b.sb("KhmTs", [64, 16, 128], BF16)
        tmps = b.sb("tmps", [128, NS], F32)
    R = Res
    r_x, r_h, r_rstd, r_const, r_gm, r_gms = [R(n) for n in "x h rstd const gm gms".split()]
    r_xsq0, r_t32a = R("xsq"), R("t32")
    r_wr = [R("wr%d" % i) for i in range(NWB)]
    r_qf, r_kf, r_bA, r_bB, r_sg, r_Oraw, r_qt, r_kt, r_kh, r_dec, r_Vt, r_At, r_Khm, r_KhmT, r_S, r_Sbf, r_Sall, r_bt = [
        R(n) for n in "qf kf bA bB sg Oraw qt kt kh dec Vt At Khm KhmT S Sbf Sall bt".split()]
    r_sr, r_S0, r_S0b, r_Ats, r_Khms, r_KhmTs, r_tmps = [R(n) for n in "sr S0 S0b Ats Khms KhmTs tmps".split()]
    r_O2T = r_x

    b.dma("sp", x[:], xT, d[0], writes=[r_x])
    cl = [(vec, vecd), (mods, modsd), (lbp, lbpd), (m01, m01d), (cm, cmd), (identf, identd)]
    if not pass1:
        cl += [(ms, msd), (cms, cmsd)] + ([] if modeA else [(ng, ngd), (dr, drd)])
    for dst, src in cl:
        b.dma("sp", dst[:], src, d[1], writes=[r_const])
    b.op("pool", lambda e: e.memset(ones[:], 1.0), writes=[r_const])
    b.op("act", lambda e: e.activation(ident[:], identf[:], AF.Copy), reads=[r_const], writes=[r_const])
    b.op("dve", lambda e: e.tensor_tensor(oml[:], lbp[:, 0, :], lbp[:, 1, :], ALU.subtract), reads=[r_const], writes=[r_const])
    b.op("act", lambda e: e.activation(oml[:], oml[:], AF.Sigmoid), reads=[r_const], writes=[r_const])

    wsem = [d[2], d[3], d[4], d[5], d[6]]
    NWT = NHH * 4 + (0 if (pass1 or modeA) else KC)
    used = [1, 2] if pass1 else [0, 1, 2, 3]
    wlist = [(hh, wh) for hh in range(NHH) for wh in used] + ([("o", i) for i in range(KC)] if not (pass1 or modeA) else [])

    def load_w(n):
        if n >= len(wlist):
            return
        a, c = wlist[n]
        src = wo[c] if a == "o" else whg[a][:, c, :, :]
        b.dma("pool", wring[n % NWB][:], src, wsem[n % NWB], writes=[r_wr[n % NWB]])
    for n in range(4):
        load_w(n)
    wcount = [0]

    def next_w():
        n = wcount[0]
        wcount[0] += 1
        return n, wring[n % NWB], r_wr[n % NWB]

    emit_norm_mod(b, pr, x, r_x, h, r_h, NT, NP, vec, 0, 1, 2, mods, 0, 1, ones, r_const,
                  ([xsq0, xsq0], [r_xsq0, r_xsq0], rstd, r_rstd, gm, r_gm, gms, r_gms, [t32a, t32a], [r_t32a, r_t32a]))

    def proj_fm(dst_fn):
        n_, w, rw = next_w()
        for (s, n) in ntiles(NT):
            pt, rp = pr.next()
            for k in range(KC):
                b.op("pe", lambda e, pt=pt, w=w, k=k, s=s, n=n: e.matmul(
                    pt[:, 0:n], w[:, k, :], h[:, k, s:s + n], start=(k == 0), stop=(k == KC - 1)),
                    reads=[rw, r_h], writes=[rp], sig=(k == KC - 1))
            dst_fn(pt, rp, s, n)
        load_w(n_ + 4)

    outs = []
    for hh in range(NHH):
        if not pass1:
            if not modeA:
                b.dma("sp", sr[:], srd[hh], d[7], writes=[r_sr])
            b.dma("sp", S0[:], s0d[hh], d[8], writes=[r_S0])
            b.dma("pool", S0b[:], s0d[hh], d[9], writes=[r_S0b])
            proj_fm(lambda pt, rp, s, n: b.op("act", lambda e: e.activation(qf[:, s:s + n], pt[:, 0:n], AF.Silu),
                                              reads=[rp], writes=[r_qf]))
        proj_fm(lambda pt, rp, s, n: b.op("act", lambda e: e.activation(kf[:, s:s + n], pt[:, 0:n], AF.Sigmoid, scale=-1.0),
                                          reads=[rp], writes=[r_kf]))
        b.op("dve", lambda e, hh=hh: e.tensor_scalar(kf[:], kf[:], oml[:, hh:hh + 1], None, ALU.mult),
             reads=[r_kf, r_const], writes=[r_kf])
        b.op("act", lambda e: e.activation(bA[:], kf[:], AF.Ln, bias=1.0, scale=-1.0), reads=[r_kf], writes=[r_bA])
        n_, w, rw = next_w()
        for blk in range(NBLK + (0 if pass1 else 1)):
            m = 128 if blk < NBLK else NS
            c0 = blk * 128
            pt, rp = pr.next()
            for k in range(KC):
                b.op("pe", lambda e, pt=pt, w=w, k=k, c0=c0, m=m: e.matmul(
                    pt[0:m, 0:128], h[:, k, c0:c0 + m], w[:, k, :], start=(k == 0), stop=(k == KC - 1)),
                    reads=[rw, r_h], writes=[rp], sig=(k == KC - 1))
            b.op("act", lambda e, pt=pt, blk=blk, m=m: e.activation(Vt[0:m, blk, :], pt[0:m, 0:128], AF.Copy),
                 reads=[rp], writes=[r_Vt])
        load_w(n_ + 4)
        if not pass1:
            proj_fm(lambda pt, rp, s, n: b.op("act", lambda e: e.activation(sg[:, s:s + n], pt[:, 0:n], AF.Silu),
                                              reads=[rp], writes=[r_sg]))
        bb, rbb = cumsum_chunks(b, ["dve", "pool"], bA, r_bA, bB, r_bB, 0, NP, CH)
        other, rother = (bB, r_bB) if bb is bA else (bA, r_bA)
        if not pass1:
            if bb is not bA:
                b.op("pool", lambda e: e.tensor_copy(bB[:, NP:NT], bA[:, NP:NT]), reads=[r_bA], writes=[r_bB])
            sb_, rsb = cumsum_chunks(b, ["pool"], bb, rbb, other, rother, NP, NS, 4)
            if sb_ is not bb:
                b.op("pool", lambda e, sb_=sb_, bb=bb: e.tensor_copy(bb[:, NP:NT], sb_[:, NP:NT]), reads=[rsb], writes=[rbb])
        bv = bb[:, 0:NP].rearrange("p (c t) -> p c t", t=CH)
        ov = other[:, 0:NP].rearrange("p (c t) -> p c t", t=CH)
        b.op("act", lambda e, bv=bv: e.activation(dec[:, 0:NCK].unsqueeze(2), bv[:, :, CH - 1:CH], AF.Exp), reads=[rbb], writes=[r_dec])
        b.op("dve", lambda e, bv=bv, ov=ov: e.tensor_tensor(ov, bv[:, :, CH - 1:CH].broadcast_to([128, NCK, CH]), bv, ALU.subtract),
             reads=[rbb], writes=[rother])
        if not pass1:
            bs_ = bb[:, NP:NT].rearrange("p (c t) -> p c t", t=4)
            os_ = other[:, NP:NT].rearrange("p (c t) -> p c t", t=4)
            b.op("act", lambda e, bs_=bs_: e.activation(dec[:, NCK:NCK + 16].unsqueeze(2), bs_[:, :, 3:4], AF.Exp), reads=[rbb], writes=[r_dec])
            b.op("dve", lambda e, bs_=bs_, os_=os_: e.tensor_tensor(os_, bs_[:, :, 3:4].broadcast_to([128, 16, 4]), bs_, ALU.subtract),
                 reads=[rbb], writes=[rother])
        b.op("act", lambda e, other=other: e.activation(other[:, 0:NT], other[:, 0:NT], AF.Exp), reads=[rother], writes=[rother])
        b.op("dve", lambda e, other=other: e.tensor_tensor(kh[:], kf[:], other[:, 0:NT], ALU.mult), reads=[rother, r_kf], writes=[r_kh])
        if pass1:
            b.op("dve", lambda e, bv=bv: e.tensor_reduce(btot[:], bv[:, :, CH - 1], AX.X, ALU.add), reads=[rbb], writes=[r_bt])
            b.op("act", lambda e, hh=hh: e.activation(Dall[:, hh:hh + 1], btot[:], AF.Exp), reads=[r_bt], writes=[r_Sall])
        else:
            b.op("act", lambda e, other=other, bb=bb: e.activation(other[:, 0:NT], bb[:, 0:NT], AF.Exp), reads=[rbb, r_kh], writes=[rother])
            b.op("dve", lambda e, other=other: e.tensor_tensor(qt[:], qf[:], other[:, 0:NT], ALU.mult), reads=[rother, r_qf], writes=[r_qt])
            b.op("act", lambda e, other=other, bb=bb: e.activation(other[:, 0:NT], bb[:, 0:NT], AF.Exp, scale=-1.0), reads=[rbb, r_qt], writes=[rother])
            b.op("dve", lambda e, other=other: e.tensor_tensor(kt[:], kf[:], other[:, 0:NT], ALU.mult), reads=[rother, r_kf], writes=[r_kt])
            if modeA:
                b.op("pool", lambda e, bv=bv: e.tensor_copy(pbA[:].unsqueeze(2), bv[:, :, CH - 1:CH]), reads=[rbb], writes=[r_pbA])
                pin, rpin = cumsum_chunks(b, ["pool"], pbA, r_pbA, pbB, r_pbB, 0, NCK, NCK)
                pex, rpex = (pbB, r_pbB) if pin is pbA else (pbA, r_pbA)
                b.op("act", lambda e, hh=hh, pin=pin: e.activation(Dall[:, hh:hh + 1], pin[:, NCK - 1:NCK], AF.Exp), reads=[rpin], writes=[r_Sall])
                b.op("pool", lambda e, pin=pin, pex=pex, bv=bv: e.tensor_tensor(pex[:].unsqueeze(2), pin[:].unsqueeze(2), bv[:, :, CH - 1:CH], ALU.subtract),
                     reads=[rpin, rbb], writes=[rpex])
                b.op("act", lambda e, pex=pex: e.activation(pex[:], pex[:], AF.Exp), reads=[rpex], writes=[rpex])
                b.op("pool", lambda e, pex=pex: e.tensor_tensor(
                    QBf[:, 0:NP].rearrange("p (c t) -> p c t", t=CH), qt[:, 0:NP].rearrange("p (c t) -> p c t", t=CH),
                    pex[:].unsqueeze(2).broadcast_to([128, NCK, CH]), ALU.mult), reads=[rpex, r_qt], writes=[r_QBf])
                b.op("pool", lambda e: e.memset(QBf[:, NP:NT], 0.0), writes=[r_QBf])
                outs.append(b.dma("sp", qbo[hh], QBf[:], d[7], reads=[r_QBf]))
                outs.append(b.dma("sp", sgo[hh], sg[:], d[13], reads=[r_sg]))
        if pass1 or modeA:
            b.op("pool", lambda e: e.memset(S[:], 0.0), writes=[r_S])
            if modeA:
                b.op("pool", lambda e: e.memset(Sbf[:], 0.0), writes=[r_Sbf])
        else:
            b.op("dve", lambda e, hh=hh: e.tensor_scalar(S[:], sr[:, 0, :], 1.0, None, ALU.mult), reads=[r_sr], writes=[r_S])
            for r in range(1, NRK):
                b.op("dve", lambda e, hh=hh, r=r: e.scalar_tensor_tensor(S[:], S[:], dr[:, hh, r:r + 1], sr[:, r, :], ALU.mult, ALU.add),
                     reads=[r_S, r_sr, r_const], writes=[r_S])
            b.op("act", lambda e: e.activation(Sbf[:], S[:], AF.Copy), reads=[r_S], writes=[r_Sbf])
        for blk in range(NBLK):
            c0 = blk * 128
            b.op("dve", lambda e, c0=c0: e.tensor_tensor(Khm[:], kh[:, c0:c0 + 128].unsqueeze(1).broadcast_to([128, 4, 128]), cm[:], ALU.mult),
                 reads=[r_kh, r_const], writes=[r_Khm])
            ptT, rpT = pr.next()
            ptTb = ptT[:, :].bitcast(BF16)
            for c in range(4):
                b.op("pe", lambda e, ptTb=ptTb, c=c: e.transpose(ptTb[:, c * 128:(c + 1) * 128], Khm[:, c, :], ident[:]),
                     reads=[r_Khm, r_const], writes=[rpT], sig=(c == 3))
            b.op("act", lambda e, ptTb=ptTb: e.activation(KhmT[:].rearrange("p c k -> p (c k)"), ptTb[:, 0:512], AF.Copy),
                 reads=[rpT], writes=[r_KhmT])
            if not pass1:
                pa, rpa = pr.next()
                b.op("pe", lambda e, pa=pa, c0=c0: e.matmul(pa[:, 0:128], kt[:, c0:c0 + 128], qt[:, c0:c0 + 128], start=True, stop=True),
                     reads=[r_kt, r_qt], writes=[rpa])
                b.op("dve", lambda e, pa=pa: e.tensor_tensor(At[:], pa[:, 0:128], m01[:], ALU.mult), reads=[rpa, r_const], writes=[r_At])
                po, rpo = pr.next()
                b.op("pe", lambda e, po=po, blk=blk: e.matmul(po[:, 0:128], Vt[:, blk, :], At[:], start=True, stop=False),
                     reads=[r_Vt, r_At], writes=[rpo], sig=False)
            for c in range(4):
                ck = blk * 4 + c
                if not pass1:
                    b.op("pe", lambda e, po=po, c=c, c0=c0: e.matmul(po[:, c * CH:(c + 1) * CH], Sbf[:], qt[:, c0 + c * CH:c0 + (c + 1) * CH],
                                                                   start=False, stop=(c == 3)),
                         reads=[r_Sbf, r_qt], writes=[rpo], sig=True)
                pu, rpu = pr.next()
                b.op("pe", lambda e, pu=pu, c=c, blk=blk: e.matmul(pu[:, 0:128], KhmT[:, c, :], Vt[:, blk, :], start=True, stop=True),
                     reads=[r_KhmT, r_Vt], writes=[rpu])
                b.op("dve", lambda e, pu=pu, ck=ck: e.scalar_tensor_tensor(S[:], S[:], dec[:, ck:ck + 1], pu[:, 0:128], ALU.mult, ALU.add),
                     reads=[rpu, r_S, r_dec], writes=[r_S])
                if not pass1:
                    b.op("act", lambda e: e.activation(Sbf[:], S[:], AF.Copy), reads=[r_S], writes=[r_Sbf])
            if not pass1:
                b.op("act", lambda e, po=po, c0=c0: e.activation(Oraw[:, c0:c0 + 128], po[:, 0:128], AF.Copy), reads=[rpo], writes=[r_Oraw])
        outs.append(b.dma("sp", send[:, hh, :], S[:], d[11], reads=[r_S]))
        if pass1:
            continue
        b.op("dve", lambda e: e.tensor_tensor(Khms[:], kh[:, NP:NT].unsqueeze(1).broadcast_to([128, 16, 64]), cms[:], ALU.mult),
             reads=[r_kh, r_const], writes=[r_Khms])
        for half in range(2):
            ptT, rpT = pr.next()
            ptTb = ptT[:, :].bitcast(BF16)
            for s8 in range(8):
                sq = half * 8 + s8
                b.op("pe", lambda e, ptTb=ptTb, s8=s8, sq=sq: e.transpose(ptTb[0:64, s8 * 128:(s8 + 1) * 128], Khms[:, sq, :], ident[:]),
                     reads=[r_Khms, r_const], writes=[rpT], sig=(s8 == 7))
            b.op("act", lambda e, ptTb=ptTb, half=half: e.activation(
                KhmTs[:, half * 8:half * 8 + 8, :].rearrange("p c k -> p (c k)"), ptTb[0:64, 0:1024], AF.Copy),
                reads=[rpT], writes=[r_KhmTs])
        pa, rpa = pr.next()
        b.op("pe", lambda e, pa=pa: e.matmul(pa[0:64, 0:64], kt[:, NP:NT], qt[:, NP:NT], start=True, stop=True),
             reads=[r_kt, r_qt], writes=[rpa])
        b.op("dve", lambda e, pa=pa: e.tensor_tensor(Ats[:], pa[0:64, 0:64], ms[:], ALU.mult), reads=[rpa, r_const], writes=[r_Ats])
        po, rpo = pr.next()
        b.op("pe", lambda e, po=po: e.matmul(po[:, 0:64], Vt[0:64, NBLK, :], Ats[:], start=True, stop=False),
             reads=[r_Vt, r_Ats], writes=[rpo], sig=False)
        for sq in range(16):
            b.op("pe", lambda e, po=po, sq=sq: e.matmul(po[:, sq * 4:sq * 4 + 4], S0b[:, sq, :], qt[:, NP + sq * 4:NP + sq * 4 + 4],
                                                       start=False, stop=(sq == 15)),
                 reads=[r_S0b, r_qt], writes=[rpo], sig=(sq == 15))
        b.op("act", lambda e, po=po: e.activation(Oraw[:, NP:NT], po[:, 0:64], AF.Copy), reads=[rpo], writes=[r_Oraw])
        for q4 in range(4):
            pu, rpu = pr.next()
            for s4 in range(4):
                sq = q4 * 4 + s4
                b.op("pe", lambda e, pu=pu, s4=s4, sq=sq: e.matmul(pu[:, s4 * 128:(s4 + 1) * 128], KhmTs[:, sq, :], Vt[0:64, NBLK, :],
                                                                 start=True, stop=True),
                     reads=[r_KhmTs, r_Vt], writes=[rpu], sig=(s4 == 3))
            for s4 in range(4):
                sq = q4 * 4 + s4
                b.op("dve", lambda e, pu=pu, s4=s4, sq=sq: e.scalar_tensor_tensor(
                    S0[:, sq, :], S0[:, sq, :], dec[:, NCK + sq:NCK + sq + 1], pu[:, s4 * 128:(s4 + 1) * 128], ALU.mult, ALU.add),
                    reads=[rpu, r_S0, r_dec], writes=[r_S0])
        outs.append(b.dma("sp", snew[hh], S0[:], d[10], reads=[r_S0]))
        if modeA:
            outs.append(b.dma("sp", oloc[hh], Oraw[:], d[14], reads=[r_Oraw]))
            continue
        b.op("act", lambda e: e.activation(xsq0[:], Oraw[:], AF.Square), reads=[r_Oraw], writes=[r_xsq0])
        tl = ntiles(NT)
        bk = [pr.next() for _ in tl]
        for ti, (s, n) in enumerate(tl):
            pt, rp = bk[ti]
            b.op("pe", lambda e, pt=pt, s=s, n=n: e.matmul(pt[:, 0:n], ones[:], xsq0[:, s:s + n], start=True, stop=True),
                 reads=[r_xsq0, r_const], writes=[rp])
            b.op("act", lambda e, pt=pt, s=s, n=n: e.activation(rstd[:, s:s + n], pt[:, 0:n], AF.Sqrt, bias=EPS, scale=1.0 / 128),
                 reads=[rp], writes=[r_rstd])
        b.op("dve", lambda e: e.reciprocal(rstd[:], rstd[:]), reads=[r_rstd], writes=[r_rstd])
        b.op("dve", lambda e, hh=hh: e.scalar_tensor_tensor(Oraw[:], Oraw[:], ng[:, hh:hh + 1], rstd[:], ALU.mult, ALU.mult),
             reads=[r_Oraw, r_rstd, r_const], writes=[r_Oraw])
        b.op("dve", lambda e, hh=hh: e.tensor_tensor(O2T[:, hh, :], Oraw[:], sg[:], ALU.mult), reads=[r_Oraw, r_sg, r_x], writes=[r_O2T])

    if pass1 or modeA:
        outs.append(b.dma("sp", dout, Dall[:], d[12], reads=[r_Sall]))
    else:
        b.op("pool", lambda e: e.memset(Dall[:], 0.0), writes=[r_Sall])
        outs.append(b.dma("sp", dout, Dall[:], d[12], reads=[r_Sall]))
        xr = [t32a, rstd]; r_xr = [r_t32a, r_rstd]
        xsem = [d[13], d[14]]; osem = [d[15], d[16]]
        for i in range(KC):
            n_, w, rw = next_w()
            xi = xr[i % 2]; rxi = r_xr[i % 2]
            b.dma("sp", xi[:], xT[:, i, :], xsem[i % 2], writes=[rxi])
            for (s, n) in ntiles(NT):
                pt, rp = pr.next()
                for k in range(KC):
                    b.op("pe", lambda e, pt=pt, w=w, k=k, s=s, n=n: e.matmul(
                        pt[:, 0:n], w[:, k, :], O2T[:, k, s:s + n], start=(k == 0), stop=(k == KC - 1)),
                        reads=[rw, r_O2T], writes=[rp], sig=(k == KC - 1))
                if s + n <= NP:
                    b.op("dve", lambda e, pt=pt, i=i, s=s, n=n, xi=xi: e.scalar_tensor_tensor(
                        xi[:, s:s + n], pt[:, 0:n], vec[:, 3, i:i + 1], xi[:, s:s + n], ALU.mult, ALU.add),
                        reads=[rp, r_const, rxi], writes=[rxi])
                else:
                    b.op("dve", lambda e, pt=pt, i=i: e.tensor_tensor(
                        tmps[:].rearrange("p (s t) -> p s t", t=4), pt[:, 0:NS].rearrange("p (s t) -> p s t", t=4),
                        mods[:, 2, i, :].unsqueeze(2).broadcast_to([128, 16, 4]), ALU.mult),
                        reads=[rp, r_const], writes=[r_tmps])
                    b.op("dve", lambda e, xi=xi: e.tensor_tensor(xi[:, NP:NT], xi[:, NP:NT], tmps[:], ALU.add),
                         reads=[r_tmps, rxi], writes=[rxi])
            outs.append(b.dma("sp", xo[:, i, :], xi[:], osem[i % 2], reads=[rxi]))
            load_w(n_ + 4)
    b.wait_all("sp", outs)
    b.emit(); b.close()
    return nc


def build_hgrnb():
    nc = bass.Bass("TRN2", target_bir_lowering=False)
    NT = NP + NS
    dram = lambda name, shape, kind="ExternalInput": nc.dram_tensor(name, list(shape), F32, kind=kind).ap()
    xT = dram("xT", [128, KC, NT])
    vecd = dram("vec", [128, 4, KC])
    modsd = dram("mods", [128, 3, KC, 16])
    olocd = dram("oloc", [NHH, 128, NT]); qbd = dram("qb", [NHH, 128, NT]); sgd = dram("sg", [NHH, 128, NT])
    srd = dram("sr", [NHH, 128, NRK, 128]); drd = dram("dr", [128, NHH, NRK])
    slocd = dram("sloc", [128, NHH, 128]); dld = dram("dl", [128, NHH])
    ngd = dram("ng", [128, NHH])
    wo = dram("wo", [KC, 128, KC, 128])
    xo = dram("xo", [128, KC, NT], "ExternalOutput")
    send = dram("send", [128, NHH, 128], "ExternalOutput")

    b = Builder(nc, n_dsem=18)
    d = b.dsems
    pr = PsumRot(b)
    O2T = b.sb("O2T", [128, KC, NT], BF16)
    ol = [b.sb("ol%d" % i, [128, NT], F32) for i in range(2)]
    qbb = [b.sb("qbb%d" % i, [128, NT], BF16) for i in range(2)]
    sgl = [b.sb("sgl%d" % i, [128, NT], F32) for i in range(2)]
    srall = b.sb("srall", [128, NHH, NRK, 128], F32)
    Sall = b.sb("Sall", [128, NHH, 128], F32)
    Oraws = [b.sb("Oraw%d" % i, [128, NT], F32) for i in range(2)]
    xsqs = [b.sb("xsq%d" % i, [128, NT], BF16) for i in range(2)]
    rstds = [b.sb("rstd%d" % i, [128, NT], F32) for i in range(2)]
    Ss = [b.sb("S%d" % i, [128, 128], F32) for i in range(2)]; Sbfs = [b.sb("Sbf%d" % i, [128, 128], BF16) for i in range(2)]
    Se = b.sb("Se", [128, NHH, 128], F32)
    sloc = b.sb("slocs", [128, NHH, 128], F32)
    dr = b.sb("drs", [128, NHH, NRK], F32); dl = b.sb("dls", [128, NHH], F32); ng = b.sb("ngs", [128, NHH], F32)
    vec = b.sb("vecs", [128, 4, KC], F32); mods = b.sb("modss", [128, 3, KC, 16], F32)
    ones = b.sb("ones", [128, 128], BF16)
    NWB = 4
    wring = [b.sb("wr%d" % i, [128, KC, 128], BF16) for i in range(NWB)]
    xr = [b.sb("xr%d" % i, [128, NT], F32) for i in range(2)]
    tmps = b.sb("tmps", [128, NS], F32)
    R = Res
    r_O2T, r_Se, r_const, r_tmps = [R(n) for n in "O2T Se const tmps".split()]
    r_Oraws = [R("a"), R("b")]; r_xsqs = [R("a"), R("b")]; r_rstds = [R("a"), R("b")]; r_Ss = [R("a"), R("b")]; r_Sbfs = [R("a"), R("b")]
    r_ol = [R("a"), R("b")]; r_qbb = [R("a"), R("b")]; r_sgl = [R("a"), R("b")]; r_srall = R("srall"); r_Sall = R("Sall")
    r_wr = [R("w%d" % i) for i in range(NWB)]; r_xr = [R("a"), R("b")]
    for dst, src in [(vec, vecd), (mods, modsd), (sloc, slocd), (dr, drd), (dl, dld), (ng, ngd)]:
        b.dma("sp", dst[:], src, d[0], writes=[r_const])
    b.op("pool", lambda e: e.memset(ones[:], 1.0), writes=[r_const])

    def load_w(i):
        if i < KC:
            b.dma("pool", wring[i % NWB][:], wo[i], d[1 + i % NWB], writes=[r_wr[i % NWB]])

    def load_head(hh):
        if hh >= NHH:
            return
        i = hh % 2
        b.dma("sp", ol[i][:], olocd[hh], d[5 + i], writes=[r_ol[i]])
        b.dma("pool", qbb[i][:], qbd[hh], d[7 + i], writes=[r_qbb[i]])
        b.dma("sp", sgl[i][:], sgd[hh], d[9 + i], writes=[r_sgl[i]])
    b.dma("sp", srall[:], srd.rearrange("h p r v -> p h r v"), d[11], writes=[r_srall])
    load_head(0)
    b.op("dve", lambda e: e.tensor_copy(Sall[:], srall[:, :, 0, :]), reads=[r_srall], writes=[r_Sall])
    for r in range(1, NRK):
        b.op("dve", lambda e, r=r: e.tensor_tensor(Sall[:], Sall[:], dr[:, :, r:r + 1].broadcast_to([128, NHH, 128]), ALU.mult),
             reads=[r_Sall, r_const], writes=[r_Sall])
        b.op("pool", lambda e, r=r: e.tensor_tensor(Sall[:], Sall[:], srall[:, :, r, :], ALU.add),
             reads=[r_Sall, r_srall], writes=[r_Sall])
    for i in range(NWB - 1):
        load_w(i)
    outs = []
    for hh in range(NHH):
        load_head(hh + 1)
        i2 = hh % 2
        Oraw, xsq0, rstd, S, Sbf = Oraws[i2], xsqs[i2], rstds[i2], Ss[i2], Sbfs[i2]
        r_Oraw, r_xsq0, r_rstd, r_S, r_Sbf = r_Oraws[i2], r_xsqs[i2], r_rstds[i2], r_Ss[i2], r_Sbfs[i2]
        b.op("act", lambda e, hh=hh: e.activation(Sbf[:], Sall[:, hh, :], AF.Copy), reads=[r_Sall], writes=[r_Sbf])
        b.op("dve", lambda e, hh=hh: e.scalar_tensor_tensor(Se[:, hh, :], Sall[:, hh, :], dl[:, hh:hh + 1], sloc[:, hh, :], ALU.mult, ALU.add),
             reads=[r_Sall, r_const], writes=[r_Se])
        for (s, n) in ntiles(NT):
            pt, rp = pr.next()
            b.op("pe", lambda e, pt=pt, s=s, n=n, i2=i2: e.matmul(pt[:, 0:n], Sbf[:], qbb[i2][:, s:s + n], start=True, stop=True),
                 reads=[r_Sbf, r_qbb[i2]], writes=[rp])
            b.op("dve", lambda e, pt=pt, s=s, n=n, i2=i2: e.tensor_tensor(Oraw[:, s:s + n], pt[:, 0:n], ol[i2][:, s:s + n], ALU.add),
                 reads=[rp, r_ol[i2]], writes=[r_Oraw])
        b.op("act", lambda e: e.activation(xsq0[:], Oraw[:], AF.Square), reads=[r_Oraw], writes=[r_xsq0])
        for (s, n) in ntiles(NT):
            pt, rp = pr.next()
            b.op("pe", lambda e, pt=pt, s=s, n=n: e.matmul(pt[:, 0:n], ones[:], xsq0[:, s:s + n], start=True, stop=True),
                 reads=[r_xsq0, r_const], writes=[rp])
            b.op("act", lambda e, pt=pt, s=s, n=n: e.activation(rstd[:, s:s + n], pt[:, 0:n], AF.Ln, bias=EPS, scale=1.0 / 128),
                 reads=[rp], writes=[r_rstd])
        b.op("act", lambda e: e.activation(rstd[:], rstd[:], AF.Exp, scale=-0.5), reads=[r_rstd], writes=[r_rstd])
        b.op("dve", lambda e, hh=hh: e.scalar_tensor_tensor(Oraw[:], Oraw[:], ng[:, hh:hh + 1], rstd[:], ALU.mult, ALU.mult),
             reads=[r_Oraw, r_rstd, r_const], writes=[r_Oraw])
        b.op("pool", lambda e, hh=hh, i2=i2: e.tensor_tensor(O2T[:, hh, :], Oraw[:], sgl[i2][:], ALU.mult),
             reads=[r_Oraw, r_sgl[i2]], writes=[r_O2T])
    outs.append(b.dma("sp", send, Se[:], d[13], reads=[r_Se]))
    for i in range(KC):
        load_w(i + NWB - 1)
        w, rw = wring[i % NWB], r_wr[i % NWB]
        xi, rxi = xr[i % 2], r_xr[i % 2]
        b.dma("sp", xi[:], xT[:, i, :], d[14 + i % 2], writes=[rxi])
        for (s, n) in ntiles(NT):
            pt, rp = pr.next()
            for k in range(KC):
                b.op("pe", lambda e, pt=pt, w=w, k=k, s=s, n=n: e.matmul(
                    pt[:, 0:n], w[:, k, :], O2T[:, k, s:s + n], start=(k == 0), stop=(k == KC - 1)),
                    reads=[rw, r_O2T], writes=[rp], sig=(k == KC - 1))
            if s + n <= NP:
                b.op("dve", lambda e, pt=pt, i=i, s=s, n=n, xi=xi: e.scalar_tensor_tensor(
                    xi[:, s:s + n], pt[:, 0:n], vec[:, 3, i:i + 1], xi[:, s:s + n], ALU.mult, ALU.add),
                    reads=[rp, r_const, rxi], writes=[rxi])
            else:
                b.op("dve", lambda e, pt=pt, i=i: e.tensor_tensor(
                    tmps[:].rearrange("p (s t) -> p s t", t=4), pt[:, 0:NS].rearrange("p (s t) -> p s t", t=4),
                    mods[:, 2, i, :].unsqueeze(2).broadcast_to([128, 16, 4]), ALU.mult),
                    reads=[rp, r_const], writes=[r_tmps])
                b.op("dve", lambda e, xi=xi: e.tensor_tensor(xi[:, NP:NT], xi[:, NP:NT], tmps[:], ALU.add),
                     reads=[r_tmps, rxi], writes=[rxi])
        outs.append(b.dma("sp", xo[:, i, :], xi[:], d[16 + i % 2], reads=[rxi]))
    b.wait_all("sp", outs)
    b.emit(); b.close()
    return nc


class _HSet:
    pass


FILL_RATIO = 1
CRIT_RATIO = 2


def build_hgrna():
    nc = bass.Bass("TRN2", target_bir_lowering=False)
    NT = NP + NS
    NBLK = NP // 128
    NCK = NP // CH
    dram = lambda name, shape, kind="ExternalInput": nc.dram_tensor(name, list(shape), F32, kind=kind).ap()
    xT = dram("xT", [128, KC, NT])
    vecd = dram("vec", [128, 4, KC]); modsd = dram("mods", [128, 3, KC, 16])
    whg = dram("whg", [NHH, 128, 4, KC, 128])
    lbpd = dram("lbp", [128, 2, NHH])
    m01d = dram("m01", [128, 128]); cmd = dram("cm", [128, 4, 128]); identd = dram("ident", [128, 128])
    s0d = dram("s0", [NHH, 128, 16, 128]); msd = dram("ms", [64, 64]); cmsd = dram("cms", [128, 16, 64])
    smd = dram("smask", [128, NT])
    oloc = dram("oloc", [NHH, 128, NT], "ExternalOutput")
    qbo = dram("qbo", [NHH, 128, NT], "ExternalOutput")
    sgo = dram("sgo", [NHH, 128, NT], "ExternalOutput")
    snew = dram("snew", [NHH, 128, 16, 128], "ExternalOutput")
    send = dram("send", [128, NHH, 128], "ExternalOutput")
    dout = dram("dout", [128, NHH], "ExternalOutput")

    b = Builder(nc, n_dsem=22)
    d = b.dsems
    pr = PsumRot(b, 4)
    pr_loop = PsumRot.__new__(PsumRot); pr_loop.tiles = pr.tiles[0:2]; pr_loop.res = pr.res[0:2]; pr_loop.i = 0
    pr_prep = PsumRot.__new__(PsumRot); pr_prep.tiles = pr.tiles[2:4]; pr_prep.res = pr.res[2:4]; pr_prep.i = 0
    po_banks = [(b.ps("pob%d" % i, [128, 512]), Res("pob%d" % i)) for i in range(2)]
    pu_banks = [(b.ps("pub%d" % i, [128, 512]), Res("pub%d" % i)) for i in range(2)]
    xbuf = b.sb("xbuf", [128, KC * NT], F32)
    x = xbuf[:].rearrange("p (k n) -> p k n", k=KC)
    h = b.sb("h", [128, KC, NT], BF16)
    rstd = b.sb("rstd", [128, NT], F32)
    xsq0 = b.sb("xsq0", [128, NT], BF16)
    t32a = b.sb("t32a", [128, NT], F32)
    NWB = 5
    wring = [b.sb("wr%d" % i, [128, KC, 128], BF16) for i in range(NWB)]
    Dall = b.sb("Dall", [128, NHH], F32)
    vec = b.sb("vecs", [128, 4, KC], F32); mods = b.sb("modss", [128, 3, KC, 16], F32)
    lbp = b.sb("lbps", [128, 2, NHH], F32); oml = b.sb("oml", [128, NHH], F32)
    m01 = b.sb("m01s", [128, 128], F32); cm = b.sb("cms_", [128, 4, 128], F32)
    ident = b.sb("idents", [128, 128], BF16); identf = b.sb("identf", [128, 128], F32)
    ones = b.sb("ones", [128, 128], BF16)
    gm = b.sb("gm", [128, KC], F32); gms = b.sb("gms", [128, KC, 16], F32)
    ms = b.sb("mss", [64, 64], F32); cms = b.sb("cmss", [128, 16, 64], F32)
    smask = b.sb("smasks", [128, NT], F32); onesf = b.sb("onesf", [128, NCK], F32)
    R = Res
    r_x, r_h, r_rstd, r_const, r_gm, r_gms, r_xsq0, r_t32a, r_D = [R(n) for n in "x h rstd const gm gms xsq t32 D".split()]
    r_wr = [R("wr%d" % i) for i in range(NWB)]

    f32_names = ["qf", "kf", "bA", "bB", "sg", "Oraw", "QBf"]
    bf_names = ["qt", "kt", "kh"]
    sets = []
    for si in range(2):
        B = _HSet()
        if si == 0:
            for nm in f32_names:
                if nm == "QBf":
                    B.QBf = t32a[:]
                elif nm == "Oraw":
                    B.Oraw = rstd[:]
                else:
                    setattr(B, nm, b.sb(nm + "0", [128, NT], F32)[:])
            for nm in bf_names:
                setattr(B, nm, b.sb(nm + "0", [128, NT], BF16)[:])
            B.Vt = b.sb("Vt0", [128, NBLK + 1, 128], BF16)[:]
            B.S0 = b.sb("S00", [128, 16, 128], F32)[:]; B.S0b = b.sb("S0b0", [128, 16, 128], BF16)[:]
            B.Khms = b.sb("Khms0", [128, 16, 64], BF16)[:]; B.KhmTs = b.sb("KhmTs0", [64, 16, 128], BF16)[:]
            B.At = b.sb("At0", [128, 128], BF16)[:]; B.Khm = b.sb("Khm0", [128, 4, 128], BF16)[:]
            B.KhmT = b.sb("KhmT0", [128, 4, 128], BF16)[:]
            B.At_b = b.sb("At0b", [128, 128], BF16)[:]; B.Khm_b = b.sb("Khm0b", [128, 4, 128], BF16)[:]
            B.KhmT_b = b.sb("KhmT0b", [128, 4, 128], BF16)[:]
            B.S = b.sb("S_0", [128, 128], F32)[:]; B.Sbf = b.sb("Sbf0", [128, 128], BF16)[:]
            B.S2 = b.sb("S2_0", [128, 128], F32)[:]; B.Sbf2 = b.sb("Sbf2_0", [128, 128], BF16)[:]
            B.dec = b.sb("dec0", [128, NCK + 16], F32)[:]
            B.pbA = b.sb("pbA0", [128, NCK], F32)[:]; B.pbB = b.sb("pbB0", [128, NCK], F32)[:]
            B.Ats = b.sb("Ats0", [64, 64], BF16)[:]
        else:
            off = [0]

            def carve(n_f32, dt, shape):
                v = xbuf[:, off[0]:off[0] + n_f32]
                off[0] += n_f32
                if dt is BF16:
                    v = v.bitcast(BF16)
                if len(shape) == 2:
                    return v[:, 0:shape[1]] if shape[0] == 128 else v[0:shape[0], 0:shape[1]]
                if len(shape) == 3:
                    vv = v[:, 0:shape[1] * shape[2]].rearrange("p (a c) -> p a c", a=shape[1])
                    return vv if shape[0] == 128 else vv[0:shape[0]]
            for nm in f32_names:
                setattr(B, nm, carve(NT, F32, [128, NT]))
            for nm in bf_names:
                setattr(B, nm, carve(NT // 2, BF16, [128, NT]))
            B.Vt = carve((NBLK + 1) * 64, BF16, [128, NBLK + 1, 128])
            B.S0 = carve(2048, F32, [128, 16, 128]); B.S0b = carve(1024, BF16, [128, 16, 128])
            B.Khms = carve(512, BF16, [128, 16, 64]); B.KhmTs = carve(1024, BF16, [64, 16, 128])
            B.At = carve(64, BF16, [128, 128]); B.Khm = carve(256, BF16, [128, 4, 128]); B.KhmT = carve(256, BF16, [128, 4, 128])
            B.At_b = carve(64, BF16, [128, 128]); B.Khm_b = carve(256, BF16, [128, 4, 128]); B.KhmT_b = carve(256, BF16, [128, 4, 128])
            B.S = carve(128, F32, [128, 128]); B.Sbf = carve(64, BF16, [128, 128])
            B.S2 = carve(128, F32, [128, 128]); B.Sbf2 = carve(64, BF16, [128, 128])
            B.dec = carve(NCK + 16, F32, [128, NCK + 16])
            B.pbA = carve(NCK, F32, [128, NCK]); B.pbB = carve(NCK, F32, [128, NCK])
            B.Ats = carve(32, BF16, [64, 64])
            assert off[0] <= KC * NT, off[0]
        for nm in f32_names + bf_names + ["Vt", "S0", "S0b", "Khms", "KhmTs", "At", "Khm", "KhmT", "At_b", "Khm_b", "KhmT_b", "S", "Sbf", "S2", "Sbf2", "dec", "pbA", "pbB", "Ats"]:
            setattr(B, "r_" + nm, Res(nm + str(si)))
        if si == 0:
            B.r_QBf = r_t32a
            B.r_Oraw = r_rstd
        sets.append(B)

    b.dma("sp", xbuf[:], xT.rearrange("p k n -> p (k n)"), d[0], writes=[r_x])
    for dst, src in [(vec, vecd), (mods, modsd), (lbp, lbpd), (m01, m01d), (cm, cmd), (identf, identd), (ms, msd), (cms, cmsd), (smask, smd)]:
        b.dma("sp", dst[:], src, d[1], writes=[r_const])
    b.op("pool", lambda e: e.memset(ones[:], 1.0), writes=[r_const])
    b.op("pool", lambda e: e.memset(Dall[:], 0.0), writes=[r_D])
    b.op("pool", lambda e: e.memset(onesf[:], 1.0), writes=[r_const])
    b.op("act", lambda e: e.activation(ident[:], identf[:], AF.Copy), reads=[r_const], writes=[r_const])
    b.op("dve", lambda e: e.tensor_tensor(oml[:], lbp[:, 0, :], lbp[:, 1, :], ALU.subtract), reads=[r_const], writes=[r_const])
    b.op("act", lambda e: e.activation(oml[:], oml[:], AF.Sigmoid), reads=[r_const], writes=[r_const])

    wsem = [d[2], d[3], d[4], d[5], d[6]]
    wlist = [(hh, wh) for hh in range(NHH) for wh in range(4)]

    def load_w(n):
        if n >= len(wlist):
            return
        a, c = wlist[n]
        b.dma("pool", wring[n % NWB][:], whg[a][:, c, :, :], wsem[n % NWB], writes=[r_wr[n % NWB]])
    for n in range(4):
        load_w(n)
    wcount = [0]

    def next_w():
        n = wcount[0]
        wcount[0] += 1
        return n, wring[n % NWB], r_wr[n % NWB]

    emit_norm_mod(b, pr, x, r_x, h, r_h, NT, NP, vec, 0, 1, 2, mods, 0, 1, ones, r_const,
                  ([xsq0[:], xsq0[:]], [r_xsq0, r_xsq0], rstd[:], r_rstd, gm, r_gm, gms, r_gms, [t32a[:], t32a[:]], [r_t32a, r_t32a]))
    B1 = sets[1]
    for nm in f32_names + bf_names + ["Vt", "S0", "S0b", "Khms", "KhmTs", "At", "Khm", "KhmT", "At_b", "Khm_b", "KhmT_b", "S", "Sbf", "S2", "Sbf2", "dec", "pbA", "pbB", "Ats"]:
        rr = getattr(B1, "r_" + nm)
        rr.r = list(r_x.r)
        rr.w = r_x.w

    outs = []
    dsem_set = [dict(s0=d[7], s0b=d[8], qb=d[9], sg=d[10], ol=d[11], sn=d[12], se=d[13]),
                dict(s0=d[14], s0b=d[15], qb=d[16], sg=d[17], ol=d[18], sn=d[19], se=d[20])]

    def proj_fm(dst_fn):
        n_, w, rw = next_w()
        for (s, n) in ntiles(NT):
            pt, rp = pr_prep.next()
            for k in range(KC):
                b.op("pe", lambda e, pt=pt, w=w, k=k, s=s, n=n: e.matmul(
                    pt[:, 0:n], w[:, k, :], h[:, k, s:s + n], start=(k == 0), stop=(k == KC - 1)),
                    reads=[rw, r_h], writes=[rp], sig=(k == KC - 1))
                if k % 4 == 3 and k != KC - 1 and n > 128:
                    yield
            dst_fn(pt, rp, s, n)
            yield
        load_w(n_ + 4)

    def prep(hh):
        B = sets[hh % 2]; ds = dsem_set[hh % 2]
        b.dma("sp", B.S0, s0d[hh], ds["s0"], writes=[B.r_S0])
        b.dma("pool", B.S0b, s0d[hh], ds["s0b"], writes=[B.r_S0b])
        yield from proj_fm(lambda pt, rp, s, n: b.op("act", lambda e: e.activation(B.qf[:, s:s + n], pt[:, 0:n], AF.Silu),
                                                     reads=[rp], writes=[B.r_qf]))
        yield from proj_fm(lambda pt, rp, s, n: b.op("act", lambda e: e.activation(B.kf[:, s:s + n], pt[:, 0:n], AF.Sigmoid, scale=-1.0),
                                                     reads=[rp], writes=[B.r_kf]))
        b.op("dve", lambda e: e.tensor_scalar(B.kf, B.kf, oml[:, hh:hh + 1], None, ALU.mult), reads=[B.r_kf, r_const], writes=[B.r_kf])
        b.op("act", lambda e: e.activation(B.bA, B.kf, AF.Ln, bias=1.0, scale=-1.0), reads=[B.r_kf], writes=[B.r_bA])
        yield
        n_, w, rw = next_w()
        for blk in range(NBLK + 1):
            m = 128 if blk < NBLK else NS
            c0 = blk * 128
            pt, rp = pr_prep.next()
            for k in range(KC):
                b.op("pe", lambda e, pt=pt, w=w, k=k, c0=c0, m=m: e.matmul(
                    pt[0:m, 0:128], h[:, k, c0:c0 + m], w[:, k, :], start=(k == 0), stop=(k == KC - 1)),
                    reads=[rw, r_h], writes=[rp], sig=(k == KC - 1))
            b.op("act", lambda e, pt=pt, blk=blk, m=m: e.activation(B.Vt[0:m, blk, :], pt[0:m, 0:128], AF.Copy),
                 reads=[rp], writes=[B.r_Vt])
            yield
        load_w(n_ + 4)
        yield from proj_fm(lambda pt, rp, s, n: b.op("act", lambda e: e.activation(B.sg[:, s:s + n], pt[:, 0:n], AF.Silu),
                                                     reads=[rp], writes=[B.r_sg]))
        outs.append(b.dma("sp", sgo[hh], B.sg, ds["sg"], reads=[B.r_sg]))
        b.op("dve", lambda e: e.tensor_tensor_scan(B.bB, smask[:], B.bA, 0.0, ALU.mult, ALU.add),
             reads=[B.r_bA, r_const], writes=[B.r_bB])
        bb, rbb = B.bB, B.r_bB
        other, rother = B.bA, B.r_bA
        yield
        bv = bb[:, 0:NP].rearrange("p (c t) -> p c t", t=CH)
        ov = other[:, 0:NP].rearrange("p (c t) -> p c t", t=CH)
        b.op("act", lambda e: e.activation(B.dec[:, 0:NCK].unsqueeze(2), bv[:, :, CH - 1:CH], AF.Exp), reads=[rbb], writes=[B.r_dec])
        b.op("dve", lambda e: e.tensor_tensor(ov, bv[:, :, CH - 1:CH].broadcast_to([128, NCK, CH]), bv, ALU.subtract),
             reads=[rbb], writes=[rother])
        bs_ = bb[:, NP:NT].rearrange("p (c t) -> p c t", t=4)
        os_ = other[:, NP:NT].rearrange("p (c t) -> p c t", t=4)
        b.op("act", lambda e: e.activation(B.dec[:, NCK:NCK + 16].unsqueeze(2), bs_[:, :, 3:4], AF.Exp), reads=[rbb], writes=[B.r_dec])
        b.op("dve", lambda e: e.tensor_tensor(os_, bs_[:, :, 3:4].broadcast_to([128, 16, 4]), bs_, ALU.subtract),
             reads=[rbb], writes=[rother])
        b.op("act", lambda e: e.activation(other[:, 0:NT], other[:, 0:NT], AF.Exp), reads=[rother], writes=[rother])
        b.op("dve", lambda e: e.tensor_tensor(B.kh, B.kf, other[:, 0:NT], ALU.mult), reads=[rother, B.r_kf], writes=[B.r_kh])
        yield
        b.op("act", lambda e: e.activation(other[:, 0:NT], bb[:, 0:NT], AF.Exp), reads=[rbb, B.r_kh], writes=[rother])
        b.op("dve", lambda e: e.tensor_tensor(B.qt, B.qf, other[:, 0:NT], ALU.mult), reads=[rother, B.r_qf], writes=[B.r_qt])
        b.op("act", lambda e: e.activation(other[:, 0:NT], bb[:, 0:NT], AF.Exp, scale=-1.0), reads=[rbb, B.r_qt], writes=[rother])
        b.op("dve", lambda e: e.tensor_tensor(B.kt, B.kf, other[:, 0:NT], ALU.mult), reads=[rother, B.r_kf], writes=[B.r_kt])
        yield
        b.op("pool", lambda e: e.tensor_copy(B.pbA.unsqueeze(2), bv[:, :, CH - 1:CH]), reads=[rbb], writes=[B.r_pbA])
        b.op("dve", lambda e: e.tensor_tensor_scan(B.pbB, onesf[:], B.pbA, 0.0, ALU.mult, ALU.add),
             reads=[B.r_pbA, r_const], writes=[B.r_pbB])
        pin, rpin = B.pbB, B.r_pbB
        pex, rpex = B.pbA, B.r_pbA
        b.op("act", lambda e: e.activation(Dall[:, hh:hh + 1], pin[:, NCK - 1:NCK], AF.Exp), reads=[rpin], writes=[r_D])
        b.op("pool", lambda e: e.tensor_tensor(pex.unsqueeze(2), pin.unsqueeze(2), bv[:, :, CH - 1:CH], ALU.subtract),
             reads=[rpin, rbb], writes=[rpex])
        b.op("act", lambda e: e.activation(pex, pex, AF.Exp), reads=[rpex], writes=[rpex])
        b.op("pool", lambda e: e.tensor_tensor(
            B.QBf[:, 0:NP].rearrange("p (c t) -> p c t", t=CH), B.qt[:, 0:NP].rearrange("p (c t) -> p c t", t=CH),
            pex.unsqueeze(2).broadcast_to([128, NCK, CH]), ALU.mult), reads=[rpex, B.r_qt], writes=[B.r_QBf])
        b.op("pool", lambda e: e.memset(B.QBf[:, NP:NT], 0.0), writes=[B.r_QBf])
        outs.append(b.dma("sp", qbo[hh], B.QBf, ds["qb"], reads=[B.r_QBf]))
        b.op("pool", lambda e: e.memset(B.S, 0.0), writes=[B.r_S])
        b.op("pool", lambda e: e.memset(B.Sbf, 0.0), writes=[B.r_Sbf])
        yield

    def loop(hh):
        B = sets[hh % 2]; ds = dsem_set[hh % 2]
        def front(blk):
            c0 = blk * 128
            Khm, rKhm, KhmT, rKhmT, At, rAt = ((B.Khm, B.r_Khm, B.KhmT, B.r_KhmT, B.At, B.r_At) if blk % 2 == 0 else
                                               (B.Khm_b, B.r_Khm_b, B.KhmT_b, B.r_KhmT_b, B.At_b, B.r_At_b))
            b.op("dve", lambda e: e.tensor_tensor(Khm, B.kh[:, c0:c0 + 128].unsqueeze(1).broadcast_to([128, 4, 128]), cm[:], ALU.mult),
                 reads=[B.r_kh, r_const], writes=[rKhm])
            ptT, rpT = pr_loop.next()
            ptTb = ptT[:, :].bitcast(BF16)
            for c in range(4):
                b.op("pe", lambda e, c=c: e.transpose(ptTb[:, c * 128:(c + 1) * 128], Khm[:, c, :], ident[:]),
                     reads=[rKhm, r_const], writes=[rpT], sig=(c == 3))
            b.op("act", lambda e: e.activation(KhmT.rearrange("p c k -> p (c k)"), ptTb[:, 0:512], AF.Copy),
                 reads=[rpT], writes=[rKhmT])
            pa, rpa = pr_loop.next()
            b.op("pe", lambda e: e.matmul(pa[:, 0:128], B.kt[:, c0:c0 + 128], B.qt[:, c0:c0 + 128], start=True, stop=True),
                 reads=[B.r_kt, B.r_qt], writes=[rpa])
            b.op("dve", lambda e: e.tensor_tensor(At, pa[:, 0:128], m01[:], ALU.mult), reads=[rpa, r_const], writes=[rAt])
            po, rpo = po_banks[blk % 2]
            b.op("pe", lambda e: e.matmul(po[:, 0:128], B.Vt[:, blk, :], At, start=True, stop=False),
                 reads=[B.r_Vt, rAt], writes=[rpo], sig=False)
            pu, rpu = pu_banks[blk % 2]
            for c in range(4):
                b.op("pe", lambda e, c=c: e.matmul(pu[:, c * 128:(c + 1) * 128], KhmT[:, c, :], B.Vt[:, blk, :], start=True, stop=True),
                     reads=[rKhmT, B.r_Vt], writes=[rpu], sig=(c == 3))

        front(0)
        for blk in range(NBLK):
            c0 = blk * 128
            if blk + 1 < NBLK:
                front(blk + 1)
            yield
            po, rpo = po_banks[blk % 2]
            pu, rpu = pu_banks[blk % 2]
            for c in range(4):
                ck = blk * 4 + c
                Sc, rSc, Sn, rSn = (B.S, B.r_S, B.S2, B.r_S2) if ck % 2 == 0 else (B.S2, B.r_S2, B.S, B.r_S)
                Sbc, rSbc, Sbn, rSbn = (B.Sbf, B.r_Sbf, B.Sbf2, B.r_Sbf2) if ck % 2 == 0 else (B.Sbf2, B.r_Sbf2, B.Sbf, B.r_Sbf)
                b.op("pe", lambda e, c=c: e.matmul(po[:, c * CH:(c + 1) * CH], Sbc, B.qt[:, c0 + c * CH:c0 + (c + 1) * CH],
                                                   start=False, stop=(c == 3)),
                     reads=[rSbc, B.r_qt], writes=[rpo], sig=True)
                b.op("dve", lambda e, c=c, ck=ck: e.scalar_tensor_tensor(Sn, Sc, B.dec[:, ck:ck + 1], pu[:, c * 128:(c + 1) * 128], ALU.mult, ALU.add),
                     reads=[rpu, rSc, B.r_dec], writes=[rSn])
                b.op("pool", lambda e: e.tensor_copy(Sbn, Sn), reads=[rSn], writes=[rSbn])
                yield
            b.op("act", lambda e: e.activation(B.Oraw[:, c0:c0 + 128], po[:, 0:128], AF.Copy), reads=[rpo], writes=[B.r_Oraw])
        outs.append(b.dma("sp", send[:, hh, :], B.S, ds["se"], reads=[B.r_S]))
        b.op("dve", lambda e: e.tensor_tensor(B.Khms, B.kh[:, NP:NT].unsqueeze(1).broadcast_to([128, 16, 64]), cms[:], ALU.mult),
             reads=[B.r_kh, r_const], writes=[B.r_Khms])
        for half in range(2):
            ptT, rpT = pr_loop.next()
            ptTb = ptT[:, :].bitcast(BF16)
            for s8 in range(8):
                sq = half * 8 + s8
                b.op("pe", lambda e, ptTb=ptTb, s8=s8, sq=sq: e.transpose(ptTb[0:64, s8 * 128:(s8 + 1) * 128], B.Khms[:, sq, :], ident[:]),
                     reads=[B.r_Khms, r_const], writes=[rpT], sig=(s8 == 7))
            b.op("act", lambda e, ptTb=ptTb, half=half: e.activation(
                B.KhmTs[:, half * 8:half * 8 + 8, :].rearrange("p c k -> p (c k)"), ptTb[0:64, 0:1024], AF.Copy),
                reads=[rpT], writes=[B.r_KhmTs])
            yield
        pa, rpa = pr_loop.next()
        b.op("pe", lambda e, pa=pa: e.matmul(pa[0:64, 0:64], B.kt[:, NP:NT], B.qt[:, NP:NT], start=True, stop=True),
             reads=[B.r_kt, B.r_qt], writes=[rpa])
        b.op("dve", lambda e, pa=pa: e.tensor_tensor(B.Ats, pa[0:64, 0:64], ms[:], ALU.mult), reads=[rpa, r_const], writes=[B.r_Ats])
        po, rpo = pr_loop.next()
        b.op("pe", lambda e, po=po: e.matmul(po[:, 0:64], B.Vt[0:64, NBLK, :], B.Ats, start=True, stop=False),
             reads=[B.r_Vt, B.r_Ats], writes=[rpo], sig=False)
        for sq in range(16):
            b.op("pe", lambda e, po=po, sq=sq: e.matmul(po[:, sq * 4:sq * 4 + 4], B.S0b[:, sq, :], B.qt[:, NP + sq * 4:NP + sq * 4 + 4],
                                                       start=False, stop=(sq == 15)),
                 reads=[B.r_S0b, B.r_qt], writes=[rpo], sig=(sq == 15))
        b.op("act", lambda e, po=po: e.activation(B.Oraw[:, NP:NT], po[:, 0:64], AF.Copy), reads=[rpo], writes=[B.r_Oraw])
        outs.append(b.dma("sp", oloc[hh], B.Oraw, ds["ol"], reads=[B.r_Oraw]))
        yield
        for q4 in range(4):
            pu, rpu = pr_loop.next()
            for s4 in range(4):
                sq = q4 * 4 + s4
                b.op("pe", lambda e, pu=pu, s4=s4, sq=sq: e.matmul(pu[:, s4 * 128:(s4 + 1) * 128], B.KhmTs[:, sq, :], B.Vt[0:64, NBLK, :],
                                                                 start=True, stop=True),
                     reads=[B.r_KhmTs, B.r_Vt], writes=[rpu], sig=(s4 == 3))
            for s4 in range(4):
                sq = q4 * 4 + s4
                b.op("dve", lambda e, pu=pu, s4=s4, sq=sq: e.scalar_tensor_tensor(
                    B.S0[:, sq, :], B.S0[:, sq, :], B.dec[:, NCK + sq:NCK + sq + 1], pu[:, s4 * 128:(s4 + 1) * 128], ALU.mult, ALU.add),
                    reads=[rpu, B.r_S0, B.r_dec], writes=[B.r_S0])
            yield
        outs.append(b.dma("sp", snew[hh], B.S0, ds["sn"], reads=[B.r_S0]))

    def drain(g):
        for _ in g:
            pass

    drain(prep(0))
    for hh in range(NHH):
        crit = loop(hh)
        fill = prep(hh + 1) if hh + 1 < NHH else iter(())
        done_c = done_f = False
        while not (done_c and done_f):
            for _ in range(CRIT_RATIO):
                if not done_c:
                    try:
                        next(crit)
                    except StopIteration:
                        done_c = True
            for _ in range(FILL_RATIO):
                if not done_f:
                    try:
                        next(fill)
                    except StopIteration:
                        done_f = True
    outs.append(b.dma("sp", dout, Dall[:], d[21], reads=[r_D]))
    b.wait_all("sp", outs)
    b.emit(); b.close()
    return nc


def fm(a):
    n = a.shape[0]
    return np.ascontiguousarray(a.T.reshape(16, 128, n).transpose(1, 0, 2))
def unfm(t):
    return np.ascontiguousarray(t.transpose(2, 1, 0).reshape(t.shape[2], D))
def vfm(v):
    return v.reshape(-1, 128).T
def mods_fm(m3):
    return np.ascontiguousarray(m3.reshape(3, 16, 16, 128).transpose(3, 0, 2, 1))
def tile_w_in(w):
    return np.ascontiguousarray(w.reshape(16, 128, 2, 44, 128).transpose(3, 1, 2, 0, 4))
def tile_w_out(w):
    return np.ascontiguousarray(w.reshape(4, 11, 128, 16, 128).transpose(0, 3, 2, 1, 4))
def tile_cols(w):
    return w.reshape(16, 128, w.shape[1]).transpose(1, 0, 2)
def tile_wqkv(w):
    out = np.empty((8, 128, 4, 16, 128), np.float32)
    for g in range(8):
        out[g, :, 0] = tile_cols(w[:, 256 * g:256 * g + 128])
        out[g, :, 1] = tile_cols(w[:, 256 * g + 128:256 * g + 256])
        kk = w[:, 2048 + 64 * g:2048 + 64 * g + 64]; vv = w[:, 2560 + 64 * g:2560 + 64 * g + 64]
        out[g, :, 2] = tile_cols(np.concatenate([kk, kk], 1))
        out[g, :, 3] = tile_cols(np.concatenate([vv, vv], 1))
    return out
def tile_sq(w):
    return np.ascontiguousarray(w.reshape(16, 128, 16, 128).transpose(2, 1, 0, 3))
NEG = -30000.0
def attn_consts():
    s = np.arange(128)[:, None]; q = np.arange(128)[None, :]
    nd = np.zeros((128, 2, 128), np.float32); mk = np.zeros((128, 2, 128), np.float32)
    nd[:, 0] = -(q - s + 128); mk[:, 0] = np.where(s > q, 0, NEG)
    nd[:, 1] = -(q - s); mk[:, 1] = np.where(s <= q, 0, NEG)
    nd = np.where(mk < 0, 0, nd).astype(np.float32)
    t = np.arange(4)[None, :]
    ndc = -(128 + t - s).astype(np.float32); mkc = np.where(s > t, 0, NEG).astype(np.float32)
    ndc = np.where(mkc < 0, 0, ndc).astype(np.float32)
    a = np.arange(64)
    same = (a[:, None] // 4) == (a[None, :] // 4)
    tp = a[:, None] % 4; tq = a[None, :] % 4
    ok = same & (tp <= tq)
    ndn = np.where(ok, -(tq - tp), 0).astype(np.float32); mkn = np.where(ok, 0, NEG).astype(np.float32)
    return dict(nd=nd, mk=mk, ndc=ndc, mkc=mkc, ndn=ndn, mkn=mkn)
def tile_whg(w):
    out = np.empty((16, 128, 4, 16, 128), np.float32)
    for hh in range(16):
        for wh in range(4):
            out[hh, :, wh] = tile_cols(w[:, wh * 2048 + hh * 128: wh * 2048 + hh * 128 + 128])
    return out
def hgrn_consts():
    a = np.arange(128)
    m01 = ((a[:, None] // 32 == a[None, :] // 32) & (a[:, None] <= a[None, :])).astype(np.float32)
    cm = np.broadcast_to((a[None, :] // 32 == np.arange(4)[:, None]).astype(np.float32)[None], (128, 4, 128)).copy()
    s = np.arange(64)
    ms = ((s[:, None] // 4 == s[None, :] // 4) & (s[:, None] <= s[None, :])).astype(np.float32)
    cms = np.broadcast_to((s[None, :] // 4 == np.arange(16)[:, None]).astype(np.float32)[None], (128, 16, 64)).copy()
    t = np.arange(1088)
    sm = np.where(t < 1024, (t % 32) != 0, ((t - 1024) % 4) != 0).astype(np.float32)
    smask = np.ascontiguousarray(np.broadcast_to(sm[None], (128, 1088)))
    return dict(m01=m01, cm=cm, ms=ms, cms=cms, ident=np.eye(128, dtype=np.float32), smask=smask)

NCORE = 8
_PROGS = {}


def _prog(name, fn):
    if name not in _PROGS:
        _PROGS[name] = fn()
    return _PROGS[name]


def _run(name, fn, in_maps):
    nc = _prog(name, fn)
    res = run_bass_kernel_spmd(nc, in_maps, core_ids=list(range(NCORE)))
    return res.results


def _f32(a):
    return np.ascontiguousarray(np.asarray(a, dtype=np.float32))


def kernel(x_prompt, x_sample, cache_swa_k, cache_swa_v, state_hgrn, state_ffn_conv, c_prompt, c_sample,
           norm1_g, norm2_g, w_ada, b_ada, attn_w_qkv, attn_w_o, attn_sinks,
           hgrn_w_in, hgrn_lower_bounds, hgrn_norm_g, hgrn_w_o,
           ffn_w_in, ffn_conv_w, ffn_conv_b, ffn_w_out, final_norm_g):
    (x_prompt, x_sample, cache_swa_k, cache_swa_v, state_hgrn, state_ffn_conv, c_prompt, c_sample,
     norm1_g, norm2_g, w_ada, b_ada, attn_w_qkv, attn_w_o, attn_sinks,
     hgrn_w_in, hgrn_lower_bounds, hgrn_norm_g, hgrn_w_o,
     ffn_w_in, ffn_conv_w, ffn_conv_b, ffn_w_out, final_norm_g) = [_f32(a) for a in (
        x_prompt, x_sample, cache_swa_k, cache_swa_v, state_hgrn, state_ffn_conv, c_prompt, c_sample,
        norm1_g, norm2_g, w_ada, b_ada, attn_w_qkv, attn_w_o, attn_sinks,
        hgrn_w_in, hgrn_lower_bounds, hgrn_norm_g, hgrn_w_o,
        ffn_w_in, ffn_conv_w, ffn_conv_b, ffn_w_out, final_norm_g)]
    Dm = 2048
    xp = x_prompt[0]
    xs = x_sample.reshape(128 * 4, Dm)

    c_all = np.concatenate([c_prompt, c_sample], 0)
    cT = fm(c_all)
    maps = []
    for c in range(NCORE):
        wt = np.empty((24, 128, 16, 128), np.float32)
        bt = np.empty((128, 24), np.float32)
        for n in range(24):
            l, ch = n // 12, 12 * c + n % 12
            wt[n] = tile_cols(w_ada[l][:, ch * 128:(ch + 1) * 128])
            bt[:, n] = b_ada[l][ch * 128:(ch + 1) * 128]
        maps.append({"cT": cT, "wada": wt, "bada": bt})
    res = _run("adaln", build_adaln, maps)
    mod = np.empty((2, 129, 6 * Dm), np.float32)
    for c in range(NCORE):
        mt = res[c]["modT"]
        for n in range(24):
            l, ch = n // 12, 12 * c + n % 12
            mod[l][:, ch * 128:(ch + 1) * 128] = mt[:, n, :].T
    mod = mod.reshape(2, 129, 6, Dm)

    def vec_for(l, g_vec, i0, extra=None):
        rows = [vfm(g_vec), vfm(mod[l, 0, i0]), vfm(mod[l, 0, i0 + 1]), vfm(mod[l, 0, i0 + 2])]
        if extra is not None:
            rows.append(vfm(extra))
        return np.ascontiguousarray(np.stack(rows, 1))

    def mods_for(l, c, i0):
        sl = slice(1 + 16 * c, 1 + 16 * c + 16)
        return mods_fm(np.stack([mod[l, sl, i0], mod[l, sl, i0 + 1], mod[l, sl, i0 + 2]], 0))

    aconst = attn_consts()
    wq_t = tile_wqkv(attn_w_qkv)
    wo_t = tile_sq(attn_w_o)
    sinks_b = np.ascontiguousarray(np.broadcast_to(attn_sinks[None], (128, 32)))
    vec0 = vec_for(0, norm1_g[0], 0)
    maps = []
    for c in range(NCORE):
        halo = xp[1024 * c - 128:1024 * c] if c > 0 else np.zeros((128, Dm), np.float32)
        xc = np.concatenate([halo, xp[1024 * c:1024 * (c + 1)], xs[64 * c:64 * (c + 1)]], 0)
        m = {"xT": fm(xc), "vec": vec0, "mods": mods_for(0, c, 0), "wqkv": wq_t, "wo": wo_t,
             "kcT": np.ascontiguousarray(cache_swa_k[16 * c:16 * c + 16].transpose(2, 3, 0, 1)),
             "vc": np.ascontiguousarray(cache_swa_v[16 * c:16 * c + 16].transpose(2, 1, 0, 3)),
             "sinks": sinks_b, "hb": np.full((128, 1), NEG if c == 0 else 0.0, np.float32)}
        m.update(aconst)
        maps.append(m)
    res = _run("attn", build_attn2, maps)
    x1 = [unfm(res[c]["xo"]) for c in range(NCORE)]
    ko = res[NCORE - 1]["kout"].reshape(2, 64, 4, 192).transpose(1, 2, 0, 3).reshape(64, 8, 192)
    swa_k_prompt = np.ascontiguousarray(ko[:, :, :128].transpose(2, 1, 0))[None]
    swa_v_prompt = np.ascontiguousarray(res[NCORE - 1]["vout"][:, 0])[None]
    swa_k_sample = np.empty((128, 4, 8, 64), np.float32)
    swa_v_sample = np.empty((128, 4, 8, 64), np.float32)
    for c in range(NCORE):
        ko = res[c]["kout"].reshape(2, 64, 4, 192).transpose(1, 2, 0, 3).reshape(64, 8, 192)
        swa_k_sample[16 * c:16 * c + 16] = ko[:, :, 128:].transpose(2, 1, 0).reshape(16, 4, 8, 64)
        swa_v_sample[16 * c:16 * c + 16] = res[c]["vout"][:64, 1].reshape(16, 4, 8, 64)

    def run_ffn(l, xin, last):
        w_in_t = tile_w_in(ffn_w_in[l])
        w_out_t = tile_w_out(ffn_w_out[l])
        convw = np.ascontiguousarray(np.concatenate([ffn_conv_w[l], ffn_conv_b[l][None]], 0).reshape(4, 44, 128).transpose(2, 1, 0))
        vec = vec_for(l, norm2_g[l], 3, extra=final_norm_g)
        maps = []
        for c in range(NCORE):
            halo = xin[c - 1][1022:1024] if c > 0 else np.zeros((2, Dm), np.float32)
            xc = np.concatenate([halo, xin[c]], 0)
            maps.append({"xT": fm(xc), "vec": vec, "mods": mods_for(l, c, 3), "w_in": w_in_t, "w_out": w_out_t,
                         "convw": convw,
                         "cstate": np.ascontiguousarray(state_ffn_conv[l, 16 * c:16 * c + 16].reshape(16, 2, 44, 128).transpose(3, 2, 0, 1)),
                         "flag": np.full((128, 1), 0.0 if c == 0 else 1.0, np.float32)})
        res = _run("ffn_last" if last else "ffn", (lambda: build_ffn(True)) if last else (lambda: build_ffn(False)), maps)
        xout = [unfm(res[c]["xo"]) for c in range(NCORE)]
        cbp = np.ascontiguousarray(res[NCORE - 1]["cbp"].transpose(2, 1, 0).reshape(2, 5632))[None]
        cbs = np.concatenate([res[c]["cbs"].transpose(2, 3, 1, 0).reshape(16, 2, 5632) for c in range(NCORE)], 0)
        return xout, cbp, cbs

    x2, cbp0, cbs0 = run_ffn(0, x1, False)

    hconst = hgrn_consts()
    whg_t = tile_whg(hgrn_w_in)
    lbp = np.ascontiguousarray(hgrn_lower_bounds.reshape(2, 16, 128).transpose(2, 0, 1))
    vec1 = vec_for(1, norm1_g[1], 0)
    maps = []
    for c in range(NCORE):
        m = {"xT": fm(x2[c]), "vec": vec1, "mods": mods_for(1, c, 0), "whg": whg_t, "lbp": lbp,
             "s0": np.ascontiguousarray(state_hgrn[16 * c:16 * c + 16].transpose(1, 2, 0, 3))}
        m.update(hconst)
        maps.append(m)
    resA = _run("hgrnA", build_hgrna, maps)
    s_loc = [resA[c]["send"] for c in range(NCORE)]
    d_loc = [resA[c]["dout"] for c in range(NCORE)]
    hgrn_state_sample = np.concatenate([resA[c]["snew"].transpose(2, 0, 1, 3) for c in range(NCORE)], 0)

    wo2_t = tile_sq(hgrn_w_o)
    ng = np.ascontiguousarray(hgrn_norm_g.reshape(16, 128).T)
    maps = []
    for c in range(NCORE):
        sr = np.zeros((16, 128, NRK, 128), np.float32)
        dr = np.ones((128, 16, NRK), np.float32)
        for r in range(c):
            sr[:, :, r, :] = s_loc[r].transpose(1, 0, 2)
            dr[:, :, r] = d_loc[r]
        maps.append({"xT": fm(x2[c]), "vec": vec1, "mods": mods_for(1, c, 0), "oloc": resA[c]["oloc"], "qb": resA[c]["qbo"],
                     "sg": resA[c]["sgo"], "sr": sr, "dr": dr, "sloc": s_loc[c], "dl": d_loc[c], "ng": ng, "wo": wo2_t})
    res = _run("hgrnB", build_hgrnb, maps)
    x3 = [unfm(res[c]["xo"]) for c in range(NCORE)]
    hgrn_state_prompt = np.ascontiguousarray(res[NCORE - 1]["send"].transpose(1, 0, 2))[None]

    y, cbp1, cbs1 = run_ffn(1, x3, True)
    y_prompt = np.concatenate([y[c][:1024] for c in range(NCORE)], 0)[None]
    y_sample = np.concatenate([y[c][1024:] for c in range(NCORE)], 0).reshape(128, 4, Dm)
    ffn_conv_prompt = np.stack([cbp0, cbp1], 0)
    ffn_conv_sample = np.stack([cbs0, cbs1], 0)
    outs = (y_prompt, y_sample, swa_k_prompt, swa_v_prompt, swa_k_sample, swa_v_sample,
            hgrn_state_prompt, hgrn_state_sample, ffn_conv_prompt, ffn_conv_sample)
    return tuple(np.ascontiguousarray(o, dtype=np.float32) for o in outs)
```

```python
from concourse.bass_utils import run_bass_kernel_spmd

import contextlib
import numpy as np
import concourse.bass as bass
import concourse.mybir as mybir

F32 = mybir.dt.float32
BF16 = mybir.dt.bfloat16
I32 = mybir.dt.int32
AF = mybir.ActivationFunctionType
ALU = mybir.AluOpType
AX = mybir.AxisListType

ENGS = ["sp", "act", "pool", "dve", "pe"]


class Res:
    __slots__ = ("name", "w", "r")

    def __init__(self, name=""):
        self.name = name
        self.w = None
        self.r = []


class DSem:
    def __init__(self, handle, name):
        self.h = handle
        self.name = name
        self.cnt = 0


class _Rec:
    def __getattr__(self, name):
        return lambda *a, **k: (name, a, k)


_REC = _Rec()


class Builder:
    def __init__(self, nc, n_dsem=12):
        self.nc = nc
        self.es = contextlib.ExitStack()
        self.q = {e: [] for e in ENGS}
        self.cnt = {e: 0 for e in ENGS}
        self.waited = {e: {} for e in ENGS}
        self.esem = {e: self.es.enter_context(nc.semaphore("s_" + e)) for e in ENGS}
        self.dsems = [DSem(self.es.enter_context(nc.semaphore("d%d" % i)), "d%d" % i)
                      for i in range(n_dsem)]
        self.pending = {e: [] for e in ENGS}
        self.n_inst = 0

    def sb(self, name, shape, dt):
        return self.es.enter_context(self.nc.sbuf_tensor(name, list(shape), dt))

    def ps(self, name, shape, dt=F32):
        return self.es.enter_context(self.nc.psum_tensor(name, list(shape), dt))

    def _deps(self, eng, reads, writes):
        need = {}

        def add(t):
            if t is None:
                return
            k, v = t
            if need.get(k, 0) < v:
                need[k] = v
        for r in reads:
            add(r.w)
        for w in writes:
            add(w.w)
            for t in w.r:
                add(t)
        waits = []
        for k, v in need.items():
            if k == "pe" and eng == "pe":
                continue
            if self.waited[eng].get(k, 0) >= v:
                continue
            self.waited[eng][k] = v
            waits.append((k, v))
        return waits

    def _semh(self, k):
        return self.esem[k] if isinstance(k, str) else k.h

    def op(self, eng, fn, reads=(), writes=(), sig=True):
        reads = [r for r in reads if r is not None]
        writes = [w for w in writes if w is not None]
        waits = self._deps(eng, reads, writes)
        for k, v in waits:
            cur = self.cnt[k] if isinstance(k, str) else k.cnt
            assert v <= cur, "forward wait %s %d > %d" % (k, v, cur)
        ticket = None
        if sig:
            self.cnt[eng] += 1
            ticket = (eng, self.cnt[eng])
            pend = self.pending[eng]
            self.pending[eng] = []
            for pr, pw in pend:
                self._commit(pr, pw, ticket)
            self._commit(reads, writes, ticket)
        else:
            t = (eng, self.cnt[eng] + 1)
            self._commit(reads, writes, t)
        self.q[eng].append((waits, fn(_REC), ticket, None))
        self.n_inst += 1
        return ticket

    def _commit(self, reads, writes, ticket):
        for r in reads:
            r.r.append(ticket)
        for w in writes:
            w.w = ticket
            w.r = []

    def dma(self, eng, out, in_, dsem, reads=(), writes=(), **kw):
        reads = [r for r in reads if r is not None]
        writes = [w for w in writes if w is not None]
        waits = self._deps(eng, reads, writes)
        for k, v in waits:
            cur = self.cnt[k] if isinstance(k, str) else k.cnt
            assert v <= cur, "forward wait %s %d > %d" % (k, v, cur)
        dsem.cnt += 16
        ticket = (dsem, dsem.cnt)
        self._commit(reads, writes, ticket)
        kw2 = dict(kw); kw2["out"] = out; kw2["in_"] = in_
        self.q[eng].append((waits, ("dma_start", (), kw2), None, (dsem, 16)))
        self.n_inst += 1
        return ticket

    def wait_all(self, eng, tickets):
        waits = []
        for t in tickets:
            if t is None:
                continue
            k, v = t
            if self.waited[eng].get(k, 0) >= v:
                continue
            self.waited[eng][k] = v
            waits.append((k, v))
        self.q[eng].append((waits, None, None, None))

    def emit(self):
        nc = self.nc
        handles = {"sp": "sync", "act": "scalar", "pool": "gpsimd", "dve": "vector", "pe": "tensor"}
        with nc.Block() as block:
            for eng in ENGS:
                items = self.q[eng]
                if not items:
                    continue

                def body(e, items=items, eng=eng):
                    for waits, fn, ticket, dinc in items:
                        for k, v in waits:
                            e.wait_ge(self._semh(k), v)
                        if fn is None:
                            continue
                        name, a, k = fn
                        ins = getattr(e, name)(*a, **k)
                        if ticket is not None:
                            ins.then_inc(self.esem[eng], 1)
                        if dinc is not None:
                            ins.then_inc(dinc[0].h, dinc[1])
                getattr(block, handles[eng])(body)

    def close(self):
        self.es.close()


D = 2048
KC = 16
DFF = 5632
FC = 44
NQ = 4
FQ = FC // NQ
NP = 1024
NS = 64
EPS = 1e-6


class PsumRot:
    def __init__(self, b, n=8):
        self.tiles = [b.ps("psb%d" % i, [128, 512]) for i in range(n)]
        self.res = [Res("psb%d" % i) for i in range(n)]
        self.i = 0

    def next(self):
        t, r = self.tiles[self.i], self.res[self.i]
        self.i = (self.i + 1) % len(self.tiles)
        return t, r


def ntiles(n, step=512):
    return [(s, min(step, n - s)) for s in range(0, n, step)]


def emit_norm_mod(b, pr, x, r_x, h, r_h, ncol, np_cols, vec, iv_g, iv_sh, iv_sc, mods, im_sh, im_sc,
                  ones, r_const, tmp, r_x_parts=None):
    xsq, r_xsq, rstd, r_rstd, gm, r_gm, gms, r_gms, t32, r_t32 = tmp
    tiles = ntiles(ncol)
    banks = [pr.next() for _ in tiles]
    for k in range(KC):
        i2 = k % 2
        rxk = r_x if r_x_parts is None else r_x_parts[k * len(r_x_parts) // KC]
        b.op("act", lambda e, k=k, i2=i2: e.activation(xsq[i2][:, 0:ncol], x[:, k, 0:ncol], AF.Square),
             reads=[rxk], writes=[r_xsq[i2]])
        for ti, (s, n) in enumerate(tiles):
            pt, rp = banks[ti]
            b.op("pe", lambda e, pt=pt, s=s, n=n, i2=i2, k=k: e.matmul(
                pt[:, 0:n], ones[:], xsq[i2][:, s:s + n], start=(k == 0), stop=(k == KC - 1)),
                reads=[r_const, r_xsq[i2]], writes=[rp], sig=(k == KC - 1) or ti == len(tiles) - 1)
    for ti, (s, n) in enumerate(tiles):
        pt, rp = banks[ti]
        b.op("act", lambda e, pt=pt, s=s, n=n: e.activation(rstd[:, s:s + n], pt[:, 0:n], AF.Ln,
                                                            bias=EPS, scale=1.0 / D),
             reads=[rp], writes=[r_rstd])
    b.op("act", lambda e: e.activation(rstd[:, 0:ncol], rstd[:, 0:ncol], AF.Exp, scale=-0.5), reads=[r_rstd], writes=[r_rstd])
    b.op("dve", lambda e: e.scalar_tensor_tensor(gm[:], vec[:, iv_sc, :], 1.0, vec[:, iv_g, :], ALU.add, ALU.mult),
         reads=[r_const], writes=[r_gm])
    ns = ncol - np_cols
    if ns:
        b.op("dve", lambda e: e.scalar_tensor_tensor(
            gms[:], mods[:, im_sc, :, :], 1.0, vec[:, iv_g, :].unsqueeze(2).broadcast_to([128, KC, 16]),
            ALU.add, ALU.mult), reads=[r_const], writes=[r_gms])
    for k in range(KC):
        i2 = k % 2
        r_hk = r_h[k] if isinstance(r_h, list) else r_h
        b.op("dve", lambda e, k=k, i2=i2: e.scalar_tensor_tensor(
            t32[i2][:, 0:np_cols], x[:, k, 0:np_cols], gm[:, k:k + 1], rstd[:, 0:np_cols], ALU.mult, ALU.mult),
            reads=[r_x, r_gm, r_rstd], writes=[r_t32[i2]])
        if ns:
            b.op("dve", lambda e, k=k, i2=i2: e.tensor_tensor(
                t32[i2][:, np_cols:ncol].rearrange("p (s t) -> p s t", t=4),
                x[:, k, np_cols:ncol].rearrange("p (s t) -> p s t", t=4),
                gms[:, k, :].unsqueeze(2).broadcast_to([128, 16, 4]), ALU.mult),
                reads=[r_x, r_gms], writes=[r_t32[i2]])
            b.op("dve", lambda e, k=k, i2=i2: e.tensor_tensor(
                t32[i2][:, np_cols:ncol], t32[i2][:, np_cols:ncol], rstd[:, np_cols:ncol], ALU.mult),
                reads=[r_t32[i2], r_rstd], writes=[r_t32[i2]])
            b.op("dve", lambda e, k=k, i2=i2: e.tensor_tensor(
                t32[i2][:, np_cols:ncol].rearrange("p (s t) -> p s t", t=4),
                t32[i2][:, np_cols:ncol].rearrange("p (s t) -> p s t", t=4),
                mods[:, im_sh, k, :].unsqueeze(2).broadcast_to([128, 16, 4]), ALU.add),
                reads=[r_t32[i2], r_const], writes=[r_t32[i2]])
            b.op("act", lambda e, k=k, i2=i2: e.activation(h[:, k, np_cols:ncol], t32[i2][:, np_cols:ncol], AF.Copy),
                 reads=[r_t32[i2]], writes=[r_hk])
        b.op("act", lambda e, k=k, i2=i2: e.activation(
            h[:, k, 0:np_cols], t32[i2][:, 0:np_cols], AF.Identity, bias=vec[:, iv_sh, k:k + 1], scale=1.0),
            reads=[r_t32[i2], r_const], writes=[r_hk])


def build_ffn(last):
    nc = bass.Bass("TRN2", target_bir_lowering=False)
    NCOL = 2 + NP + NS
    UW = 2 + NP + 16 * 6
    AW = UW - 2
    dram = lambda name, shape, kind="ExternalInput": nc.dram_tensor(name, list(shape), F32, kind=kind).ap()
    xT = dram("xT", [128, KC, NCOL])
    vecd = dram("vec", [128, 5, KC])
    modsd = dram("mods", [128, 3, KC, 16])
    w_in = dram("w_in", [FC, 128, 2, KC, 128])
    w_out = dram("w_out", [NQ, KC, 128, FQ, 128])
    convd = dram("convw", [128, FC, 4])
    cstd = dram("cstate", [128, FC, 16, 2])
    flagd = dram("flag", [128, 1])
    xo = dram("xo", [128, KC, NP + NS], "ExternalOutput")
    cbp = dram("cbp", [128, FC, 2], "ExternalOutput")
    cbs = dram("cbs", [128, FC, 16, 2], "ExternalOutput")

    b = Builder(nc, n_dsem=14)
    d = b.dsems
    pr = PsumRot(b)
    x = b.sb("x", [128, KC, NCOL], F32)
    h = b.sb("h", [128, KC, NCOL], BF16)
    act = b.sb("act", [128, FQ, NP + NS], BF16)
    U = [b.sb("U%d" % i, [128, UW], F32) for i in range(2)]
    G = [b.sb("G%d" % i, [128, NP + NS], F32) for i in range(2)]
    A = b.sb("A", [128, UW], F32)
    rstd = b.sb("rstd", [128, NCOL], F32)
    xsq = [b.sb("xsq%d" % i, [128, NCOL], BF16) for i in range(2)]
    t32 = [b.sb("t32%d" % i, [128, NCOL], F32) for i in range(2)]
    win = [b.sb("win%d" % i, [128, 2, KC, 128], BF16) for i in range(2)]
    wout = [b.sb("wout%d" % i, [128, FQ, 128], BF16) for i in range(2)]
    vec = b.sb("vecs", [128, 5, KC], F32)
    mods = b.sb("modss", [128, 3, KC, 16], F32)
    convw = b.sb("convws", [128, FC, 4], F32)
    cst = b.sb("csts", [128, FC, 16, 2], F32)
    cbps = b.sb("cbps", [128, FC, 2], F32)
    cbss = b.sb("cbss", [128, FC, 16, 2], F32)
    flag = b.sb("flags", [128, 1], F32)
    ones = b.sb("ones", [128, 128], BF16)
    gm = b.sb("gm", [128, KC], F32)
    gms = b.sb("gms", [128, KC, 16], F32)
    tmps = b.sb("tmps", [128, NS], F32)

    R = lambda n: Res(n)
    r_x, r_h, r_act, r_A, r_rstd, r_const, r_gm, r_gms, r_cb, r_tmps = [R(n) for n in
        "x h act A rstd const gm gms cb tmps".split()]
    r_h = [R("h%d" % k) for k in range(KC)]
    r_U = [R("U0"), R("U1")]; r_G = [R("G0"), R("G1")]
    r_xsq = [R("xsq0"), R("xsq1")]; r_t32 = [R("t0"), R("t1")]
    r_win = [R("win0"), R("win1")]; r_wout = [R("wo0"), R("wo1")]

    r_xp = [Res("xp%d" % i) for i in range(4)]
    xsems = [d[0], d[9], d[10], d[11]]
    for i in range(4):
        b.dma("sp", x[:, 4 * i:4 * i + 4, :], xT[:, 4 * i:4 * i + 4, :], xsems[i], writes=[r_xp[i]])
    joind = b.sb("joind", [128, 1], F32)
    b.op("pool", lambda e: e.memset(joind[:], 0.0), reads=r_xp, writes=[r_x])
    b.dma("sp", vec[:], vecd, d[1], writes=[r_const])
    b.dma("sp", mods[:], modsd, d[1], writes=[r_const])
    b.dma("sp", convw[:], convd, d[1], writes=[r_const])
    b.dma("sp", cst[:], cstd, d[1], writes=[r_const])
    b.dma("sp", flag[:], flagd, d[1], writes=[r_const])
    b.op("pool", lambda e: e.memset(ones[:], 1.0), writes=[r_const])

    win_sem = [d[2], d[3]]
    wout_sem = [d[4], d[5]]

    def load_win(j):
        b.dma("pool", win[j % 2][:], w_in[j], win_sem[j % 2], writes=[r_win[j % 2]])

    def load_wout(q, i):
        n = q * KC + i
        b.dma("pool", wout[n % 2][:], w_out[q, i], wout_sem[n % 2], writes=[r_wout[n % 2]])

    load_win(0)
    emit_norm_mod(b, pr, x, r_x, h, r_h, NCOL, 2 + NP, vec, 0, 1, 2, mods, 0, 1, ones, r_const,
                  (xsq, r_xsq, rstd, r_rstd, gm, r_gm, gms, r_gms, t32, r_t32), r_x_parts=r_xp)

    tiles = ntiles(NCOL)
    for q in range(NQ):
        for jj in range(FQ):
            j = q * FQ + jj
            if j + 1 < FC:
                load_win(j + 1)
            w = win[j % 2]; rw = r_win[j % 2]
            Uj, rU = U[j % 2], r_U[j % 2]
            Gj, rG = G[j % 2], r_G[j % 2]
            for which in range(2):
                for (s, n) in tiles:
                    pt, rp = pr.next()
                    for k in range(KC):
                        b.op("pe", lambda e, pt=pt, w=w, which=which, k=k, s=s, n=n: e.matmul(
                            pt[:, 0:n], w[:, which, k, :], h[:, k, s:s + n], start=(k == 0), stop=(k == KC - 1)),
                            reads=[rw, r_h[k]], writes=[rp], sig=(k == KC - 1))
                    if which == 0:
                        if s + n <= 2 + NP:
                            b.op("act", lambda e, pt=pt, s=s, n=n, Uj=Uj: e.activation(Uj[:, s:s + n], pt[:, 0:n], AF.Copy),
                                 reads=[rp], writes=[rU])
                        else:
                            npart = 2 + NP - s
                            b.op("act", lambda e, pt=pt, s=s, npart=npart, Uj=Uj: e.activation(
                                Uj[:, s:s + npart], pt[:, 0:npart], AF.Copy), reads=[rp], writes=[rU])
                            b.op("act", lambda e, pt=pt, npart=npart, Uj=Uj: e.activation(
                                Uj[:, 2 + NP:UW].rearrange("p (s c) -> p s c", c=6)[:, :, 2:6],
                                pt[:, npart:npart + NS].rearrange("p (s t) -> p s t", t=4), AF.Copy),
                                reads=[rp], writes=[rU])
                    else:
                        if s == 0:
                            b.op("act", lambda e, pt=pt, n=n, Gj=Gj: e.activation(Gj[:, 0:n - 2], pt[:, 2:n], AF.Copy),
                                 reads=[rp], writes=[rG])
                        else:
                            b.op("act", lambda e, pt=pt, s=s, n=n, Gj=Gj: e.activation(Gj[:, s - 2:s - 2 + n], pt[:, 0:n], AF.Copy),
                                 reads=[rp], writes=[rG])
            b.op("pool", lambda e, Uj=Uj: e.tensor_scalar(Uj[:, 0:2], Uj[:, 0:2], flag[:, 0:1], None, ALU.mult),
                 reads=[rU, r_const], writes=[rU])
            b.op("pool", lambda e, Uj=Uj, j=j: e.tensor_copy(
                Uj[:, 2 + NP:UW].rearrange("p (s c) -> p s c", c=6)[:, :, 0:2], cst[:, j, :, :]),
                reads=[r_const], writes=[rU])
            b.op("dve", lambda e, Uj=Uj, j=j: e.tensor_scalar(A[:, 0:AW], Uj[:, 2:UW], convw[:, j, 2:3], None, ALU.mult),
                 reads=[rU, r_const], writes=[r_A])
            b.op("dve", lambda e, Uj=Uj, j=j: e.scalar_tensor_tensor(A[:, 0:AW], Uj[:, 1:UW - 1], convw[:, j, 1:2], A[:, 0:AW],
                                                                 ALU.mult, ALU.add), reads=[rU, r_A, r_const], writes=[r_A])
            b.op("dve", lambda e, Uj=Uj, j=j: e.scalar_tensor_tensor(A[:, 0:AW], Uj[:, 0:AW], convw[:, j, 0:1], A[:, 0:AW],
                                                                 ALU.mult, ALU.add), reads=[rU, r_A, r_const], writes=[r_A])
            b.op("act", lambda e, j=j: e.activation(A[:, 0:AW], A[:, 0:AW], AF.Gelu, bias=convw[:, j, 3:4], scale=1.0),
                 reads=[r_A, r_const], writes=[r_A])
            b.op("dve", lambda e, jj=jj, Gj=Gj: e.tensor_tensor(act[:, jj, 0:NP], A[:, 0:NP], Gj[:, 0:NP], ALU.mult),
                 reads=[r_A, rG], writes=[r_act])
            b.op("dve", lambda e, jj=jj, Gj=Gj: e.tensor_tensor(
                act[:, jj, NP:NP + NS].rearrange("p (s t) -> p s t", t=4),
                A[:, NP + 2:UW].rearrange("p (s c) -> p s c", c=6)[:, :, 0:4],
                Gj[:, NP:NP + NS].rearrange("p (s t) -> p s t", t=4), ALU.mult),
                reads=[r_A, rG], writes=[r_act])
            b.op("pool", lambda e, Uj=Uj, j=j: e.tensor_copy(cbps[:, j, :], Uj[:, NP:NP + 2]), reads=[rU], writes=[r_cb])
            b.op("pool", lambda e, Uj=Uj, j=j: e.tensor_copy(
                cbss[:, j, :, :], Uj[:, 2 + NP:UW].rearrange("p (s c) -> p s c", c=6)[:, :, 4:6]), reads=[rU], writes=[r_cb])
        load_wout(q, 0)
        for i in range(KC):
            if i + 1 < KC:
                load_wout(q, i + 1)
            n_ = q * KC + i
            w = wout[n_ % 2]; rw = r_wout[n_ % 2]
            for (s, n) in ntiles(NP + NS):
                pt, rp = pr.next()
                for jj in range(FQ):
                    b.op("pe", lambda e, pt=pt, w=w, jj=jj, s=s, n=n: e.matmul(
                        pt[:, 0:n], w[:, jj, :], act[:, jj, s:s + n], start=(jj == 0), stop=(jj == FQ - 1)),
                        reads=[rw, r_act], writes=[rp], sig=(jj == FQ - 1))
                if s + n <= NP:
                    b.op("dve", lambda e, pt=pt, i=i, s=s, n=n: e.scalar_tensor_tensor(
                        x[:, i, 2 + s:2 + s + n], pt[:, 0:n], vec[:, 3, i:i + 1], x[:, i, 2 + s:2 + s + n], ALU.mult, ALU.add),
                        reads=[rp, r_const, r_x], writes=[r_x])
                else:
                    assert s == NP and n == NS
                    b.op("dve", lambda e, pt=pt, i=i: e.tensor_tensor(
                        tmps[:].rearrange("p (s t) -> p s t", t=4), pt[:, 0:NS].rearrange("p (s t) -> p s t", t=4),
                        mods[:, 2, i, :].unsqueeze(2).broadcast_to([128, 16, 4]), ALU.mult),
                        reads=[rp, r_const], writes=[r_tmps])
                    b.op("dve", lambda e, i=i: e.tensor_tensor(x[:, i, 2 + NP:NCOL], x[:, i, 2 + NP:NCOL], tmps[:], ALU.add),
                         reads=[r_tmps, r_x], writes=[r_x])
    outs = []
    if last:
        tl = ntiles(NCOL)
        banks = [pr.next() for _ in tl]
        for k in range(KC):
            i2 = k % 2
            b.op("act", lambda e, k=k, i2=i2: e.activation(xsq[i2][:, 0:NCOL], x[:, k, 0:NCOL], AF.Square),
                 reads=[r_x], writes=[r_xsq[i2]])
            for ti, (s, n) in enumerate(tl):
                pt, rp = banks[ti]
                b.op("pe", lambda e, pt=pt, s=s, n=n, i2=i2, k=k: e.matmul(
                    pt[:, 0:n], ones[:], xsq[i2][:, s:s + n], start=(k == 0), stop=(k == KC - 1)),
                    reads=[r_const, r_xsq[i2]], writes=[rp], sig=True)
        for ti, (s, n) in enumerate(tl):
            pt, rp = banks[ti]
            b.op("act", lambda e, pt=pt, s=s, n=n: e.activation(rstd[:, s:s + n], pt[:, 0:n], AF.Sqrt, bias=EPS, scale=1.0 / D),
                 reads=[rp], writes=[r_rstd])
        b.op("dve", lambda e: e.reciprocal(rstd[:, 0:NCOL], rstd[:, 0:NCOL]), reads=[r_rstd], writes=[r_rstd])
        for k in range(KC):
            b.op("dve", lambda e, k=k: e.scalar_tensor_tensor(
                x[:, k, :], x[:, k, :], vec[:, 4, k:k + 1], rstd[:, 0:NCOL], ALU.mult, ALU.mult),
                reads=[r_x, r_rstd, r_const], writes=[r_x])
    outs.append(b.dma("sp", xo, x[:, :, 2:NCOL], d[6], reads=[r_x]))
    outs.append(b.dma("sp", cbp, cbps[:], d[7], reads=[r_cb]))
    outs.append(b.dma("sp", cbs, cbss[:], d[8], reads=[r_cb]))
    b.wait_all("sp", outs)
    b.emit()
    b.close()
    return nc


NKV = 8
SCALE = 64 ** -0.5
NEG = -30000.0


def alibi_slope(h):
    return float(2.0 ** (-8.0 * (h + 1) / 32))


def build_attn2():
    nc = bass.Bass("TRN2", target_bir_lowering=False)
    NH = 128
    NCOL = NH + NP + NS
    NQC = NP + NS
    NB = 8
    dram = lambda name, shape, kind="ExternalInput": nc.dram_tensor(name, list(shape), F32, kind=kind).ap()
    xT = dram("xT", [128, KC, NCOL])
    vecd = dram("vec", [128, 4, KC])
    modsd = dram("mods", [128, 3, KC, 16])
    wqkv = dram("wqkv", [NKV, 128, 4, KC, 128])
    wo = dram("wo", [KC, 128, KC, 128])
    kcT = dram("kcT", [NKV, 64, 16, 128])
    vc = dram("vc", [NKV, 128, 16, 64])
    ndd = dram("nd", [128, 2, 128]); mkd = dram("mk", [128, 2, 128])
    ndcd = dram("ndc", [128, 4]); mkcd = dram("mkc", [128, 4])
    ndnd = dram("ndn", [64, 64]); mknd = dram("mkn", [64, 64])
    sinkd = dram("sinks", [128, 32])
    hbd = dram("hb", [128, 1])
    xo = dram("xo", [128, KC, NQC], "ExternalOutput")
    kout = dram("kout", [128, NKV // 2, 192], "ExternalOutput")
    vout = dram("vout", [128, 2, NKV, 64], "ExternalOutput")

    b = Builder(nc, n_dsem=16)
    d = b.dsems
    pr = PsumRot(b)
    xbuf = b.sb("xbuf", [128, KC, NCOL], F32)
    x = xbuf
    OT = xbuf[:].rearrange("p k n -> p (k n)").bitcast(BF16)[:, 0:KC * NQC].rearrange("p (k n) -> p k n", k=KC)
    h = b.sb("h", [128, KC, NCOL], BF16)
    rstd = b.sb("rstd", [128, NCOL], F32)
    xsq0 = b.sb("xsq0", [128, NCOL], BF16); xsq = [xsq0, xsq0]
    t32a = b.sb("t32a", [128, NCOL], F32); t32 = [t32a, t32a]
    NWB = 4
    wring = [b.sb("wr%d" % i, [128, KC, 128], BF16) for i in range(NWB)]
    Qgs = [b.sb("Qg%d" % i, [128, 2, NQC], BF16) for i in range(2)]
    Klos = [b.sb("Klo%d" % i, [128, NCOL], BF16) for i in range(2)]
    Khis = [b.sb("Khi%d" % i, [128, NCOL], BF16) for i in range(2)]
    Vds = [b.sb("Vd%d" % i, [128, 10, 64], BF16) for i in range(2)]
    Kclo = b.sb("Kclo", [128, 16, 128], BF16); Kchi = b.sb("Kchi", [128, 16, 128], BF16)
    Vcd = b.sb("Vcd", [128, 16, 64], BF16)
    sc = [b.sb("sc%d" % i, [128, 512], F32) for i in range(2)]
    P = [b.sb("P%d" % i, [128, 512], BF16) for i in range(4)]
    rden = b.sb("rden", [128, 512], F32)
    biasg = b.sb("biasg", [128, 4, 2, 128], F32)
    biasc = b.sb("biasc", [128, 4, 4], F32)
    biasn = b.sb("biasn", [64, 4, 64], F32)
    Pc = b.sb("Pc", [128, 16, 16], BF16)
    Pn = b.sb("Pn", [64, 16, 4, 4], BF16)
    scc = b.sb("scc", [128, 16, 16], F32)
    scn = b.sb("scn", [64, 4, 64], F32)
    vec = b.sb("vecs", [128, 4, KC], F32)
    mods = b.sb("modss", [128, 3, KC, 16], F32)
    nd = b.sb("nds", [128, 2, 128], F32); mk = b.sb("mks", [128, 2, 128], F32)
    ndc = b.sb("ndcs", [128, 4], F32); mkc = b.sb("mkcs", [128, 4], F32)
    ndn = b.sb("ndns", [64, 64], F32); mkn = b.sb("mkns", [64, 64], F32)
    esink = b.sb("esink", [128, 32], F32)
    hb = b.sb("hbs", [128, 1], F32)
    esg = b.sb("esg", [128, 4], F32)
    ones = b.sb("ones", [128, 128], BF16)
    gm = b.sb("gm", [128, KC], F32)
    gms = b.sb("gms", [128, KC, 16], F32)
    koutS = b.sb("koutS", [128, NKV // 2, 192], F32)
    voutS = b.sb("voutS", [128, 2, NKV, 64], F32)
    xr = [t32a[:, 0:NQC], rstd[:, 0:NQC]]
    tmps = b.sb("tmps", [128, NS], F32)

    R = Res
    r_x, r_h, r_rstd, r_const, r_gm, r_gms, r_Q, r_K, r_V, r_Kc, r_Vc, r_rden, r_bias, r_OT = [R(n) for n in
        "x h rstd const gm gms Q K V Kc Vc rden bias OT".split()]
    r_Pc, r_Pn, r_scc, r_scn, r_ko, r_vo, r_tmps = [R(n) for n in "Pc Pn scc scn ko vo tmps".split()]
    r_xsq0 = R("xsq"); r_xsq = [r_xsq0, r_xsq0]; r_t32a = R("t32"); r_t32 = [r_t32a, r_t32a]
    r_wr = [R("wr%d" % i) for i in range(NWB)]
    r_sc = [R("a"), R("b")]; r_P = [R("a") for _ in range(4)]
    r_xr = [r_t32a, r_rstd]
    r_OT = r_x

    b.dma("sp", x[:], xT, d[0], writes=[r_x])
    for i, (dst, src) in enumerate([(vec, vecd), (mods, modsd), (nd, ndd), (mk, mkd), (ndc, ndcd), (mkc, mkcd),
                                    (ndn, ndnd), (mkn, mknd), (esink, sinkd), (hb, hbd)]):
        b.dma("sp", dst[:], src, d[1], writes=[r_const])
    b.op("pool", lambda e: e.memset(ones[:], 1.0), writes=[r_const])
    b.op("pool", lambda e: e.memset(voutS[:], 0.0), writes=[r_vo])
    r_h = [Res("h%d" % k) for k in range(KC)]
    r_Qs = [Res("Q0"), Res("Q1")]; r_Ks = [Res("K0"), Res("K1")]; r_Vs = [Res("V0"), Res("V1")]
    for i_ in range(2):
        b.op("pool", lambda e, i_=i_: e.memset(Klos[i_][:], 0.0), writes=[r_Ks[i_]])
        b.op("pool", lambda e, i_=i_: e.memset(Khis[i_][:], 0.0), writes=[r_Ks[i_]])
    b.op("pool", lambda e: e.memset(Kclo[:], 0.0), writes=[r_Kc])
    b.op("pool", lambda e: e.memset(Kchi[:], 0.0), writes=[r_Kc])
    b.op("act", lambda e: e.activation(esink[:], esink[:], AF.Exp), reads=[r_const], writes=[r_const])

    wsem = [d[2], d[3], d[8], d[9]]
    NWT = NKV * 4 + KC

    def load_w(n):
        if n >= NWT:
            return
        src = wqkv[n // 4][:, n % 4, :, :] if n < NKV * 4 else wo[n - NKV * 4]
        b.dma("pool", wring[n % NWB][:], src, wsem[n % NWB], writes=[r_wr[n % NWB]])

    for n in range(4):
        load_w(n)
    emit_norm_mod(b, pr, x, r_x, h, r_h, NCOL, NH + NP, vec, 0, 1, 2, mods, 0, 1, ones, r_const,
                  (xsq, r_xsq, rstd, r_rstd, gm, r_gm, gms, r_gms, t32, r_t32))

    def proj(g):
        Qg = Qgs[g % 2]; Klo = Klos[g % 2]; Khi = Khis[g % 2]; Vd = Vds[g % 2]
        r_Q = r_Qs[g % 2]; r_K = r_Ks[g % 2]; r_V = r_Vs[g % 2]
        for which in range(2):
            wn = g * 4 + which; w = wring[wn % NWB]; rw = r_wr[wn % NWB]
            for (s, n) in ntiles(NQC):
                yield
                pt, rp = pr.next()
                for k in range(KC):
                    b.op("pe", lambda e, pt=pt, w=w, which=which, k=k, s=s, n=n: e.matmul(
                        pt[:, 0:n], w[:, k, :], h[:, k, NH + s:NH + s + n], start=(k == 0), stop=(k == KC - 1)),
                        reads=[rw, r_h[k]], writes=[rp], sig=(k == KC - 1))
                b.op("act", lambda e, pt=pt, which=which, s=s, n=n: e.activation(Qg[:, which, s:s + n], pt[:, 0:n], AF.Copy),
                     reads=[rp], writes=[r_Q])
            load_w(wn + 4)
        wn = g * 4 + 2; w = wring[wn % NWB]; rw = r_wr[wn % NWB]
        for (s, n) in ntiles(NCOL):
            yield
            pt, rp = pr.next()
            for k in range(KC):
                b.op("pe", lambda e, pt=pt, w=w, k=k, s=s, n=n: e.matmul(
                    pt[:, 0:n], w[:, k, :], h[:, k, s:s + n], start=(k == 0), stop=(k == KC - 1)),
                    reads=[rw, r_h[k]], writes=[rp], sig=(k == KC - 1))
            b.op("act", lambda e, pt=pt, s=s, n=n: e.activation(Klo[0:64, s:s + n], pt[0:64, 0:n], AF.Copy),
                 reads=[rp], writes=[r_K])
            b.op("act", lambda e, pt=pt, s=s, n=n: e.activation(Khi[64:128, s:s + n], pt[64:128, 0:n], AF.Copy),
                 reads=[rp], writes=[r_K])
            if s == 1024:
                lo = 64 * (g % 2)
                if lo == 0:
                    b.op("act", lambda e, pt=pt, g=g, lo=lo: e.activation(koutS[lo:lo + 64, g // 2, :], pt[lo:lo + 64, 0:192], AF.Copy),
                         reads=[rp], writes=[r_ko])
                else:
                    b.op("act", lambda e, pt=pt, g=g, lo=lo: e.activation(koutS[lo:lo + 64, g // 2, :], pt[lo:lo + 64, 0:192], AF.Copy),
                         reads=[rp], writes=[r_ko])
        load_w(wn + 4)
        wn = g * 4 + 3; w = wring[wn % NWB]; rw = r_wr[wn % NWB]
        for blk in range(10):
            m = 128 if blk < 9 else NS
            c0 = blk * 128
            yield
            pt, rp = pr.next()
            for k in range(KC):
                b.op("pe", lambda e, pt=pt, w=w, k=k, c0=c0, m=m: e.matmul(
                    pt[0:m, 0:64], h[:, k, c0:c0 + m], w[:, k, 0:64], start=(k == 0), stop=(k == KC - 1)),
                    reads=[rw, r_h[k]], writes=[rp], sig=(k == KC - 1))
            b.op("dve", lambda e, pt=pt, blk=blk, m=m: e.tensor_copy(Vd[0:m, blk, :], pt[0:m, 0:64]),
                 reads=[rp], writes=[r_V])
            if blk >= 8:
                b.op("dve", lambda e, pt=pt, blk=blk, m=m, g=g: e.tensor_copy(voutS[0:m, blk - 8, g, :], pt[0:m, 0:64]),
                     reads=[rp], writes=[r_vo])
        load_w(wn + 4)
        yield

    def attn(g):
        Qg = Qgs[g % 2]; Klo = Klos[g % 2]; Khi = Khis[g % 2]; Vd = Vds[g % 2]
        r_Q = r_Qs[g % 2]; r_K = r_Ks[g % 2]; r_V = r_Vs[g % 2]
        b.dma("pool", Kclo[0:64, :, :], kcT[g], d[4], writes=[r_Kc])
        b.dma("pool", Kchi[64:128, :, :], kcT[g], d[5], writes=[r_Kc])
        b.dma("pool", Vcd[:], vc[g], d[6], writes=[r_Vc])
        b.op("dve", lambda e, g=g: e.tensor_copy(esg[:], esink[:, 4 * g:4 * g + 4]), reads=[r_const], writes=[r_bias])
        for hq in range(4):
            sl = alibi_slope(4 * g + hq)
            b.op("dve", lambda e, hq=hq, sl=sl: e.scalar_tensor_tensor(biasg[:, hq, :, :], nd[:], sl, mk[:], ALU.mult, ALU.add),
                 reads=[r_const], writes=[r_bias])
            b.op("dve", lambda e, hq=hq, sl=sl: e.scalar_tensor_tensor(biasc[:, hq, :], ndc[:], sl, mkc[:], ALU.mult, ALU.add),
                 reads=[r_const], writes=[r_bias])
            b.op("dve", lambda e, hq=hq, sl=sl: e.scalar_tensor_tensor(biasn[:, hq, :], ndn[:], sl, mkn[:], ALU.mult, ALU.add),
                 reads=[r_const], writes=[r_bias])
        for i in range(1, NB + 1):
            yield
            qc = (i - 1) * 128
            ptd, rpd = pr.next()
            ptv, rpv = pr.next()
            Pp = []
            for pair in range(2):
                pts, rps = pr.next()
                for hh in range(2):
                    hq = pair * 2 + hh
                    Kx = Klo if hh == 0 else Khi
                    for j in range(2):
                        kc0 = (i - 1 + j) * 128
                        b.op("pe", lambda e, pts=pts, hh=hh, j=j, Kx=Kx, kc0=kc0, pair=pair, qc=qc: e.matmul(
                            pts[:, (hh * 2 + j) * 128:(hh * 2 + j + 1) * 128], Kx[:, kc0:kc0 + 128], Qg[:, pair, qc:qc + 128],
                            start=True, stop=True), reads=[r_K, r_Q], writes=[rps], sig=(hh == 1 and j == 1))
                si = pair
                b.op("dve", lambda e, pts=pts, si=si, pair=pair: e.scalar_tensor_tensor(
                    sc[si][:], pts[:, 0:512], SCALE, biasg[:, pair * 2:pair * 2 + 2, :, :].rearrange("p a b c -> p (a b c)"),
                    ALU.mult, ALU.add), reads=[rps, r_bias], writes=[r_sc[si]])
                if i == 1:
                    b.op("dve", lambda e, si=si: e.tensor_scalar(
                        sc[si][:].rearrange("p (a b c) -> p a b c", a=2, b=2)[:, :, 0, :],
                        sc[si][:].rearrange("p (a b c) -> p a b c", a=2, b=2)[:, :, 0, :], hb[:, 0:1], None, ALU.add),
                        reads=[r_sc[si], r_const], writes=[r_sc[si]])
                pi = (i % 2) * 2 + pair
                b.op("act", lambda e, pi=pi, si=si: e.activation(P[pi][:], sc[si][:], AF.Exp),
                     reads=[r_sc[si]], writes=[r_P[pi]])
                Pp.append(pi)
            for pair in range(2):
                pi = Pp[pair]
                for hh in range(2):
                    hq = pair * 2 + hh
                    for j in range(2):
                        b.op("pe", lambda e, pi=pi, hh=hh, j=j, hq=hq: e.matmul(
                            ptd[:, hq * 128:(hq + 1) * 128], ones[:], P[pi][:, (hh * 2 + j) * 128:(hh * 2 + j + 1) * 128],
                            start=(j == 0), stop=(j == 1)), reads=[r_P[pi], r_const], writes=[rpd],
                            sig=(pair == 1 and hh == 1 and j == 1))
            for pair in range(2):
                pi = Pp[pair]
                for hh in range(2):
                    hq = pair * 2 + hh
                    for j in range(2):
                        blk = i - 1 + j
                        b.op("pe", lambda e, pi=pi, hh=hh, j=j, hq=hq, blk=blk: e.matmul(
                            ptv[64 * hh:64 * hh + 64, hq * 128:(hq + 1) * 128], Vd[:, blk, :], P[pi][:, (hh * 2 + j) * 128:(hh * 2 + j + 1) * 128],
                            start=(j == 0), stop=(j == 1)), reads=[r_P[pi], r_V], writes=[rpv],
                            sig=(pair == 1 and hh == 1 and j == 1))
            b.op("dve", lambda e, ptd=ptd, g=g: e.tensor_tensor(
                rden[:].rearrange("p (a q) -> p a q", a=4), ptd[:, 0:512].rearrange("p (a q) -> p a q", a=4),
                esg[:].unsqueeze(2).broadcast_to([128, 4, 128]), ALU.add),
                reads=[rpd, r_bias], writes=[r_rden])
            b.op("act", lambda e: e.activation(rden[:], rden[:], AF.Ln), reads=[r_rden], writes=[r_rden])
            b.op("act", lambda e: e.activation(rden[:], rden[:], AF.Exp, scale=-1.0), reads=[r_rden], writes=[r_rden])
            for hq in range(4):
                lo = 0 if hq % 2 == 0 else 64
                ch = (2 * g + hq // 2)
                b.op("dve", lambda e, ptv=ptv, hq=hq, lo=lo, ch=ch, qc=qc: e.tensor_tensor(
                    OT[lo:lo + 64, ch, qc:qc + 128], ptv[lo:lo + 64, hq * 128:(hq + 1) * 128],
                    rden[lo:lo + 64, hq * 128:(hq + 1) * 128], ALU.mult),
                    reads=[rpv, r_rden, r_x], writes=[r_OT])
        yield
        ptc, rpc = pr.next()
        ptn, rpn = pr.next()
        for sq in range(16):
            for hq in range(4):
                Kx = Kclo if hq % 2 == 0 else Kchi
                b.op("pe", lambda e, sq=sq, hq=hq, Kx=Kx: e.matmul(
                    ptc[:, sq * 16 + hq * 4:sq * 16 + hq * 4 + 4], Kx[:, sq, :], Qg[:, hq // 2, NP + sq * 4:NP + sq * 4 + 4],
                    start=True, stop=True), reads=[r_Kc, r_Q], writes=[rpc], sig=(sq == 15 and hq == 3))
        for hq in range(4):
            Kx = Klo if hq % 2 == 0 else Khi
            b.op("pe", lambda e, hq=hq, Kx=Kx: e.matmul(
                ptn[0:64, hq * 64:(hq + 1) * 64], Kx[:, NH + NP:NCOL], Qg[:, hq // 2, NP:NQC],
                start=True, stop=True), reads=[r_K, r_Q], writes=[rpn], sig=(hq == 3))
        b.op("dve", lambda e: e.scalar_tensor_tensor(
            scc[:].rearrange("p s (a t) -> p s a t", a=4), ptc[:, 0:256].rearrange("p (s a t) -> p s a t", s=16, a=4),
            SCALE, biasc[:].unsqueeze(1).broadcast_to([128, 16, 4, 4]), ALU.mult, ALU.add),
            reads=[rpc, r_bias], writes=[r_scc])
        b.op("act", lambda e: e.activation(Pc[:], scc[:], AF.Exp), reads=[r_scc], writes=[r_Pc])
        b.op("dve", lambda e: e.scalar_tensor_tensor(
            scn[:].rearrange("p a n -> p (a n)"), ptn[0:64, 0:256], SCALE, biasn[:].rearrange("p a n -> p (a n)"),
            ALU.mult, ALU.add), reads=[rpn, r_bias], writes=[r_scn])
        b.op("act", lambda e: e.activation(
            Pn[:].rearrange("p s a t -> p a s t"), scn[:].rearrange("p a (s t) -> p a s t", t=4), AF.Exp),
            reads=[r_scn], writes=[r_Pn])
        ptd, rpd = pr.next()
        ptv, rpv = pr.next()
        b.op("pe", lambda e: e.matmul(ptd[:, 0:256], ones[:], Pc[:].rearrange("p s c -> p (s c)"), start=True, stop=False),
             reads=[r_Pc, r_const], writes=[rpd], sig=False)
        b.op("pe", lambda e: e.matmul(ptd[:, 0:256], ones[0:64, :], Pn[:].rearrange("p s a t -> p (s a t)"), start=False, stop=True),
             reads=[r_Pn, r_const], writes=[rpd])
        for sq in range(16):
            for lo in (0, 64):
                b.op("pe", lambda e, sq=sq, lo=lo: e.matmul(ptv[lo:lo + 64, sq * 16:(sq + 1) * 16], Vcd[:, sq, :], Pc[:, sq, :], start=True, stop=False),
                     reads=[r_Pc, r_Vc], writes=[rpv], sig=False)
                b.op("pe", lambda e, sq=sq, lo=lo: e.matmul(ptv[lo:lo + 64, sq * 16:(sq + 1) * 16], Vd[0:64, 9, :],
                                                     Pn[:, sq, :, :].rearrange("p a t -> p (a t)"), start=False, stop=True),
                     reads=[r_Pn, r_V], writes=[rpv], sig=(sq == 15 and lo == 64))
        b.op("dve", lambda e, g=g: e.tensor_tensor(
            rden[:, 0:256].rearrange("p (s a t) -> p s a t", s=16, a=4), ptd[:, 0:256].rearrange("p (s a t) -> p s a t", s=16, a=4),
            esg[:].unsqueeze(1).unsqueeze(3).broadcast_to([128, 16, 4, 4]), ALU.add),
            reads=[rpd, r_bias], writes=[r_rden])
        b.op("act", lambda e: e.activation(rden[:, 0:256], rden[:, 0:256], AF.Ln), reads=[r_rden], writes=[r_rden])
        b.op("act", lambda e: e.activation(rden[:, 0:256], rden[:, 0:256], AF.Exp, scale=-1.0), reads=[r_rden], writes=[r_rden])
        for hq in range(4):
            lo = 0 if hq % 2 == 0 else 64
            ch = (2 * g + hq // 2)
            b.op("dve", lambda e, hq=hq, lo=lo, ch=ch: e.tensor_tensor(
                OT[lo:lo + 64, ch, NP:NQC].rearrange("p (s t) -> p s t", t=4),
                ptv[lo:lo + 64, 0:256].rearrange("p (s a t) -> p s a t", s=16, a=4)[:, :, hq, :],
                rden[lo:lo + 64, 0:256].rearrange("p (s a t) -> p s a t", s=16, a=4)[:, :, hq, :], ALU.mult),
                reads=[rpv, r_rden, r_x], writes=[r_OT])

        yield

    def drain(gen):
        for _ in gen:
            pass

    drain(proj(0))
    for g in range(NKV):
        crit = attn(g)
        fill = proj(g + 1) if g + 1 < NKV else iter(())
        done_c = done_f = False
        while not (done_c and done_f):
            if not done_c:
                try:
                    next(crit)
                except StopIteration:
                    done_c = True
            for _ in range(2):
                if not done_f:
                    try:
                        next(fill)
                    except StopIteration:
                        done_f = True
    xsem = [d[11], d[12]]
    osem = [d[13], d[14]]
    outs = []
    for i in range(KC):
        wn = NKV * 4 + i; w = wring[wn % NWB]; rw = r_wr[wn % NWB]
        xi = xr[i % 2]; rxi = r_xr[i % 2]
        b.dma("sp", xi, xT[:, i, NH:NCOL], xsem[i % 2], writes=[rxi])
        for (s, n) in ntiles(NQC):
            pt, rp = pr.next()
            for k in range(KC):
                b.op("pe", lambda e, pt=pt, w=w, k=k, s=s, n=n: e.matmul(
                    pt[:, 0:n], w[:, k, :], OT[:, k, s:s + n], start=(k == 0), stop=(k == KC - 1)),
                    reads=[rw, r_OT], writes=[rp], sig=(k == KC - 1))
            if s + n <= NP:
                b.op("dve", lambda e, pt=pt, i=i, s=s, n=n, xi=xi: e.scalar_tensor_tensor(
                    xi[:, s:s + n], pt[:, 0:n], vec[:, 3, i:i + 1], xi[:, s:s + n], ALU.mult, ALU.add),
                    reads=[rp, r_const, rxi], writes=[rxi])
            else:
                b.op("dve", lambda e, pt=pt, i=i: e.tensor_tensor(
                    tmps[:].rearrange("p (s t) -> p s t", t=4), pt[:, 0:NS].rearrange("p (s t) -> p s t", t=4),
                    mods[:, 2, i, :].unsqueeze(2).broadcast_to([128, 16, 4]), ALU.mult),
                    reads=[rp, r_const], writes=[r_tmps])
                b.op("dve", lambda e, xi=xi: e.tensor_tensor(xi[:, NP:NQC], xi[:, NP:NQC], tmps[:], ALU.add),
                     reads=[r_tmps, rxi], writes=[rxi])
        outs.append(b.dma("sp", xo[:, i, :], xi, osem[i % 2], reads=[rxi]))
        load_w(wn + 4)
    outs.append(b.dma("sp", kout, koutS[:], d[15], reads=[r_ko]))
    outs.append(b.dma("sp", vout, voutS[:], d[7], reads=[r_vo]))
    b.wait_all("sp", outs)
    b.emit()
    b.close()
    return nc


def build_adaln():
    nc = bass.Bass("TRN2", target_bir_lowering=False)
    NSEQ = 129
    NCH = 24
    dram = lambda name, shape, kind="ExternalInput": nc.dram_tensor(name, list(shape), F32, kind=kind).ap()
    cT = dram("cT", [128, KC, NSEQ])
    wada = dram("wada", [NCH, 128, KC, 128])
    bada = dram("bada", [128, NCH])
    modT = dram("modT", [128, NCH, NSEQ], "ExternalOutput")
    b = Builder(nc, n_dsem=8)
    d = b.dsems
    pr = PsumRot(b)
    cs = b.sb("cs", [128, KC, NSEQ], F32)
    sc = b.sb("sc", [128, KC, NSEQ], BF16)
    bs = b.sb("bs", [128, NCH], F32)
    outT = b.sb("outT", [128, NCH, NSEQ], F32)
    NWB = 4
    wr = [b.sb("wr%d" % i, [128, KC, 128], BF16) for i in range(NWB)]
    r_c, r_sc, r_b, r_out = Res(), Res(), Res(), Res()
    r_wr = [Res() for _ in range(NWB)]
    b.dma("sp", cs[:], cT, d[0], writes=[r_c])
    b.dma("sp", bs[:], bada, d[1], writes=[r_b])

    def load_w(n):
        if n < NCH:
            b.dma("pool", wr[n % NWB][:], wada[n], d[2 + n % NWB], writes=[r_wr[n % NWB]])
    for n in range(NWB - 1):
        load_w(n)
    b.op("act", lambda e: e.activation(sc[:], cs[:], AF.Silu), reads=[r_c], writes=[r_sc])
    for n in range(NCH):
        load_w(n + NWB - 1)
        w, rw = wr[n % NWB], r_wr[n % NWB]
        pt, rp = pr.next()
        for k in range(KC):
            b.op("pe", lambda e, pt=pt, w=w, k=k: e.matmul(pt[:, 0:NSEQ], w[:, k, :], sc[:, k, :], start=(k == 0), stop=(k == KC - 1)),
                 reads=[rw, r_sc], writes=[rp], sig=(k == KC - 1))
        b.op("act", lambda e, pt=pt, n=n: e.activation(outT[:, n, :], pt[:, 0:NSEQ], AF.Identity, bias=bs[:, n:n + 1], scale=1.0),
             reads=[rp, r_b], writes=[r_out])
    t = b.dma("sp", modT, outT[:], d[6], reads=[r_out])
    b.wait_all("sp", [t])
    b.emit(); b.close()
    return nc


NHH = 16
CH = 32
NRK = 7


def cumsum_chunks(b, engs, bufA, rA, bufB, rB, ncol0, ncols, clen):
    src, rs, dst, rd = bufA, rA, bufB, rB
    s = 1
    i = 0
    while s < clen:
        sv = src[:, ncol0:ncol0 + ncols].rearrange("p (c t) -> p c t", t=clen)
        dv = dst[:, ncol0:ncol0 + ncols].rearrange("p (c t) -> p c t", t=clen)
        eng = engs[i % len(engs)]
        b.op(eng, lambda e, dv=dv, sv=sv, s=s: e.tensor_tensor(dv[:, :, s:clen], sv[:, :, s:clen], sv[:, :, 0:clen - s], ALU.add),
             reads=[rs], writes=[rd])
        b.op(eng, lambda e, dv=dv, sv=sv, s=s: e.tensor_copy(dv[:, :, 0:s], sv[:, :, 0:s]), reads=[rs], writes=[rd])
        src, rs, dst, rd = dst, rd, src, rs
        s *= 2
        i += 1
    return src, rs


def build_hgrn(pass1, modeA=False):
    nc = bass.Bass("TRN2", target_bir_lowering=False)
    NT = NP if pass1 else NP + NS
    NBLK = NP // 128
    NCK = NP // CH
    dram = lambda name, shape, kind="ExternalInput": nc.dram_tensor(name, list(shape), F32, kind=kind).ap()
    xT = dram("xT", [128, KC, NT])
    vecd = dram("vec", [128, 4, KC])
    modsd = dram("mods", [128, 3, KC, 16])
    whg = dram("whg", [NHH, 128, 4, KC, 128])
    lbpd = dram("lbp", [128, 2, NHH])
    m01d = dram("m01", [128, 128]); cmd = dram("cm", [128, 4, 128])
    identd = dram("ident", [128, 128])
    if not pass1:
        if not modeA:
            wo = dram("wo", [KC, 128, KC, 128])
            ngd = dram("ng", [128, NHH])
            srd = dram("sr", [NHH, 128, NRK, 128])
            drd = dram("dr", [128, NHH, NRK])
            xo = dram("xo", [128, KC, NT], "ExternalOutput")
        else:
            oloc = dram("oloc", [NHH, 128, NT], "ExternalOutput")
            qbo = dram("qbo", [NHH, 128, NT], "ExternalOutput")
            sgo = dram("sgo", [NHH, 128, NT], "ExternalOutput")
        s0d = dram("s0", [NHH, 128, 16, 128])
        msd = dram("ms", [64, 64]); cmsd = dram("cms", [128, 16, 64])
        snew = dram("snew", [NHH, 128, 16, 128], "ExternalOutput")
    send = dram("send", [128, NHH, 128], "ExternalOutput")
    dout = dram("dout", [128, NHH], "ExternalOutput")

    b = Builder(nc, n_dsem=18)
    d = b.dsems
    pr = PsumRot(b)
    xbuf = b.sb("xbuf", [128, KC, NT], F32)
    x = xbuf
    O2T = xbuf[:].rearrange("p k n -> p (k n)").bitcast(BF16)[:, 0:KC * NT].rearrange("p (k n) -> p k n", k=KC)
    h = b.sb("h", [128, KC, NT], BF16)
    rstd = b.sb("rstd", [128, NT], F32)
    xsq0 = b.sb("xsq0", [128, NT], BF16)
    t32a = b.sb("t32a", [128, NT], F32)
    NWB = 5
    wring = [b.sb("wr%d" % i, [128, KC, 128], BF16) for i in range(NWB)]
    qf = b.sb("qf", [128, NT], F32); kf = b.sb("kf", [128, NT], F32)
    bA = b.sb("bA", [128, NT], F32); bB = b.sb("bB", [128, NT], F32)
    sg = b.sb("sg", [128, NT], F32); Oraw = b.sb("Oraw", [128, NT], F32)
    qt = b.sb("qt", [128, NT], BF16); kt = b.sb("kt", [128, NT], BF16); kh = b.sb("kh", [128, NT], BF16)
    dec = b.sb("dec", [128, NCK + 16], F32)
    Vt = b.sb("Vt", [128, NBLK + 1, 128], BF16)
    At = b.sb("At", [128, 128], BF16)
    Khm = b.sb("Khm", [128, 4, 128], BF16)
    KhmT = b.sb("KhmT", [128, 4, 128], BF16)
    S = b.sb("S", [128, 128], F32); Sbf = b.sb("Sbf", [128, 128], BF16)
    Dall = b.sb("Dall", [128, NHH], F32)
    btot = b.sb("btot", [128, 1], F32)
    vec = b.sb("vecs", [128, 4, KC], F32)
    mods = b.sb("modss", [128, 3, KC, 16], F32)
    lbp = b.sb("lbps", [128, 2, NHH], F32)
    oml = b.sb("oml", [128, NHH], F32)
    m01 = b.sb("m01s", [128, 128], F32); cm = b.sb("cms_", [128, 4, 128], F32)
    ident = b.sb("idents", [128, 128], BF16); identf = b.sb("identf", [128, 128], F32)
    ones = b.sb("ones", [128, 128], BF16)
    gm = b.sb("gm", [128, KC], F32); gms = b.sb("gms", [128, KC, 16], F32)
    if not pass1:
        if not modeA:
            ng = b.sb("ngs", [128, NHH], F32)
            sr = b.sb("srs", [128, NRK, 128], F32)
            dr = b.sb("drs", [128, NHH, NRK], F32)
        else:
            QBf = b.sb("QBf", [128, NT], F32)
            pbA = b.sb("pbA", [128, NCK], F32); pbB = b.sb("pbB", [128, NCK], F32)
            r_QBf, r_pbA, r_pbB = Res("QBf"), Res("pbA"), Res("pbB")
        S0 = b.sb("S0", [128, 16, 128], F32); S0b = b.sb("S0b", [128, 16, 128], BF16)
        ms = b.sb("mss", [64, 64], F32); cms = b.sb("cmss", [128, 16, 64], F32)
        Ats = b.sb("Ats", [64, 64], BF16)
        Khms = b.sb("Khms", [128, 16, 64], BF16)
        KhmTs = b.sb("KhmTs", [64, 16, 128], BF16)
        tmps = b.sb("tmps", [128, NS], F32)
    R = Res
    r_x, r_h, r_rstd, r_const, r_gm, r_gms = [R(n) for n in "x h rstd const gm gms".split()]
    r_xsq0, r_t32a = R("xsq"), R("t32")
    r_wr = [R("wr%d" % i) for i in range(NWB)]
    r_qf, r_kf, r_bA, r_bB, r_sg, r_Oraw, r_qt, r_kt, r_kh, r_dec, r_Vt, r_At, r_Khm, r_KhmT, r_S, r_Sbf, r_Sall, r_bt = [
        R(n) for n in "qf kf bA bB sg Oraw qt kt kh dec Vt At Khm KhmT S Sbf Sall bt".split()]
    r_sr, r_S0, r_S0b, r_Ats, r_Khms, r_KhmTs, r_tmps = [R(n) for n in "sr S0 S0b Ats Khms KhmTs tmps".split()]
    r_O2T = r_x

    b.dma("sp", x[:], xT, d[0], writes=[r_x])
    cl = [(vec, vecd), (mods, modsd), (lbp, lbpd), (m01, m01d), (cm, cmd), (identf, identd)]
    if not pass1:
        cl += [(ms, msd), (cms, cmsd)] + ([] if modeA else [(ng, ngd), (dr, drd)])
    for dst, src in cl:
        b.dma("sp", dst[:], src, d[1], writes=[r_const])
    b.op("pool", lambda e: e.memset(ones[:], 1.0), writes=[r_const])
    b.op("act", lambda e: e.activation(ident[:], identf[:], AF.Copy), reads=[r_const], writes=[r_const])
    b.op("dve", lambda e: e.tensor_tensor(oml[:], lbp[:, 0, :], lbp[:, 1, :], ALU.subtract), reads=[r_const], writes=[r_const])
    b.op("act", lambda e: e.activation(oml[:], oml[:], AF.Sigmoid), reads=[r_const], writes=[r_const])

    wsem = [d[2], d[3], d[4], d[5], d[6]]
    NWT = NHH * 4 + (0 if (pass1 or modeA) else KC)
    used = [1, 2] if pass1 else [0, 1, 2, 3]
    wlist = [(hh, wh) for hh in range(NHH) for wh in used] + ([("o", i) for i in range(KC)] if not (pass1 or modeA) else [])

    def load_w(n):
        if n >= len(wlist):
            return
        a, c = wlist[n]
        src = wo[c] if a == "o" else whg[a][:, c, :, :]
        b.dma("pool", wring[n % NWB][:], src, wsem[n % NWB], writes=[r_wr[n % NWB]])
    for n in range(4):
        load_w(n)
    wcount = [0]

    def next_w():
        n = wcount[0]
        wcount[0] += 1
        return n, wring[n % NWB], r_wr[n % NWB]

    emit_norm_mod(b, pr, x, r_x, h, r_h, NT, NP, vec, 0, 1, 2, mods, 0, 1, ones, r_const,
                  ([xsq0, xsq0], [r_xsq0, r_xsq0], rstd, r_rstd, gm, r_gm, gms, r_gms, [t32a, t32a], [r_t32a, r_t32a]))

    def proj_fm(dst_fn):
        n_, w, rw = next_w()
        for (s, n) in ntiles(NT):
            pt, rp = pr.next()
            for k in range(KC):
                b.op("pe", lambda e, pt=pt, w=w, k=k, s=s, n=n: e.matmul(
                    pt[:, 0:n], w[:, k, :], h[:, k, s:s + n], start=(k == 0), stop=(k == KC - 1)),
                    reads=[rw, r_h], writes=[rp], sig=(k == KC - 1))
            dst_fn(pt, rp, s, n)
        load_w(n_ + 4)

    outs = []
    for hh in range(NHH):
        if not pass1:
            if not modeA:
                b.dma("sp", sr[:], srd[hh], d[7], writes=[r_sr])
            b.dma("sp", S0[:], s0d[hh], d[8], writes=[r_S0])
            b.dma("pool", S0b[:], s0d[hh], d[9], writes=[r_S0b])
            proj_fm(lambda pt, rp, s, n: b.op("act", lambda e: e.activation(qf[:, s:s + n], pt[:, 0:n], AF.Silu),
                                              reads=[rp], writes=[r_qf]))
        proj_fm(lambda pt, rp, s, n: b.op("act", lambda e: e.activation(kf[:, s:s + n], pt[:, 0:n], AF.Sigmoid, scale=-1.0),
                                          reads=[rp], writes=[r_kf]))
        b.op("dve", lambda e, hh=hh: e.tensor_scalar(kf[:], kf[:], oml[:, hh:hh + 1], None, ALU.mult),
             reads=[r_kf, r_const], writes=[r_kf])
        b.op("act", lambda e: e.activation(bA[:], kf[:], AF.Ln, bias=1.0, scale=-1.0), reads=[r_kf], writes=[r_bA])
        n_, w, rw = next_w()
        for blk in range(NBLK + (0 if pass1 else 1)):
            m = 128 if blk < NBLK else NS
            c0 = blk * 128
            pt, rp = pr.next()
            for k in range(KC):
                b.op("pe", lambda e, pt=pt, w=w, k=k, c0=c0, m=m: e.matmul(
                    pt[0:m, 0:128], h[:, k, c0:c0 + m], w[:, k, :], start=(k == 0), stop=(k == KC - 1)),
                    reads=[rw, r_h], writes=[rp], sig=(k == KC - 1))
            b.op("act", lambda e, pt=pt, blk=blk, m=m: e.activation(Vt[0:m, blk, :], pt[0:m, 0:128], AF.Copy),
                 reads=[rp], writes=[r_Vt])
        load_w(n_ + 4)
        if not pass1:
            proj_fm(lambda pt, rp, s, n: b.op("act", lambda e: e.activation(sg[:, s:s + n], pt[:, 0:n], AF.Silu),
                                              reads=[rp], writes=[r_sg]))
        bb, rbb = cumsum_chunks(b, ["dve", "pool"], bA, r_bA, bB, r_bB, 0, NP, CH)
        other, rother = (bB, r_bB) if bb is bA else (bA, r_bA)
        if not pass1:
            if bb is not bA:
                b.op("pool", lambda e: e.tensor_copy(bB[:, NP:NT], bA[:, NP:NT]), reads=[r_bA], writes=[r_bB])
            sb_, rsb = cumsum_chunks(b, ["pool"], bb, rbb, other, rother, NP, NS, 4)
            if sb_ is not bb:
                b.op("pool", lambda e, sb_=sb_, bb=bb: e.tensor_copy(bb[:, NP:NT], sb_[:, NP:NT]), reads=[rsb], writes=[rbb])
        bv = bb[:, 0:NP].rearrange("p (c t) -> p c t", t=CH)
        ov = other[:, 0:NP].rearrange("p (c t) -> p c t", t=CH)
        b.op("act", lambda e, bv=bv: e.activation(dec[:, 0:NCK].unsqueeze(2), bv[:, :, CH - 1:CH], AF.Exp), reads=[rbb], writes=[r_dec])
        b.op("dve", lambda e, bv=bv, ov=ov: e.tensor_tensor(ov, bv[:, :, CH - 1:CH].broadcast_to([128, NCK, CH]), bv, ALU.subtract),
             reads=[rbb], writes=[rother])
        if not pass1:
            bs_ = bb[:, NP:NT].rearrange("p (c t) -> p c t", t=4)
            os_ = other[:, NP:NT].rearrange("p (c t) -> p c t", t=4)
            b.op("act", lambda e, bs_=bs_: e.activation(dec[:, NCK:NCK + 16].unsqueeze(2), bs_[:, :, 3:4], AF.Exp), reads=[rbb], writes=[r_dec])
            b.op("dve", lambda e, bs_=bs_, os_=os_: e.tensor_tensor(os_, bs_[:, :, 3:4].broadcast_to([128, 16, 4]), bs_, ALU.subtract),
                 reads=[rbb], writes=[rother])
        b.op("act", lambda e, other=other: e.activation(other[:, 0:NT], other[:, 0:NT], AF.Exp), reads=[rother], writes=[rother])
        b.op("dve", lambda e, other=other: e.tensor_tensor(kh[:], kf[:], other[:, 0:NT], ALU.mult), reads=[rother, r_kf], writes=[r_kh])
        if pass1:
            b.op("dve", lambda e, bv=bv: e.tensor_reduce(btot[:], bv[:, :, CH - 1], AX.X, ALU.add), reads=[rbb], writes=[r_bt])
            b.op("act", lambda e, hh=hh: e.activation(Dall[:, hh:hh + 1], btot[:], AF.Exp), reads=[r_bt], writes=[r_Sall])
        else:
            b.op("act", lambda e, other=other, bb=bb: e.activation(other[:, 0:NT], bb[:, 0:NT], AF.Exp), reads=[rbb, r_kh], writes=[rother])
            b.op("dve", lambda e, other=other: e.tensor_tensor(qt[:], qf[:], other[:, 0:NT], ALU.mult), reads=[rother, r_qf], writes=[r_qt])
            b.op("act", lambda e, other=other, bb=bb: e.activation(other[:, 0:NT], bb[:, 0:NT], AF.Exp, scale=-1.0), reads=[rbb, r_qt], writes=[rother])
            b.op("dve", lambda e, other=other: e.tensor_tensor(kt[:], kf[:], other[:, 0:NT], ALU.mult), reads=[rother, r_kf], writes=[r_kt])
            if modeA:
                b.op("pool", lambda e, bv=bv: e.tensor_copy(pbA[:].unsqueeze(2), bv[:, :, CH - 1:CH]), reads=[rbb], writes=[r_pbA])
                pin, rpin = cumsum_chunks(b, ["pool"], pbA, r_pbA, pbB, r_pbB, 0, NCK, NCK)
                pex, rpex = (pbB, r_pbB) if pin is pbA else (pbA, r_pbA)
                b.op("act", lambda e, hh=hh, pin=pin: e.activation(Dall[:, hh:hh + 1], pin[:, NCK - 1:NCK], AF.Exp), reads=[rpin], writes=[r_Sall])
                b.op("pool", lambda e, pin=pin, pex=pex, bv=bv: e.tensor_tensor(pex[:].unsqueeze(2), pin[:].unsqueeze(2), bv[:, :, CH - 1:CH], ALU.subtract),
                     reads=[rpin, rbb], writes=[rpex])
                b.op("act", lambda e, pex=pex: e.activation(pex[:], pex[:], AF.Exp), reads=[rpex], writes=[rpex])
                b.op("pool", lambda e, pex=pex: e.tensor_tensor(
                    QBf[:, 0:NP].rearrange("p (c t) -> p c t", t=CH), qt[:, 0:NP].rearrange("p (c t) -> p c t", t=CH),
                    pex[:].unsqueeze(2).broadcast_to([128, NCK, CH]), ALU.mult), reads=[rpex, r_qt], writes=[r_QBf])
                b.op("pool", lambda e: e.memset(QBf[:, NP:NT], 0.0), writes=[r_QBf])
                outs.append(b.dma("sp", qbo[hh], QBf[:], d[7], reads=[r_QBf]))
                outs.append(b.dma("sp", sgo[hh], sg[:], d[13], reads=[r_sg]))
        if pass1 or modeA:
            b.op("pool", lambda e: e.memset(S[:], 0.0), writes=[r_S])
            if modeA:
                b.op("pool", lambda e: e.memset(Sbf[:], 0.0), writes=[r_Sbf])
        else:
            b.op("dve", lambda e, hh=hh: e.tensor_scalar(S[:], sr[:, 0, :], 1.0, None, ALU.mult), reads=[r_sr], writes=[r_S])
            for r in range(1, NRK):
                b.op("dve", lambda e, hh=hh, r=r: e.scalar_tensor_tensor(S[:], S[:], dr[:, hh, r:r + 1], sr[:, r, :], ALU.mult, ALU.add),
                     reads=[r_S, r_sr, r_const], writes=[r_S])
            b.op("act", lambda e: e.activation(Sbf[:], S[:], AF.Copy), reads=[r_S], writes=[r_Sbf])
        for blk in range(NBLK):
            c0 = blk * 128
            b.op("dve", lambda e, c0=c0: e.tensor_tensor(Khm[:], kh[:, c0:c0 + 128].unsqueeze(1).broadcast_to([128, 4, 128]), cm[:], ALU.mult),
                 reads=[r_kh, r_const], writes=[r_Khm])
            ptT, rpT = pr.next()
            ptTb = ptT[:, :].bitcast(BF16)
            for c in range(4):
                b.op("pe", lambda e, ptTb=ptTb, c=c: e.transpose(ptTb[:, c * 128:(c + 1) * 128], Khm[:, c, :], ident[:]),
                     reads=[r_Khm, r_const], writes=[rpT], sig=(c == 3))
            b.op("act", lambda e, ptTb=ptTb: e.activation(KhmT[:].rearrange("p c k -> p (c k)"), ptTb[:, 0:512], AF.Copy),
                 reads=[rpT], writes=[r_KhmT])
            if not pass1:
                pa, rpa = pr.next()
                b.op("pe", lambda e, pa=pa, c0=c0: e.matmul(pa[:, 0:128], kt[:, c0:c0 + 128], qt[:, c0:c0 + 128], start=True, stop=True),
                     reads=[r_kt, r_qt], writes=[rpa])
                b.op("dve", lambda e, pa=pa: e.tensor_tensor(At[:], pa[:, 0:128], m01[:], ALU.mult), reads=[rpa, r_const], writes=[r_At])
                po, rpo = pr.next()
                b.op("pe", lambda e, po=po, blk=blk: e.matmul(po[:, 0:128], Vt[:, blk, :], At[:], start=True, stop=False),
                     reads=[r_Vt, r_At], writes=[rpo], sig=False)
            for c in range(4):
                ck = blk * 4 + c
                if not pass1:
                    b.op("pe", lambda e, po=po, c=c, c0=c0: e.matmul(po[:, c * CH:(c + 1) * CH], Sbf[:], qt[:, c0 + c * CH:c0 + (c + 1) * CH],
                                                                   start=False, stop=(c == 3)),
                         reads=[r_Sbf, r_qt], writes=[rpo], sig=True)
                pu, rpu = pr.next()
                b.op("pe", lambda e, pu=pu, c=c, blk=blk: e.matmul(pu[:, 0:128], KhmT[:, c, :], Vt[:, blk, :], start=True, stop=True),
                     reads=[r_KhmT, r_Vt], writes=[rpu])
                b.op("dve", lambda e, pu=pu, ck=ck: e.scalar_tensor_tensor(S[:], S[:], dec[:, ck:ck + 1], pu[:, 0:128], ALU.mult, ALU.add),
                     reads=[rpu, r_S, r_dec], writes=[r_S])
                if not pass1:
                    b.op("act", lambda e: e.activation(Sbf[:], S[:], AF.Copy), reads=[r_S], writes=[r_Sbf])
            if not pass1:
                b.op("act", lambda e, po=po, c0=c0: e.activation(Oraw[:, c0:c0 + 128], po[:, 0:128], AF.Copy), reads=[rpo], writes=[r_Oraw])
        outs.append(b.dma("sp", send[:, hh, :], S[:], d[11], reads=[r_S]))
        if pass1:
            continue
        b.op("dve", lambda e: e.tensor_tensor(Khms[:], kh[:, NP:NT].unsqueeze(1).broadcast_to([128, 16, 64]), cms[:], ALU.mult),
             reads=[r_kh, r_const], writes=[r_Khms])
        for half in range(2):
            ptT, rpT = pr.next()
            ptTb = ptT[:, :].bitcast(BF16)
            for s8 in range(8):
                sq = half * 8 + s8
                b.op("pe", lambda e, ptTb=ptTb, s8=s8, sq=sq: e.transpose(ptTb[0:64, s8 * 128:(s8 + 1) * 128], Khms[:, sq, :], ident[:]),
                     reads=[r_Khms, r_const], writes=[rpT], sig=(s8 == 7))
            b.op("act", lambda e, ptTb=ptTb, half=half: e.activation(
                KhmTs[:, half * 8:half * 8 + 8, :].rearrange("p c k -> p (c k)"), ptTb[0:64, 0:1024], AF.Copy),
                reads=[rpT], writes=[r_KhmTs])
        pa, rpa = pr.next()
        b.op("pe", lambda e, pa=pa: e.matmul(pa[0:64, 0:64], kt[:, NP:NT], qt[:, NP:NT], start=True, stop=True),
             reads=[r_kt, r_qt], writes=[rpa])
        b.op("dve", lambda e, pa=pa: e.tensor_tensor(Ats[:], pa[0:64, 0:64], ms[:], ALU.mult), reads=[rpa, r_const], writes=[r_Ats])
        po, rpo = pr.next()
        b.op("pe", lambda e, po=po: e.matmul(po[:, 0:64], Vt[0:64, NBLK, :], Ats[:], start=True, stop=False),
             reads=[r_Vt, r_Ats], writes=[rpo], sig=False)
        for sq in range(16):
            b.op("pe", lambda e, po=po, sq=sq: e.matmul(po[:, sq * 4:sq * 4 + 4], S0b[:, sq, :], qt[:, NP + sq * 4:NP + sq * 4 + 4],
                                                       start=False, stop=(sq == 15)),
                 reads=[r_S0b, r_qt], writes=[rpo], sig=(sq == 15))
        b.op("act", lambda e, po=po: e.activation(Oraw[:, NP:NT], po[:, 0:64], AF.Copy), reads=[rpo], writes=[r_Oraw])
        for q4 in range(4):
            pu, rpu = pr.next()
            for s4 in range(4):
                sq = q4 * 4 + s4
                b.op("pe", lambda e, pu=pu, s4=s4, sq=sq: e.matmul(pu[:, s4 * 128:(s4 + 1) * 128], KhmTs[:, sq, :], Vt[0:64, NBLK, :],
                                                                 start=True, stop=True),
                     reads=[r_KhmTs, r_Vt], writes=[rpu], sig=(s4 == 3))
            for s4 in range(4):
                sq = q4 * 4 + s4
                b.op("dve", lambda e, pu=pu, s4=s4, sq=sq: e.scalar_tensor_tensor(
                    S0[:, sq, :], S0[:, sq, :], dec[:, NCK + sq:NCK + sq + 1], pu[:, s4 * 128:(s4 + 1) * 128], ALU.mult, ALU.add),
                    reads=[rpu, r_S0, r_dec], writes=[r_S0])
        outs.append(b.dma("sp", snew[hh], S0[:], d[10], reads=[r_S0]))
        if modeA:
            outs.append(b.dma("sp", oloc[hh], Oraw[:], d[14], reads=[r_Oraw]))
            continue
        b.op("act", lambda e: e.activation(xsq0[:], Oraw[:], AF.Square), reads=[r_Oraw], writes=[r_xsq0])
        tl = ntiles(NT)
        bk = [pr.next() for _ in tl]
        for ti, (s, n) in enumerate(tl):
            pt, rp = bk[ti]
            b.op("pe", lambda e, pt=pt, s=s, n=n: e.matmul(pt[:, 0:n], ones[:], xsq0[:, s:s + n], start=True, stop=True),
                 reads=[r_xsq0, r_const], writes=[rp])
            b.op("act", lambda e, pt=pt, s=s, n=n: e.activation(rstd[:, s:s + n], pt[:, 0:n], AF.Sqrt, bias=EPS, scale=1.0 / 128),
                 reads=[rp], writes=[r_rstd])
        b.op("dve", lambda e: e.reciprocal(rstd[:], rstd[:]), reads=[r_rstd], writes=[r_rstd])
        b.op("dve", lambda e, hh=hh: e.scalar_tensor_tensor(Oraw[:], Oraw[:], ng[:, hh:hh + 1], rstd[:], ALU.mult, ALU.mult),
             reads=[r_Oraw, r_rstd, r_const], writes=[r_Oraw])
        b.op("dve", lambda e, hh=hh: e.tensor_tensor(O2T[:, hh, :], Oraw[:], sg[:], ALU.mult), reads=[r_Oraw, r_sg, r_x], writes=[r_O2T])

    if pass1 or modeA:
        outs.append(b.dma("sp", dout, Dall[:], d[12], reads=[r_Sall]))
    else:
        b.op("pool", lambda e: e.memset(Dall[:], 0.0), writes=[r_Sall])
        outs.append(b.dma("sp", dout, Dall[:], d[12], reads=[r_Sall]))
        xr = [t32a, rstd]; r_xr = [r_t32a, r_rstd]
        xsem = [d[13], d[14]]; osem = [d[15], d[16]]
        for i in range(KC):
            n_, w, rw = next_w()
            xi = xr[i % 2]; rxi = r_xr[i % 2]
            b.dma("sp", xi[:], xT[:, i, :], xsem[i % 2], writes=[rxi])
            for (s, n) in ntiles(NT):
                pt, rp = pr.next()
                for k in range(KC):
                    b.op("pe", lambda e, pt=pt, w=w, k=k, s=s, n=n: e.matmul(
                        pt[:, 0:n], w[:, k, :], O2T[:, k, s:s + n], start=(k == 0), stop=(k == KC - 1)),
                        reads=[rw, r_O2T], writes=[rp], sig=(k == KC - 1))
                if s + n <= NP:
                    b.op("dve", lambda e, pt=pt, i=i, s=s, n=n, xi=xi: e.scalar_tensor_tensor(
                        xi[:, s:s + n], pt[:, 0:n], vec[:, 3, i:i + 1], xi[:, s:s + n], ALU.mult, ALU.add),
                        reads=[rp, r_const, rxi], writes=[rxi])
                else:
                    b.op("dve", lambda e, pt=pt, i=i: e.tensor_tensor(
                        tmps[:].rearrange("p (s t) -> p s t", t=4), pt[:, 0:NS].rearrange("p (s t) -> p s t", t=4),
                        mods[:, 2, i, :].unsqueeze(2).broadcast_to([128, 16, 4]), ALU.mult),
                        reads=[rp, r_const], writes=[r_tmps])
                    b.op("dve", lambda e, xi=xi: e.tensor_tensor(xi[:, NP:NT], xi[:, NP:NT], tmps[:], ALU.add),
                         reads=[r_tmps, rxi], writes=[rxi])
            outs.append(b.dma("sp", xo[:, i, :], xi[:], osem[i % 2], reads=[rxi]))
            load_w(n_ + 4)
    b.wait_all("sp", outs)
    b.emit(); b.close()
    return nc


def build_hgrnb():
    nc = bass.Bass("TRN2", target_bir_lowering=False)
    NT = NP + NS
    dram = lambda name, shape, kind="ExternalInput": nc.dram_tensor(name, list(shape), F32, kind=kind).ap()
    xT = dram("xT", [128, KC, NT])
    vecd = dram("vec", [128, 4, KC])
    modsd = dram("mods", [128, 3, KC, 16])
    olocd = dram("oloc", [NHH, 128, NT]); qbd = dram("qb", [NHH, 128, NT]); sgd = dram("sg", [NHH, 128, NT])
    srd = dram("sr", [NHH, 128, NRK, 128]); drd = dram("dr", [128, NHH, NRK])
    slocd = dram("sloc", [128, NHH, 128]); dld = dram("dl", [128, NHH])
    ngd = dram("ng", [128, NHH])
    wo = dram("wo", [KC, 128, KC, 128])
    xo = dram("xo", [128, KC, NT], "ExternalOutput")
    send = dram("send", [128, NHH, 128], "ExternalOutput")

    b = Builder(nc, n_dsem=18)
    d = b.dsems
    pr = PsumRot(b)
    O2T = b.sb("O2T", [128, KC, NT], BF16)
    ol = [b.sb("ol%d" % i, [128, NT], F32) for i in range(2)]
    qbb = [b.sb("qbb%d" % i, [128, NT], BF16) for i in range(2)]
    sgl = [b.sb("sgl%d" % i, [128, NT], F32) for i in range(2)]
    srall = b.sb("srall", [128, NHH, NRK, 128], F32)
    Sall = b.sb("Sall", [128, NHH, 128], F32)
    Oraws = [b.sb("Oraw%d" % i, [128, NT], F32) for i in range(2)]
    xsqs = [b.sb("xsq%d" % i, [128, NT], BF16) for i in range(2)]
    rstds = [b.sb("rstd%d" % i, [128, NT], F32) for i in range(2)]
    Ss = [b.sb("S%d" % i, [128, 128], F32) for i in range(2)]; Sbfs = [b.sb("Sbf%d" % i, [128, 128], BF16) for i in range(2)]
    Se = b.sb("Se", [128, NHH, 128], F32)
    sloc = b.sb("slocs", [128, NHH, 128], F32)
    dr = b.sb("drs", [128, NHH, NRK], F32); dl = b.sb("dls", [128, NHH], F32); ng = b.sb("ngs", [128, NHH], F32)
    vec = b.sb("vecs", [128, 4, KC], F32); mods = b.sb("modss", [128, 3, KC, 16], F32)
    ones = b.sb("ones", [128, 128], BF16)
    NWB = 4
    wring = [b.sb("wr%d" % i, [128, KC, 128], BF16) for i in range(NWB)]
    xr = [b.sb("xr%d" % i, [128, NT], F32) for i in range(2)]
    tmps = b.sb("tmps", [128, NS], F32)
    R = Res
    r_O2T, r_Se, r_const, r_tmps = [R(n) for n in "O2T Se const tmps".split()]
    r_Oraws = [R("a"), R("b")]; r_xsqs = [R("a"), R("b")]; r_rstds = [R("a"), R("b")]; r_Ss = [R("a"), R("b")]; r_Sbfs = [R("a"), R("b")]
    r_ol = [R("a"), R("b")]; r_qbb = [R("a"), R("b")]; r_sgl = [R("a"), R("b")]; r_srall = R("srall"); r_Sall = R("Sall")
    r_wr = [R("w%d" % i) for i in range(NWB)]; r_xr = [R("a"), R("b")]
    for dst, src in [(vec, vecd), (mods, modsd), (sloc, slocd), (dr, drd), (dl, dld), (ng, ngd)]:
        b.dma("sp", dst[:], src, d[0], writes=[r_const])
    b.op("pool", lambda e: e.memset(ones[:], 1.0), writes=[r_const])

    def load_w(i):
        if i < KC:
            b.dma("pool", wring[i % NWB][:], wo[i], d[1 + i % NWB], writes=[r_wr[i % NWB]])

    def load_head(hh):
        if hh >= NHH:
            return
        i = hh % 2
        b.dma("sp", ol[i][:], olocd[hh], d[5 + i], writes=[r_ol[i]])
        b.dma("pool", qbb[i][:], qbd[hh], d[7 + i], writes=[r_qbb[i]])
        b.dma("sp", sgl[i][:], sgd[hh], d[9 + i], writes=[r_sgl[i]])
    b.dma("sp", srall[:], srd.rearrange("h p r v -> p h r v"), d[11], writes=[r_srall])
    load_head(0)
    b.op("dve", lambda e: e.tensor_copy(Sall[:], srall[:, :, 0, :]), reads=[r_srall], writes=[r_Sall])
    for r in range(1, NRK):
        b.op("dve", lambda e, r=r: e.tensor_tensor(Sall[:], Sall[:], dr[:, :, r:r + 1].broadcast_to([128, NHH, 128]), ALU.mult),
             reads=[r_Sall, r_const], writes=[r_Sall])
        b.op("pool", lambda e, r=r: e.tensor_tensor(Sall[:], Sall[:], srall[:, :, r, :], ALU.add),
             reads=[r_Sall, r_srall], writes=[r_Sall])
    for i in range(NWB - 1):
        load_w(i)
    outs = []
    for hh in range(NHH):
        load_head(hh + 1)
        i2 = hh % 2
        Oraw, xsq0, rstd, S, Sbf = Oraws[i2], xsqs[i2], rstds[i2], Ss[i2], Sbfs[i2]
        r_Oraw, r_xsq0, r_rstd, r_S, r_Sbf = r_Oraws[i2], r_xsqs[i2], r_rstds[i2], r_Ss[i2], r_Sbfs[i2]
        b.op("act", lambda e, hh=hh: e.activation(Sbf[:], Sall[:, hh, :], AF.Copy), reads=[r_Sall], writes=[r_Sbf])
        b.op("dve", lambda e, hh=hh: e.scalar_tensor_tensor(Se[:, hh, :], Sall[:, hh, :], dl[:, hh:hh + 1], sloc[:, hh, :], ALU.mult, ALU.add),
             reads=[r_Sall, r_const], writes=[r_Se])
        for (s, n) in ntiles(NT):
            pt, rp = pr.next()
            b.op("pe", lambda e, pt=pt, s=s, n=n, i2=i2: e.matmul(pt[:, 0:n], Sbf[:], qbb[i2][:, s:s + n], start=True, stop=True),
                 reads=[r_Sbf, r_qbb[i2]], writes=[rp])
            b.op("dve", lambda e, pt=pt, s=s, n=n, i2=i2: e.tensor_tensor(Oraw[:, s:s + n], pt[:, 0:n], ol[i2][:, s:s + n], ALU.add),
                 reads=[rp, r_ol[i2]], writes=[r_Oraw])
        b.op("act", lambda e: e.activation(xsq0[:], Oraw[:], AF.Square), reads=[r_Oraw], writes=[r_xsq0])
        for (s, n) in ntiles(NT):
            pt, rp = pr.next()
            b.op("pe", lambda e, pt=pt, s=s, n=n: e.matmul(pt[:, 0:n], ones[:], xsq0[:, s:s + n], start=True, stop=True),
                 reads=[r_xsq0, r_const], writes=[rp])
            b.op("act", lambda e, pt=pt, s=s, n=n: e.activation(rstd[:, s:s + n], pt[:, 0:n], AF.Ln, bias=EPS, scale=1.0 / 128),
                 reads=[rp], writes=[r_rstd])
        b.op("act", lambda e: e.activation(rstd[:], rstd[:], AF.Exp, scale=-0.5), reads=[r_rstd], writes=[r_rstd])
        b.op("dve", lambda e, hh=hh: e.scalar_tensor_tensor(Oraw[:], Oraw[:], ng[:, hh:hh + 1], rstd[:], ALU.mult, ALU.mult),
             reads=[r_Oraw, r_rstd, r_const], writes=[r_Oraw])
        b.op("pool", lambda e, hh=hh, i2=i2: e.tensor_tensor(O2T[:, hh, :], Oraw[:], sgl[i2][:], ALU.mult),
             reads=[r_Oraw, r_sgl[i2]], writes=[r_O2T])
    outs.append(b.dma("sp", send, Se[:], d[13], reads=[r_Se]))
    for i in range(KC):
        load_w(i + NWB - 1)
        w, rw = wring[i % NWB], r_wr[i % NWB]
        xi, rxi = xr[i % 2], r_xr[i % 2]
        b.dma("sp", xi[:], xT[:, i, :], d[14 + i % 2], writes=[rxi])
        for (s, n) in ntiles(NT):
            pt, rp = pr.next()
            for k in range(KC):
                b.op("pe", lambda e, pt=pt, w=w, k=k, s=s, n=n: e.matmul(
                    pt[:, 0:n], w[:, k, :], O2T[:, k, s:s + n], start=(k == 0), stop=(k == KC - 1)),
                    reads=[rw, r_O2T], writes=[rp], sig=(k == KC - 1))
            if s + n <= NP:
                b.op("dve", lambda e, pt=pt, i=i, s=s, n=n, xi=xi: e.scalar_tensor_tensor(
                    xi[:, s:s + n], pt[:, 0:n], vec[:, 3, i:i + 1], xi[:, s:s + n], ALU.mult, ALU.add),
                    reads=[rp, r_const, rxi], writes=[rxi])
            else:
                b.op("dve", lambda e, pt=pt, i=i: e.tensor_tensor(
                    tmps[:].rearrange("p (s t) -> p s t", t=4), pt[:, 0:NS].rearrange("p (s t) -> p s t", t=4),
                    mods[:, 2, i, :].unsqueeze(2).broadcast_to([128, 16, 4]), ALU.mult),
                    reads=[rp, r_const], writes=[r_tmps])
                b.op("dve", lambda e, xi=xi: e.tensor_tensor(xi[:, NP:NT], xi[:, NP:NT], tmps[:], ALU.add),
                     reads=[r_tmps, rxi], writes=[rxi])
        outs.append(b.dma("sp", xo[:, i, :], xi[:], d[16 + i % 2], reads=[rxi]))
    b.wait_all("sp", outs)
    b.emit(); b.close()
    return nc


class _HSet:
    pass


FILL_RATIO = 1
CRIT_RATIO = 2


def build_hgrna():
    nc = bass.Bass("TRN2", target_bir_lowering=False)
    NT = NP + NS
    NBLK = NP // 128
    NCK = NP // CH
    dram = lambda name, shape, kind="ExternalInput": nc.dram_tensor(name, list(shape), F32, kind=kind).ap()
    xT = dram("xT", [128, KC, NT])
    vecd = dram("vec", [128, 4, KC]); modsd = dram("mods", [128, 3, KC, 16])
    whg = dram("whg", [NHH, 128, 4, KC, 128])
    lbpd = dram("lbp", [128, 2, NHH])
    m01d = dram("m01", [128, 128]); cmd = dram("cm", [128, 4, 128]); identd = dram("ident", [128, 128])
    s0d = dram("s0", [NHH, 128, 16, 128]); msd = dram("ms", [64, 64]); cmsd = dram("cms", [128, 16, 64])
    smd = dram("smask", [128, NT])
    oloc = dram("oloc", [NHH, 128, NT], "ExternalOutput")
    qbo = dram("qbo", [NHH, 128, NT], "ExternalOutput")
    sgo = dram("sgo", [NHH, 128, NT], "ExternalOutput")
    snew = dram("snew", [NHH, 128, 16, 128], "ExternalOutput")
    send = dram("send", [128, NHH, 128], "ExternalOutput")
    dout = dram("dout", [128, NHH], "ExternalOutput")

    b = Builder(nc, n_dsem=22)
    d = b.dsems
    pr = PsumRot(b, 4)
    pr_loop = PsumRot.__new__(PsumRot); pr_loop.tiles = pr.tiles[0:2]; pr_loop.res = pr.res[0:2]; pr_loop.i = 0
    pr_prep = PsumRot.__new__(PsumRot); pr_prep.tiles = pr.tiles[2:4]; pr_prep.res = pr.res[2:4]; pr_prep.i = 0
    po_banks = [(b.ps("pob%d" % i, [128, 512]), Res("pob%d" % i)) for i in range(2)]
    pu_banks = [(b.ps("pub%d" % i, [128, 512]), Res("pub%d" % i)) for i in range(2)]
    xbuf = b.sb("xbuf", [128, KC * NT], F32)
    x = xbuf[:].rearrange("p (k n) -> p k n", k=KC)
    h = b.sb("h", [128, KC, NT], BF16)
    rstd = b.sb("rstd", [128, NT], F32)
    xsq0 = b.sb("xsq0", [128, NT], BF16)
    t32a = b.sb("t32a", [128, NT], F32)
    NWB = 5
    wring = [b.sb("wr%d" % i, [128, KC, 128], BF16) for i in range(NWB)]
    Dall = b.sb("Dall", [128, NHH], F32)
    vec = b.sb("vecs", [128, 4, KC], F32); mods = b.sb("modss", [128, 3, KC, 16], F32)
    lbp = b.sb("lbps", [128, 2, NHH], F32); oml = b.sb("oml", [128, NHH], F32)
    m01 = b.sb("m01s", [128, 128], F32); cm = b.sb("cms_", [128, 4, 128], F32)
    ident = b.sb("idents", [128, 128], BF16); identf = b.sb("identf", [128, 128], F32)
    ones = b.sb("ones", [128, 128], BF16)
    gm = b.sb("gm", [128, KC], F32); gms = b.sb("gms", [128, KC, 16], F32)
    ms = b.sb("mss", [64, 64], F32); cms = b.sb("cmss", [128, 16, 64], F32)
    smask = b.sb("smasks", [128, NT], F32); onesf = b.sb("onesf", [128, NCK], F32)
    R = Res
    r_x, r_h, r_rstd, r_const, r_gm, r_gms, r_xsq0, r_t32a, r_D = [R(n) for n in "x h rstd const gm gms xsq t32 D".split()]
    r_wr = [R("wr%d" % i) for i in range(NWB)]

    f32_names = ["qf", "kf", "bA", "bB", "sg", "Oraw", "QBf"]
    bf_names = ["qt", "kt", "kh"]
    sets = []
    for si in range(2):
        B = _HSet()
        if si == 0:
            for nm in f32_names:
                if nm == "QBf":
                    B.QBf = t32a[:]
                elif nm == "Oraw":
                    B.Oraw = rstd[:]
                else:
                    setattr(B, nm, b.sb(nm + "0", [128, NT], F32)[:])
            for nm in bf_names:
                setattr(B, nm, b.sb(nm + "0", [128, NT], BF16)[:])
            B.Vt = b.sb("Vt0", [128, NBLK + 1, 128], BF16)[:]
            B.S0 = b.sb("S00", [128, 16, 128], F32)[:]; B.S0b = b.sb("S0b0", [128, 16, 128], BF16)[:]
            B.Khms = b.sb("Khms0", [128, 16, 64], BF16)[:]; B.KhmTs = b.sb("KhmTs0", [64, 16, 128], BF16)[:]
            B.At = b.sb("At0", [128, 128], BF16)[:]; B.Khm = b.sb("Khm0", [128, 4, 128], BF16)[:]
            B.KhmT = b.sb("KhmT0", [128, 4, 128], BF16)[:]
            B.At_b = b.sb("At0b", [128, 128], BF16)[:]; B.Khm_b = b.sb("Khm0b", [128, 4, 128], BF16)[:]
            B.KhmT_b = b.sb("KhmT0b", [128, 4, 128], BF16)[:]
            B.S = b.sb("S_0", [128, 128], F32)[:]; B.Sbf = b.sb("Sbf0", [128, 128], BF16)[:]
            B.S2 = b.sb("S2_0", [128, 128], F32)[:]; B.Sbf2 = b.sb("Sbf2_0", [128, 128], BF16)[:]
            B.dec = b.sb("dec0", [128, NCK + 16], F32)[:]
            B.pbA = b.sb("pbA0", [128, NCK], F32)[:]; B.pbB = b.sb("pbB0", [128, NCK], F32)[:]
            B.Ats = b.sb("Ats0", [64, 64], BF16)[:]
        else:
            off = [0]

            def carve(n_f32, dt, shape):
                v = xbuf[:, off[0]:off[0] + n_f32]
                off[0] += n_f32
                if dt is BF16:
                    v = v.bitcast(BF16)
                if len(shape) == 2:
                    return v[:, 0:shape[1]] if shape[0] == 128 else v[0:shape[0], 0:shape[1]]
                if len(shape) == 3:
                    vv = v[:, 0:shape[1] * shape[2]].rearrange("p (a c) -> p a c", a=shape[1])
                    return vv if shape[0] == 128 else vv[0:shape[0]]
            for nm in f32_names:
                setattr(B, nm, carve(NT, F32, [128, NT]))
            for nm in bf_names:
                setattr(B, nm, carve(NT // 2, BF16, [128, NT]))
            B.Vt = carve((NBLK + 1) * 64, BF16, [128, NBLK + 1, 128])
            B.S0 = carve(2048, F32, [128, 16, 128]); B.S0b = carve(1024, BF16, [128, 16, 128])
            B.Khms = carve(512, BF16, [128, 16, 64]); B.KhmTs = carve(1024, BF16, [64, 16, 128])
            B.At = carve(64, BF16, [128, 128]); B.Khm = carve(256, BF16, [128, 4, 128]); B.KhmT = carve(256, BF16, [128, 4, 128])
            B.At_b = carve(64, BF16, [128, 128]); B.Khm_b = carve(256, BF16, [128, 4, 128]); B.KhmT_b = carve(256, BF16, [128, 4, 128])
            B.S = carve(128, F32, [128, 128]); B.Sbf = carve(64, BF16, [128, 128])
            B.S2 = carve(128, F32, [128, 128]); B.Sbf2 = carve(64, BF16, [128, 128])
            B.dec = carve(NCK + 16, F32, [128, NCK + 16])
            B.pbA = carve(NCK, F32, [128, NCK]); B.pbB = carve(NCK, F32, [128, NCK])
            B.Ats = carve(32, BF16, [64, 64])
            assert off[0] <= KC * NT, off[0]
        for nm in f32_names + bf_names + ["Vt", "S0", "S0b", "Khms", "KhmTs", "At", "Khm", "KhmT", "At_b", "Khm_b", "KhmT_b", "S", "Sbf", "S2", "Sbf2", "dec", "pbA", "pbB", "Ats"]:
            setattr(B, "r_" + nm, Res(nm + str(si)))
        if si == 0:
            B.r_QBf = r_t32a
            B.r_Oraw = r_rstd
        sets.append(B)

    b.dma("sp", xbuf[:], xT.rearrange("p k n -> p (k n)"), d[0], writes=[r_x])
    for dst, src in [(vec, vecd), (mods, modsd), (lbp, lbpd), (m01, m01d), (cm, cmd), (identf, identd), (ms, msd), (cms, cmsd), (smask, smd)]:
        b.dma("sp", dst[:], src, d[1], writes=[r_const])
    b.op("pool", lambda e: e.memset(ones[:], 1.0), writes=[r_const])
    b.op("pool", lambda e: e.memset(Dall[:], 0.0), writes=[r_D])
    b.op("pool", lambda e: e.memset(onesf[:], 1.0), writes=[r_const])
    b.op("act", lambda e: e.activation(ident[:], identf[:], AF.Copy), reads=[r_const], writes=[r_const])
    b.op("dve", lambda e: e.tensor_tensor(oml[:], lbp[:, 0, :], lbp[:, 1, :], ALU.subtract), reads=[r_const], writes=[r_const])
    b.op("act", lambda e: e.activation(oml[:], oml[:], AF.Sigmoid), reads=[r_const], writes=[r_const])

    wsem = [d[2], d[3], d[4], d[5], d[6]]
    wlist = [(hh, wh) for hh in range(NHH) for wh in range(4)]

    def load_w(n):
        if n >= len(wlist):
            return
        a, c = wlist[n]
        b.dma("pool", wring[n % NWB][:], whg[a][:, c, :, :], wsem[n % NWB], writes=[r_wr[n % NWB]])
    for n in range(4):
        load_w(n)
    wcount = [0]

    def next_w():
        n = wcount[0]
        wcount[0] += 1
        return n, wring[n % NWB], r_wr[n % NWB]

    emit_norm_mod(b, pr, x, r_x, h, r_h, NT, NP, vec, 0, 1, 2, mods, 0, 1, ones, r_const,
                  ([xsq0[:], xsq0[:]], [r_xsq0, r_xsq0], rstd[:], r_rstd, gm, r_gm, gms, r_gms, [t32a[:], t32a[:]], [r_t32a, r_t32a]))
    B1 = sets[1]
    for nm in f32_names + bf_names + ["Vt", "S0", "S0b", "Khms", "KhmTs", "At", "Khm", "KhmT", "At_b", "Khm_b", "KhmT_b", "S", "Sbf", "S2", "Sbf2", "dec", "pbA", "pbB", "Ats"]:
        rr = getattr(B1, "r_" + nm)
        rr.r = list(r_x.r)
        rr.w = r_x.w

    outs = []
    dsem_set = [dict(s0=d[7], s0b=d[8], qb=d[9], sg=d[10], ol=d[11], sn=d[12], se=d[13]),
                dict(s0=d[14], s0b=d[15], qb=d[16], sg=d[17], ol=d[18], sn=d[19], se=d[20])]

    def proj_fm(dst_fn):
        n_, w, rw = next_w()
        for (s, n) in ntiles(NT):
            pt, rp = pr_prep.next()
            for k in range(KC):
                b.op("pe", lambda e, pt=pt, w=w, k=k, s=s, n=n: e.matmul(
                    pt[:, 0:n], w[:, k, :], h[:, k, s:s + n], start=(k == 0), stop=(k == KC - 1)),
                    reads=[rw, r_h], writes=[rp], sig=(k == KC - 1))
                if k % 4 == 3 and k != KC - 1 and n > 128:
                    yield
            dst_fn(pt, rp, s, n)
            yield
        load_w(n_ + 4)

    def prep(hh):
        B = sets[hh % 2]; ds = dsem_set[hh % 2]
        b.dma("sp", B.S0, s0d[hh], ds["s0"], writes=[B.r_S0])
        b.dma("pool", B.S0b, s0d[hh], ds["s0b"], writes=[B.r_S0b])
        yield from proj_fm(lambda pt, rp, s, n: b.op("act", lambda e: e.activation(B.qf[:, s:s + n], pt[:, 0:n], AF.Silu),
                                                     reads=[rp], writes=[B.r_qf]))
        yield from proj_fm(lambda pt, rp, s, n: b.op("act", lambda e: e.activation(B.kf[:, s:s + n], pt[:, 0:n], AF.Sigmoid, scale=-1.0),
                                                     reads=[rp], writes=[B.r_kf]))
        b.op("dve", lambda e: e.tensor_scalar(B.kf, B.kf, oml[:, hh:hh + 1], None, ALU.mult), reads=[B.r_kf, r_const], writes=[B.r_kf])
        b.op("act", lambda e: e.activation(B.bA, B.kf, AF.Ln, bias=1.0, scale=-1.0), reads=[B.r_kf], writes=[B.r_bA])
        yield
        n_, w, rw = next_w()
        for blk in range(NBLK + 1):
            m = 128 if blk < NBLK else NS
            c0 = blk * 128
            pt, rp = pr_prep.next()
            for k in range(KC):
                b.op("pe", lambda e, pt=pt, w=w, k=k, c0=c0, m=m: e.matmul(
                    pt[0:m, 0:128], h[:, k, c0:c0 + m], w[:, k, :], start=(k == 0), stop=(k == KC - 1)),
                    reads=[rw, r_h], writes=[rp], sig=(k == KC - 1))
            b.op("act", lambda e, pt=pt, blk=blk, m=m: e.activation(B.Vt[0:m, blk, :], pt[0:m, 0:128], AF.Copy),
                 reads=[rp], writes=[B.r_Vt])
            yield
        load_w(n_ + 4)
        yield from proj_fm(lambda pt, rp, s, n: b.op("act", lambda e: e.activation(B.sg[:, s:s + n], pt[:, 0:n], AF.Silu),
                                                     reads=[rp], writes=[B.r_sg]))
        outs.append(b.dma("sp", sgo[hh], B.sg, ds["sg"], reads=[B.r_sg]))
        b.op("dve", lambda e: e.tensor_tensor_scan(B.bB, smask[:], B.bA, 0.0, ALU.mult, ALU.add),
             reads=[B.r_bA, r_const], writes=[B.r_bB])
        bb, rbb = B.bB, B.r_bB
        other, rother = B.bA, B.r_bA
        yield
        bv = bb[:, 0:NP].rearrange("p (c t) -> p c t", t=CH)
        ov = other[:, 0:NP].rearrange("p (c t) -> p c t", t=CH)
        b.op("act", lambda e: e.activation(B.dec[:, 0:NCK].unsqueeze(2), bv[:, :, CH - 1:CH], AF.Exp), reads=[rbb], writes=[B.r_dec])
        b.op("dve", lambda e: e.tensor_tensor(ov, bv[:, :, CH - 1:CH].broadcast_to([128, NCK, CH]), bv, ALU.subtract),
             reads=[rbb], writes=[rother])
        bs_ = bb[:, NP:NT].rearrange("p (c t) -> p c t", t=4)
        os_ = other[:, NP:NT].rearrange("p (c t) -> p c t", t=4)
        b.op("act", lambda e: e.activation(B.dec[:, NCK:NCK + 16].unsqueeze(2), bs_[:, :, 3:4], AF.Exp), reads=[rbb], writes=[B.r_dec])
        b.op("dve", lambda e: e.tensor_tensor(os_, bs_[:, :, 3:4].broadcast_to([128, 16, 4]), bs_, ALU.subtract),
             reads=[rbb], writes=[rother])
        b.op("act", lambda e: e.activation(other[:, 0:NT], other[:, 0:NT], AF.Exp), reads=[rother], writes=[rother])
        b.op("dve", lambda e: e.tensor_tensor(B.kh, B.kf, other[:, 0:NT], ALU.mult), reads=[rother, B.r_kf], writes=[B.r_kh])
        yield
        b.op("act", lambda e: e.activation(other[:, 0:NT], bb[:, 0:NT], AF.Exp), reads=[rbb, B.r_kh], writes=[rother])
        b.op("dve", lambda e: e.tensor_tensor(B.qt, B.qf, other[:, 0:NT], ALU.mult), reads=[rother, B.r_qf], writes=[B.r_qt])
        b.op("act", lambda e: e.activation(other[:, 0:NT], bb[:, 0:NT], AF.Exp, scale=-1.0), reads=[rbb, B.r_qt], writes=[rother])
        b.op("dve", lambda e: e.tensor_tensor(B.kt, B.kf, other[:, 0:NT], ALU.mult), reads=[rother, B.r_kf], writes=[B.r_kt])
        yield
        b.op("pool", lambda e: e.tensor_copy(B.pbA.unsqueeze(2), bv[:, :, CH - 1:CH]), reads=[rbb], writes=[B.r_pbA])
        b.op("dve", lambda e: e.tensor_tensor_scan(B.pbB, onesf[:], B.pbA, 0.0, ALU.mult, ALU.add),
             reads=[B.r_pbA, r_const], writes=[B.r_pbB])
        pin, rpin = B.pbB, B.r_pbB
        pex, rpex = B.pbA, B.r_pbA
        b.op("act", lambda e: e.activation(Dall[:, hh:hh + 1], pin[:, NCK - 1:NCK], AF.Exp), reads=[rpin], writes=[r_D])
        b.op("pool", lambda e: e.tensor_tensor(pex.unsqueeze(2), pin.unsqueeze(2), bv[:, :, CH - 1:CH], ALU.subtract),
             reads=[rpin, rbb], writes=[rpex])
        b.op("act", lambda e: e.activation(pex, pex, AF.Exp), reads=[rpex], writes=[rpex])
        b.op("pool", lambda e: e.tensor_tensor(
            B.QBf[:, 0:NP].rearrange("p (c t) -> p c t", t=CH), B.qt[:, 0:NP].rearrange("p (c t) -> p c t", t=CH),
            pex.unsqueeze(2).broadcast_to([128, NCK, CH]), ALU.mult), reads=[rpex, B.r_qt], writes=[B.r_QBf])
        b.op("pool", lambda e: e.memset(B.QBf[:, NP:NT], 0.0), writes=[B.r_QBf])
        outs.append(b.dma("sp", qbo[hh], B.QBf, ds["qb"], reads=[B.r_QBf]))
        b.op("pool", lambda e: e.memset(B.S, 0.0), writes=[B.r_S])
        b.op("pool", lambda e: e.memset(B.Sbf, 0.0), writes=[B.r_Sbf])
        yield

    def loop(hh):
        B = sets[hh % 2]; ds = dsem_set[hh % 2]
        def front(blk):
            c0 = blk * 128
            Khm, rKhm, KhmT, rKhmT, At, rAt = ((B.Khm, B.r_Khm, B.KhmT, B.r_KhmT, B.At, B.r_At) if blk % 2 == 0 else
                                               (B.Khm_b, B.r_Khm_b, B.KhmT_b, B.r_KhmT_b, B.At_b, B.r_At_b))
            b.op("dve", lambda e: e.tensor_tensor(Khm, B.kh[:, c0:c0 + 128].unsqueeze(1).broadcast_to([128, 4, 128]), cm[:], ALU.mult),
                 reads=[B.r_kh, r_const], writes=[rKhm])
            ptT, rpT = pr_loop.next()
            ptTb = ptT[:, :].bitcast(BF16)
            for c in range(4):
                b.op("pe", lambda e, c=c: e.transpose(ptTb[:, c * 128:(c + 1) * 128], Khm[:, c, :], ident[:]),
                     reads=[rKhm, r_const], writes=[rpT], sig=(c == 3))
            b.op("act", lambda e: e.activation(KhmT.rearrange("p c k -> p (c k)"), ptTb[:, 0:512], AF.Copy),
                 reads=[rpT], writes=[rKhmT])
            pa, rpa = pr_loop.next()
            b.op("pe", lambda e: e.matmul(pa[:, 0:128], B.kt[:, c0:c0 + 128], B.qt[:, c0:c0 + 128], start=True, stop=True),
                 reads=[B.r_kt, B.r_qt], writes=[rpa])
            b.op("dve", lambda e: e.tensor_tensor(At, pa[:, 0:128], m01[:], ALU.mult), reads=[rpa, r_const], writes=[rAt])
            po, rpo = po_banks[blk % 2]
            b.op("pe", lambda e: e.matmul(po[:, 0:128], B.Vt[:, blk, :], At, start=True, stop=False),
                 reads=[B.r_Vt, rAt], writes=[rpo], sig=False)
            pu, rpu = pu_banks[blk % 2]
            for c in range(4):
                b.op("pe", lambda e, c=c: e.matmul(pu[:, c * 128:(c + 1) * 128], KhmT[:, c, :], B.Vt[:, blk, :], start=True, stop=True),
                     reads=[rKhmT, B.r_Vt], writes=[rpu], sig=(c == 3))

        front(0)
        for blk in range(NBLK):
            c0 = blk * 128
            if blk + 1 < NBLK:
                front(blk + 1)
            yield
            po, rpo = po_banks[blk % 2]
            pu, rpu = pu_banks[blk % 2]
            for c in range(4):
                ck = blk * 4 + c
                Sc, rSc, Sn, rSn = (B.S, B.r_S, B.S2, B.r_S2) if ck % 2 == 0 else (B.S2, B.r_S2, B.S, B.r_S)
                Sbc, rSbc, Sbn, rSbn = (B.Sbf, B.r_Sbf, B.Sbf2, B.r_Sbf2) if ck % 2 == 0 else (B.Sbf2, B.r_Sbf2, B.Sbf, B.r_Sbf)
                b.op("pe", lambda e, c=c: e.matmul(po[:, c * CH:(c + 1) * CH], Sbc, B.qt[:, c0 + c * CH:c0 + (c + 1) * CH],
                                                   start=False, stop=(c == 3)),
                     reads=[rSbc, B.r_qt], writes=[rpo], sig=True)
                b.op("dve", lambda e, c=c, ck=ck: e.scalar_tensor_tensor(Sn, Sc, B.dec[:, ck:ck + 1], pu[:, c * 128:(c + 1) * 128], ALU.mult, ALU.add),
                     reads=[rpu, rSc, B.r_dec], writes=[rSn])
                b.op("pool", lambda e: e.tensor_copy(Sbn, Sn), reads=[rSn], writes=[rSbn])
                yield
            b.op("act", lambda e: e.activation(B.Oraw[:, c0:c0 + 128], po[:, 0:128], AF.Copy), reads=[rpo], writes=[B.r_Oraw])
        outs.append(b.dma("sp", send[:, hh, :], B.S, ds["se"], reads=[B.r_S]))
        b.op("dve", lambda e: e.tensor_tensor(B.Khms, B.kh[:, NP:NT].unsqueeze(1).broadcast_to([128, 16, 64]), cms[:], ALU.mult),
             reads=[B.r_kh, r_const], writes=[B.r_Khms])
        for half in range(2):
            ptT, rpT = pr_loop.next()
            ptTb = ptT[:, :].bitcast(BF16)
            for s8 in range(8):
                sq = half * 8 + s8
                b.op("pe", lambda e, ptTb=ptTb, s8=s8, sq=sq: e.transpose(ptTb[0:64, s8 * 128:(s8 + 1) * 128], B.Khms[:, sq, :], ident[:]),
                     reads=[B.r_Khms, r_const], writes=[rpT], sig=(s8 == 7))
            b.op("act", lambda e, ptTb=ptTb, half=half: e.activation(
                B.KhmTs[:, half * 8:half * 8 + 8, :].rearrange("p c k -> p (c k)"), ptTb[0:64, 0:1024], AF.Copy),
                reads=[rpT], writes=[B.r_KhmTs])
            yield
        pa, rpa = pr_loop.next()
        b.op("pe", lambda e, pa=pa: e.matmul(pa[0:64, 0:64], B.kt[:, NP:NT], B.qt[:, NP:NT], start=True, stop=True),
             reads=[B.r_kt, B.r_qt], writes=[rpa])
        b.op("dve", lambda e, pa=pa: e.tensor_tensor(B.Ats, pa[0:64, 0:64], ms[:], ALU.mult), reads=[rpa, r_const], writes=[B.r_Ats])
        po, rpo = pr_loop.next()
        b.op("pe", lambda e, po=po: e.matmul(po[:, 0:64], B.Vt[0:64, NBLK, :], B.Ats, start=True, stop=False),
             reads=[B.r_Vt, B.r_Ats], writes=[rpo], sig=False)
        for sq in range(16):
            b.op("pe", lambda e, po=po, sq=sq: e.matmul(po[:, sq * 4:sq * 4 + 4], B.S0b[:, sq, :], B.qt[:, NP + sq * 4:NP + sq * 4 + 4],
                                                       start=False, stop=(sq == 15)),
                 reads=[B.r_S0b, B.r_qt], writes=[rpo], sig=(sq == 15))
        b.op("act", lambda e, po=po: e.activation(B.Oraw[:, NP:NT], po[:, 0:64], AF.Copy), reads=[rpo], writes=[B.r_Oraw])
        outs.append(b.dma("sp", oloc[hh], B.Oraw, ds["ol"], reads=[B.r_Oraw]))
        yield
        for q4 in range(4):
            pu, rpu = pr_loop.next()
            for s4 in range(4):
                sq = q4 * 4 + s4
                b.op("pe", lambda e, pu=pu, s4=s4, sq=sq: e.matmul(pu[:, s4 * 128:(s4 + 1) * 128], B.KhmTs[:, sq, :], B.Vt[0:64, NBLK, :],
                                                                 start=True, stop=True),
                     reads=[B.r_KhmTs, B.r_Vt], writes=[rpu], sig=(s4 == 3))
            for s4 in range(4):
                sq = q4 * 4 + s4
                b.op("dve", lambda e, pu=pu, s4=s4, sq=sq: e.scalar_tensor_tensor(
                    B.S0[:, sq, :], B.S0[:, sq, :], B.dec[:, NCK + sq:NCK + sq + 1], pu[:, s4 * 128:(s4 + 1) * 128], ALU.mult, ALU.add),
                    reads=[rpu, B.r_S0, B.r_dec], writes=[B.r_S0])
            yield
        outs.append(b.dma("sp", snew[hh], B.S0, ds["sn"], reads=[B.r_S0]))

    def drain(g):
        for _ in g:
            pass

    drain(prep(0))
    for hh in range(NHH):
        crit = loop(hh)
        fill = prep(hh + 1) if hh + 1 < NHH else iter(())
        done_c = done_f = False
        while not (done_c and done_f):
            for _ in range(CRIT_RATIO):
                if not done_c:
                    try:
                        next(crit)
                    except StopIteration:
                        done_c = True
            for _ in range(FILL_RATIO):
                if not done_f:
                    try:
                        next(fill)
                    except StopIteration:
                        done_f = True
    outs.append(b.dma("sp", dout, Dall[:], d[21], reads=[r_D]))
    b.wait_all("sp", outs)
    b.emit(); b.close()
    return nc


def fm(a):
    n = a.shape[0]
    return np.ascontiguousarray(a.T.reshape(16, 128, n).transpose(1, 0, 2))
def unfm(t):
    return np.ascontiguousarray(t.transpose(2, 1, 0).reshape(t.shape[2], D))
def vfm(v):
    return v.reshape(-1, 128).T
def mods_fm(m3):
    return np.ascontiguousarray(m3.reshape(3, 16, 16, 128).transpose(3, 0, 2, 1))
def tile_w_in(w):
    return np.ascontiguousarray(w.reshape(16, 128, 2, 44, 128).transpose(3, 1, 2, 0, 4))
def tile_w_out(w):
    return np.ascontiguousarray(w.reshape(4, 11, 128, 16, 128).transpose(0, 3, 2, 1, 4))
def tile_cols(w):
    return w.reshape(16, 128, w.shape[1]).transpose(1, 0, 2)
def tile_wqkv(w):
    out = np.empty((8, 128, 4, 16, 128), np.float32)
    for g in range(8):
        out[g, :, 0] = tile_cols(w[:, 256 * g:256 * g + 128])
        out[g, :, 1] = tile_cols(w[:, 256 * g + 128:256 * g + 256])
        kk = w[:, 2048 + 64 * g:2048 + 64 * g + 64]; vv = w[:, 2560 + 64 * g:2560 + 64 * g + 64]
        out[g, :, 2] = tile_cols(np.concatenate([kk, kk], 1))
        out[g, :, 3] = tile_cols(np.concatenate([vv, vv], 1))
    return out
def tile_sq(w):
    return np.ascontiguousarray(w.reshape(16, 128, 16, 128).transpose(2, 1, 0, 3))
NEG = -30000.0
def attn_consts():
    s = np.arange(128)[:, None]; q = np.arange(128)[None, :]
    nd = np.zeros((128, 2, 128), np.float32); mk = np.zeros((128, 2, 128), np.float32)
    nd[:, 0] = -(q - s + 128); mk[:, 0] = np.where(s > q, 0, NEG)
    nd[:, 1] = -(q - s); mk[:, 1] = np.where(s <= q, 0, NEG)
    nd = np.where(mk < 0, 0, nd).astype(np.float32)
    t = np.arange(4)[None, :]
    ndc = -(128 + t - s).astype(np.float32); mkc = np.where(s > t, 0, NEG).astype(np.float32)
    ndc = np.where(mkc < 0, 0, ndc).astype(np.float32)
    a = np.arange(64)
    same = (a[:, None] // 4) == (a[None, :] // 4)
    tp = a[:, None] % 4; tq = a[None, :] % 4
    ok = same & (tp <= tq)
    ndn = np.where(ok, -(tq - tp), 0).astype(np.float32); mkn = np.where(ok, 0, NEG).astype(np.float32)
    return dict(nd=nd, mk=mk, ndc=ndc, mkc=mkc, ndn=ndn, mkn=mkn)
def tile_whg(w):
    out = np.empty((16, 128, 4, 16, 128), np.float32)
    for hh in range(16):
        for wh in range(4):
            out[hh, :, wh] = tile_cols(w[:, wh * 2048 + hh * 128: wh * 2048 + hh * 128 + 128])
    return out
def hgrn_consts():
    a = np.arange(128)
    m01 = ((a[:, None] // 32 == a[None, :] // 32) & (a[:, None] <= a[None, :])).astype(np.float32)
    cm = np.broadcast_to((a[None, :] // 32 == np.arange(4)[:, None]).astype(np.float32)[None], (128, 4, 128)).copy()
    s = np.arange(64)
    ms = ((s[:, None] // 4 == s[None, :] // 4) & (s[:, None] <= s[None, :])).astype(np.float32)
    cms = np.broadcast_to((s[None, :] // 4 == np.arange(16)[:, None]).astype(np.float32)[None], (128, 16, 64)).copy()
    t = np.arange(1088)
    sm = np.where(t < 1024, (t % 32) != 0, ((t - 1024) % 4) != 0).astype(np.float32)
    smask = np.ascontiguousarray(np.broadcast_to(sm[None], (128, 1088)))
    return dict(m01=m01, cm=cm, ms=ms, cms=cms, ident=np.eye(128, dtype=np.float32), smask=smask)

NCORE = 8
_PROGS = {}


def _prog(name, fn):
    if name not in _PROGS:
        _PROGS[name] = fn()
    return _PROGS[name]


def _run(name, fn, in_maps):
    nc = _prog(name, fn)
    res = run_bass_kernel_spmd(nc, in_maps, core_ids=list(range(NCORE)))
    return res.results


def _f32(a):
    return np.ascontiguousarray(np.asarray(a, dtype=np.float32))


def kernel(x_prompt, x_sample, cache_swa_k, cache_swa_v, state_hgrn, state_ffn_conv, c_prompt, c_sample,
           norm1_g, norm2_g, w_ada, b_ada, attn_w_qkv, attn_w_o, attn_sinks,
           hgrn_w_in, hgrn_lower_bounds, hgrn_norm_g, hgrn_w_o,
           ffn_w_in, ffn_conv_w, ffn_conv_b, ffn_w_out, final_norm_g):
    (x_prompt, x_sample, cache_swa_k, cache_swa_v, state_hgrn, state_ffn_conv, c_prompt, c_sample,
     norm1_g, norm2_g, w_ada, b_ada, attn_w_qkv, attn_w_o, attn_sinks,
     hgrn_w_in, hgrn_lower_bounds, hgrn_norm_g, hgrn_w_o,
     ffn_w_in, ffn_conv_w, ffn_conv_b, ffn_w_out, final_norm_g) = [_f32(a) for a in (
        x_prompt, x_sample, cache_swa_k, cache_swa_v, state_hgrn, state_ffn_conv, c_prompt, c_sample,
        norm1_g, norm2_g, w_ada, b_ada, attn_w_qkv, attn_w_o, attn_sinks,
        hgrn_w_in, hgrn_lower_bounds, hgrn_norm_g, hgrn_w_o,
        ffn_w_in, ffn_conv_w, ffn_conv_b, ffn_w_out, final_norm_g)]
    Dm = 2048
    xp = x_prompt[0]
    xs = x_sample.reshape(128 * 4, Dm)

    c_all = np.concatenate([c_prompt, c_sample], 0)
    cT = fm(c_all)
    maps = []
    for c in range(NCORE):
        wt = np.empty((24, 128, 16, 128), np.float32)
        bt = np.empty((128, 24), np.float32)
        for n in range(24):
            l, ch = n // 12, 12 * c + n % 12
            wt[n] = tile_cols(w_ada[l][:, ch * 128:(ch + 1) * 128])
            bt[:, n] = b_ada[l][ch * 128:(ch + 1) * 128]
        maps.append({"cT": cT, "wada": wt, "bada": bt})
    res = _run("adaln", build_adaln, maps)
    mod = np.empty((2, 129, 6 * Dm), np.float32)
    for c in range(NCORE):
        mt = res[c]["modT"]
        for n in range(24):
            l, ch = n // 12, 12 * c + n % 12
            mod[l][:, ch * 128:(ch + 1) * 128] = mt[:, n, :].T
    mod = mod.reshape(2, 129, 6, Dm)

    def vec_for(l, g_vec, i0, extra=None):
        rows = [vfm(g_vec), vfm(mod[l, 0, i0]), vfm(mod[l, 0, i0 + 1]), vfm(mod[l, 0, i0 + 2])]
        if extra is not None:
            rows.append(vfm(extra))
        return np.ascontiguousarray(np.stack(rows, 1))

    def mods_for(l, c, i0):
        sl = slice(1 + 16 * c, 1 + 16 * c + 16)
        return mods_fm(np.stack([mod[l, sl, i0], mod[l, sl, i0 + 1], mod[l, sl, i0 + 2]], 0))

    aconst = attn_consts()
    wq_t = tile_wqkv(attn_w_qkv)
    wo_t = tile_sq(attn_w_o)
    sinks_b = np.ascontiguousarray(np.broadcast_to(attn_sinks[None], (128, 32)))
    vec0 = vec_for(0, norm1_g[0], 0)
    maps = []
    for c in range(NCORE):
        halo = xp[1024 * c - 128:1024 * c] if c > 0 else np.zeros((128, Dm), np.float32)
        xc = np.concatenate([halo, xp[1024 * c:1024 * (c + 1)], xs[64 * c:64 * (c + 1)]], 0)
        m = {"xT": fm(xc), "vec": vec0, "mods": mods_for(0, c, 0), "wqkv": wq_t, "wo": wo_t,
             "kcT": np.ascontiguousarray(cache_swa_k[16 * c:16 * c + 16].transpose(2, 3, 0, 1)),
             "vc": np.ascontiguousarray(cache_swa_v[16 * c:16 * c + 16].transpose(2, 1, 0, 3)),
             "sinks": sinks_b, "hb": np.full((128, 1), NEG if c == 0 else 0.0, np.float32)}
        m.update(aconst)
        maps.append(m)
    res = _run("attn", build_attn2, maps)
    x1 = [unfm(res[c]["xo"]) for c in range(NCORE)]
    ko = res[NCORE - 1]["kout"].reshape(2, 64, 4, 192).transpose(1, 2, 0, 3).reshape(64, 8, 192)
    swa_k_prompt = np.ascontiguousarray(ko[:, :, :128].transpose(2, 1, 0))[None]
    swa_v_prompt = np.ascontiguousarray(res[NCORE - 1]["vout"][:, 0])[None]
    swa_k_sample = np.empty((128, 4, 8, 64), np.float32)
    swa_v_sample = np.empty((128, 4, 8, 64), np.float32)
    for c in range(NCORE):
        ko = res[c]["kout"].reshape(2, 64, 4, 192).transpose(1, 2, 0, 3).reshape(64, 8, 192)
        swa_k_sample[16 * c:16 * c + 16] = ko[:, :, 128:].transpose(2, 1, 0).reshape(16, 4, 8, 64)
        swa_v_sample[16 * c:16 * c + 16] = res[c]["vout"][:64, 1].reshape(16, 4, 8, 64)

    def run_ffn(l, xin, last):
        w_in_t = tile_w_in(ffn_w_in[l])
        w_out_t = tile_w_out(ffn_w_out[l])
        convw = np.ascontiguousarray(np.concatenate([ffn_conv_w[l], ffn_conv_b[l][None]], 0).reshape(4, 44, 128).transpose(2, 1, 0))
        vec = vec_for(l, norm2_g[l], 3, extra=final_norm_g)
        maps = []
        for c in range(NCORE):
            halo = xin[c - 1][1022:1024] if c > 0 else np.zeros((2, Dm), np.float32)
            xc = np.concatenate([halo, xin[c]], 0)
            maps.append({"xT": fm(xc), "vec": vec, "mods": mods_for(l, c, 3), "w_in": w_in_t, "w_out": w_out_t,
                         "convw": convw,
                         "cstate": np.ascontiguousarray(state_ffn_conv[l, 16 * c:16 * c + 16].reshape(16, 2, 44, 128).transpose(3, 2, 0, 1)),
                         "flag": np.full((128, 1), 0.0 if c == 0 else 1.0, np.float32)})
        res = _run("ffn_last" if last else "ffn", (lambda: build_ffn(True)) if last else (lambda: build_ffn(False)), maps)
        xout = [unfm(res[c]["xo"]) for c in range(NCORE)]
        cbp = np.ascontiguousarray(res[NCORE - 1]["cbp"].transpose(2, 1, 0).reshape(2, 5632))[None]
        cbs = np.concatenate([res[c]["cbs"].transpose(2, 3, 1, 0).reshape(16, 2, 5632) for c in range(NCORE)], 0)
        return xout, cbp, cbs

    x2, cbp0, cbs0 = run_ffn(0, x1, False)

    hconst = hgrn_consts()
    whg_t = tile_whg(hgrn_w_in)
    lbp = np.ascontiguousarray(hgrn_lower_bounds.reshape(2, 16, 128).transpose(2, 0, 1))
    vec1 = vec_for(1, norm1_g[1], 0)
    maps = []
    for c in range(NCORE):
        m = {"xT": fm(x2[c]), "vec": vec1, "mods": mods_for(1, c, 0), "whg": whg_t, "lbp": lbp,
             "s0": np.ascontiguousarray(state_hgrn[16 * c:16 * c + 16].transpose(1, 2, 0, 3))}
        m.update(hconst)
        maps.append(m)
    resA = _run("hgrnA", build_hgrna, maps)
    s_loc = [resA[c]["send"] for c in range(NCORE)]
    d_loc = [resA[c]["dout"] for c in range(NCORE)]
    hgrn_state_sample = np.concatenate([resA[c]["snew"].transpose(2, 0, 1, 3) for c in range(NCORE)], 0)

    wo2_t = tile_sq(hgrn_w_o)
    ng = np.ascontiguousarray(hgrn_norm_g.reshape(16, 128).T)
    maps = []
    for c in range(NCORE):
        sr = np.zeros((16, 128, NRK, 128), np.float32)
        dr = np.ones((128, 16, NRK), np.float32)
        for r in range(c):
            sr[:, :, r, :] = s_loc[r].transpose(1, 0, 2)
            dr[:, :, r] = d_loc[r]
        maps.append({"xT": fm(x2[c]), "vec": vec1, "mods": mods_for(1, c, 0), "oloc": resA[c]["oloc"], "qb": resA[c]["qbo"],
                     "sg": resA[c]["sgo"], "sr": sr, "dr": dr, "sloc": s_loc[c], "dl": d_loc[c], "ng": ng, "wo": wo2_t})
    res = _run("hgrnB", build_hgrnb, maps)
    x3 = [unfm(res[c]["xo"]) for c in range(NCORE)]
    hgrn_state_prompt = np.ascontiguousarray(res[NCORE - 1]["send"].transpose(1, 0, 2))[None]

    y, cbp1, cbs1 = run_ffn(1, x3, True)
    y_prompt = np.concatenate([y[c][:1024] for c in range(NCORE)], 0)[None]
    y_sample = np.concatenate([y[c][1024:] for c in range(NCORE)], 0).reshape(128, 4, Dm)
    ffn_conv_prompt = np.stack([cbp0, cbp1], 0)
    ffn_conv_sample = np.stack([cbs0, cbs1], 0)
    outs = (y_prompt, y_sample, swa_k_prompt, swa_v_prompt, swa_k_sample, swa_v_sample,
            hgrn_state_prompt, hgrn_state_sample, ffn_conv_prompt, ffn_conv_sample)
    return tuple(np.ascontiguousarray(o, dtype=np.float32) for o in outs)
```

```python
from concourse.bass_utils import run_bass_kernel_spmd

import contextlib
import numpy as np
import concourse.bass as bass
import concourse.mybir as mybir

F32 = mybir.dt.float32
BF16 = mybir.dt.bfloat16
I32 = mybir.dt.int32
AF = mybir.ActivationFunctionType
ALU = mybir.AluOpType
AX = mybir.AxisListType

ENGS = ["sp", "act", "pool", "dve", "pe"]


class Res:
    __slots__ = ("name", "w", "r")

    def __init__(self, name=""):
        self.name = name
        self.w = None
        self.r = []


class DSem:
    def __init__(self, handle, name):
        self.h = handle
        self.name = name
        self.cnt = 0


class _Rec:
    def __getattr__(self, name):
        return lambda *a, **k: (name, a, k)


_REC = _Rec()


class Builder:
    def __init__(self, nc, n_dsem=12):
        self.nc = nc
        self.es = contextlib.ExitStack()
        self.q = {e: [] for e in ENGS}
        self.cnt = {e: 0 for e in ENGS}
        self.waited = {e: {} for e in ENGS}
        self.esem = {e: self.es.enter_context(nc.semaphore("s_" + e)) for e in ENGS}
        self.dsems = [DSem(self.es.enter_context(nc.semaphore("d%d" % i)), "d%d" % i)
                      for i in range(n_dsem)]
        self.pending = {e: [] for e in ENGS}
        self.n_inst = 0

    def sb(self, name, shape, dt):
        return self.es.enter_context(self.nc.sbuf_tensor(name, list(shape), dt))

    def ps(self, name, shape, dt=F32):
        return self.es.enter_context(self.nc.psum_tensor(name, list(shape), dt))

    def _deps(self, eng, reads, writes):
        need = {}

        def add(t):
            if t is None:
                return
            k, v = t
            if need.get(k, 0) < v:
                need[k] = v
        for r in reads:
            add(r.w)
        for w in writes:
            add(w.w)
            for t in w.r:
                add(t)
        waits = []
        for k, v in need.items():
            if k == "pe" and eng == "pe":
                continue
            if self.waited[eng].get(k, 0) >= v:
                continue
            self.waited[eng][k] = v
            waits.append((k, v))
        return waits

    def _semh(self, k):
        return self.esem[k] if isinstance(k, str) else k.h

    def op(self, eng, fn, reads=(), writes=(), sig=True):
        reads = [r for r in reads if r is not None]
        writes = [w for w in writes if w is not None]
        waits = self._deps(eng, reads, writes)
        for k, v in waits:
            cur = self.cnt[k] if isinstance(k, str) else k.cnt
            assert v <= cur, "forward wait %s %d > %d" % (k, v, cur)
        ticket = None
        if sig:
            self.cnt[eng] += 1
            ticket = (eng, self.cnt[eng])
            pend = self.pending[eng]
            self.pending[eng] = []
            for pr, pw in pend:
                self._commit(pr, pw, ticket)
            self._commit(reads, writes, ticket)
        else:
            t = (eng, self.cnt[eng] + 1)
            self._commit(reads, writes, t)
        self.q[eng].append((waits, fn(_REC), ticket, None))
        self.n_inst += 1
        return ticket

    def _commit(self, reads, writes, ticket):
        for r in reads:
            r.r.append(ticket)
        for w in writes:
            w.w = ticket
            w.r = []

    def dma(self, eng, out, in_, dsem, reads=(), writes=(), **kw):
        reads = [r for r in reads if r is not None]
        writes = [w for w in writes if w is not None]
        waits = self._deps(eng, reads, writes)
        for k, v in waits:
            cur = self.cnt[k] if isinstance(k, str) else k.cnt
            assert v <= cur, "forward wait %s %d > %d" % (k, v, cur)
        dsem.cnt += 16
        ticket = (dsem, dsem.cnt)
        self._commit(reads, writes, ticket)
        kw2 = dict(kw); kw2["out"] = out; kw2["in_"] = in_
        self.q[eng].append((waits, ("dma_start", (), kw2), None, (dsem, 16)))
        self.n_inst += 1
        return ticket

    def wait_all(self, eng, tickets):
        waits = []
        for t in tickets:
            if t is None:
                continue
            k, v = t
            if self.waited[eng].get(k, 0) >= v:
                continue
            self.waited[eng][k] = v
            waits.append((k, v))
        self.q[eng].append((waits, None, None, None))

    def emit(self):
        nc = self.nc
        handles = {"sp": "sync", "act": "scalar", "pool": "gpsimd", "dve": "vector", "pe": "tensor"}
        with nc.Block() as block:
            for eng in ENGS:
                items = self.q[eng]
                if not items:
                    continue

                def body(e, items=items, eng=eng):
                    for waits, fn, ticket, dinc in items:
                        for k, v in waits:
                            e.wait_ge(self._semh(k), v)
                        if fn is None:
                            continue
                        name, a, k = fn
                        ins = getattr(e, name)(*a, **k)
                        if ticket is not None:
                            ins.then_inc(self.esem[eng], 1)
                        if dinc is not None:
                            ins.then_inc(dinc[0].h, dinc[1])
                getattr(block, handles[eng])(body)

    def close(self):
        self.es.close()


D = 2048
KC = 16
DFF = 5632
FC = 44
NQ = 4
FQ = FC // NQ
NP = 1024
NS = 64
EPS = 1e-6


class PsumRot:
    def __init__(self, b, n=8):
        self.tiles = [b.ps("psb%d" % i, [128, 512]) for i in range(n)]
        self.res = [Res("psb%d" % i) for i in range(n)]
        self.i = 0

    def next(self):
        t, r = self.tiles[self.i], self.res[self.i]
        self.i = (self.i + 1) % len(self.tiles)
        return t, r


def ntiles(n, step=512):
    return [(s, min(step, n - s)) for s in range(0, n, step)]


def emit_norm_mod(b, pr, x, r_x, h, r_h, ncol, np_cols, vec, iv_g, iv_sh, iv_sc, mods, im_sh, im_sc,
                  ones, r_const, tmp, r_x_parts=None):
    xsq, r_xsq, rstd, r_rstd, gm, r_gm, gms, r_gms, t32, r_t32 = tmp
    tiles = ntiles(ncol)
    banks = [pr.next() for _ in tiles]
    for k in range(KC):
        i2 = k % 2
        rxk = r_x if r_x_parts is None else r_x_parts[k * len(r_x_parts) // KC]
        b.op("act", lambda e, k=k, i2=i2: e.activation(xsq[i2][:, 0:ncol], x[:, k, 0:ncol], AF.Square),
             reads=[rxk], writes=[r_xsq[i2]])
        for ti, (s, n) in enumerate(tiles):
            pt, rp = banks[ti]
            b.op("pe", lambda e, pt=pt, s=s, n=n, i2=i2, k=k: e.matmul(
                pt[:, 0:n], ones[:], xsq[i2][:, s:s + n], start=(k == 0), stop=(k == KC - 1)),
                reads=[r_const, r_xsq[i2]], writes=[rp], sig=(k == KC - 1) or ti == len(tiles) - 1)
    for ti, (s, n) in enumerate(tiles):
        pt, rp = banks[ti]
        b.op("act", lambda e, pt=pt, s=s, n=n: e.activation(rstd[:, s:s + n], pt[:, 0:n], AF.Ln,
                                                            bias=EPS, scale=1.0 / D),
             reads=[rp], writes=[r_rstd])
    b.op("act", lambda e: e.activation(rstd[:, 0:ncol], rstd[:, 0:ncol], AF.Exp, scale=-0.5), reads=[r_rstd], writes=[r_rstd])
    b.op("dve", lambda e: e.scalar_tensor_tensor(gm[:], vec[:, iv_sc, :], 1.0, vec[:, iv_g, :], ALU.add, ALU.mult),
         reads=[r_const], writes=[r_gm])
    ns = ncol - np_cols
    if ns:
        b.op("dve", lambda e: e.scalar_tensor_tensor(
            gms[:], mods[:, im_sc, :, :], 1.0, vec[:, iv_g, :].unsqueeze(2).broadcast_to([128, KC, 16]),
            ALU.add, ALU.mult), reads=[r_const], writes=[r_gms])
    for k in range(KC):
        i2 = k % 2
        r_hk = r_h[k] if isinstance(r_h, list) else r_h
        b.op("dve", lambda e, k=k, i2=i2: e.scalar_tensor_tensor(
            t32[i2][:, 0:np_cols], x[:, k, 0:np_cols], gm[:, k:k + 1], rstd[:, 0:np_cols], ALU.mult, ALU.mult),
            reads=[r_x, r_gm, r_rstd], writes=[r_t32[i2]])
        if ns:
            b.op("dve", lambda e, k=k, i2=i2: e.tensor_tensor(
                t32[i2][:, np_cols:ncol].rearrange("p (s t) -> p s t", t=4),
                x[:, k, np_cols:ncol].rearrange("p (s t) -> p s t", t=4),
                gms[:, k, :].unsqueeze(2).broadcast_to([128, 16, 4]), ALU.mult),
                reads=[r_x, r_gms], writes=[r_t32[i2]])
            b.op("dve", lambda e, k=k, i2=i2: e.tensor_tensor(
                t32[i2][:, np_cols:ncol], t32[i2][:, np_cols:ncol], rstd[:, np_cols:ncol], ALU.mult),
                reads=[r_t32[i2], r_rstd], writes=[r_t32[i2]])
            b.op("dve", lambda e, k=k, i2=i2: e.tensor_tensor(
                t32[i2][:, np_cols:ncol].rearrange("p (s t) -> p s t", t=4),
                t32[i2][:, np_cols:ncol].rearrange("p (s t) -> p s t", t=4),
                mods[:, im_sh, k, :].unsqueeze(2).broadcast_to([128, 16, 4]), ALU.add),
                reads=[r_t32[i2], r_const], writes=[r_t32[i2]])
            b.op("act", lambda e, k=k, i2=i2: e.activation(h[:, k, np_cols:ncol], t32[i2][:, np_cols:ncol], AF.Copy),
                 reads=[r_t32[i2]], writes=[r_hk])
        b.op("act", lambda e, k=k, i2=i2: e.activation(
            h[:, k, 0:np_cols], t32[i2][:, 0:np_cols], AF.Identity, bias=vec[:, iv_sh, k:k + 1], scale=1.0),
            reads=[r_t32[i2], r_const], writes=[r_hk])


def build_ffn(last):
    nc = bass.Bass("TRN2", target_bir_lowering=False)
    NCOL = 2 + NP + NS
    UW = 2 + NP + 16 * 6
    AW = UW - 2
    dram = lambda name, shape, kind="ExternalInput": nc.dram_tensor(name, list(shape), F32, kind=kind).ap()
    xT = dram("xT", [128, KC, NCOL])
    vecd = dram("vec", [128, 5, KC])
    modsd = dram("mods", [128, 3, KC, 16])
    w_in = dram("w_in", [FC, 128, 2, KC, 128])
    w_out = dram("w_out", [NQ, KC, 128, FQ, 128])
    convd = dram("convw", [128, FC, 4])
    cstd = dram("cstate", [128, FC, 16, 2])
    flagd = dram("flag", [128, 1])
    xo = dram("xo", [128, KC, NP + NS], "ExternalOutput")
    cbp = dram("cbp", [128, FC, 2], "ExternalOutput")
    cbs = dram("cbs", [128, FC, 16, 2], "ExternalOutput")

    b = Builder(nc, n_dsem=14)
    d = b.dsems
    pr = PsumRot(b)
    x = b.sb("x", [128, KC, NCOL], F32)
    h = b.sb("h", [128, KC, NCOL], BF16)
    act = b.sb("act", [128, FQ, NP + NS], BF16)
    U = [b.sb("U%d" % i, [128, UW], F32) for i in range(2)]
    G = [b.sb("G%d" % i, [128, NP + NS], F32) for i in range(2)]
    A = b.sb("A", [128, UW], F32)
    rstd = b.sb("rstd", [128, NCOL], F32)
    xsq = [b.sb("xsq%d" % i, [128, NCOL], BF16) for i in range(2)]
    t32 = [b.sb("t32%d" % i, [128, NCOL], F32) for i in range(2)]
    win = [b.sb("win%d" % i, [128, 2, KC, 128], BF16) for i in range(2)]
    wout = [b.sb("wout%d" % i, [128, FQ, 128], BF16) for i in range(2)]
    vec = b.sb("vecs", [128, 5, KC], F32)
    mods = b.sb("modss", [128, 3, KC, 16], F32)
    convw = b.sb("convws", [128, FC, 4], F32)
    cst = b.sb("csts", [128, FC, 16, 2], F32)
    cbps = b.sb("cbps", [128, FC, 2], F32)
    cbss = b.sb("cbss", [128, FC, 16, 2], F32)
    flag = b.sb("flags", [128, 1], F32)
    ones = b.sb("ones", [128, 128], BF16)
    gm = b.sb("gm", [128, KC], F32)
    gms = b.sb("gms", [128, KC, 16], F32)
    tmps = b.sb("tmps", [128, NS], F32)

    R = lambda n: Res(n)
    r_x, r_h, r_act, r_A, r_rstd, r_const, r_gm, r_gms, r_cb, r_tmps = [R(n) for n in
        "x h act A rstd const gm gms cb tmps".split()]
    r_h = [R("h%d" % k) for k in range(KC)]
    r_U = [R("U0"), R("U1")]; r_G = [R("G0"), R("G1")]
    r_xsq = [R("xsq0"), R("xsq1")]; r_t32 = [R("t0"), R("t1")]
    r_win = [R("win0"), R("win1")]; r_wout = [R("wo0"), R("wo1")]

    r_xp = [Res("xp%d" % i) for i in range(4)]
    xsems = [d[0], d[9], d[10], d[11]]
    for i in range(4):
        b.dma("sp", x[:, 4 * i:4 * i + 4, :], xT[:, 4 * i:4 * i + 4, :], xsems[i], writes=[r_xp[i]])
    joind = b.sb("joind", [128, 1], F32)
    b.op("pool", lambda e: e.memset(joind[:], 0.0), reads=r_xp, writes=[r_x])
    b.dma("sp", vec[:], vecd, d[1], writes=[r_const])
    b.dma("sp", mods[:], modsd, d[1], writes=[r_const])
    b.dma("sp", convw[:], convd, d[1], writes=[r_const])
    b.dma("sp", cst[:], cstd, d[1], writes=[r_const])
    b.dma("sp", flag[:], flagd, d[1], writes=[r_const])
    b.op("pool", lambda e: e.memset(ones[:], 1.0), writes=[r_const])

    win_sem = [d[2], d[3]]
    wout_sem = [d[4], d[5]]

    def load_win(j):
        b.dma("pool", win[j % 2][:], w_in[j], win_sem[j % 2], writes=[r_win[j % 2]])

    def load_wout(q, i):
        n = q * KC + i
        b.dma("pool", wout[n % 2][:], w_out[q, i], wout_sem[n % 2], writes=[r_wout[n % 2]])

    load_win(0)
    emit_norm_mod(b, pr, x, r_x, h, r_h, NCOL, 2 + NP, vec, 0, 1, 2, mods, 0, 1, ones, r_const,
                  (xsq, r_xsq, rstd, r_rstd, gm, r_gm, gms, r_gms, t32, r_t32), r_x_parts=r_xp)

    tiles = ntiles(NCOL)
    for q in range(NQ):
        for jj in range(FQ):
            j = q * FQ + jj
            if j + 1 < FC:
                load_win(j + 1)
            w = win[j % 2]; rw = r_win[j % 2]
            Uj, rU = U[j % 2], r_U[j % 2]
            Gj, rG = G[j % 2], r_G[j % 2]
            for which in range(2):
                for (s, n) in tiles:
                    pt, rp = pr.next()
                    for k in range(KC):
                        b.op("pe", lambda e, pt=pt, w=w, which=which, k=k, s=s, n=n: e.matmul(
                            pt[:, 0:n], w[:, which, k, :], h[:, k, s:s + n], start=(k == 0), stop=(k == KC - 1)),
                            reads=[rw, r_h[k]], writes=[rp], sig=(k == KC - 1))
                    if which == 0:
                        if s + n <= 2 + NP:
                            b.op("act", lambda e, pt=pt, s=s, n=n, Uj=Uj: e.activation(Uj[:, s:s + n], pt[:, 0:n], AF.Copy),
                                 reads=[rp], writes=[rU])
                        else:
                            npart = 2 + NP - s
                            b.op("act", lambda e, pt=pt, s=s, npart=npart, Uj=Uj: e.activation(
                                Uj[:, s:s + npart], pt[:, 0:npart], AF.Copy), reads=[rp], writes=[rU])
                            b.op("act", lambda e, pt=pt, npart=npart, Uj=Uj: e.activation(
                                Uj[:, 2 + NP:UW].rearrange("p (s c) -> p s c", c=6)[:, :, 2:6],
                                pt[:, npart:npart + NS].rearrange("p (s t) -> p s t", t=4), AF.Copy),
                                reads=[rp], writes=[rU])
                    else:
                        if s == 0:
                            b.op("act", lambda e, pt=pt, n=n, Gj=Gj: e.activation(Gj[:, 0:n - 2], pt[:, 2:n], AF.Copy),
                                 reads=[rp], writes=[rG])
                        else:
                            b.op("act", lambda e, pt=pt, s=s, n=n, Gj=Gj: e.activation(Gj[:, s - 2:s - 2 + n], pt[:, 0:n], AF.Copy),
                                 reads=[rp], writes=[rG])
            b.op("pool", lambda e, Uj=Uj: e.tensor_scalar(Uj[:, 0:2], Uj[:, 0:2], flag[:, 0:1], None, ALU.mult),
                 reads=[rU, r_const], writes=[rU])
            b.op("pool", lambda e, Uj=Uj, j=j: e.tensor_copy(
                Uj[:, 2 + NP:UW].rearrange("p (s c) -> p s c", c=6)[:, :, 0:2], cst[:, j, :, :]),
                reads=[r_const], writes=[rU])
            b.op("dve", lambda e, Uj=Uj, j=j: e.tensor_scalar(A[:, 0:AW], Uj[:, 2:UW], convw[:, j, 2:3], None, ALU.mult),
                 reads=[rU, r_const], writes=[r_A])
            b.op("dve", lambda e, Uj=Uj, j=j: e.scalar_tensor_tensor(A[:, 0:AW], Uj[:, 1:UW - 1], convw[:, j, 1:2], A[:, 0:AW],
                                                                 ALU.mult, ALU.add), reads=[rU, r_A, r_const], writes=[r_A])
            b.op("dve", lambda e, Uj=Uj, j=j: e.scalar_tensor_tensor(A[:, 0:AW], Uj[:, 0:AW], convw[:, j, 0:1], A[:, 0:AW],
                                                                 ALU.mult, ALU.add), reads=[rU, r_A, r_const], writes=[r_A])
            b.op("act", lambda e, j=j: e.activation(A[:, 0:AW], A[:, 0:AW], AF.Gelu, bias=convw[:, j, 3:4], scale=1.0),
                 reads=[r_A, r_const], writes=[r_A])
            b.op("dve", lambda e, jj=jj, Gj=Gj: e.tensor_tensor(act[:, jj, 0:NP], A[:, 0:NP], Gj[:, 0:NP], ALU.mult),
                 reads=[r_A, rG], writes=[r_act])
            b.op("dve", lambda e, jj=jj, Gj=Gj: e.tensor_tensor(
                act[:, jj, NP:NP + NS].rearrange("p (s t) -> p s t", t=4),
                A[:, NP + 2:UW].rearrange("p (s c) -> p s c", c=6)[:, :, 0:4],
                Gj[:, NP:NP + NS].rearrange("p (s t) -> p s t", t=4), ALU.mult),
                reads=[r_A, rG], writes=[r_act])
            b.op("pool", lambda e, Uj=Uj, j=j: e.tensor_copy(cbps[:, j, :], Uj[:, NP:NP + 2]), reads=[rU], writes=[r_cb])
            b.op("pool", lambda e, Uj=Uj, j=j: e.tensor_copy(
                cbss[:, j, :, :], Uj[:, 2 + NP:UW].rearrange("p (s c) -> p s c", c=6)[:, :, 4:6]), reads=[rU], writes=[r_cb])
        load_wout(q, 0)
        for i in range(KC):
            if i + 1 < KC:
                load_wout(q, i + 1)
            n_ = q * KC + i
            w = wout[n_ % 2]; rw = r_wout[n_ % 2]
            for (s, n) in ntiles(NP + NS):
                pt, rp = pr.next()
                for jj in range(FQ):
                    b.op("pe", lambda e, pt=pt, w=w, jj=jj, s=s, n=n: e.matmul(
                        pt[:, 0:n], w[:, jj, :], act[:, jj, s:s + n], start=(jj == 0), stop=(jj == FQ - 1)),
                        reads=[rw, r_act], writes=[rp], sig=(jj == FQ - 1))
                if s + n <= NP:
                    b.op("dve", lambda e, pt=pt, i=i, s=s, n=n: e.scalar_tensor_tensor(
                        x[:, i, 2 + s:2 + s + n], pt[:, 0:n], vec[:, 3, i:i + 1], x[:, i, 2 + s:2 + s + n], ALU.mult, ALU.add),
                        reads=[rp, r_const, r_x], writes=[r_x])
                else:
                    assert s == NP and n == NS
                    b.op("dve", lambda e, pt=pt, i=i: e.tensor_tensor(
                        tmps[:].rearrange("p (s t) -> p s t", t=4), pt[:, 0:NS].rearrange("p (s t) -> p s t", t=4),
                        mods[:, 2, i, :].unsqueeze(2).broadcast_to([128, 16, 4]), ALU.mult),
                        reads=[rp, r_const], writes=[r_tmps])
                    b.op("dve", lambda e, i=i: e.tensor_tensor(x[:, i, 2 + NP:NCOL], x[:, i, 2 + NP:NCOL], tmps[:], ALU.add),
                         reads=[r_tmps, r_x], writes=[r_x])
    outs = []
    if last:
        tl = ntiles(NCOL)
        banks = [pr.next() for _ in tl]
        for k in range(KC):
            i2 = k % 2
            b.op("act", lambda e, k=k, i2=i2: e.activation(xsq[i2][:, 0:NCOL], x[:, k, 0:NCOL], AF.Square),
                 reads=[r_x], writes=[r_xsq[i2]])
            for ti, (s, n) in enumerate(tl):
                pt, rp = banks[ti]
                b.op("pe", lambda e, pt=pt, s=s, n=n, i2=i2, k=k: e.matmul(
                    pt[:, 0:n], ones[:], xsq[i2][:, s:s + n], start=(k == 0), stop=(k == KC - 1)),
                    reads=[r_const, r_xsq[i2]], writes=[rp], sig=True)
        for ti, (s, n) in enumerate(tl):
            pt, rp = banks[ti]
            b.op("act", lambda e, pt=pt, s=s, n=n: e.activation(rstd[:, s:s + n], pt[:, 0:n], AF.Sqrt, bias=EPS, scale=1.0 / D),
                 reads=[rp], writes=[r_rstd])
        b.op("dve", lambda e: e.reciprocal(rstd[:, 0:NCOL], rstd[:, 0:NCOL]), reads=[r_rstd], writes=[r_rstd])
        for k in range(KC):
            b.op("dve", lambda e, k=k: e.scalar_tensor_tensor(
                x[:, k, :], x[:, k, :], vec[:, 4, k:k + 1], rstd[:, 0:NCOL], ALU.mult, ALU.mult),
                reads=[r_x, r_rstd, r_const], writes=[r_x])
    outs.append(b.dma("sp", xo, x[:, :, 2:NCOL], d[6], reads=[r_x]))
    outs.append(b.dma("sp", cbp, cbps[:], d[7], reads=[r_cb]))
    outs.append(b.dma("sp", cbs, cbss[:], d[8], reads=[r_cb]))
    b.wait_all("sp", outs)
    b.emit()
    b.close()
    return nc


NKV = 8
SCALE = 64 ** -0.5
NEG = -30000.0


def alibi_slope(h):
    return float(2.0 ** (-8.0 * (h + 1) / 32))


def build_attn2():
    nc = bass.Bass("TRN2", target_bir_lowering=False)
    NH = 128
    NCOL = NH + NP + NS
    NQC = NP + NS
    NB = 8
    dram = lambda name, shape, kind="ExternalInput": nc.dram_tensor(name, list(shape), F32, kind=kind).ap()
    xT = dram("xT", [128, KC, NCOL])
    vecd = dram("vec", [128, 4, KC])
    modsd = dram("mods", [128, 3, KC, 16])
    wqkv = dram("wqkv", [NKV, 128, 4, KC, 128])
    wo = dram("wo", [KC, 128, KC, 128])
    kcT = dram("kcT", [NKV, 64, 16, 128])
    vc = dram("vc", [NKV, 128, 16, 64])
    ndd = dram("nd", [128, 2, 128]); mkd = dram("mk", [128, 2, 128])
    ndcd = dram("ndc", [128, 4]); mkcd = dram("mkc", [128, 4])
    ndnd = dram("ndn", [64, 64]); mknd = dram("mkn", [64, 64])
    sinkd = dram("sinks", [128, 32])
    hbd = dram("hb", [128, 1])
    xo = dram("xo", [128, KC, NQC], "ExternalOutput")
    kout = dram("kout", [128, NKV // 2, 192], "ExternalOutput")
    vout = dram("vout", [128, 2, NKV, 64], "ExternalOutput")

    b = Builder(nc, n_dsem=16)
    d = b.dsems
    pr = PsumRot(b)
    xbuf = b.sb("xbuf", [128, KC, NCOL], F32)
    x = xbuf
    OT = xbuf[:].rearrange("p k n -> p (k n)").bitcast(BF16)[:, 0:KC * NQC].rearrange("p (k n) -> p k n", k=KC)
    h = b.sb("h", [128, KC, NCOL], BF16)
    rstd = b.sb("rstd", [128, NCOL], F32)
    xsq0 = b.sb("xsq0", [128, NCOL], BF16); xsq = [xsq0, xsq0]
    t32a = b.sb("t32a", [128, NCOL], F32); t32 = [t32a, t32a]
    NWB = 4
    wring = [b.sb("wr%d" % i, [128, KC, 128], BF16) for i in range(NWB)]
    Qgs = [b.sb("Qg%d" % i, [128, 2, NQC], BF16) for i in range(2)]
    Klos = [b.sb("Klo%d" % i, [128, NCOL], BF16) for i in range(2)]
    Khis = [b.sb("Khi%d" % i, [128, NCOL], BF16) for i in range(2)]
    Vds = [b.sb("Vd%d" % i, [128, 10, 64], BF16) for i in range(2)]
    Kclo = b.sb("Kclo", [128, 16, 128], BF16); Kchi = b.sb("Kchi", [128, 16, 128], BF16)
    Vcd = b.sb("Vcd", [128, 16, 64], BF16)
    sc = [b.sb("sc%d" % i, [128, 512], F32) for i in range(2)]
    P = [b.sb("P%d" % i, [128, 512], BF16) for i in range(4)]
    rden = b.sb("rden", [128, 512], F32)
    biasg = b.sb("biasg", [128, 4, 2, 128], F32)
    biasc = b.sb("biasc", [128, 4, 4], F32)
    biasn = b.sb("biasn", [64, 4, 64], F32)
    Pc = b.sb("Pc", [128, 16, 16], BF16)
    Pn = b.sb("Pn", [64, 16, 4, 4], BF16)
    scc = b.sb("scc", [128, 16, 16], F32)
    scn = b.sb("scn", [64, 4, 64], F32)
    vec = b.sb("vecs", [128, 4, KC], F32)
    mods = b.sb("modss", [128, 3, KC, 16], F32)
    nd = b.sb("nds", [128, 2, 128], F32); mk = b.sb("mks", [128, 2, 128], F32)
    ndc = b.sb("ndcs", [128, 4], F32); mkc = b.sb("mkcs", [128, 4], F32)
    ndn = b.sb("ndns", [64, 64], F32); mkn = b.sb("mkns", [64, 64], F32)
    esink = b.sb("esink", [128, 32], F32)
    hb = b.sb("hbs", [128, 1], F32)
    esg = b.sb("esg", [128, 4], F32)
    ones = b.sb("ones", [128, 128], BF16)
    gm = b.sb("gm", [128, KC], F32)
    gms = b.sb("gms", [128, KC, 16], F32)
    koutS = b.sb("koutS", [128, NKV // 2, 192], F32)
    voutS = b.sb("voutS", [128, 2, NKV, 64], F32)
    xr = [t32a[:, 0:NQC], rstd[:, 0:NQC]]
    tmps = b.sb("tmps", [128, NS], F32)

    R = Res
    r_x, r_h, r_rstd, r_const, r_gm, r_gms, r_Q, r_K, r_V, r_Kc, r_Vc, r_rden, r_bias, r_OT = [R(n) for n in
        "x h rstd const gm gms Q K V Kc Vc rden bias OT".split()]
    r_Pc, r_Pn, r_scc, r_scn, r_ko, r_vo, r_tmps = [R(n) for n in "Pc Pn scc scn ko vo tmps".split()]
    r_xsq0 = R("xsq"); r_xsq = [r_xsq0, r_xsq0]; r_t32a = R("t32"); r_t32 = [r_t32a, r_t32a]
    r_wr = [R("wr%d" % i) for i in range(NWB)]
    r_sc = [R("a"), R("b")]; r_P = [R("a") for _ in range(4)]
    r_xr = [r_t32a, r_rstd]
    r_OT = r_x

    b.dma("sp", x[:], xT, d[0], writes=[r_x])
    for i, (dst, src) in enumerate([(vec, vecd), (mods, modsd), (nd, ndd), (mk, mkd), (ndc, ndcd), (mkc, mkcd),
                                    (ndn, ndnd), (mkn, mknd), (esink, sinkd), (hb, hbd)]):
        b.dma("sp", dst[:], src, d[1], writes=[r_const])
    b.op("pool", lambda e: e.memset(ones[:], 1.0), writes=[r_const])
    b.op("pool", lambda e: e.memset(voutS[:], 0.0), writes=[r_vo])
    r_h = [Res("h%d" % k) for k in range(KC)]
    r_Qs = [Res("Q0"), Res("Q1")]; r_Ks = [Res("K0"), Res("K1")]; r_Vs = [Res("V0"), Res("V1")]
    for i_ in range(2):
        b.op("pool", lambda e, i_=i_: e.memset(Klos[i_][:], 0.0), writes=[r_Ks[i_]])
        b.op("pool", lambda e, i_=i_: e.memset(Khis[i_][:], 0.0), writes=[r_Ks[i_]])
    b.op("pool", lambda e: e.memset(Kclo[:], 0.0), writes=[r_Kc])
    b.op("pool", lambda e: e.memset(Kchi[:], 0.0), writes=[r_Kc])
    b.op("act", lambda e: e.activation(esink[:], esink[:], AF.Exp), reads=[r_const], writes=[r_const])

    wsem = [d[2], d[3], d[8], d[9]]
    NWT = NKV * 4 + KC

    def load_w(n):
        if n >= NWT:
            return
        src = wqkv[n // 4][:, n % 4, :, :] if n < NKV * 4 else wo[n - NKV * 4]
        b.dma("pool", wring[n % NWB][:], src, wsem[n % NWB], writes=[r_wr[n % NWB]])

    for n in range(4):
        load_w(n)
    emit_norm_mod(b, pr, x, r_x, h, r_h, NCOL, NH + NP, vec, 0, 1, 2, mods, 0, 1, ones, r_const,
                  (xsq, r_xsq, rstd, r_rstd, gm, r_gm, gms, r_gms, t32, r_t32))

    def proj(g):
        Qg = Qgs[g % 2]; Klo = Klos[g % 2]; Khi = Khis[g % 2]; Vd = Vds[g % 2]
        r_Q = r_Qs[g % 2]; r_K = r_Ks[g % 2]; r_V = r_Vs[g % 2]
        for which in range(2):
            wn = g * 4 + which; w = wring[wn % NWB]; rw = r_wr[wn % NWB]
            for (s, n) in ntiles(NQC):
                yield
                pt, rp = pr.next()
                for k in range(KC):
                    b.op("pe", lambda e, pt=pt, w=w, which=which, k=k, s=s, n=n: e.matmul(
                        pt[:, 0:n], w[:, k, :], h[:, k, NH + s:NH + s + n], start=(k == 0), stop=(k == KC - 1)),
                        reads=[rw, r_h[k]], writes=[rp], sig=(k == KC - 1))
                b.op("act", lambda e, pt=pt, which=which, s=s, n=n: e.activation(Qg[:, which, s:s + n], pt[:, 0:n], AF.Copy),
                     reads=[rp], writes=[r_Q])
            load_w(wn + 4)
        wn = g * 4 + 2; w = wring[wn % NWB]; rw = r_wr[wn % NWB]
        for (s, n) in ntiles(NCOL):
            yield
            pt, rp = pr.next()
            for k in range(KC):
                b.op("pe", lambda e, pt=pt, w=w, k=k, s=s, n=n: e.matmul(
                    pt[:, 0:n], w[:, k, :], h[:, k, s:s + n], start=(k == 0), stop=(k == KC - 1)),
                    reads=[rw, r_h[k]], writes=[rp], sig=(k == KC - 1))
            b.op("act", lambda e, pt=pt, s=s, n=n: e.activation(Klo[0:64, s:s + n], pt[0:64, 0:n], AF.Copy),
                 reads=[rp], writes=[r_K])
            b.op("act", lambda e, pt=pt, s=s, n=n: e.activation(Khi[64:128, s:s + n], pt[64:128, 0:n], AF.Copy),
                 reads=[rp], writes=[r_K])
            if s == 1024:
                lo = 64 * (g % 2)
                if lo == 0:
                    b.op("act", lambda e, pt=pt, g=g, lo=lo: e.activation(koutS[lo:lo + 64, g // 2, :], pt[lo:lo + 64, 0:192], AF.Copy),
                         reads=[rp], writes=[r_ko])
                else:
                    b.op("act", lambda e, pt=pt, g=g, lo=lo: e.activation(koutS[lo:lo + 64, g // 2, :], pt[lo:lo + 64, 0:192], AF.Copy),
                         reads=[rp], writes=[r_ko])
        load_w(wn + 4)
        wn = g * 4 + 3; w = wring[wn % NWB]; rw = r_wr[wn % NWB]
        for blk in range(10):
            m = 128 if blk < 9 else NS
            c0 = blk * 128
            yield
            pt, rp = pr.next()
            for k in range(KC):
                b.op("pe", lambda e, pt=pt, w=w, k=k, c0=c0, m=m: e.matmul(
                    pt[0:m, 0:64], h[:, k, c0:c0 + m], w[:, k, 0:64], start=(k == 0), stop=(k == KC - 1)),
                    reads=[rw, r_h[k]], writes=[rp], sig=(k == KC - 1))
            b.op("dve", lambda e, pt=pt, blk=blk, m=m: e.tensor_copy(Vd[0:m, blk, :], pt[0:m, 0:64]),
                 reads=[rp], writes=[r_V])
            if blk >= 8:
                b.op("dve", lambda e, pt=pt, blk=blk, m=m, g=g: e.tensor_copy(voutS[0:m, blk - 8, g, :], pt[0:m, 0:64]),
                     reads=[rp], writes=[r_vo])
        load_w(wn + 4)
        yield

    def attn(g):
        Qg = Qgs[g % 2]; Klo = Klos[g % 2]; Khi = Khis[g % 2]; Vd = Vds[g % 2]
        r_Q = r_Qs[g % 2]; r_K = r_Ks[g % 2]; r_V = r_Vs[g % 2]
        b.dma("pool", Kclo[0:64, :, :], kcT[g], d[4], writes=[r_Kc])
        b.dma("pool", Kchi[64:128, :, :], kcT[g], d[5], writes=[r_Kc])
        b.dma("pool", Vcd[:], vc[g], d[6], writes=[r_Vc])
        b.op("dve", lambda e, g=g: e.tensor_copy(esg[:], esink[:, 4 * g:4 * g + 4]), reads=[r_const], writes=[r_bias])
        for hq in range(4):
            sl = alibi_slope(4 * g + hq)
            b.op("dve", lambda e, hq=hq, sl=sl: e.scalar_tensor_tensor(biasg[:, hq, :, :], nd[:], sl, mk[:], ALU.mult, ALU.add),
                 reads=[r_const], writes=[r_bias])
            b.op("dve", lambda e, hq=hq, sl=sl: e.scalar_tensor_tensor(biasc[:, hq, :], ndc[:], sl, mkc[:], ALU.mult, ALU.add),
                 reads=[r_const], writes=[r_bias])
            b.op("dve", lambda e, hq=hq, sl=sl: e.scalar_tensor_tensor(biasn[:, hq, :], ndn[:], sl, mkn[:], ALU.mult, ALU.add),
                 reads=[r_const], writes=[r_bias])
        for i in range(1, NB + 1):
            yield
            qc = (i - 1) * 128
            ptd, rpd = pr.next()
            ptv, rpv = pr.next()
            Pp = []
            for pair in range(2):
                pts, rps = pr.next()
                for hh in range(2):
                    hq = pair * 2 + hh
                    Kx = Klo if hh == 0 else Khi
                    for j in range(2):
                        kc0 = (i - 1 + j) * 128
                        b.op("pe", lambda e, pts=pts, hh=hh, j=j, Kx=Kx, kc0=kc0, pair=pair, qc=qc: e.matmul(
                            pts[:, (hh * 2 + j) * 128:(hh * 2 + j + 1) * 128], Kx[:, kc0:kc0 + 128], Qg[:, pair, qc:qc + 128],
                            start=True, stop=True), reads=[r_K, r_Q], writes=[rps], sig=(hh == 1 and j == 1))
                si = pair
                b.op("dve", lambda e, pts=pts, si=si, pair=pair: e.scalar_tensor_tensor(
                    sc[si][:], pts[:, 0:512], SCALE, biasg[:, pair * 2:pair * 2 + 2, :, :].rearrange("p a b c -> p (a b c)"),
                    ALU.mult, ALU.add), reads=[rps, r_bias], writes=[r_sc[si]])
                if i == 1:
                    b.op("dve", lambda e, si=si: e.tensor_scalar(
                        sc[si][:].rearrange("p (a b c) -> p a b c", a=2, b=2)[:, :, 0, :],
                        sc[si][:].rearrange("p (a b c) -> p a b c", a=2, b=2)[:, :, 0, :], hb[:, 0:1], None, ALU.add),
                        reads=[r_sc[si], r_const], writes=[r_sc[si]])
                pi = (i % 2) * 2 + pair
                b.op("act", lambda e, pi=pi, si=si: e.activation(P[pi][:], sc[si][:], AF.Exp),
                     reads=[r_sc[si]], writes=[r_P[pi]])
                Pp.append(pi)
            for pair in range(2):
                pi = Pp[pair]
                for hh in range(2):
                    hq = pair * 2 + hh
                    for j in range(2):
                        b.op("pe", lambda e, pi=pi, hh=hh, j=j, hq=hq: e.matmul(
                            ptd[:, hq * 128:(hq + 1) * 128], ones[:], P[pi][:, (hh * 2 + j) * 128:(hh * 2 + j + 1) * 128],
                            start=(j == 0), stop=(j == 1)), reads=[r_P[pi], r_const], writes=[rpd],
                            sig=(pair == 1 and hh == 1 and j == 1))
            for pair in range(2):
                pi = Pp[pair]
                for hh in range(2):
                    hq = pair * 2 + hh
                    for j in range(2):
                        blk = i - 1 + j
                        b.op("pe", lambda e, pi=pi, hh=hh, j=j, hq=hq, blk=blk: e.matmul(
                            ptv[64 * hh:64 * hh + 64, hq * 128:(hq + 1) * 128], Vd[:, blk, :], P[pi][:, (hh * 2 + j) * 128:(hh * 2 + j + 1) * 128],
                            start=(j == 0), stop=(j == 1)), reads=[r_P[pi], r_V], writes=[rpv],
                            sig=(pair == 1 and hh == 1 and j == 1))
            b.op("dve", lambda e, ptd=ptd, g=g: e.tensor_tensor(
                rden[:].rearrange("p (a q) -> p a q", a=4), ptd[:, 0:512].rearrange("p (a q) -> p a q", a=4),
                esg[:].unsqueeze(2).broadcast_to([128, 4, 128]), ALU.add),
                reads=[rpd, r_bias], writes=[r_rden])
            b.op("act", lambda e: e.activation(rden[:], rden[:], AF.Ln), reads=[r_rden], writes=[r_rden])
            b.op("act", lambda e: e.activation(rden[:], rden[:], AF.Exp, scale=-1.0), reads=[r_rden], writes=[r_rden])
            for hq in range(4):
                lo = 0 if hq % 2 == 0 else 64
                ch = (2 * g + hq // 2)
                b.op("dve", lambda e, ptv=ptv, hq=hq, lo=lo, ch=ch, qc=qc: e.tensor_tensor(
                    OT[lo:lo + 64, ch, qc:qc + 128], ptv[lo:lo + 64, hq * 128:(hq + 1) * 128],
                    rden[lo:lo + 64, hq * 128:(hq + 1) * 128], ALU.mult),
                    reads=[rpv, r_rden, r_x], writes=[r_OT])
        yield
        ptc, rpc = pr.next()
        ptn, rpn = pr.next()
        for sq in range(16):
            for hq in range(4):
                Kx = Kclo if hq % 2 == 0 else Kchi
                b.op("pe", lambda e, sq=sq, hq=hq, Kx=Kx: e.matmul(
                    ptc[:, sq * 16 + hq * 4:sq * 16 + hq * 4 + 4], Kx[:, sq, :], Qg[:, hq // 2, NP + sq * 4:NP + sq * 4 + 4],
                    start=True, stop=True), reads=[r_Kc, r_Q], writes=[rpc], sig=(sq == 15 and hq == 3))
        for hq in range(4):
            Kx = Klo if hq % 2 == 0 else Khi
            b.op("pe", lambda e, hq=hq, Kx=Kx: e.matmul(
                ptn[0:64, hq * 64:(hq + 1) * 64], Kx[:, NH + NP:NCOL], Qg[:, hq // 2, NP:NQC],
                start=True, stop=True), reads=[r_K, r_Q], writes=[rpn], sig=(hq == 3))
        b.op("dve", lambda e: e.scalar_tensor_tensor(
            scc[:].rearrange("p s (a t) -> p s a t", a=4), ptc[:, 0:256].rearrange("p (s a t) -> p s a t", s=16, a=4),
            SCALE, biasc[:].unsqueeze(1).broadcast_to([128, 16, 4, 4]), ALU.mult, ALU.add),
            reads=[rpc, r_bias], writes=[r_scc])
        b.op("act", lambda e: e.activation(Pc[:], scc[:], AF.Exp), reads=[r_scc], writes=[r_Pc])
        b.op("dve", lambda e: e.scalar_tensor_tensor(
            scn[:].rearrange("p a n -> p (a n)"), ptn[0:64, 0:256], SCALE, biasn[:].rearrange("p a n -> p (a n)"),
            ALU.mult, ALU.add), reads=[rpn, r_bias], writes=[r_scn])
        b.op("act", lambda e: e.activation(
            Pn[:].rearrange("p s a t -> p a s t"), scn[:].rearrange("p a (s t) -> p a s t", t=4), AF.Exp),
            reads=[r_scn], writes=[r_Pn])
        ptd, rpd = pr.next()
        ptv, rpv = pr.next()
        b.op("pe", lambda e: e.matmul(ptd[:, 0:256], ones[:], Pc[:].rearrange("p s c -> p (s c)"), start=True, stop=False),
             reads=[r_Pc, r_const], writes=[rpd], sig=False)
        b.op("pe", lambda e: e.matmul(ptd[:, 0:256], ones[0:64, :], Pn[:].rearrange("p s a t -> p (s a t)"), start=False, stop=True),
             reads=[r_Pn, r_const], writes=[rpd])
        for sq in range(16):
            for lo in (0, 64):
                b.op("pe", lambda e, sq=sq, lo=lo: e.matmul(ptv[lo:lo + 64, sq * 16:(sq + 1) * 16], Vcd[:, sq, :], Pc[:, sq, :], start=True, stop=False),
                     reads=[r_Pc, r_Vc], writes=[rpv], sig=False)
                b.op("pe", lambda e, sq=sq, lo=lo: e.matmul(ptv[lo:lo + 64, sq * 16:(sq + 1) * 16], Vd[0:64, 9, :],
                                                     Pn[:, sq, :, :].rearrange("p a t -> p (a t)"), start=False, stop=True),
                     reads=[r_Pn, r_V], writes=[rpv], sig=(sq == 15 and lo == 64))
        b.op("dve", lambda e, g=g: e.tensor_tensor(
            rden[:, 0:256].rearrange("p (s a t) -> p s a t", s=16, a=4), ptd[:, 0:256].rearrange("p (s a t) -> p s a t", s=16, a=4),
            esg[:].unsqueeze(1).unsqueeze(3).broadcast_to([128, 16, 4, 4]), ALU.add),
            reads=[rpd, r_bias], writes=[r_rden])
        b.op("act", lambda e: e.activation(rden[:, 0:256], rden[:, 0:256], AF.Ln), reads=[r_rden], writes=[r_rden])
        b.op("act", lambda e: e.activation(rden[:, 0:256], rden[:, 0:256], AF.Exp, scale=-1.0), reads=[r_rden], writes=[r_rden])
        for hq in range(4):
            lo = 0 if hq % 2 == 0 else 64
            ch = (2 * g + hq // 2)
            b.op("dve", lambda e, hq=hq, lo=lo, ch=ch: e.tensor_tensor(
                OT[lo:lo + 64, ch, NP:NQC].rearrange("p (s t) -> p s t", t=4),
                ptv[lo:lo + 64, 0:256].rearrange("p (s a t) -> p s a t", s=16, a=4)[:, :, hq, :],
                rden[lo:lo + 64, 0:256].rearrange("p (s a t) -> p s a t", s=16, a=4)[:, :, hq, :], ALU.mult),
                reads=[rpv, r_rden, r_x], writes=[r_OT])

        yield

    def drain(gen):
        for _ in gen:
            pass

    drain(proj(0))
    for g in range(NKV):
        crit = attn(g)
        fill = proj(g + 1) if g + 1 < NKV else iter(())
        done_c = done_f = False
        while not (done_c and done_f):
            if not done_c:
                try:
                    next(crit)
                except StopIteration:
                    done_c = True
            for _ in range(2):
                if not done_f:
                    try:
                        next(fill)
                    except StopIteration:
                        done_f = True
    xsem = [d[11], d[12]]
    osem = [d[13], d[14]]
    outs = []
    for i in range(KC):
        wn = NKV * 4 + i; w = wring[wn % NWB]; rw = r_wr[wn % NWB]
        xi = xr[i % 2]; rxi = r_xr[i % 2]
        b.dma("sp", xi, xT[:, i, NH:NCOL], xsem[i % 2], writes=[rxi])
        for (s, n) in ntiles(NQC):
            pt, rp = pr.next()
            for k in range(KC):
                b.op("pe", lambda e, pt=pt, w=w, k=k, s=s, n=n: e.matmul(
                    pt[:, 0:n], w[:, k, :], OT[:, k, s:s + n], start=(k == 0), stop=(k == KC - 1)),
                    reads=[rw, r_OT], writes=[rp], sig=(k == KC - 1))
            if s + n <= NP:
                b.op("dve", lambda e, pt=pt, i=i, s=s, n=n, xi=xi: e.scalar_tensor_tensor(
                    xi[:, s:s + n], pt[:, 0:n], vec[:, 3, i:i + 1], xi[:, s:s + n], ALU.mult, ALU.add),
                    reads=[rp, r_const, rxi], writes=[rxi])
            else:
                b.op("dve", lambda e, pt=pt, i=i: e.tensor_tensor(
                    tmps[:].rearrange("p (s t) -> p s t", t=4), pt[:, 0:NS].rearrange("p (s t) -> p s t", t=4),
                    mods[:, 2, i, :].unsqueeze(2).broadcast_to([128, 16, 4]), ALU.mult),
                    reads=[rp, r_const], writes=[r_tmps])
                b.op("dve", lambda e, xi=xi: e.tensor_tensor(xi[:, NP:NQC], xi[:, NP:NQC], tmps[:], ALU.add),
                     reads=[r_tmps, rxi], writes=[rxi])
        outs.append(b.dma("sp", xo[:, i, :], xi, osem[i % 2], reads=[rxi]))
        load_w(wn + 4)
    outs.append(b.dma("sp", kout, koutS[:], d[15], reads=[r_ko]))
    outs.append(b.dma("sp", vout, voutS[:], d[7], reads=[r_vo]))
    b.wait_all("sp", outs)
    b.emit()
    b.close()
    return nc


def build_adaln():
    nc = bass.Bass("TRN2", target_bir_lowering=False)
    NSEQ = 129
    NCH = 24
    dram = lambda name, shape, kind="ExternalInput": nc.dram_tensor(name, list(shape), F32, kind=kind).ap()
    cT = dram("cT", [128, KC, NSEQ])
    wada = dram("wada", [NCH, 128, KC, 128])
    bada = dram("bada", [128, NCH])
    modT = dram("modT", [128, NCH, NSEQ], "ExternalOutput")
    b = Builder(nc, n_dsem=8)
    d = b.dsems
    pr = PsumRot(b)
    cs = b.sb("cs", [128, KC, NSEQ], F32)
    sc = b.sb("sc", [128, KC, NSEQ], BF16)
    bs = b.sb("bs", [128, NCH], F32)
    outT = b.sb("outT", [128, NCH, NSEQ], F32)
    NWB = 4
    wr = [b.sb("wr%d" % i, [128, KC, 128], BF16) for i in range(NWB)]
    r_c, r_sc, r_b, r_out = Res(), Res(), Res(), Res()
    r_wr = [Res() for _ in range(NWB)]
    b.dma("sp", cs[:], cT, d[0], writes=[r_c])
    b.dma("sp", bs[:], bada, d[1], writes=[r_b])

    def load_w(n):
        if n < NCH:
            b.dma("pool", wr[n % NWB][:], wada[n], d[2 + n % NWB], writes=[r_wr[n % NWB]])
    for n in range(NWB - 1):
        load_w(n)
    b.op("act", lambda e: e.activation(sc[:], cs[:], AF.Silu), reads=[r_c], writes=[r_sc])
    for n in range(NCH):
        load_w(n + NWB - 1)
        w, rw = wr[n % NWB], r_wr[n % NWB]
        pt, rp = pr.next()
        for k in range(KC):
            b.op("pe", lambda e, pt=pt, w=w, k=k: e.matmul(pt[:, 0:NSEQ], w[:, k, :], sc[:, k, :], start=(k == 0), stop=(k == KC - 1)),
                 reads=[rw, r_sc], writes=[rp], sig=(k == KC - 1))
        b.op("act", lambda e, pt=pt, n=n: e.activation(outT[:, n, :], pt[:, 0:NSEQ], AF.Identity, bias=bs[:, n:n + 1], scale=1.0),
             reads=[rp, r_b], writes=[r_out])
    t = b.dma("sp", modT, outT[:], d[6], reads=[r_out])
    b.wait_all("sp", [t])
    b.emit(); b.close()
    return nc


NHH = 16
CH = 32
NRK = 7


def cumsum_chunks(b, engs, bufA, rA, bufB, rB, ncol0, ncols, clen):
    src, rs, dst, rd = bufA, rA, bufB, rB
    s = 1
    i = 0
    while s < clen:
        sv = src[:, ncol0:ncol0 + ncols].rearrange("p (c t) -> p c t", t=clen)
        dv = dst[:, ncol0:ncol0 + ncols].rearrange("p (c t) -> p c t", t=clen)
        eng = engs[i % len(engs)]
        b.op(eng, lambda e, dv=dv, sv=sv, s=s: e.tensor_tensor(dv[:, :, s:clen], sv[:, :, s:clen], sv[:, :, 0:clen - s], ALU.add),
             reads=[rs], writes=[rd])
        b.op(eng, lambda e, dv=dv, sv=sv, s=s: e.tensor_copy(dv[:, :, 0:s], sv[:, :, 0:s]), reads=[rs], writes=[rd])
        src, rs, dst, rd = dst, rd, src, rs
        s *= 2
        i += 1
    return src, rs


def build_hgrn(pass1, modeA=False):
    nc = bass.Bass("TRN2", target_bir_lowering=False)
    NT = NP if pass1 else NP + NS
    NBLK = NP // 128
    NCK = NP // CH
    dram = lambda name, shape, kind="ExternalInput": nc.dram_tensor(name, list(shape), F32, kind=kind).ap()
    xT = dram("xT", [128, KC, NT])
    vecd = dram("vec", [128, 4, KC])
    modsd = dram("mods", [128, 3, KC, 16])
    whg = dram("whg", [NHH, 128, 4, KC, 128])
    lbpd = dram("lbp", [128, 2, NHH])
    m01d = dram("m01", [128, 128]); cmd = dram("cm", [128, 4, 128])
    identd = dram("ident", [128, 128])
    if not pass1:
        if not modeA:
            wo = dram("wo", [KC, 128, KC, 128])
            ngd = dram("ng", [128, NHH])
            srd = dram("sr", [NHH, 128, NRK, 128])
            drd = dram("dr", [128, NHH, NRK])
            xo = dram("xo", [128, KC, NT], "ExternalOutput")
        else:
            oloc = dram("oloc", [NHH, 128, NT], "ExternalOutput")
            qbo = dram("qbo", [NHH, 128, NT], "ExternalOutput")
            sgo = dram("sgo", [NHH, 128, NT], "ExternalOutput")
        s0d = dram("s0", [NHH, 128, 16, 128])
        msd = dram("ms", [64, 64]); cmsd = dram("cms", [128, 16, 64])
        snew = dram("snew", [NHH, 128, 16, 128], "ExternalOutput")
    send = dram("send", [128, NHH, 128], "ExternalOutput")
    dout = dram("dout", [128, NHH], "ExternalOutput")

    b = Builder(nc, n_dsem=18)
    d = b.dsems
    pr = PsumRot(b)
    xbuf = b.sb("xbuf", [128, KC, NT], F32)
    x = xbuf
    O2T = xbuf[:].rearrange("p k n -> p (k n)").bitcast(BF16)[:, 0:KC * NT].rearrange("p (k n) -> p k n", k=KC)
    h = b.sb("h", [128, KC, NT], BF16)
    rstd = b.sb("rstd", [128, NT], F32)
    xsq0 = b.sb("xsq0", [128, NT], BF16)
    t32a = b.sb("t32a", [128, NT], F32)
    NWB = 5
    wring = [b.sb("wr%d" % i, [128, KC, 128], BF16) for i in range(NWB)]
    qf = b.sb("qf", [128, NT], F32); kf = b.sb("kf", [128, NT], F32)
    bA = b.sb("bA", [128, NT], F32); bB = b.sb("bB", [128, NT], F32)
    sg = b.sb("sg", [128, NT], F32); Oraw = b.sb("Oraw", [128, NT], F32)
    qt = b.sb("qt", [128, NT], BF16); kt = b.sb("kt", [128, NT], BF16); kh = b.sb("kh", [128, NT], BF16)
    dec = b.sb("dec", [128, NCK + 16], F32)
    Vt = b.sb("Vt", [128, NBLK + 1, 128], BF16)
    At = b.sb("At", [128, 128], BF16)
    Khm = b.sb("Khm", [128, 4, 128], BF16)
    KhmT = b.sb("KhmT", [128, 4, 128], BF16)
    S = b.sb("S", [128, 128], F32); Sbf = b.sb("Sbf", [128, 128], BF16)
    Dall = b.sb("Dall", [128, NHH], F32)
    btot = b.sb("btot", [128, 1], F32)
    vec = b.sb("vecs", [128, 4, KC], F32)
    mods = b.sb("modss", [128, 3, KC, 16], F32)
    lbp = b.sb("lbps", [128, 2, NHH], F32)
    oml = b.sb("oml", [128, NHH], F32)
    m01 = b.sb("m01s", [128, 128], F32); cm = b.sb("cms_", [128, 4, 128], F32)
    ident = b.sb("idents", [128, 128], BF16); identf = b.sb("identf", [128, 128], F32)
    ones = b.sb("ones", [128, 128], BF16)
    gm = b.sb("gm", [128, KC], F32); gms = b.sb("gms", [128, KC, 16], F32)
    if not pass1:
        if not modeA:
            ng = b.sb("ngs", [128, NHH], F32)
            sr = b.sb("srs", [128, NRK, 128], F32)
            dr = b.sb("drs", [128, NHH, NRK], F32)
        else:
            QBf = b.sb("QBf", [128, NT], F32)
            pbA = b.sb("pbA", [128, NCK], F32); pbB = b.sb("pbB", [128, NCK], F32)
            r_QBf, r_pbA, r_pbB = Res("QBf"), Res("pbA"), Res("pbB")
        S0 = b.sb("S0", [128, 16, 128], F32); S0b = b.sb("S0b", [128, 16, 128], BF16)
        ms = b.sb("mss", [64, 64], F32); cms = b.sb("cmss", [128, 16, 64], F32)
        Ats = b.sb("Ats", [64, 64], BF16)
        Khms = b.sb("Khms", [128, 16, 64], BF16)
        KhmTs = b.sb("KhmTs", [64, 16, 128], BF16)
        tmps = b.sb("tmps", [128, NS], F32)
    R = Res
    r_x, r_h, r_rstd, r_const, r_gm, r_gms = [R(n) for n in "x h rstd const gm gms".split()]
    r_xsq0, r_t32a = R("xsq"), R("t32")
    r_wr = [R("wr%d" % i) for i in range(NWB)]
    r_qf, r_kf, r_bA, r_bB, r_sg, r_Oraw, r_qt, r_kt, r_kh, r_dec, r_Vt, r_At, r_Khm, r_KhmT, r_S, r_Sbf, r_Sall, r_bt = [
        R(n) for n in "qf kf bA bB sg Oraw qt kt kh dec Vt At Khm KhmT S Sbf Sall bt".split()]
    r_sr, r_S0, r_S0b, r_Ats, r_Khms, r_KhmTs, r_tmps = [R(n) for n in "sr S0 S0b Ats Khms KhmTs tmps".split()]
    r_O2T = r_x

    b.dma("sp", x[:], xT, d[0], writes=[r_x])
    cl = [(vec, vecd), (mods, modsd), (lbp, lbpd), (m01, m01d), (cm, cmd), (identf, identd)]
    if not pass1:
        cl += [(ms, msd), (cms, cmsd)] + ([] if modeA else [(ng, ngd), (dr, drd)])
    for dst, src in cl:
        b.dma("sp", dst[:], src, d[1], writes=[r_const])
    b.op("pool", lambda e: e.memset(ones[:], 1.0), writes=[r_const])
    b.op("act", lambda e: e.activation(ident[:], identf[:], AF.Copy), reads=[r_const], writes=[r_const])
    b.op("dve", lambda e: e.tensor_tensor(oml[:], lbp[:, 0, :], lbp[:, 1, :], ALU.subtract), reads=[r_const], writes=[r_const])
    b.op("act", lambda e: e.activation(oml[:], oml[:], AF.Sigmoid), reads=[r_const], writes=[r_const])

    wsem = [d[2], d[3], d[4], d[5], d[6]]
    NWT = NHH * 4 + (0 if (pass1 or modeA) else KC)
    used = [1, 2] if pass1 else [0, 1, 2, 3]
    wlist = [(hh, wh) for hh in range(NHH) for wh in used] + ([("o", i) for i in range(KC)] if not (pass1 or modeA) else [])

    def load_w(n):
        if n >= len(wlist):
            return
        a, c = wlist[n]
        src = wo[c] if a == "o" else whg[a][:, c, :, :]
        b.dma("pool", wring[n % NWB][:], src, wsem[n % NWB], writes=[r_wr[n % NWB]])
    for n in range(4):
        load_w(n)
    wcount = [0]

    def next_w():
        n = wcount[0]
        wcount[0] += 1
        return n, wring[n % NWB], r_wr[n % NWB]

    emit_norm_mod(b, pr, x, r_x, h, r_h, NT, NP, vec, 0, 1, 2, mods, 0, 1, ones, r_const,
                  ([xsq0, xsq0], [r_xsq0, r_xsq0], rstd, r_rstd, gm, r_gm, gms, r_gms, [t32a, t32a], [r_t32a, r_t32a]))

    def proj_fm(dst_fn):
        n_, w, rw = next_w()
        for (s, n) in ntiles(NT):
            pt, rp = pr.next()
            for k in range(KC):
                b.op("pe", lambda e, pt=pt, w=w, k=k, s=s, n=n: e.matmul(
                    pt[:, 0:n], w[:, k, :], h[:, k, s:s + n], start=(k == 0), stop=(k == KC - 1)),
                    reads=[rw, r_h], writes=[rp], sig=(k == KC - 1))
            dst_fn(pt, rp, s, n)
        load_w(n_ + 4)

    outs = []
    for hh in range(NHH):
        if not pass1:
            if not modeA:
                b.dma("sp", sr[:], srd[hh], d[7], writes=[r_sr])
            b.dma("sp", S0[:], s0d[hh], d[8], writes=[r_S0])
            b.dma("pool", S0b[:], s0d[hh], d[9], writes=[r_S0b])
            proj_fm(lambda pt, rp, s, n: b.op("act", lambda e: e.activation(qf[:, s:s + n], pt[:, 0:n], AF.Silu),
                                              reads=[rp], writes=[r_qf]))
        proj_fm(lambda pt, rp, s, n: b.op("act", lambda e: e.activation(kf[:, s:s + n], pt[:, 0:n], AF.Sigmoid, scale=-1.0),
                                          reads=[rp], writes=[r_kf]))
        b.op("dve", lambda e, hh=hh: e.tensor_scalar(kf[:], kf[:], oml[:, hh:hh + 1], None, ALU.mult),
             reads=[r_kf, r_const], writes=[r_kf])
        b.op("act", lambda e: e.activation(bA[:], kf[:], AF.Ln, bias=1.0, scale=-1.0), reads=[r_kf], writes=[r_bA])
        n_, w, rw = next_w()
        for blk in range(NBLK + (0 if pass1 else 1)):
            m = 128 if blk < NBLK else NS
            c0 = blk * 128
            pt, rp = pr.next()
            for k in range(KC):
                b.op("pe", lambda e, pt=pt, w=w, k=k, c0=c0, m=m: e.matmul(
                    pt[0:m, 0:128], h[:, k, c0:c0 + m], w[:, k, :], start=(k == 0), stop=(k == KC - 1)),
                    reads=[rw, r_h], writes=[rp], sig=(k == KC - 1))
            b.op("act", lambda e, pt=pt, blk=blk, m=m: e.activation(Vt[0:m, blk, :], pt[0:m, 0:128], AF.Copy),
                 reads=[rp], writes=[r_Vt])
        load_w(n_ + 4)
        if not pass1:
            proj_fm(lambda pt, rp, s, n: b.op("act", lambda e: e.activation(sg[:, s:s + n], pt[:, 0:n], AF.Silu),
                                              reads=[rp], writes=[r_sg]))
        bb, rbb = cumsum_chunks(b, ["dve", "pool"], bA, r_bA, bB, r_bB, 0, NP, CH)
        other, rother = (bB, r_bB) if bb is bA else (bA, r_bA)
        if not pass1:
            if bb is not bA:
                b.op("pool", lambda e: e.tensor_copy(bB[:, NP:NT], bA[:, NP:NT]), reads=[r_bA], writes=[r_bB])
            sb_, rsb = cumsum_chunks(b, ["pool"], bb, rbb, other, rother, NP, NS, 4)
            if sb_ is not bb:
                b.op("pool", lambda e, sb_=sb_, bb=bb: e.tensor_copy(bb[:, NP:NT], sb_[:, NP:NT]), reads=[rsb], writes=[rbb])
        bv = bb[:, 0:NP].rearrange("p (c t) -> p c t", t=CH)
        ov = other[:, 0:NP].rearrange("p (c t) -> p c t", t=CH)
        b.op("act", lambda e, bv=bv: e.activation(dec[:, 0:NCK].unsqueeze(2), bv[:, :, CH - 1:CH], AF.Exp), reads=[rbb], writes=[r_dec])
        b.op("dve", lambda e, bv=bv, ov=ov: e.tensor_tensor(ov, bv[:, :, CH - 1:CH].broadcast_to([128, NCK, CH]), bv, ALU.subtract),
             reads=[rbb], writes=[rother])
        if not pass1:
            bs_ = bb[:, NP:NT].rearrange("p (c t) -> p c t", t=4)
            os_ = other[:, NP:NT].rearrange("p (c t) -> p c t", t=4)
            b.op("act", lambda e, bs_=bs_: e.activation(dec[:, NCK:NCK + 16].unsqueeze(2), bs_[:, :, 3:4], AF.Exp), reads=[rbb], writes=[r_dec])
            b.op("dve", lambda e, bs_=bs_, os_=os_: e.tensor_tensor(os_, bs_[:, :, 3:4].broadcast_to([128, 16, 4]), bs_, ALU.subtract),
                 reads=[rbb], writes=[rother])
        b.op("act", lambda e, other=other: e.activation(other[:, 0:NT], other[:, 0:NT], AF.Exp), reads=[rother], writes=[rother])
        b.op("dve", lambda e, other=other: e.tensor_tensor(kh[:], kf[:], other[:, 0:NT], ALU.mult), reads=[rother, r_kf], writes=[r_kh])
        if pass1:
            b.op("dve", lambda e, bv=bv: e.tensor_reduce(btot[:], bv[:, :, CH - 1], AX.X, ALU.add), reads=[rbb], writes=[r_bt])
            b.op("act", lambda e, hh=hh: e.activation(Dall[:, hh:hh + 1], btot[:], AF.Exp), reads=[r_bt], writes=[r_Sall])
        else:
            b.op("act", lambda e, other=other, bb=bb: e.activation(other[:, 0:NT], bb[:, 0:NT], AF.Exp), reads=[rbb, r_kh], writes=[rother])
            b.op("dve", lambda e, other=other: e.tensor_tensor(qt[:], qf[:], other[:, 0:NT], ALU.mult), reads=[rother, r_qf], writes=[r_qt])
            b.op("act", lambda e, other=other, bb=bb: e.activation(other[:, 0:NT], bb[:, 0:NT], AF.Exp, scale=-1.0), reads=[rbb, r_qt], writes=[rother])
            b.op("dve", lambda e, other=other: e.tensor_tensor(kt[:], kf[:], other[:, 0:NT], ALU.mult), reads=[rother, r_kf], writes=[r_kt])
            if modeA:
                b.op("pool", lambda e, bv=bv: e.tensor_copy(pbA[:].unsqueeze(2), bv[:, :, CH - 1:CH]), reads=[rbb], writes=[r_pbA])
                pin, rpin = cumsum_chunks(b, ["pool"], pbA, r_pbA, pbB, r_pbB, 0, NCK, NCK)
                pex, rpex = (pbB, r_pbB) if pin is pbA else (pbA, r_pbA)
                b.op("act", lambda e, hh=hh, pin=pin: e.activation(Dall[:, hh:hh + 1], pin[:, NCK - 1:NCK], AF.Exp), reads=[rpin], writes=[r_Sall])
                b.op("pool", lambda e, pin=pin, pex=pex, bv=bv: e.tensor_tensor(pex[:].unsqueeze(2), pin[:].unsqueeze(2), bv[:, :, CH - 1:CH], ALU.subtract),
                     reads=[rpin, rbb], writes=[rpex])
                b.op("act", lambda e, pex=pex: e.activation(pex[:], pex[:], AF.Exp), reads=[rpex], writes=[rpex])
                b.op("pool", lambda e, pex=pex: e.tensor_tensor(
                    QBf[:, 0:NP].rearrange("p (c t) -> p c t", t=CH), qt[:, 0:NP].rearrange("p (c t) -> p c t", t=CH),
                    pex[:].unsqueeze(2).broadcast_to([128, NCK, CH]), ALU.mult), reads=[rpex, r_qt], writes=[r_QBf])
                b.op("pool", lambda e: e.memset(QBf[:, NP:NT], 0.0), writes=[r_QBf])
                outs.append(b.dma("sp", qbo[hh], QBf[:], d[7], reads=[r_QBf]))
                outs.append(b.dma("sp", sgo[hh], sg[:], d[13], reads=[r_sg]))
        if pass1 or modeA:
            b.op("pool", lambda e: e.memset(S[:], 0.0), writes=[r_S])
            if modeA:
                b.op("pool", lambda e: e.memset(Sbf[:], 0.0), writes=[r_Sbf])
        else:
            b.op("dve", lambda e, hh=hh: e.tensor_scalar(S[:], sr[:, 0, :], 1.0, None, ALU.mult), reads=[r_sr], writes=[r_S])
            for r in range(1, NRK):
                b.op("dve", lambda e, hh=hh, r=r: e.scalar_tensor_tensor(S[:], S[:], dr[:, hh, r:r + 1], sr[:, r, :], ALU.mult, ALU.add),
                     reads=[r_S, r_sr, r_const], writes=[r_S])
            b.op("act", lambda e: e.activation(Sbf[:], S[:], AF.Copy), reads=[r_S], writes=[r_Sbf])
        for blk in range(NBLK):
            c0 = blk * 128
            b.op("dve", lambda e, c0=c0: e.tensor_tensor(Khm[:], kh[:, c0:c0 + 128].unsqueeze(1).broadcast_to([128, 4, 128]), cm[:], ALU.mult),
                 reads=[r_kh, r_const], writes=[r_Khm])
            ptT, rpT = pr.next()
            ptTb = ptT[:, :].bitcast(BF16)
            for c in range(4):
                b.op("pe", lambda e, ptTb=ptTb, c=c: e.transpose(ptTb[:, c * 128:(c + 1) * 128], Khm[:, c, :], ident[:]),
                     reads=[r_Khm, r_const], writes=[rpT], sig=(c == 3))
            b.op("act", lambda e, ptTb=ptTb: e.activation(KhmT[:].rearrange("p c k -> p (c k)"), ptTb[:, 0:512], AF.Copy),
                 reads=[rpT], writes=[r_KhmT])
            if not pass1:
                pa, rpa = pr.next()
                b.op("pe", lambda e, pa=pa, c0=c0: e.matmul(pa[:, 0:128], kt[:, c0:c0 + 128], qt[:, c0:c0 + 128], start=True, stop=True),
                     reads=[r_kt, r_qt], writes=[rpa])
                b.op("dve", lambda e, pa=pa: e.tensor_tensor(At[:], pa[:, 0:128], m01[:], ALU.mult), reads=[rpa, r_const], writes=[r_At])
                po, rpo = pr.next()
                b.op("pe", lambda e, po=po, blk=blk: e.matmul(po[:, 0:128], Vt[:, blk, :], At[:], start=True, stop=False),
                     reads=[r_Vt, r_At], writes=[rpo], sig=False)
            for c in range(4):
                ck = blk * 4 + c
                if not pass1:
                    b.op("pe", lambda e, po=po, c=c, c0=c0: e.matmul(po[:, c * CH:(c + 1) * CH], Sbf[:], qt[:, c0 + c * CH:c0 + (c + 1) * CH],
                                                                   start=False, stop=(c == 3)),
                         reads=[r_Sbf, r_qt], writes=[rpo], sig=True)
                pu, rpu = pr.next()
                b.op("pe", lambda e, pu=pu, c=c, blk=blk: e.matmul(pu[:, 0:128], KhmT[:, c, :], Vt[:, blk, :], start=True, stop=True),
                     reads=[r_KhmT, r_Vt], writes=[rpu])
                b.op("dve", lambda e, pu=pu, ck=ck: e.scalar_tensor_tensor(S[:], S[:], dec[:, ck:ck + 1], pu[:, 0:128], ALU.mult, ALU.add),
                     reads=[rpu, r_S, r_dec], writes=[r_S])
                if not pass1:
                    b.op("act", lambda e: e.activation(Sbf[:], S[:], AF.Copy), reads=[r_S], writes=[r_Sbf])
            if not pass1:
                b.op("act", lambda e, po=po, c0=c0: e.activation(Oraw[:, c0:c0 + 128], po[:, 0:128], AF.Copy), reads=[rpo], writes=[r_Oraw])
        outs.append(b.dma("sp", send[:, hh, :], S[:], d[11], reads=[r_S]))
        if pass1:
            continue
        b.op("dve", lambda e: e.tensor_tensor(Khms[:], kh[:, NP:NT].unsqueeze(1).broadcast_to([128, 16, 64]), cms[:], ALU.mult),
             reads=[r_kh, r_const], writes=[r_Khms])
        for half in range(2):
            ptT, rpT = pr.next()
            ptTb = ptT[:, :].bitcast(BF16)
            for s8 in range(8):
                sq = half * 8 + s8
                b.op("pe", lambda e, ptTb=ptTb, s8=s8, sq=sq: e.transpose(ptTb[0:64, s8 * 128:(s8 + 1) * 128], Khms[:, sq, :], ident[:]),
                     reads=[r_Khms, r_const], writes=[rpT], sig=(s8 == 7))
            b.op("act", lambda e, ptTb=ptTb, half=half: e.activation(
                KhmTs[:, half * 8:half * 8 + 8, :].rearrange("p c k -> p (c k)"), ptTb[0:64, 0:1024], AF.Copy),
                reads=[rpT], writes=[r_KhmTs])
        pa, rpa = pr.next()
        b.op("pe", lambda e, pa=pa: e.matmul(pa[0:64, 0:64], kt[:, NP:NT], qt[:, NP:NT], start=True, stop=True),
             reads=[r_kt, r_qt], writes=[rpa])
        b.op("dve", lambda e, pa=pa: e.tensor_tensor(Ats[:], pa[0:64, 0:64], ms[:], ALU.mult), reads=[rpa, r_const], writes=[r_Ats])
        po, rpo = pr.next()
        b.op("pe", lambda e, po=po: e.matmul(po[:, 0:64], Vt[0:64, NBLK, :], Ats[:], start=True, stop=False),
             reads=[r_Vt, r_Ats], writes=[rpo], sig=False)
        for sq in range(16):
            b.op("pe", lambda e, po=po, sq=sq: e.matmul(po[:, sq * 4:sq * 4 + 4], S0b[:, sq, :], qt[:, NP + sq * 4:NP + sq * 4 + 4],
                                                       start=False, stop=(sq == 15)),
                 reads=[r_S0b, r_qt], writes=[rpo], sig=(sq == 15))
        b.op("act", lambda e, po=po: e.activation(Oraw[:, NP:NT], po[:, 0:64], AF.Copy), reads=[rpo], writes=[r_Oraw])
        for q4 in range(4):
            pu, rpu = pr.next()
            for s4 in range(4):
                sq = q4 * 4 + s4
                b.op("pe", lambda e, pu=pu, s4=s4, sq=sq: e.matmul(pu[:, s4 * 128:(s4 + 1) * 128], KhmTs[:, sq, :], Vt[0:64, NBLK, :],
                                                                 start=True, stop=True),
                     reads=[r_KhmTs, r_Vt], writes=[rpu], sig=(s4 == 3))
            for s4 in range(4):
                sq = q4 * 4 + s4
                b.op("dve", lambda e, pu=pu, s4=s4, sq=sq: e.scalar_tensor_tensor(
                    S0[:, sq, :], S0[:, sq, :], dec[:, NCK + sq:NCK + sq + 1], pu[:, s4 * 128:(s4 + 1) * 128], ALU.mult, ALU.add),
                    reads=[rpu, r_S0, r_dec], writes=[r_S0])
        outs.append(b.dma("sp", snew[hh], S0[:], d[10], reads=[r_S0]))
        if modeA:
            outs.append(b.dma("sp", oloc[hh], Oraw[:], d[14], reads=[r_Oraw]))
            continue
        b.op("act", lambda e: e.activation(xsq0[:], Oraw[:], AF.Square), reads=[r_Oraw], writes=[r_xsq0])
        tl = ntiles(NT)
        bk = [pr.next() for _ in tl]
        for ti, (s, n) in enumerate(tl):
            pt, rp = bk[ti]
            b.op("pe", lambda e, pt=pt, s=s, n=n: e.matmul(pt[:, 0:n], ones[:], xsq0[:, s:s + n], start=True, stop=True),
                 reads=[r_xsq0, r_const], writes=[rp])
            b.op("act", lambda e, pt=pt, s=s, n=n: e.activation(rstd[:, s:s + n], pt[:, 0:n], AF.Sqrt, bias=EPS, scale=1.0 / 128),
                 reads=[rp], writes=[r_rstd])
        b.op("dve", lambda e: e.reciprocal(rstd[:], rstd[:]), reads=[r_rstd], writes=[r_rstd])
        b.op("dve", lambda e, hh=hh: e.scalar_tensor_tensor(Oraw[:], Oraw[:], ng[:, hh:hh + 1], rstd[:], ALU.mult, ALU.mult),
             reads=[r_Oraw, r_rstd, r_const], writes=[r_Oraw])
        b.op("dve", lambda e, hh=hh: e.tensor_tensor(O2T[:, hh, :], Oraw[:], sg[:], ALU.mult), reads=[r_Oraw, r_sg, r_x], writes=[r_O2T])

    if pass1 or modeA:
        outs.append(b.dma("sp", dout, Dall[:], d[12], reads=[r_Sall]))
    else:
        b.op("pool", lambda e: e.memset(Dall[:], 0.0), writes=[r_Sall])
        outs.append(b.dma("sp", dout, Dall[:], d[12], reads=[r_Sall]))
        xr = [t32a, rstd]; r_xr = [r_t32a, r_rstd]
        xsem = [d[13], d[14]]; osem = [d[15], d[16]]
        for i in range(KC):
            n_, w, rw = next_w()
            xi = xr[i % 2]; rxi = r_xr[i % 2]
            b.dma("sp", xi[:], xT[:, i, :], xsem[i % 2], writes=[rxi])
            for (s, n) in ntiles(NT):
                pt, rp = pr.next()
                for k in range(KC):
                    b.op("pe", lambda e, pt=pt, w=w, k=k, s=s, n=n: e.matmul(
                        pt[:, 0:n], w[:, k, :], O2T[:, k, s:s + n], start=(k == 0), stop=(k == KC - 1)),
                        reads=[rw, r_O2T], writes=[rp], sig=(k == KC - 1))
                if s + n <= NP:
                    b.op("dve", lambda e, pt=pt, i=i, s=s, n=n, xi=xi: e.scalar_tensor_tensor(
                        xi[:, s:s + n], pt[:, 0:n], vec[:, 3, i:i + 1], xi[:, s:s + n], ALU.mult, ALU.add),
                        reads=[rp, r_const, rxi], writes=[rxi])
                else:
                    b.op("dve", lambda e, pt=pt, i=i: e.tensor_tensor(
                        tmps[:].rearrange("p (s t) -> p s t", t=4), pt[:, 0:NS].rearrange("p (s t) -> p s t", t=4),
                        mods[:, 2, i, :].unsqueeze(2).broadcast_to([128, 16, 4]), ALU.mult),
                        reads=[rp, r_const], writes=[r_tmps])
                    b.op("dve", lambda e, xi=xi: e.tensor_tensor(xi[:, NP:NT], xi[:, NP:NT], tmps[:], ALU.add),
                         reads=[r_tmps, rxi], writes=[rxi])
            outs.append(b.dma("sp", xo[:, i, :], xi[:], osem[i % 2], reads=[rxi]))
            load_w(n_ + 4)
    b.wait_all("sp", outs)
    b.emit(); b.close()
    return nc


def build_hgrnb():
    nc = bass.Bass("TRN2", target_bir_lowering=False)
    NT = NP + NS
    dram = lambda name, shape, kind="ExternalInput": nc.dram_tensor(name, list(shape), F32, kind=kind).ap()
    xT = dram("xT", [128, KC, NT])
    vecd = dram("vec", [128, 4, KC])
    modsd = dram("mods", [128, 3, KC, 16])
    olocd = dram("oloc", [NHH, 128, NT]); qbd = dram("qb", [NHH, 128, NT]); sgd = dram("sg", [NHH, 128, NT])
    srd = dram("sr", [NHH, 128, NRK, 128]); drd = dram("dr", [128, NHH, NRK])
    slocd = dram("sloc", [128, NHH, 128]); dld = dram("dl", [128, NHH])
    ngd = dram("ng", [128, NHH])
    wo = dram("wo", [KC, 128, KC, 128])
    xo = dram("xo", [128, KC, NT], "ExternalOutput")
    send = dram("send", [128, NHH, 128], "ExternalOutput")

    b = Builder(nc, n_dsem=18)
    d = b.dsems
    pr = PsumRot(b)
    O2T = b.sb("O2T", [128, KC, NT], BF16)
    ol = [b.sb("ol%d" % i, [128, NT], F32) for i in range(2)]
    qbb = [b.sb("qbb%d" % i, [128, NT], BF16) for i in range(2)]
    sgl = [b.sb("sgl%d" % i, [128, NT], F32) for i in range(2)]
    srall = b.sb("srall", [128, NHH, NRK, 128], F32)
    Sall = b.sb("Sall", [128, NHH, 128], F32)
    Oraws = [b.sb("Oraw%d" % i, [128, NT], F32) for i in range(2)]
    xsqs = [b.sb("xsq%d" % i, [128, NT], BF16) for i in range(2)]
    rstds = [b.sb("rstd%d" % i, [128, NT], F32) for i in range(2)]
    Ss = [b.sb("S%d" % i, [128, 128], F32) for i in range(2)]; Sbfs = [b.sb("Sbf%d" % i, [128, 128], BF16) for i in range(2)]
    Se = b.sb("Se", [128, NHH, 128], F32)
    sloc = b.sb("slocs", [128, NHH, 128], F32)
    dr = b.sb("drs", [128, NHH, NRK], F32); dl = b.sb("dls", [128, NHH], F32); ng = b.sb("ngs", [128, NHH], F32)
    vec = b.sb("vecs", [128, 4, KC], F32); mods = b.sb("modss", [128, 3, KC, 16], F32)
    ones = b.sb("ones", [128, 128], BF16)
    NWB = 4
    wring = [b.sb("wr%d" % i, [128, KC, 128], BF16) for i in range(NWB)]
    xr = [b.sb("xr%d" % i, [128, NT], F32) for i in range(2)]
    tmps = b.sb("tmps", [128, NS], F32)
    R = Res
    r_O2T, r_Se, r_const, r_tmps = [R(n) for n in "O2T Se const tmps".split()]
    r_Oraws = [R("a"), R("b")]; r_xsqs = [R("a"), R("b")]; r_rstds = [R("a"), R("b")]; r_Ss = [R("a"), R("b")]; r_Sbfs = [R("a"), R("b")]
    r_ol = [R("a"), R("b")]; r_qbb = [R("a"), R("b")]; r_sgl = [R("a"), R("b")]; r_srall = R("srall"); r_Sall = R("Sall")
    r_wr = [R("w%d" % i) for i in range(NWB)]; r_xr = [R("a"), R("b")]
    for dst, src in [(vec, vecd), (mods, modsd), (sloc, slocd), (dr, drd), (dl, dld), (ng, ngd)]:
        b.dma("sp", dst[:], src, d[0], writes=[r_const])
    b.op("pool", lambda e: e.memset(ones[:], 1.0), writes=[r_const])

    def load_w(i):
        if i < KC:
            b.dma("pool", wring[i % NWB][:], wo[i], d[1 + i % NWB], writes=[r_wr[i % NWB]])

    def load_head(hh):
        if hh >= NHH:
            return
        i = hh % 2
        b.dma("sp", ol[i][:], olocd[hh], d[5 + i], writes=[r_ol[i]])
        b.dma("pool", qbb[i][:], qbd[hh], d[7 + i], writes=[r_qbb[i]])
        b.dma("sp", sgl[i][:], sgd[hh], d[9 + i], writes=[r_sgl[i]])
    b.dma("sp", srall[:], srd.rearrange("h p r v -> p h r v"), d[11], writes=[r_srall])
    load_head(0)
    b.op("dve", lambda e: e.tensor_copy(Sall[:], srall[:, :, 0, :]), reads=[r_srall], writes=[r_Sall])
    for r in range(1, NRK):
        b.op("dve", lambda e, r=r: e.tensor_tensor(Sall[:], Sall[:], dr[:, :, r:r + 1].broadcast_to([128, NHH, 128]), ALU.mult),
             reads=[r_Sall, r_const], writes=[r_Sall])
        b.op("pool", lambda e, r=r: e.tensor_tensor(Sall[:], Sall[:], srall[:, :, r, :], ALU.add),
             reads=[r_Sall, r_srall], writes=[r_Sall])
    for i in range(NWB - 1):
        load_w(i)
    outs = []
    for hh in range(NHH):
        load_head(hh + 1)
        i2 = hh % 2
        Oraw, xsq0, rstd, S, Sbf = Oraws[i2], xsqs[i2], rstds[i2], Ss[i2], Sbfs[i2]
        r_Oraw, r_xsq0, r_rstd, r_S, r_Sbf = r_Oraws[i2], r_xsqs[i2], r_rstds[i2], r_Ss[i2], r_Sbfs[i2]
        b.op("act", lambda e, hh=hh: e.activation(Sbf[:], Sall[:, hh, :], AF.Copy), reads=[r_Sall], writes=[r_Sbf])
        b.op("dve", lambda e, hh=hh: e.scalar_tensor_tensor(Se[:, hh, :], Sall[:, hh, :], dl[:, hh:hh + 1], sloc[:, hh, :], ALU.mult, ALU.add),
             reads=[r_Sall, r_const], writes=[r_Se])
        for (s, n) in ntiles(NT):
            pt, rp = pr.next()
            b.op("pe", lambda e, pt=pt, s=s, n=n, i2=i2: e.matmul(pt[:, 0:n], Sbf[:], qbb[i2][:, s:s + n], start=True, stop=True),
                 reads=[r_Sbf, r_qbb[i2]], writes=[rp])
            b.op("dve", lambda e, pt=pt, s=s, n=n, i2=i2: e.tensor_tensor(Oraw[:, s:s + n], pt[:, 0:n], ol[i2][:, s:s + n], ALU.add),
                 reads=[rp, r_ol[i2]], writes=[r_Oraw])
        b.op("act", lambda e: e.activation(xsq0[:], Oraw[:], AF.Square), reads=[r_Oraw], writes=[r_xsq0])
        for (s, n) in ntiles(NT):
            pt, rp = pr.next()
            b.op("pe", lambda e, pt=pt, s=s, n=n: e.matmul(pt[:, 0:n], ones[:], xsq0[:, s:s + n], start=True, stop=True),
                 reads=[r_xsq0, r_const], writes=[rp])
            b.op("act", lambda e, pt=pt, s=s, n=n: e.activation(rstd[:, s:s + n], pt[:, 0:n], AF.Ln, bias=EPS, scale=1.0 / 128),
                 reads=[rp], writes=[r_rstd])
        b.op("act", lambda e: e.activation(rstd[:], rstd[:], AF.Exp, scale=-0.5), reads=[r_rstd], writes=[r_rstd])
        b.op("dve", lambda e, hh=hh: e.scalar_tensor_tensor(Oraw[:], Oraw[:], ng[:, hh:hh + 1], rstd[:], ALU.mult, ALU.mult),
             reads=[r_Oraw, r_rstd, r_const], writes=[r_Oraw])
        b.op("pool", lambda e, hh=hh, i2=i2: e.tensor_tensor(O2T[:, hh, :], Oraw[:], sgl[i2][:], ALU.mult),
             reads=[r_Oraw, r_sgl[i2]], writes=[r_O2T])
    outs.append(b.dma("sp", send, Se[:], d[13], reads=[r_Se]))
    for i in range(KC):
        load_w(i + NWB - 1)
        w, rw = wring[i % NWB], r_wr[i % NWB]
        xi, rxi = xr[i % 2], r_xr[i % 2]
        b.dma("sp", xi[:], xT[:, i, :], d[14 + i % 2], writes=[rxi])
        for (s, n) in ntiles(NT):
            pt, rp = pr.next()
            for k in range(KC):
                b.op("pe", lambda e, pt=pt, w=w, k=k, s=s, n=n: e.matmul(
                    pt[:, 0:n], w[:, k, :], O2T[:, k, s:s + n], start=(k == 0), stop=(k == KC - 1)),
                    reads=[rw, r_O2T], writes=[rp], sig=(k == KC - 1))
            if s + n <= NP:
                b.op("dve", lambda e, pt=pt, i=i, s=s, n=n, xi=xi: e.scalar_tensor_tensor(
                    xi[:, s:s + n], pt[:, 0:n], vec[:, 3, i:i + 1], xi[:, s:s + n], ALU.mult, ALU.add),
                    reads=[rp, r_const, rxi], writes=[rxi])
            else:
                b.op("dve", lambda e, pt=pt, i=i: e.tensor_tensor(
                    tmps[:].rearrange("p (s t) -> p s t", t=4), pt[:, 0:NS].rearrange("p (s t) -> p s t", t=4),
                    mods[:, 2, i, :].unsqueeze(2).broadcast_to([128, 16, 4]), ALU.mult),
                    reads=[rp, r_const], writes=[r_tmps])
                b.op("dve", lambda e, xi=xi: e.tensor_tensor(xi[:, NP:NT], xi[:, NP:NT], tmps[:], ALU.add),
                     reads=[r_tmps, rxi], writes=[rxi])
        outs.append(b.dma("sp", xo[:, i, :], xi[:], d[16 + i % 2], reads=[rxi]))
    b.wait_all("sp", outs)
    b.emit(); b.close()
    return nc


class _HSet:
    pass


FILL_RATIO = 1
CRIT_RATIO = 2


def build_hgrna():
    nc = bass.Bass("TRN2", target_bir_lowering=False)
    NT = NP + NS
    NBLK = NP // 128
    NCK = NP // CH
    dram = lambda name, shape, kind="ExternalInput": nc.dram_tensor(name, list(shape), F32, kind=kind).ap()
    xT = dram("xT", [128, KC, NT])
    vecd = dram("vec", [128, 4, KC]); modsd = dram("mods", [128, 3, KC, 16])
    whg = dram("whg", [NHH, 128, 4, KC, 128])
    lbpd = dram("lbp", [128, 2, NHH])
    m01d = dram("m01", [128, 128]); cmd = dram("cm", [128, 4, 128]); identd = dram("ident", [128, 128])
    s0d = dram("s0", [NHH, 128, 16, 128]); msd = dram("ms", [64, 64]); cmsd = dram("cms", [128, 16, 64])
    smd = dram("smask", [128, NT])
    oloc = dram("oloc", [NHH, 128, NT], "ExternalOutput")
    qbo = dram("qbo", [NHH, 128, NT], "ExternalOutput")
    sgo = dram("sgo", [NHH, 128, NT], "ExternalOutput")
    snew = dram("snew", [NHH, 128, 16, 128], "ExternalOutput")
    send = dram("send", [128, NHH, 128], "ExternalOutput")
    dout = dram("dout", [128, NHH], "ExternalOutput")

    b = Builder(nc, n_dsem=22)
    d = b.dsems
    pr = PsumRot(b, 4)
    pr_loop = PsumRot.__new__(PsumRot); pr_loop.tiles = pr.tiles[0:2]; pr_loop.res = pr.res[0:2]; pr_loop.i = 0
    pr_prep = PsumRot.__new__(PsumRot); pr_prep.tiles = pr.tiles[2:4]; pr_prep.res = pr.res[2:4]; pr_prep.i = 0
    po_banks = [(b.ps("pob%d" % i, [128, 512]), Res("pob%d" % i)) for i in range(2)]
    pu_banks = [(b.ps("pub%d" % i, [128, 512]), Res("pub%d" % i)) for i in range(2)]
    xbuf = b.sb("xbuf", [128, KC * NT], F32)
    x = xbuf[:].rearrange("p (k n) -> p k n", k=KC)
    h = b.sb("h", [128, KC, NT], BF16)
    rstd = b.sb("rstd", [128, NT], F32)
    xsq0 = b.sb("xsq0", [128, NT], BF16)
    t32a = b.sb("t32a", [128, NT], F32)
    NWB = 5
    wring = [b.sb("wr%d" % i, [128, KC, 128], BF16) for i in range(NWB)]
    Dall = b.sb("Dall", [128, NHH], F32)
    vec = b.sb("vecs", [128, 4, KC], F32); mods = b.sb("modss", [128, 3, KC, 16], F32)
    lbp = b.sb("lbps", [128, 2, NHH], F32); oml = b.sb("oml", [128, NHH], F32)
    m01 = b.sb("m01s", [128, 128], F32); cm = b.sb("cms_", [128, 4, 128], F32)
    ident = b.sb("idents", [128, 128], BF16); identf = b.sb("identf", [128, 128], F32)
    ones = b.sb("ones", [128, 128], BF16)
    gm = b.sb("gm", [128, KC], F32); gms = b.sb("gms", [128, KC, 16], F32)
    ms = b.sb("mss", [64, 64], F32); cms = b.sb("cmss", [128, 16, 64], F32)
    smask = b.sb("smasks", [128, NT], F32); onesf = b.sb("onesf", [128, NCK], F32)
    R = Res
    r_x, r_h, r_rstd, r_const, r_gm, r_gms, r_xsq0, r_t32a, r_D = [R(n) for n in "x h rstd const gm gms xsq t32 D".split()]
    r_wr = [R("wr%d" % i) for i in range(NWB)]

    f32_names = ["qf", "kf", "bA", "bB", "sg", "Oraw", "QBf"]
    bf_names = ["qt", "kt", "kh"]
    sets = []
    for si in range(2):
        B = _HSet()
        if si == 0:
            for nm in f32_names:
                if nm == "QBf":
                    B.QBf = t32a[:]
                elif nm == "Oraw":
                    B.Oraw = rstd[:]
                else:
                    setattr(B, nm, b.sb(nm + "0", [128, NT], F32)[:])
            for nm in bf_names:
                setattr(B, nm, b.sb(nm + "0", [128, NT], BF16)[:])
            B.Vt = b.sb("Vt0", [128, NBLK + 1, 128], BF16)[:]
            B.S0 = b.sb("S00", [128, 16, 128], F32)[:]; B.S0b = b.sb("S0b0", [128, 16, 128], BF16)[:]
            B.Khms = b.sb("Khms0", [128, 16, 64], BF16)[:]; B.KhmTs = b.sb("KhmTs0", [64, 16, 128], BF16)[:]
            B.At = b.sb("At0", [128, 128], BF16)[:]; B.Khm = b.sb("Khm0", [128, 4, 128], BF16)[:]
            B.KhmT = b.sb("KhmT0", [128, 4, 128], BF16)[:]
            B.At_b = b.sb("At0b", [128, 128], BF16)[:]; B.Khm_b = b.sb("Khm0b", [128, 4, 128], BF16)[:]
            B.KhmT_b = b.sb("KhmT0b", [128, 4, 128], BF16)[:]
            B.S = b.sb("S_0", [128, 128], F32)[:]; B.Sbf = b.sb("Sbf0", [128, 128], BF16)[:]
            B.S2 = b.sb("S2_0", [128, 128], F32)[:]; B.Sbf2 = b.sb("Sbf2_0", [128, 128], BF16)[:]
            B.dec = b.sb("dec0", [128, NCK + 16], F32)[:]
            B.pbA = b.sb("pbA0", [128, NCK], F32)[:]; B.pbB = b.sb("pbB0", [128, NCK], F32)[:]
            B.Ats = b.sb("Ats0", [64, 64], BF16)[:]
        else:
            off = [0]

            def carve(n_f32, dt, shape):
                v = xbuf[:, off[0]:off[0] + n_f32]
                off[0] += n_f32
                if dt is BF16:
                    v = v.bitcast(BF16)
                if len(shape) == 2:
                    return v[:, 0:shape[1]] if shape[0] == 128 else v[0:shape[0], 0:shape[1]]
                if len(shape) == 3:
                    vv = v[:, 0:shape[1] * shape[2]].rearrange("p (a c) -> p a c", a=shape[1])
                    return vv if shape[0] == 128 else vv[0:shape[0]]
            for nm in f32_names:
                setattr(B, nm, carve(NT, F32, [128, NT]))
            for nm in bf_names:
                setattr(B, nm, carve(NT // 2, BF16, [128, NT]))
            B.Vt = carve((NBLK + 1) * 64, BF16, [128, NBLK + 1, 128])
            B.S0 = carve(2048, F32, [128, 16, 128]); B.S0b = carve(1024, BF16, [128, 16, 128])
            B.Khms = carve(512, BF16, [128, 16, 64]); B.KhmTs = carve(1024, BF16, [64, 16, 128])
            B.At = carve(64, BF16, [128, 128]); B.Khm = carve(256, BF16, [128, 4, 128]); B.KhmT = carve(256, BF16, [128, 4, 128])
            B.At_b = carve(64, BF16, [128, 128]); B.Khm_b = carve(256, BF16, [128, 4, 128]); B.KhmT_b = carve(256, BF16, [128, 4, 128])
            B.S = carve(128, F32, [128, 128]); B.Sbf = carve(64, BF16, [128, 128])
            B.S2 = carve(128, F32, [128, 128]); B.Sbf2 = carve(64, BF16, [128, 128])
            B.dec = carve(NCK + 16, F32, [128, NCK + 16])
            B.pbA = carve(NCK, F32, [128, NCK]); B.pbB = carve(NCK, F32, [128, NCK])
            B.Ats = carve(32, BF16, [64, 64])
            assert off[0] <= KC * NT, off[0]
        for nm in f32_names + bf_names + ["Vt", "S0", "S0b", "Khms", "KhmTs", "At", "Khm", "KhmT", "At_b", "Khm_b", "KhmT_b", "S", "Sbf", "S2", "Sbf2", "dec", "pbA", "pbB", "Ats"]:
            setattr(B, "r_" + nm, Res(nm + str(si)))
        if si == 0:
            B.r_QBf = r_t32a
            B.r_Oraw = r_rstd
        sets.append(B)

    b.dma("sp", xbuf[:], xT.rearrange("p k n -> p (k n)"), d[0], writes=[r_x])
    for dst, src in [(vec, vecd), (mods, modsd), (lbp, lbpd), (m01, m01d), (cm, cmd), (identf, identd), (ms, msd), (cms, cmsd), (smask, smd)]:
        b.dma("sp", dst[:], src, d[1], writes=[r_const])
    b.op("pool", lambda e: e.memset(ones[:], 1.0), writes=[r_const])
    b.op("pool", lambda e: e.memset(Dall[:], 0.0), writes=[r_D])
    b.op("pool", lambda e: e.memset(onesf[:], 1.0), writes=[r_const])
    b.op("act", lambda e: e.activation(ident[:], identf[:], AF.Copy), reads=[r_const], writes=[r_const])
    b.op("dve", lambda e: e.tensor_tensor(oml[:], lbp[:, 0, :], lbp[:, 1, :], ALU.subtract), reads=[r_const], writes=[r_const])
    b.op("act", lambda e: e.activation(oml[:], oml[:], AF.Sigmoid), reads=[r_const], writes=[r_const])

    wsem = [d[2], d[3], d[4], d[5], d[6]]
    wlist = [(hh, wh) for hh in range(NHH) for wh in range(4)]

    def load_w(n):
        if n >= len(wlist):
            return
        a, c = wlist[n]
        b.dma("pool", wring[n % NWB][:], whg[a][:, c, :, :], wsem[n % NWB], writes=[r_wr[n % NWB]])
    for n in range(4):
        load_w(n)
    wcount = [0]

    def next_w():
        n = wcount[0]
        wcount[0] += 1
        return n, wring[n % NWB], r_wr[n % NWB]

    emit_norm_mod(b, pr, x, r_x, h, r_h, NT, NP, vec, 0, 1, 2, mods, 0, 1, ones, r_const,
                  ([xsq0[:], xsq0[:]], [r_xsq0, r_xsq0], rstd[:], r_rstd, gm, r_gm, gms, r_gms, [t32a[:], t32a[:]], [r_t32a, r_t32a]))
    B1 = sets[1]
    for nm in f32_names + bf_names + ["Vt", "S0", "S0b", "Khms", "KhmTs", "At", "Khm", "KhmT", "At_b", "Khm_b", "KhmT_b", "S", "Sbf", "S2", "Sbf2", "dec", "pbA", "pbB", "Ats"]:
        rr = getattr(B1, "r_" + nm)
        rr.r = list(r_x.r)
        rr.w = r_x.w

    outs = []
    dsem_set = [dict(s0=d[7], s0b=d[8], qb=d[9], sg=d[10], ol=d[11], sn=d[12], se=d[13]),
                dict(s0=d[14], s0b=d[15], qb=d[16], sg=d[17], ol=d[18], sn=d[19], se=d[20])]

    def proj_fm(dst_fn):
        n_, w, rw = next_w()
        for (s, n) in ntiles(NT):
            pt, rp = pr_prep.next()
            for k in range(KC):
                b.op("pe", lambda e, pt=pt, w=w, k=k, s=s, n=n: e.matmul(
                    pt[:, 0:n], w[:, k, :], h[:, k, s:s + n], start=(k == 0), stop=(k == KC - 1)),
                    reads=[rw, r_h], writes=[rp], sig=(k == KC - 1))
                if k % 4 == 3 and k != KC - 1 and n > 128:
                    yield
            dst_fn(pt, rp, s, n)
            yield
        load_w(n_ + 4)

    def prep(hh):
        B = sets[hh % 2]; ds = dsem_set[hh % 2]
        b.dma("sp", B.S0, s0d[hh], ds["s0"], writes=[B.r_S0])
        b.dma("pool", B.S0b, s0d[hh], ds["s0b"], writes=[B.r_S0b])
        yield from proj_fm(lambda pt, rp, s, n: b.op("act", lambda e: e.activation(B.qf[:, s:s + n], pt[:, 0:n], AF.Silu),
                                                     reads=[rp], writes=[B.r_qf]))
        yield from proj_fm(lambda pt, rp, s, n: b.op("act", lambda e: e.activation(B.kf[:, s:s + n], pt[:, 0:n], AF.Sigmoid, scale=-1.0),
                                                     reads=[rp], writes=[B.r_kf]))
        b.op("dve", lambda e: e.tensor_scalar(B.kf, B.kf, oml[:, hh:hh + 1], None, ALU.mult), reads=[B.r_kf, r_const], writes=[B.r_kf])
        b.op("act", lambda e: e.activation(B.bA, B.kf, AF.Ln, bias=1.0, scale=-1.0), reads=[B.r_kf], writes=[B.r_bA])
        yield
        n_, w, rw = next_w()
        for blk in range(NBLK + 1):
            m = 128 if blk < NBLK else NS
            c0 = blk * 128
            pt, rp = pr_prep.next()
            for k in range(KC):
                b.op("pe", lambda e, pt=pt, w=w, k=k, c0=c0, m=m: e.matmul(
                    pt[0:m, 0:128], h[:, k, c0:c0 + m], w[:, k, :], start=(k == 0), stop=(k == KC - 1)),
                    reads=[rw, r_h], writes=[rp], sig=(k == KC - 1))
            b.op("act", lambda e, pt=pt, blk=blk, m=m: e.activation(B.Vt[0:m, blk, :], pt[0:m, 0:128], AF.Copy),
                 reads=[rp], writes=[B.r_Vt])
            yield
        load_w(n_ + 4)
        yield from proj_fm(lambda pt, rp, s, n: b.op("act", lambda e: e.activation(B.sg[:, s:s + n], pt[:, 0:n], AF.Silu),
                                                     reads=[rp], writes=[B.r_sg]))
        outs.append(b.dma("sp", sgo[hh], B.sg, ds["sg"], reads=[B.r_sg]))
        b.op("dve", lambda e: e.tensor_tensor_scan(B.bB, smask[:], B.bA, 0.0, ALU.mult, ALU.add),
             reads=[B.r_bA, r_const], writes=[B.r_bB])
        bb, rbb = B.bB, B.r_bB
        other, rother = B.bA, B.r_bA
        yield
        bv = bb[:, 0:NP].rearrange("p (c t) -> p c t", t=CH)
        ov = other[:, 0:NP].rearrange("p (c t) -> p c t", t=CH)
        b.op("act", lambda e: e.activation(B.dec[:, 0:NCK].unsqueeze(2), bv[:, :, CH - 1:CH], AF.Exp), reads=[rbb], writes=[B.r_dec])
        b.op("dve", lambda e: e.tensor_tensor(ov, bv[:, :, CH - 1:CH].broadcast_to([128, NCK, CH]), bv, ALU.subtract),
             reads=[rbb], writes=[rother])
        bs_ = bb[:, NP:NT].rearrange("p (c t) -> p c t", t=4)
        os_ = other[:, NP:NT].rearrange("p (c t) -> p c t", t=4)
        b.op("act", lambda e: e.activation(B.dec[:, NCK:NCK + 16].unsqueeze(2), bs_[:, :, 3:4], AF.Exp), reads=[rbb], writes=[B.r_dec])
        b.op("dve", lambda e: e.tensor_tensor(os_, bs_[:, :, 3:4].broadcast_to([128, 16, 4]), bs_, ALU.subtract),
             reads=[rbb], writes=[rother])
        b.op("act", lambda e: e.activation(other[:, 0:NT], other[:, 0:NT], AF.Exp), reads=[rother], writes=[rother])
        b.op("dve", lambda e: e.tensor_tensor(B.kh, B.kf, other[:, 0:NT], ALU.mult), reads=[rother, B.r_kf], writes=[B.r_kh])
        yield
        b.op("act", lambda e: e.activation(other[:, 0:NT], bb[:, 0:NT], AF.Exp), reads=[rbb, B.r_kh], writes=[rother])
        b.op("dve", lambda e: e.tensor_tensor(B.qt, B.qf, other[:, 0:NT], ALU.mult), reads=[rother, B.r_qf], writes=[B.r_qt])
        b.op("act", lambda e: e.activation(other[:, 0:NT], bb[:, 0:NT], AF.Exp, scale=-1.0), reads=[rbb, B.r_qt], writes=[rother])
        b.op("dve", lambda e: e.tensor_tensor(B.kt, B.kf, other[:, 0:NT], ALU.mult), reads=[rother, B.r_kf], writes=[B.r_kt])
        yield
        b.op("pool", lambda e: e.tensor_copy(B.pbA.unsqueeze(2), bv[:, :, CH - 1:CH]), reads=[rbb], writes=[B.r_pbA])
        b.op("dve", lambda e: e.tensor_tensor_scan(B.pbB, onesf[:], B.pbA, 0.0, ALU.mult, ALU.add),
             reads=[B.r_pbA, r_const], writes=[B.r_pbB])
        pin, rpin = B.pbB, B.r_pbB
        pex, rpex = B.pbA, B.r_pbA
        b.op("act", lambda e: e.activation(Dall[:, hh:hh + 1], pin[:, NCK - 1:NCK], AF.Exp), reads=[rpin], writes=[r_D])
        b.op("pool", lambda e: e.tensor_tensor(pex.unsqueeze(2), pin.unsqueeze(2), bv[:, :, CH - 1:CH], ALU.subtract),
             reads=[rpin, rbb], writes=[rpex])
        b.op("act", lambda e: e.activation(pex, pex, AF.Exp), reads=[rpex], writes=[rpex])
        b.op("pool", lambda e: e.tensor_tensor(
            B.QBf[:, 0:NP].rearrange("p (c t) -> p c t", t=CH), B.qt[:, 0:NP].rearrange("p (c t) -> p c t", t=CH),
            pex.unsqueeze(2).broadcast_to([128, NCK, CH]), ALU.mult), reads=[rpex, B.r_qt], writes=[B.r_QBf])
        b.op("pool", lambda e: e.memset(B.QBf[:, NP:NT], 0.0), writes=[B.r_QBf])
        outs.append(b.dma("sp", qbo[hh], B.QBf, ds["qb"], reads=[B.r_QBf]))
        b.op("pool", lambda e: e.memset(B.S, 0.0), writes=[B.r_S])
        b.op("pool", lambda e: e.memset(B.Sbf, 0.0), writes=[B.r_Sbf])
        yield

    def loop(hh):
        B = sets[hh % 2]; ds = dsem_set[hh % 2]
        def front(blk):
            c0 = blk * 128
            Khm, rKhm, KhmT, rKhmT, At, rAt = ((B.Khm, B.r_Khm, B.KhmT, B.r_KhmT, B.At, B.r_At) if blk % 2 == 0 else
                                               (B.Khm_b, B.r_Khm_b, B.KhmT_b, B.r_KhmT_b, B.At_b, B.r_At_b))
            b.op("dve", lambda e: e.tensor_tensor(Khm, B.kh[:, c0:c0 + 128].unsqueeze(1).broadcast_to([128, 4, 128]), cm[:], ALU.mult),
                 reads=[B.r_kh, r_const], writes=[rKhm])
            ptT, rpT = pr_loop.next()
            ptTb = ptT[:, :].bitcast(BF16)
            for c in range(4):
                b.op("pe", lambda e, c=c: e.transpose(ptTb[:, c * 128:(c + 1) * 128], Khm[:, c, :], ident[:]),
                     reads=[rKhm, r_const], writes=[rpT], sig=(c == 3))
            b.op("act", lambda e: e.activation(KhmT.rearrange("p c k -> p (c k)"), ptTb[:, 0:512], AF.Copy),
                 reads=[rpT], writes=[rKhmT])
            yield
            pa, rpa = pr_loop.next()
            b.op("pe", lambda e: e.matmul(pa[:, 0:128], B.kt[:, c0:c0 + 128], B.qt[:, c0:c0 + 128], start=True, stop=True),
                 reads=[B.r_kt, B.r_qt], writes=[rpa])
            b.op("dve", lambda e: e.tensor_tensor(At, pa[:, 0:128], m01[:], ALU.mult), reads=[rpa, r_const], writes=[rAt])
            yield
            po, rpo = po_banks[blk % 2]
            b.op("pe", lambda e: e.matmul(po[:, 0:128], B.Vt[:, blk, :], At, start=True, stop=False),
                 reads=[B.r_Vt, rAt], writes=[rpo], sig=False)
            pu, rpu = pu_banks[blk % 2]
            for c in range(4):
                b.op("pe", lambda e, c=c: e.matmul(pu[:, c * 128:(c + 1) * 128], KhmT[:, c, :], B.Vt[:, blk, :], start=True, stop=True),
                     reads=[rKhmT, B.r_Vt], writes=[rpu], sig=(c == 3))

        for _ in front(0):
            pass
        for blk in range(NBLK):
            c0 = blk * 128
            nxt = front(blk + 1) if blk + 1 < NBLK else iter(())
            yield
            po, rpo = po_banks[blk % 2]
            pu, rpu = pu_banks[blk % 2]
            for c in range(4):
                ck = blk * 4 + c
                Sc, rSc, Sn, rSn = (B.S, B.r_S, B.S2, B.r_S2) if ck % 2 == 0 else (B.S2, B.r_S2, B.S, B.r_S)
                Sbc, rSbc, Sbn, rSbn = (B.Sbf, B.r_Sbf, B.Sbf2, B.r_Sbf2) if ck % 2 == 0 else (B.Sbf2, B.r_Sbf2, B.Sbf, B.r_Sbf)
                b.op("pe", lambda e, c=c: e.matmul(po[:, c * CH:(c + 1) * CH], Sbc, B.qt[:, c0 + c * CH:c0 + (c + 1) * CH],
                                                   start=False, stop=(c == 3)),
                     reads=[rSbc, B.r_qt], writes=[rpo], sig=True)
                b.op("dve", lambda e, c=c, ck=ck: e.scalar_tensor_tensor(Sn, Sc, B.dec[:, ck:ck + 1], pu[:, c * 128:(c + 1) * 128], ALU.mult, ALU.add),
                     reads=[rpu, rSc, B.r_dec], writes=[rSn])
                b.op("pool", lambda e: e.tensor_copy(Sbn, Sn), reads=[rSn], writes=[rSbn])
                next(nxt, None)
                yield
            for _ in nxt:
                pass
            b.op("act", lambda e: e.activation(B.Oraw[:, c0:c0 + 128], po[:, 0:128], AF.Copy), reads=[rpo], writes=[B.r_Oraw])
        outs.append(b.dma("sp", send[:, hh, :], B.S, ds["se"], reads=[B.r_S]))
        b.op("dve", lambda e: e.tensor_tensor(B.Khms, B.kh[:, NP:NT].unsqueeze(1).broadcast_to([128, 16, 64]), cms[:], ALU.mult),
             reads=[B.r_kh, r_const], writes=[B.r_Khms])
        for half in range(2):
            ptT, rpT = pr_loop.next()
            ptTb = ptT[:, :].bitcast(BF16)
            for s8 in range(8):
                sq = half * 8 + s8
                b.op("pe", lambda e, ptTb=ptTb, s8=s8, sq=sq: e.transpose(ptTb[0:64, s8 * 128:(s8 + 1) * 128], B.Khms[:, sq, :], ident[:]),
                     reads=[B.r_Khms, r_const], writes=[rpT], sig=(s8 == 7))
            b.op("act", lambda e, ptTb=ptTb, half=half: e.activation(
                B.KhmTs[:, half * 8:half * 8 + 8, :].rearrange("p c k -> p (c k)"), ptTb[0:64, 0:1024], AF.Copy),
                reads=[rpT], writes=[B.r_KhmTs])
            yield
        pa, rpa = pr_loop.next()
        b.op("pe", lambda e, pa=pa: e.matmul(pa[0:64, 0:64], B.kt[:, NP:NT], B.qt[:, NP:NT], start=True, stop=True),
             reads=[B.r_kt, B.r_qt], writes=[rpa])
        b.op("dve", lambda e, pa=pa: e.tensor_tensor(B.Ats, pa[0:64, 0:64], ms[:], ALU.mult), reads=[rpa, r_const], writes=[B.r_Ats])
        po, rpo = pr_loop.next()
        b.op("pe", lambda e, po=po: e.matmul(po[:, 0:64], B.Vt[0:64, NBLK, :], B.Ats, start=True, stop=False),
             reads=[B.r_Vt, B.r_Ats], writes=[rpo], sig=False)
        for sq in range(16):
            b.op("pe", lambda e, po=po, sq=sq: e.matmul(po[:, sq * 4:sq * 4 + 4], B.S0b[:, sq, :], B.qt[:, NP + sq * 4:NP + sq * 4 + 4],
                                                       start=False, stop=(sq == 15)),
                 reads=[B.r_S0b, B.r_qt], writes=[rpo], sig=(sq == 15))
        b.op("act", lambda e, po=po: e.activation(B.Oraw[:, NP:NT], po[:, 0:64], AF.Copy), reads=[rpo], writes=[B.r_Oraw])
        outs.append(b.dma("sp", oloc[hh], B.Oraw, ds["ol"], reads=[B.r_Oraw]))
        yield
        for q4 in range(4):
            pu, rpu = pr_loop.next()
            for s4 in range(4):
                sq = q4 * 4 + s4
                b.op("pe", lambda e, pu=pu, s4=s4, sq=sq: e.matmul(pu[:, s4 * 128:(s4 + 1) * 128], B.KhmTs[:, sq, :], B.Vt[0:64, NBLK, :],
                                                                 start=True, stop=True),
                     reads=[B.r_KhmTs, B.r_Vt], writes=[rpu], sig=(s4 == 3))
            for s4 in range(4):
                sq = q4 * 4 + s4
                b.op("dve", lambda e, pu=pu, s4=s4, sq=sq: e.scalar_tensor_tensor(
                    B.S0[:, sq, :], B.S0[:, sq, :], B.dec[:, NCK + sq:NCK + sq + 1], pu[:, s4 * 128:(s4 + 1) * 128], ALU.mult, ALU.add),
                    reads=[rpu, B.r_S0, B.r_dec], writes=[B.r_S0])
            yield
        outs.append(b.dma("sp", snew[hh], B.S0, ds["sn"], reads=[B.r_S0]))

    def drain(g):
        for _ in g:
            pass

    drain(prep(0))
    for hh in range(NHH):
        crit = loop(hh)
        fill = prep(hh + 1) if hh + 1 < NHH else iter(())
        done_c = done_f = False
        while not (done_c and done_f):
            for _ in range(CRIT_RATIO):
                if not done_c:
                    try:
                        next(crit)
                    except StopIteration:
                        done_c = True
            for _ in range(FILL_RATIO):
                if not done_f:
                    try:
                        next(fill)
                    except StopIteration:
                        done_f = True
    outs.append(b.dma("sp", dout, Dall[:], d[21], reads=[r_D]))
    b.wait_all("sp", outs)
    b.emit(); b.close()
    return nc


def fm(a):
    n = a.shape[0]
    return np.ascontiguousarray(a.T.reshape(16, 128, n).transpose(1, 0, 2))
def unfm(t):
    return np.ascontiguousarray(t.transpose(2, 1, 0).reshape(t.shape[2], D))
def vfm(v):
    return v.reshape(-1, 128).T
def mods_fm(m3):
    return np.ascontiguousarray(m3.reshape(3, 16, 16, 128).transpose(3, 0, 2, 1))
def tile_w_in(w):
    return np.ascontiguousarray(w.reshape(16, 128, 2, 44, 128).transpose(3, 1, 2, 0, 4))
def tile_w_out(w):
    return np.ascontiguousarray(w.reshape(4, 11, 128, 16, 128).transpose(0, 3, 2, 1, 4))
def tile_cols(w):
    return w.reshape(16, 128, w.shape[1]).transpose(1, 0, 2)
def tile_wqkv(w):
    out = np.empty((8, 128, 4, 16, 128), np.float32)
    for g in range(8):
        out[g, :, 0] = tile_cols(w[:, 256 * g:256 * g + 128])
        out[g, :, 1] = tile_cols(w[:, 256 * g + 128:256 * g + 256])
        kk = w[:, 2048 + 64 * g:2048 + 64 * g + 64]; vv = w[:, 2560 + 64 * g:2560 + 64 * g + 64]
        out[g, :, 2] = tile_cols(np.concatenate([kk, kk], 1))
        out[g, :, 3] = tile_cols(np.concatenate([vv, vv], 1))
    return out
def tile_sq(w):
    return np.ascontiguousarray(w.reshape(16, 128, 16, 128).transpose(2, 1, 0, 3))
NEG = -30000.0
def attn_consts():
    s = np.arange(128)[:, None]; q = np.arange(128)[None, :]
    nd = np.zeros((128, 2, 128), np.float32); mk = np.zeros((128, 2, 128), np.float32)
    nd[:, 0] = -(q - s + 128); mk[:, 0] = np.where(s > q, 0, NEG)
    nd[:, 1] = -(q - s); mk[:, 1] = np.where(s <= q, 0, NEG)
    nd = np.where(mk < 0, 0, nd).astype(np.float32)
    t = np.arange(4)[None, :]
    ndc = -(128 + t - s).astype(np.float32); mkc = np.where(s > t, 0, NEG).astype(np.float32)
    ndc = np.where(mkc < 0, 0, ndc).astype(np.float32)
    a = np.arange(64)
    same = (a[:, None] // 4) == (a[None, :] // 4)
    tp = a[:, None] % 4; tq = a[None, :] % 4
    ok = same & (tp <= tq)
    ndn = np.where(ok, -(tq - tp), 0).astype(np.float32); mkn = np.where(ok, 0, NEG).astype(np.float32)
    return dict(nd=nd, mk=mk, ndc=ndc, mkc=mkc, ndn=ndn, mkn=mkn)
def tile_whg(w):
    out = np.empty((16, 128, 4, 16, 128), np.float32)
    for hh in range(16):
        for wh in range(4):
            out[hh, :, wh] = tile_cols(w[:, wh * 2048 + hh * 128: wh * 2048 + hh * 128 + 128])
    return out
def hgrn_consts():
    a = np.arange(128)
    m01 = ((a[:, None] // 32 == a[None, :] // 32) & (a[:, None] <= a[None, :])).astype(np.float32)
    cm = np.broadcast_to((a[None, :] // 32 == np.arange(4)[:, None]).astype(np.float32)[None], (128, 4, 128)).copy()
    s = np.arange(64)
    ms = ((s[:, None] // 4 == s[None, :] // 4) & (s[:, None] <= s[None, :])).astype(np.float32)
    cms = np.broadcast_to((s[None, :] // 4 == np.arange(16)[:, None]).astype(np.float32)[None], (128, 16, 64)).copy()
    t = np.arange(1088)
    sm = np.where(t < 1024, (t % 32) != 0, ((t - 1024) % 4) != 0).astype(np.float32)
    smask = np.ascontiguousarray(np.broadcast_to(sm[None], (128, 1088)))
    return dict(m01=m01, cm=cm, ms=ms, cms=cms, ident=np.eye(128, dtype=np.float32), smask=smask)

NCORE = 8
_PROGS = {}


def _prog(name, fn):
    if name not in _PROGS:
        _PROGS[name] = fn()
    return _PROGS[name]


def _run(name, fn, in_maps):
    nc = _prog(name, fn)
    res = run_bass_kernel_spmd(nc, in_maps, core_ids=list(range(NCORE)))
    return res.results


def _f32(a):
    return np.ascontiguousarray(np.asarray(a, dtype=np.float32))


def kernel(x_prompt, x_sample, cache_swa_k, cache_swa_v, state_hgrn, state_ffn_conv, c_prompt, c_sample,
           norm1_g, norm2_g, w_ada, b_ada, attn_w_qkv, attn_w_o, attn_sinks,
           hgrn_w_in, hgrn_lower_bounds, hgrn_norm_g, hgrn_w_o,
           ffn_w_in, ffn_conv_w, ffn_conv_b, ffn_w_out, final_norm_g):
    (x_prompt, x_sample, cache_swa_k, cache_swa_v, state_hgrn, state_ffn_conv, c_prompt, c_sample,
     norm1_g, norm2_g, w_ada, b_ada, attn_w_qkv, attn_w_o, attn_sinks,
     hgrn_w_in, hgrn_lower_bounds, hgrn_norm_g, hgrn_w_o,
     ffn_w_in, ffn_conv_w, ffn_conv_b, ffn_w_out, final_norm_g) = [_f32(a) for a in (
        x_prompt, x_sample, cache_swa_k, cache_swa_v, state_hgrn, state_ffn_conv, c_prompt, c_sample,
        norm1_g, norm2_g, w_ada, b_ada, attn_w_qkv, attn_w_o, attn_sinks,
        hgrn_w_in, hgrn_lower_bounds, hgrn_norm_g, hgrn_w_o,
        ffn_w_in, ffn_conv_w, ffn_conv_b, ffn_w_out, final_norm_g)]
    Dm = 2048
    xp = x_prompt[0]
    xs = x_sample.reshape(128 * 4, Dm)

    c_all = np.concatenate([c_prompt, c_sample], 0)
    cT = fm(c_all)
    maps = []
    for c in range(NCORE):
        wt = np.empty((24, 128, 16, 128), np.float32)
        bt = np.empty((128, 24), np.float32)
        for n in range(24):
            l, ch = n // 12, 12 * c + n % 12
            wt[n] = tile_cols(w_ada[l][:, ch * 128:(ch + 1) * 128])
            bt[:, n] = b_ada[l][ch * 128:(ch + 1) * 128]
        maps.append({"cT": cT, "wada": wt, "bada": bt})
    res = _run("adaln", build_adaln, maps)
    mod = np.empty((2, 129, 6 * Dm), np.float32)
    for c in range(NCORE):
        mt = res[c]["modT"]
        for n in range(24):
            l, ch = n // 12, 12 * c + n % 12
            mod[l][:, ch * 128:(ch + 1) * 128] = mt[:, n, :].T
    mod = mod.reshape(2, 129, 6, Dm)

    def vec_for(l, g_vec, i0, extra=None):
        rows = [vfm(g_vec), vfm(mod[l, 0, i0]), vfm(mod[l, 0, i0 + 1]), vfm(mod[l, 0, i0 + 2])]
        if extra is not None:
            rows.append(vfm(extra))
        return np.ascontiguousarray(np.stack(rows, 1))

    def mods_for(l, c, i0):
        sl = slice(1 + 16 * c, 1 + 16 * c + 16)
        return mods_fm(np.stack([mod[l, sl, i0], mod[l, sl, i0 + 1], mod[l, sl, i0 + 2]], 0))

    aconst = attn_consts()
    wq_t = tile_wqkv(attn_w_qkv)
    wo_t = tile_sq(attn_w_o)
    sinks_b = np.ascontiguousarray(np.broadcast_to(attn_sinks[None], (128, 32)))
    vec0 = vec_for(0, norm1_g[0], 0)
    maps = []
    for c in range(NCORE):
        halo = xp[1024 * c - 128:1024 * c] if c > 0 else np.zeros((128, Dm), np.float32)
        xc = np.concatenate([halo, xp[1024 * c:1024 * (c + 1)], xs[64 * c:64 * (c + 1)]], 0)
        m = {"xT": fm(xc), "vec": vec0, "mods": mods_for(0, c, 0), "wqkv": wq_t, "wo": wo_t,
             "kcT": np.ascontiguousarray(cache_swa_k[16 * c:16 * c + 16].transpose(2, 3, 0, 1)),
             "vc": np.ascontiguousarray(cache_swa_v[16 * c:16 * c + 16].transpose(2, 1, 0, 3)),
             "sinks": sinks_b, "hb": np.full((128, 1), NEG if c == 0 else 0.0, np.float32)}
        m.update(aconst)
        maps.append(m)
    res = _run("attn", build_attn2, maps)
    x1 = [unfm(res[c]["xo"]) for c in range(NCORE)]
    ko = res[NCORE - 1]["kout"].reshape(2, 64, 4, 192).transpose(1, 2, 0, 3).reshape(64, 8, 192)
    swa_k_prompt = np.ascontiguousarray(ko[:, :, :128].transpose(2, 1, 0))[None]
    swa_v_prompt = np.ascontiguousarray(res[NCORE - 1]["vout"][:, 0])[None]
    swa_k_sample = np.empty((128, 4, 8, 64), np.float32)
    swa_v_sample = np.empty((128, 4, 8, 64), np.float32)
    for c in range(NCORE):
        ko = res[c]["kout"].reshape(2, 64, 4, 192).transpose(1, 2, 0, 3).reshape(64, 8, 192)
        swa_k_sample[16 * c:16 * c + 16] = ko[:, :, 128:].transpose(2, 1, 0).reshape(16, 4, 8, 64)
        swa_v_sample[16 * c:16 * c + 16] = res[c]["vout"][:64, 1].reshape(16, 4, 8, 64)

    def run_ffn(l, xin, last):
        w_in_t = tile_w_in(ffn_w_in[l])
        w_out_t = tile_w_out(ffn_w_out[l])
        convw = np.ascontiguousarray(np.concatenate([ffn_conv_w[l], ffn_conv_b[l][None]], 0).reshape(4, 44, 128).transpose(2, 1, 0))
        vec = vec_for(l, norm2_g[l], 3, extra=final_norm_g)
        maps = []
        for c in range(NCORE):
            halo = xin[c - 1][1022:1024] if c > 0 else np.zeros((2, Dm), np.float32)
            xc = np.concatenate([halo, xin[c]], 0)
            maps.append({"xT": fm(xc), "vec": vec, "mods": mods_for(l, c, 3), "w_in": w_in_t, "w_out": w_out_t,
                         "convw": convw,
                         "cstate": np.ascontiguousarray(state_ffn_conv[l, 16 * c:16 * c + 16].reshape(16, 2, 44, 128).transpose(3, 2, 0, 1)),
                         "flag": np.full((128, 1), 0.0 if c == 0 else 1.0, np.float32)})
        res = _run("ffn_last" if last else "ffn", (lambda: build_ffn(True)) if last else (lambda: build_ffn(False)), maps)
        xout = [unfm(res[c]["xo"]) for c in range(NCORE)]
        cbp = np.ascontiguousarray(res[NCORE - 1]["cbp"].transpose(2, 1, 0).reshape(2, 5632))[None]
        cbs = np.concatenate([res[c]["cbs"].transpose(2, 3, 1, 0).reshape(16, 2, 5632) for c in range(NCORE)], 0)
        return xout, cbp, cbs

    x2, cbp0, cbs0 = run_ffn(0, x1, False)

    hconst = hgrn_consts()
    whg_t = tile_whg(hgrn_w_in)
    lbp = np.ascontiguousarray(hgrn_lower_bounds.reshape(2, 16, 128).transpose(2, 0, 1))
    vec1 = vec_for(1, norm1_g[1], 0)
    maps = []
    for c in range(NCORE):
        m = {"xT": fm(x2[c]), "vec": vec1, "mods": mods_for(1, c, 0), "whg": whg_t, "lbp": lbp,
             "s0": np.ascontiguousarray(state_hgrn[16 * c:16 * c + 16].transpose(1, 2, 0, 3))}
        m.update(hconst)
        maps.append(m)
    resA = _run("hgrnA", build_hgrna, maps)
    s_loc = [resA[c]["send"] for c in range(NCORE)]
    d_loc = [resA[c]["dout"] for c in range(NCORE)]
    hgrn_state_sample = np.concatenate([resA[c]["snew"].transpose(2, 0, 1, 3) for c in range(NCORE)], 0)

    wo2_t = tile_sq(hgrn_w_o)
    ng = np.ascontiguousarray(hgrn_norm_g.reshape(16, 128).T)
    maps = []
    for c in range(NCORE):
        sr = np.zeros((16, 128, NRK, 128), np.float32)
        dr = np.ones((128, 16, NRK), np.float32)
        for r in range(c):
            sr[:, :, r, :] = s_loc[r].transpose(1, 0, 2)
            dr[:, :, r] = d_loc[r]
        maps.append({"xT": fm(x2[c]), "vec": vec1, "mods": mods_for(1, c, 0), "oloc": resA[c]["oloc"], "qb": resA[c]["qbo"],
                     "sg": resA[c]["sgo"], "sr": sr, "dr": dr, "sloc": s_loc[c], "dl": d_loc[c], "ng": ng, "wo": wo2_t})
    res = _run("hgrnB", build_hgrnb, maps)
    x3 = [unfm(res[c]["xo"]) for c in range(NCORE)]
    hgrn_state_prompt = np.ascontiguousarray(res[NCORE - 1]["send"].transpose(1, 0, 2))[None]

    y, cbp1, cbs1 = run_ffn(1, x3, True)
    y_prompt = np.concatenate([y[c][:1024] for c in range(NCORE)], 0)[None]
    y_sample = np.concatenate([y[c][1024:] for c in range(NCORE)], 0).reshape(128, 4, Dm)
    ffn_conv_prompt = np.stack([cbp0, cbp1], 0)
    ffn_conv_sample = np.stack([cbs0, cbs1], 0)
    outs = (y_prompt, y_sample, swa_k_prompt, swa_v_prompt, swa_k_sample, swa_v_sample,
            hgrn_state_prompt, hgrn_state_sample, ffn_conv_prompt, ffn_conv_sample)
    return tuple(np.ascontiguousarray(o, dtype=np.float32) for o in outs)
```

```python
from concourse.bass_utils import run_bass_kernel_spmd

import contextlib
import numpy as np
import concourse.bass as bass
import concourse.mybir as mybir

F32 = mybir.dt.float32
BF16 = mybir.dt.bfloat16
I32 = mybir.dt.int32
AF = mybir.ActivationFunctionType
ALU = mybir.AluOpType
AX = mybir.AxisListType

ENGS = ["sp", "act", "pool", "dve", "pe"]


class Res:
    __slots__ = ("name", "w", "r")

    def __init__(self, name=""):
        self.name = name
        self.w = None
        self.r = []


class DSem:
    def __init__(self, handle, name):
        self.h = handle
        self.name = name
        self.cnt = 0


class _Rec:
    def __getattr__(self, name):
        return lambda *a, **k: (name, a, k)


_REC = _Rec()


class Builder:
    def __init__(self, nc, n_dsem=12):
        self.nc = nc
        self.es = contextlib.ExitStack()
        self.q = {e: [] for e in ENGS}
        self.cnt = {e: 0 for e in ENGS}
        self.waited = {e: {} for e in ENGS}
        self.esem = {e: self.es.enter_context(nc.semaphore("s_" + e)) for e in ENGS}
        self.dsems = [DSem(self.es.enter_context(nc.semaphore("d%d" % i)), "d%d" % i)
                      for i in range(n_dsem)]
        self.pending = {e: [] for e in ENGS}
        self.n_inst = 0

    def sb(self, name, shape, dt):
        return self.es.enter_context(self.nc.sbuf_tensor(name, list(shape), dt))

    def ps(self, name, shape, dt=F32):
        return self.es.enter_context(self.nc.psum_tensor(name, list(shape), dt))

    def _deps(self, eng, reads, writes):
        need = {}

        def add(t):
            if t is None:
                return
            k, v = t
            if need.get(k, 0) < v:
                need[k] = v
        for r in reads:
            add(r.w)
        for w in writes:
            add(w.w)
            for t in w.r:
                add(t)
        waits = []
        for k, v in need.items():
            if k == "pe" and eng == "pe":
                continue
            if self.waited[eng].get(k, 0) >= v:
                continue
            self.waited[eng][k] = v
            waits.append((k, v))
        return waits

    def _semh(self, k):
        return self.esem[k] if isinstance(k, str) else k.h

    def op(self, eng, fn, reads=(), writes=(), sig=True):
        reads = [r for r in reads if r is not None]
        writes = [w for w in writes if w is not None]
        waits = self._deps(eng, reads, writes)
        for k, v in waits:
            cur = self.cnt[k] if isinstance(k, str) else k.cnt
            assert v <= cur, "forward wait %s %d > %d" % (k, v, cur)
        ticket = None
        if sig:
            self.cnt[eng] += 1
            ticket = (eng, self.cnt[eng])
            pend = self.pending[eng]
            self.pending[eng] = []
            for pr, pw in pend:
                self._commit(pr, pw, ticket)
            self._commit(reads, writes, ticket)
        else:
            t = (eng, self.cnt[eng] + 1)
            self._commit(reads, writes, t)
        self.q[eng].append((waits, fn(_REC), ticket, None))
        self.n_inst += 1
        return ticket

    def _commit(self, reads, writes, ticket):
        for r in reads:
            r.r.append(ticket)
        for w in writes:
            w.w = ticket
            w.r = []

    def dma(self, eng, out, in_, dsem, reads=(), writes=(), **kw):
        reads = [r for r in reads if r is not None]
        writes = [w for w in writes if w is not None]
        waits = self._deps(eng, reads, writes)
        for k, v in waits:
            cur = self.cnt[k] if isinstance(k, str) else k.cnt
            assert v <= cur, "forward wait %s %d > %d" % (k, v, cur)
        dsem.cnt += 16
        ticket = (dsem, dsem.cnt)
        self._commit(reads, writes, ticket)
        kw2 = dict(kw); kw2["out"] = out; kw2["in_"] = in_
        self.q[eng].append((waits, ("dma_start", (), kw2), None, (dsem, 16)))
        self.n_inst += 1
        return ticket

    def wait_all(self, eng, tickets):
        waits = []
        for t in tickets:
            if t is None:
                continue
            k, v = t
            if self.waited[eng].get(k, 0) >= v:
                continue
            self.waited[eng][k] = v
            waits.append((k, v))
        self.q[eng].append((waits, None, None, None))

    def emit(self):
        nc = self.nc
        handles = {"sp": "sync", "act": "scalar", "pool": "gpsimd", "dve": "vector", "pe": "tensor"}
        with nc.Block() as block:
            for eng in ENGS:
                items = self.q[eng]
                if not items:
                    continue

                def body(e, items=items, eng=eng):
                    for waits, fn, ticket, dinc in items:
                        for k, v in waits:
                            e.wait_ge(self._semh(k), v)
                        if fn is None:
                            continue
                        name, a, k = fn
                        ins = getattr(e, name)(*a, **k)
                        if ticket is not None:
                            ins.then_inc(self.esem[eng], 1)
                        if dinc is not None:
                            ins.then_inc(dinc[0].h, dinc[1])
                getattr(block, handles[eng])(body)

    def close(self):
        self.es.close()


D = 2048
KC = 16
DFF = 5632
FC = 44
NQ = 4
FQ = FC // NQ
NP = 1024
NS = 64
EPS = 1e-6


class PsumRot:
    def __init__(self, b, n=8):
        self.tiles = [b.ps("psb%d" % i, [128, 512]) for i in range(n)]
        self.res = [Res("psb%d" % i) for i in range(n)]
        self.i = 0

    def next(self):
        t, r = self.tiles[self.i], self.res[self.i]
        self.i = (self.i + 1) % len(self.tiles)
        return t, r


def ntiles(n, step=512):
    return [(s, min(step, n - s)) for s in range(0, n, step)]


def emit_norm_mod(b, pr, x, r_x, h, r_h, ncol, np_cols, vec, iv_g, iv_sh, iv_sc, mods, im_sh, im_sc,
                  ones, r_const, tmp, r_x_parts=None):
    xsq, r_xsq, rstd, r_rstd, gm, r_gm, gms, r_gms, t32, r_t32 = tmp
    tiles = ntiles(ncol)
    banks = [pr.next() for _ in tiles]
    for k in range(KC):
        i2 = k % 2
        rxk = r_x if r_x_parts is None else r_x_parts[k * len(r_x_parts) // KC]
        b.op("act", lambda e, k=k, i2=i2: e.activation(xsq[i2][:, 0:ncol], x[:, k, 0:ncol], AF.Square),
             reads=[rxk], writes=[r_xsq[i2]])
        for ti, (s, n) in enumerate(tiles):
            pt, rp = banks[ti]
            b.op("pe", lambda e, pt=pt, s=s, n=n, i2=i2, k=k: e.matmul(
                pt[:, 0:n], ones[:], xsq[i2][:, s:s + n], start=(k == 0), stop=(k == KC - 1)),
                reads=[r_const, r_xsq[i2]], writes=[rp], sig=(k == KC - 1) or ti == len(tiles) - 1)
    for ti, (s, n) in enumerate(tiles):
        pt, rp = banks[ti]
        b.op("act", lambda e, pt=pt, s=s, n=n: e.activation(rstd[:, s:s + n], pt[:, 0:n], AF.Ln,
                                                            bias=EPS, scale=1.0 / D),
             reads=[rp], writes=[r_rstd])
    b.op("act", lambda e: e.activation(rstd[:, 0:ncol], rstd[:, 0:ncol], AF.Exp, scale=-0.5), reads=[r_rstd], writes=[r_rstd])
    b.op("dve", lambda e: e.scalar_tensor_tensor(gm[:], vec[:, iv_sc, :], 1.0, vec[:, iv_g, :], ALU.add, ALU.mult),
         reads=[r_const], writes=[r_gm])
    ns = ncol - np_cols
    if ns:
        b.op("dve", lambda e: e.scalar_tensor_tensor(
            gms[:], mods[:, im_sc, :, :], 1.0, vec[:, iv_g, :].unsqueeze(2).broadcast_to([128, KC, 16]),
            ALU.add, ALU.mult), reads=[r_const], writes=[r_gms])
    for k in range(KC):
        i2 = k % 2
        r_hk = r_h[k] if isinstance(r_h, list) else r_h
        b.op("dve", lambda e, k=k, i2=i2: e.scalar_tensor_tensor(
            t32[i2][:, 0:np_cols], x[:, k, 0:np_cols], gm[:, k:k + 1], rstd[:, 0:np_cols], ALU.mult, ALU.mult),
            reads=[r_x, r_gm, r_rstd], writes=[r_t32[i2]])
        if ns:
            b.op("dve", lambda e, k=k, i2=i2: e.tensor_tensor(
                t32[i2][:, np_cols:ncol].rearrange("p (s t) -> p s t", t=4),
                x[:, k, np_cols:ncol].rearrange("p (s t) -> p s t", t=4),
                gms[:, k, :].unsqueeze(2).broadcast_to([128, 16, 4]), ALU.mult),
                reads=[r_x, r_gms], writes=[r_t32[i2]])
            b.op("dve", lambda e, k=k, i2=i2: e.tensor_tensor(
                t32[i2][:, np_cols:ncol], t32[i2][:, np_cols:ncol], rstd[:, np_cols:ncol], ALU.mult),
                reads=[r_t32[i2], r_rstd], writes=[r_t32[i2]])
            b.op("dve", lambda e, k=k, i2=i2: e.tensor_tensor(
                t32[i2][:, np_cols:ncol].rearrange("p (s t) -> p s t", t=4),
                t32[i2][:, np_cols:ncol].rearrange("p (s t) -> p s t", t=4),
                mods[:, im_sh, k, :].unsqueeze(2).broadcast_to([128, 16, 4]), ALU.add),
                reads=[r_t32[i2], r_const], writes=[r_t32[i2]])
            b.op("act", lambda e, k=k, i2=i2: e.activation(h[:, k, np_cols:ncol], t32[i2][:, np_cols:ncol], AF.Copy),
                 reads=[r_t32[i2]], writes=[r_hk])
        b.op("act", lambda e, k=k, i2=i2: e.activation(
            h[:, k, 0:np_cols], t32[i2][:, 0:np_cols], AF.Identity, bias=vec[:, iv_sh, k:k + 1], scale=1.0),
            reads=[r_t32[i2], r_const], writes=[r_hk])


def build_ffn(last):
    nc = bass.Bass("TRN2", target_bir_lowering=False)
    NCOL = 2 + NP + NS
    UW = 2 + NP + 16 * 6
    AW = UW - 2
    dram = lambda name, shape, kind="ExternalInput": nc.dram_tensor(name, list(shape), F32, kind=kind).ap()
    xT = dram("xT", [128, KC, NCOL])
    vecd = dram("vec", [128, 5, KC])
    modsd = dram("mods", [128, 3, KC, 16])
    w_in = dram("w_in", [FC, 128, 2, KC, 128])
    w_out = dram("w_out", [NQ, KC, 128, FQ, 128])
    convd = dram("convw", [128, FC, 4])
    cstd = dram("cstate", [128, FC, 16, 2])
    flagd = dram("flag", [128, 1])
    xo = dram("xo", [128, KC, NP + NS], "ExternalOutput")
    cbp = dram("cbp", [128, FC, 2], "ExternalOutput")
    cbs = dram("cbs", [128, FC, 16, 2], "ExternalOutput")

    b = Builder(nc, n_dsem=14)
    d = b.dsems
    pr = PsumRot(b)
    x = b.sb("x", [128, KC, NCOL], F32)
    h = b.sb("h", [128, KC, NCOL], BF16)
    act = b.sb("act", [128, FQ, NP + NS], BF16)
    U = [b.sb("U%d" % i, [128, UW], F32) for i in range(2)]
    G = [b.sb("G%d" % i, [128, NP + NS], F32) for i in range(2)]
    A = b.sb("A", [128, UW], F32)
    rstd = b.sb("rstd", [128, NCOL], F32)
    xsq = [b.sb("xsq%d" % i, [128, NCOL], BF16) for i in range(2)]
    t32 = [b.sb("t32%d" % i, [128, NCOL], F32) for i in range(2)]
    win = [b.sb("win%d" % i, [128, 2, KC, 128], BF16) for i in range(2)]
    wout = [b.sb("wout%d" % i, [128, FQ, 128], BF16) for i in range(2)]
    vec = b.sb("vecs", [128, 5, KC], F32)
    mods = b.sb("modss", [128, 3, KC, 16], F32)
    convw = b.sb("convws", [128, FC, 4], F32)
    cst = b.sb("csts", [128, FC, 16, 2], F32)
    cbps = b.sb("cbps", [128, FC, 2], F32)
    cbss = b.sb("cbss", [128, FC, 16, 2], F32)
    flag = b.sb("flags", [128, 1], F32)
    ones = b.sb("ones", [128, 128], BF16)
    gm = b.sb("gm", [128, KC], F32)
    gms = b.sb("gms", [128, KC, 16], F32)
    tmps = b.sb("tmps", [128, NS], F32)

    R = lambda n: Res(n)
    r_x, r_h, r_act, r_A, r_rstd, r_const, r_gm, r_gms, r_cb, r_tmps = [R(n) for n in
        "x h act A rstd const gm gms cb tmps".split()]
    r_h = [R("h%d" % k) for k in range(KC)]
    r_U = [R("U0"), R("U1")]; r_G = [R("G0"), R("G1")]
    r_xsq = [R("xsq0"), R("xsq1")]; r_t32 = [R("t0"), R("t1")]
    r_win = [R("win0"), R("win1")]; r_wout = [R("wo0"), R("wo1")]

    r_xp = [Res("xp%d" % i) for i in range(4)]
    xsems = [d[0], d[9], d[10], d[11]]
    for i in range(4):
        b.dma("sp", x[:, 4 * i:4 * i + 4, :], xT[:, 4 * i:4 * i + 4, :], xsems[i], writes=[r_xp[i]])
    joind = b.sb("joind", [128, 1], F32)
    b.op("pool", lambda e: e.memset(joind[:], 0.0), reads=r_xp, writes=[r_x])
    b.dma("sp", vec[:], vecd, d[1], writes=[r_const])
    b.dma("sp", mods[:], modsd, d[1], writes=[r_const])
    b.dma("sp", convw[:], convd, d[1], writes=[r_const])
    b.dma("sp", cst[:], cstd, d[1], writes=[r_const])
    b.dma("sp", flag[:], flagd, d[1], writes=[r_const])
    b.op("pool", lambda e: e.memset(ones[:], 1.0), writes=[r_const])

    win_sem = [d[2], d[3]]
    wout_sem = [d[4], d[5]]

    def load_win(j):
        b.dma("pool", win[j % 2][:], w_in[j], win_sem[j % 2], writes=[r_win[j % 2]])

    def load_wout(q, i):
        n = q * KC + i
        b.dma("pool", wout[n % 2][:], w_out[q, i], wout_sem[n % 2], writes=[r_wout[n % 2]])

    load_win(0)
    emit_norm_mod(b, pr, x, r_x, h, r_h, NCOL, 2 + NP, vec, 0, 1, 2, mods, 0, 1, ones, r_const,
                  (xsq, r_xsq, rstd, r_rstd, gm, r_gm, gms, r_gms, t32, r_t32), r_x_parts=r_xp)

    tiles = ntiles(NCOL)
    for q in range(NQ):
        for jj in range(FQ):
            j = q * FQ + jj
            if j + 1 < FC:
                load_win(j + 1)
            w = win[j % 2]; rw = r_win[j % 2]
            Uj, rU = U[j % 2], r_U[j % 2]
            Gj, rG = G[j % 2], r_G[j % 2]
            for which in range(2):
                for (s, n) in tiles:
                    pt, rp = pr.next()
                    for k in range(KC):
                        b.op("pe", lambda e, pt=pt, w=w, which=which, k=k, s=s, n=n: e.matmul(
                            pt[:, 0:n], w[:, which, k, :], h[:, k, s:s + n], start=(k == 0), stop=(k == KC - 1)),
                            reads=[rw, r_h[k]], writes=[rp], sig=(k == KC - 1))
                    if which == 0:
                        if s + n <= 2 + NP:
                            b.op("act", lambda e, pt=pt, s=s, n=n, Uj=Uj: e.activation(Uj[:, s:s + n], pt[:, 0:n], AF.Copy),
                                 reads=[rp], writes=[rU])
                        else:
                            npart = 2 + NP - s
                            b.op("act", lambda e, pt=pt, s=s, npart=npart, Uj=Uj: e.activation(
                                Uj[:, s:s + npart], pt[:, 0:npart], AF.Copy), reads=[rp], writes=[rU])
                            b.op("act", lambda e, pt=pt, npart=npart, Uj=Uj: e.activation(
                                Uj[:, 2 + NP:UW].rearrange("p (s c) -> p s c", c=6)[:, :, 2:6],
                                pt[:, npart:npart + NS].rearrange("p (s t) -> p s t", t=4), AF.Copy),
                                reads=[rp], writes=[rU])
                    else:
                        if s == 0:
                            b.op("act", lambda e, pt=pt, n=n, Gj=Gj: e.activation(Gj[:, 0:n - 2], pt[:, 2:n], AF.Copy),
                                 reads=[rp], writes=[rG])
                        else:
                            b.op("act", lambda e, pt=pt, s=s, n=n, Gj=Gj: e.activation(Gj[:, s - 2:s - 2 + n], pt[:, 0:n], AF.Copy),
                                 reads=[rp], writes=[rG])
            b.op("pool", lambda e, Uj=Uj: e.tensor_scalar(Uj[:, 0:2], Uj[:, 0:2], flag[:, 0:1], None, ALU.mult),
                 reads=[rU, r_const], writes=[rU])
            b.op("pool", lambda e, Uj=Uj, j=j: e.tensor_copy(
                Uj[:, 2 + NP:UW].rearrange("p (s c) -> p s c", c=6)[:, :, 0:2], cst[:, j, :, :]),
                reads=[r_const], writes=[rU])
            b.op("dve", lambda e, Uj=Uj, j=j: e.tensor_scalar(A[:, 0:AW], Uj[:, 2:UW], convw[:, j, 2:3], None, ALU.mult),
                 reads=[rU, r_const], writes=[r_A])
            b.op("dve", lambda e, Uj=Uj, j=j: e.scalar_tensor_tensor(A[:, 0:AW], Uj[:, 1:UW - 1], convw[:, j, 1:2], A[:, 0:AW],
                                                                 ALU.mult, ALU.add), reads=[rU, r_A, r_const], writes=[r_A])
            b.op("dve", lambda e, Uj=Uj, j=j: e.scalar_tensor_tensor(A[:, 0:AW], Uj[:, 0:AW], convw[:, j, 0:1], A[:, 0:AW],
                                                                 ALU.mult, ALU.add), reads=[rU, r_A, r_const], writes=[r_A])
            b.op("act", lambda e, j=j: e.activation(A[:, 0:AW], A[:, 0:AW], AF.Gelu, bias=convw[:, j, 3:4], scale=1.0),
                 reads=[r_A, r_const], writes=[r_A])
            b.op("dve", lambda e, jj=jj, Gj=Gj: e.tensor_tensor(act[:, jj, 0:NP], A[:, 0:NP], Gj[:, 0:NP], ALU.mult),
                 reads=[r_A, rG], writes=[r_act])
            b.op("dve", lambda e, jj=jj, Gj=Gj: e.tensor_tensor(
                act[:, jj, NP:NP + NS].rearrange("p (s t) -> p s t", t=4),
                A[:, NP + 2:UW].rearrange("p (s c) -> p s c", c=6)[:, :, 0:4],
                Gj[:, NP:NP + NS].rearrange("p (s t) -> p s t", t=4), ALU.mult),
                reads=[r_A, rG], writes=[r_act])
            b.op("pool", lambda e, Uj=Uj, j=j: e.tensor_copy(cbps[:, j, :], Uj[:, NP:NP + 2]), reads=[rU], writes=[r_cb])
            b.op("pool", lambda e, Uj=Uj, j=j: e.tensor_copy(
                cbss[:, j, :, :], Uj[:, 2 + NP:UW].rearrange("p (s c) -> p s c", c=6)[:, :, 4:6]), reads=[rU], writes=[r_cb])
        load_wout(q, 0)
        for i in range(KC):
            if i + 1 < KC:
                load_wout(q, i + 1)
            n_ = q * KC + i
            w = wout[n_ % 2]; rw = r_wout[n_ % 2]
            for (s, n) in ntiles(NP + NS):
                pt, rp = pr.next()
                for jj in range(FQ):
                    b.op("pe", lambda e, pt=pt, w=w, jj=jj, s=s, n=n: e.matmul(
                        pt[:, 0:n], w[:, jj, :], act[:, jj, s:s + n], start=(jj == 0), stop=(jj == FQ - 1)),
                        reads=[rw, r_act], writes=[rp], sig=(jj == FQ - 1))
                if s + n <= NP:
                    b.op("dve", lambda e, pt=pt, i=i, s=s, n=n: e.scalar_tensor_tensor(
                        x[:, i, 2 + s:2 + s + n], pt[:, 0:n], vec[:, 3, i:i + 1], x[:, i, 2 + s:2 + s + n], ALU.mult, ALU.add),
                        reads=[rp, r_const, r_x], writes=[r_x])
                else:
                    assert s == NP and n == NS
                    b.op("dve", lambda e, pt=pt, i=i: e.tensor_tensor(
                        tmps[:].rearrange("p (s t) -> p s t", t=4), pt[:, 0:NS].rearrange("p (s t) -> p s t", t=4),
                        mods[:, 2, i, :].unsqueeze(2).broadcast_to([128, 16, 4]), ALU.mult),
                        reads=[rp, r_const], writes=[r_tmps])
                    b.op("dve", lambda e, i=i: e.tensor_tensor(x[:, i, 2 + NP:NCOL], x[:, i, 2 + NP:NCOL], tmps[:], ALU.add),
                         reads=[r_tmps, r_x], writes=[r_x])
    outs = []
    if last:
        tl = ntiles(NCOL)
        banks = [pr.next() for _ in tl]
        for k in range(KC):
            i2 = k % 2
            b.op("act", lambda e, k=k, i2=i2: e.activation(xsq[i2][:, 0:NCOL], x[:, k, 0:NCOL], AF.Square),
                 reads=[r_x], writes=[r_xsq[i2]])
            for ti, (s, n) in enumerate(tl):
                pt, rp = banks[ti]
                b.op("pe", lambda e, pt=pt, s=s, n=n, i2=i2, k=k: e.matmul(
                    pt[:, 0:n], ones[:], xsq[i2][:, s:s + n], start=(k == 0), stop=(k == KC - 1)),
                    reads=[r_const, r_xsq[i2]], writes=[rp], sig=True)
        for ti, (s, n) in enumerate(tl):
            pt, rp = banks[ti]
            b.op("act", lambda e, pt=pt, s=s, n=n: e.activation(rstd[:, s:s + n], pt[:, 0:n], AF.Sqrt, bias=EPS, scale=1.0 / D),
                 reads=[rp], writes=[r_rstd])
        b.op("dve", lambda e: e.reciprocal(rstd[:, 0:NCOL], rstd[:, 0:NCOL]), reads=[r_rstd], writes=[r_rstd])
        for k in range(KC):
            b.op("dve", lambda e, k=k: e.scalar_tensor_tensor(
                x[:, k, :], x[:, k, :], vec[:, 4, k:k + 1], rstd[:, 0:NCOL], ALU.mult, ALU.mult),
                reads=[r_x, r_rstd, r_const], writes=[r_x])
    outs.append(b.dma("sp", xo, x[:, :, 2:NCOL], d[6], reads=[r_x]))
    outs.append(b.dma("sp", cbp, cbps[:], d[7], reads=[r_cb]))
    outs.append(b.dma("sp", cbs, cbss[:], d[8], reads=[r_cb]))
    b.wait_all("sp", outs)
    b.emit()
    b.close()
    return nc


NKV = 8
SCALE = 64 ** -0.5
NEG = -30000.0


def alibi_slope(h):
    return float(2.0 ** (-8.0 * (h + 1) / 32))


def build_attn2():
    nc = bass.Bass("TRN2", target_bir_lowering=False)
    NH = 128
    NCOL = NH + NP + NS
    NQC = NP + NS
    NB = 8
    dram = lambda name, shape, kind="ExternalInput": nc.dram_tensor(name, list(shape), F32, kind=kind).ap()
    xT = dram("xT", [128, KC, NCOL])
    vecd = dram("vec", [128, 4, KC])
    modsd = dram("mods", [128, 3, KC, 16])
    wqkv = dram("wqkv", [NKV, 128, 4, KC, 128])
    wo = dram("wo", [KC, 128, KC, 128])
    kcT = dram("kcT", [NKV, 64, 16, 128])
    vc = dram("vc", [NKV, 128, 16, 64])
    ndd = dram("nd", [128, 2, 128]); mkd = dram("mk", [128, 2, 128])
    ndcd = dram("ndc", [128, 4]); mkcd = dram("mkc", [128, 4])
    ndnd = dram("ndn", [64, 64]); mknd = dram("mkn", [64, 64])
    sinkd = dram("sinks", [128, 32])
    hbd = dram("hb", [128, 1])
    xo = dram("xo", [128, KC, NQC], "ExternalOutput")
    kout = dram("kout", [128, NKV // 2, 192], "ExternalOutput")
    vout = dram("vout", [128, 2, NKV, 64], "ExternalOutput")

    b = Builder(nc, n_dsem=16)
    d = b.dsems
    pr = PsumRot(b)
    xbuf = b.sb("xbuf", [128, KC, NCOL], F32)
    x = xbuf
    OT = xbuf[:].rearrange("p k n -> p (k n)").bitcast(BF16)[:, 0:KC * NQC].rearrange("p (k n) -> p k n", k=KC)
    h = b.sb("h", [128, KC, NCOL], BF16)
    rstd = b.sb("rstd", [128, NCOL], F32)
    xsq0 = b.sb("xsq0", [128, NCOL], BF16); xsq = [xsq0, xsq0]
    t32a = b.sb("t32a", [128, NCOL], F32); t32 = [t32a, t32a]
    NWB = 4
    wring = [b.sb("wr%d" % i, [128, KC, 128], BF16) for i in range(NWB)]
    Qgs = [b.sb("Qg%d" % i, [128, 2, NQC], BF16) for i in range(2)]
    Klos = [b.sb("Klo%d" % i, [128, NCOL], BF16) for i in range(2)]
    Khis = [b.sb("Khi%d" % i, [128, NCOL], BF16) for i in range(2)]
    Vds = [b.sb("Vd%d" % i, [128, 10, 64], BF16) for i in range(2)]
    Kclo = b.sb("Kclo", [128, 16, 128], BF16); Kchi = b.sb("Kchi", [128, 16, 128], BF16)
    Vcd = b.sb("Vcd", [128, 16, 64], BF16)
    sc = [b.sb("sc%d" % i, [128, 512], F32) for i in range(2)]
    P = [b.sb("P%d" % i, [128, 512], BF16) for i in range(4)]
    rden = b.sb("rden", [128, 512], F32)
    biasg = b.sb("biasg", [128, 4, 2, 128], F32)
    biasc = b.sb("biasc", [128, 4, 4], F32)
    biasn = b.sb("biasn", [64, 4, 64], F32)
    Pc = b.sb("Pc", [128, 16, 16], BF16)
    Pn = b.sb("Pn", [64, 16, 4, 4], BF16)
    scc = b.sb("scc", [128, 16, 16], F32)
    scn = b.sb("scn", [64, 4, 64], F32)
    vec = b.sb("vecs", [128, 4, KC], F32)
    mods = b.sb("modss", [128, 3, KC, 16], F32)
    nd = b.sb("nds", [128, 2, 128], F32); mk = b.sb("mks", [128, 2, 128], F32)
    ndc = b.sb("ndcs", [128, 4], F32); mkc = b.sb("mkcs", [128, 4], F32)
    ndn = b.sb("ndns", [64, 64], F32); mkn = b.sb("mkns", [64, 64], F32)
    esink = b.sb("esink", [128, 32], F32)
    hb = b.sb("hbs", [128, 1], F32)
    esg = b.sb("esg", [128, 4], F32)
    ones = b.sb("ones", [128, 128], BF16)
    gm = b.sb("gm", [128, KC], F32)
    gms = b.sb("gms", [128, KC, 16], F32)
    koutS = b.sb("koutS", [128, NKV // 2, 192], F32)
    voutS = b.sb("voutS", [128, 2, NKV, 64], F32)
    xr = [t32a[:, 0:NQC], rstd[:, 0:NQC]]
    tmps = b.sb("tmps", [128, NS], F32)

    R = Res
    r_x, r_h, r_rstd, r_const, r_gm, r_gms, r_Q, r_K, r_V, r_Kc, r_Vc, r_rden, r_bias, r_OT = [R(n) for n in
        "x h rstd const gm gms Q K V Kc Vc rden bias OT".split()]
    r_Pc, r_Pn, r_scc, r_scn, r_ko, r_vo, r_tmps = [R(n) for n in "Pc Pn scc scn ko vo tmps".split()]
    r_xsq0 = R("xsq"); r_xsq = [r_xsq0, r_xsq0]; r_t32a = R("t32"); r_t32 = [r_t32a, r_t32a]
    r_wr = [R("wr%d" % i) for i in range(NWB)]
    r_sc = [R("a"), R("b")]; r_P = [R("a") for _ in range(4)]
    r_xr = [r_t32a, r_rstd]
    r_OT = r_x

    b.dma("sp", x[:], xT, d[0], writes=[r_x])
    for i, (dst, src) in enumerate([(vec, vecd), (mods, modsd), (nd, ndd), (mk, mkd), (ndc, ndcd), (mkc, mkcd),
                                    (ndn, ndnd), (mkn, mknd), (esink, sinkd), (hb, hbd)]):
        b.dma("sp", dst[:], src, d[1], writes=[r_const])
    b.op("pool", lambda e: e.memset(ones[:], 1.0), writes=[r_const])
    b.op("pool", lambda e: e.memset(voutS[:], 0.0), writes=[r_vo])
    r_h = [Res("h%d" % k) for k in range(KC)]
    r_Qs = [Res("Q0"), Res("Q1")]; r_Ks = [Res("K0"), Res("K1")]; r_Vs = [Res("V0"), Res("V1")]
    for i_ in range(2):
        b.op("pool", lambda e, i_=i_: e.memset(Klos[i_][:], 0.0), writes=[r_Ks[i_]])
        b.op("pool", lambda e, i_=i_: e.memset(Khis[i_][:], 0.0), writes=[r_Ks[i_]])
    b.op("pool", lambda e: e.memset(Kclo[:], 0.0), writes=[r_Kc])
    b.op("pool", lambda e: e.memset(Kchi[:], 0.0), writes=[r_Kc])
    b.op("act", lambda e: e.activation(esink[:], esink[:], AF.Exp), reads=[r_const], writes=[r_const])

    wsem = [d[2], d[3], d[8], d[9]]
    NWT = NKV * 4 + KC

    def load_w(n):
        if n >= NWT:
            return
        src = wqkv[n // 4][:, n % 4, :, :] if n < NKV * 4 else wo[n - NKV * 4]
        b.dma("pool", wring[n % NWB][:], src, wsem[n % NWB], writes=[r_wr[n % NWB]])

    for n in range(4):
        load_w(n)
    emit_norm_mod(b, pr, x, r_x, h, r_h, NCOL, NH + NP, vec, 0, 1, 2, mods, 0, 1, ones, r_const,
                  (xsq, r_xsq, rstd, r_rstd, gm, r_gm, gms, r_gms, t32, r_t32))

    def proj(g):
        Qg = Qgs[g % 2]; Klo = Klos[g % 2]; Khi = Khis[g % 2]; Vd = Vds[g % 2]
        r_Q = r_Qs[g % 2]; r_K = r_Ks[g % 2]; r_V = r_Vs[g % 2]
        for which in range(2):
            wn = g * 4 + which; w = wring[wn % NWB]; rw = r_wr[wn % NWB]
            for (s, n) in ntiles(NQC):
                yield
                pt, rp = pr.next()
                for k in range(KC):
                    b.op("pe", lambda e, pt=pt, w=w, which=which, k=k, s=s, n=n: e.matmul(
                        pt[:, 0:n], w[:, k, :], h[:, k, NH + s:NH + s + n], start=(k == 0), stop=(k == KC - 1)),
                        reads=[rw, r_h[k]], writes=[rp], sig=(k == KC - 1))
                b.op("act", lambda e, pt=pt, which=which, s=s, n=n: e.activation(Qg[:, which, s:s + n], pt[:, 0:n], AF.Copy),
                     reads=[rp], writes=[r_Q])
            load_w(wn + 4)
        wn = g * 4 + 2; w = wring[wn % NWB]; rw = r_wr[wn % NWB]
        for (s, n) in ntiles(NCOL):
            yield
            pt, rp = pr.next()
            for k in range(KC):
                b.op("pe", lambda e, pt=pt, w=w, k=k, s=s, n=n: e.matmul(
                    pt[:, 0:n], w[:, k, :], h[:, k, s:s + n], start=(k == 0), stop=(k == KC - 1)),
                    reads=[rw, r_h[k]], writes=[rp], sig=(k == KC - 1))
            b.op("act", lambda e, pt=pt, s=s, n=n: e.activation(Klo[0:64, s:s + n], pt[0:64, 0:n], AF.Copy),
                 reads=[rp], writes=[r_K])
            b.op("act", lambda e, pt=pt, s=s, n=n: e.activation(Khi[64:128, s:s + n], pt[64:128, 0:n], AF.Copy),
                 reads=[rp], writes=[r_K])
            if s == 1024:
                lo = 64 * (g % 2)
                if lo == 0:
                    b.op("act", lambda e, pt=pt, g=g, lo=lo: e.activation(koutS[lo:lo + 64, g // 2, :], pt[lo:lo + 64, 0:192], AF.Copy),
                         reads=[rp], writes=[r_ko])
                else:
                    b.op("act", lambda e, pt=pt, g=g, lo=lo: e.activation(koutS[lo:lo + 64, g // 2, :], pt[lo:lo + 64, 0:192], AF.Copy),
                         reads=[rp], writes=[r_ko])
        load_w(wn + 4)
        wn = g * 4 + 3; w = wring[wn % NWB]; rw = r_wr[wn % NWB]
        for blk in range(10):
            m = 128 if blk < 9 else NS
            c0 = blk * 128
            yield
            pt, rp = pr.next()
            for k in range(KC):
                b.op("pe", lambda e, pt=pt, w=w, k=k, c0=c0, m=m: e.matmul(
                    pt[0:m, 0:64], h[:, k, c0:c0 + m], w[:, k, 0:64], start=(k == 0), stop=(k == KC - 1)),
                    reads=[rw, r_h[k]], writes=[rp], sig=(k == KC - 1))
            b.op("dve", lambda e, pt=pt, blk=blk, m=m: e.tensor_copy(Vd[0:m, blk, :], pt[0:m, 0:64]),
                 reads=[rp], writes=[r_V])
            if blk >= 8:
                b.op("dve", lambda e, pt=pt, blk=blk, m=m, g=g: e.tensor_copy(voutS[0:m, blk - 8, g, :], pt[0:m, 0:64]),
                     reads=[rp], writes=[r_vo])
        load_w(wn + 4)
        yield

    def attn(g):
        Qg = Qgs[g % 2]; Klo = Klos[g % 2]; Khi = Khis[g % 2]; Vd = Vds[g % 2]
        r_Q = r_Qs[g % 2]; r_K = r_Ks[g % 2]; r_V = r_Vs[g % 2]
        b.dma("pool", Kclo[0:64, :, :], kcT[g], d[4], writes=[r_Kc])
        b.dma("pool", Kchi[64:128, :, :], kcT[g], d[5], writes=[r_Kc])
        b.dma("pool", Vcd[:], vc[g], d[6], writes=[r_Vc])
        b.op("dve", lambda e, g=g: e.tensor_copy(esg[:], esink[:, 4 * g:4 * g + 4]), reads=[r_const], writes=[r_bias])
        for hq in range(4):
            sl = alibi_slope(4 * g + hq)
            b.op("dve", lambda e, hq=hq, sl=sl: e.scalar_tensor_tensor(biasg[:, hq, :, :], nd[:], sl, mk[:], ALU.mult, ALU.add),
                 reads=[r_const], writes=[r_bias])
            b.op("dve", lambda e, hq=hq, sl=sl: e.scalar_tensor_tensor(biasc[:, hq, :], ndc[:], sl, mkc[:], ALU.mult, ALU.add),
                 reads=[r_const], writes=[r_bias])
            b.op("dve", lambda e, hq=hq, sl=sl: e.scalar_tensor_tensor(biasn[:, hq, :], ndn[:], sl, mkn[:], ALU.mult, ALU.add),
                 reads=[r_const], writes=[r_bias])
        for i in range(1, NB + 1):
            yield
            qc = (i - 1) * 128
            ptd, rpd = pr.next()
            ptv, rpv = pr.next()
            Pp = []
            for pair in range(2):
                pts, rps = pr.next()
                for hh in range(2):
                    hq = pair * 2 + hh
                    Kx = Klo if hh == 0 else Khi
                    for j in range(2):
                        kc0 = (i - 1 + j) * 128
                        b.op("pe", lambda e, pts=pts, hh=hh, j=j, Kx=Kx, kc0=kc0, pair=pair, qc=qc: e.matmul(
                            pts[:, (hh * 2 + j) * 128:(hh * 2 + j + 1) * 128], Kx[:, kc0:kc0 + 128], Qg[:, pair, qc:qc + 128],
                            start=True, stop=True), reads=[r_K, r_Q], writes=[rps], sig=(hh == 1 and j == 1))
                si = pair
                b.op("dve", lambda e, pts=pts, si=si, pair=pair: e.scalar_tensor_tensor(
                    sc[si][:], pts[:, 0:512], SCALE, biasg[:, pair * 2:pair * 2 + 2, :, :].rearrange("p a b c -> p (a b c)"),
                    ALU.mult, ALU.add), reads=[rps, r_bias], writes=[r_sc[si]])
                if i == 1:
                    b.op("dve", lambda e, si=si: e.tensor_scalar(
                        sc[si][:].rearrange("p (a b c) -> p a b c", a=2, b=2)[:, :, 0, :],
                        sc[si][:].rearrange("p (a b c) -> p a b c", a=2, b=2)[:, :, 0, :], hb[:, 0:1], None, ALU.add),
                        reads=[r_sc[si], r_const], writes=[r_sc[si]])
                pi = (i % 2) * 2 + pair
                b.op("act", lambda e, pi=pi, si=si: e.activation(P[pi][:], sc[si][:], AF.Exp),
                     reads=[r_sc[si]], writes=[r_P[pi]])
                Pp.append(pi)
            for pair in range(2):
                pi = Pp[pair]
                for hh in range(2):
                    hq = pair * 2 + hh
                    for j in range(2):
                        b.op("pe", lambda e, pi=pi, hh=hh, j=j, hq=hq: e.matmul(
                            ptd[:, hq * 128:(hq + 1) * 128], ones[:], P[pi][:, (hh * 2 + j) * 128:(hh * 2 + j + 1) * 128],
                            start=(j == 0), stop=(j == 1)), reads=[r_P[pi], r_const], writes=[rpd],
                            sig=(pair == 1 and hh == 1 and j == 1))
            for pair in range(2):
                pi = Pp[pair]
                for hh in range(2):
                    hq = pair * 2 + hh
                    for j in range(2):
                        blk = i - 1 + j
                        b.op("pe", lambda e, pi=pi, hh=hh, j=j, hq=hq, blk=blk: e.matmul(
                            ptv[64 * hh:64 * hh + 64, hq * 128:(hq + 1) * 128], Vd[:, blk, :], P[pi][:, (hh * 2 + j) * 128:(hh * 2 + j + 1) * 128],
                            start=(j == 0), stop=(j == 1)), reads=[r_P[pi], r_V], writes=[rpv],
                            sig=(pair == 1 and hh == 1 and j == 1))
            b.op("dve", lambda e, ptd=ptd, g=g: e.tensor_tensor(
                rden[:].rearrange("p (a q) -> p a q", a=4), ptd[:, 0:512].rearrange("p (a q) -> p a q", a=4),
                esg[:].unsqueeze(2).broadcast_to([128, 4, 128]), ALU.add),
                reads=[rpd, r_bias], writes=[r_rden])
            b.op("act", lambda e: e.activation(rden[:], rden[:], AF.Ln), reads=[r_rden], writes=[r_rden])
            b.op("act", lambda e: e.activation(rden[:], rden[:], AF.Exp, scale=-1.0), reads=[r_rden], writes=[r_rden])
            for hq in range(4):
                lo = 0 if hq % 2 == 0 else 64
                ch = (2 * g + hq // 2)
                b.op("dve", lambda e, ptv=ptv, hq=hq, lo=lo, ch=ch, qc=qc: e.tensor_tensor(
                    OT[lo:lo + 64, ch, qc:qc + 128], ptv[lo:lo + 64, hq * 128:(hq + 1) * 128],
                    rden[lo:lo + 64, hq * 128:(hq + 1) * 128], ALU.mult),
                    reads=[rpv, r_rden, r_x], writes=[r_OT])
        yield
        ptc, rpc = pr.next()
        ptn, rpn = pr.next()
        for sq in range(16):
            for hq in range(4):
                Kx = Kclo if hq % 2 == 0 else Kchi
                b.op("pe", lambda e, sq=sq, hq=hq, Kx=Kx: e.matmul(
                    ptc[:, sq * 16 + hq * 4:sq * 16 + hq * 4 + 4], Kx[:, sq, :], Qg[:, hq // 2, NP + sq * 4:NP + sq * 4 + 4],
                    start=True, stop=True), reads=[r_Kc, r_Q], writes=[rpc], sig=(sq == 15 and hq == 3))
        for hq in range(4):
            Kx = Klo if hq % 2 == 0 else Khi
            b.op("pe", lambda e, hq=hq, Kx=Kx: e.matmul(
                ptn[0:64, hq * 64:(hq + 1) * 64], Kx[:, NH + NP:NCOL], Qg[:, hq // 2, NP:NQC],
                start=True, stop=True), reads=[r_K, r_Q], writes=[rpn], sig=(hq == 3))
        b.op("dve", lambda e: e.scalar_tensor_tensor(
            scc[:].rearrange("p s (a t) -> p s a t", a=4), ptc[:, 0:256].rearrange("p (s a t) -> p s a t", s=16, a=4),
            SCALE, biasc[:].unsqueeze(1).broadcast_to([128, 16, 4, 4]), ALU.mult, ALU.add),
            reads=[rpc, r_bias], writes=[r_scc])
        b.op("act", lambda e: e.activation(Pc[:], scc[:], AF.Exp), reads=[r_scc], writes=[r_Pc])
        b.op("dve", lambda e: e.scalar_tensor_tensor(
            scn[:].rearrange("p a n -> p (a n)"), ptn[0:64, 0:256], SCALE, biasn[:].rearrange("p a n -> p (a n)"),
            ALU.mult, ALU.add), reads=[rpn, r_bias], writes=[r_scn])
        b.op("act", lambda e: e.activation(
            Pn[:].rearrange("p s a t -> p a s t"), scn[:].rearrange("p a (s t) -> p a s t", t=4), AF.Exp),
            reads=[r_scn], writes=[r_Pn])
        ptd, rpd = pr.next()
        ptv, rpv = pr.next()
        b.op("pe", lambda e: e.matmul(ptd[:, 0:256], ones[:], Pc[:].rearrange("p s c -> p (s c)"), start=True, stop=False),
             reads=[r_Pc, r_const], writes=[rpd], sig=False)
        b.op("pe", lambda e: e.matmul(ptd[:, 0:256], ones[0:64, :], Pn[:].rearrange("p s a t -> p (s a t)"), start=False, stop=True),
             reads=[r_Pn, r_const], writes=[rpd])
        for sq in range(16):
            for lo in (0, 64):
                b.op("pe", lambda e, sq=sq, lo=lo: e.matmul(ptv[lo:lo + 64, sq * 16:(sq + 1) * 16], Vcd[:, sq, :], Pc[:, sq, :], start=True, stop=False),
                     reads=[r_Pc, r_Vc], writes=[rpv], sig=False)
                b.op("pe", lambda e, sq=sq, lo=lo: e.matmul(ptv[lo:lo + 64, sq * 16:(sq + 1) * 16], Vd[0:64, 9, :],
                                                     Pn[:, sq, :, :].rearrange("p a t -> p (a t)"), start=False, stop=True),
                     reads=[r_Pn, r_V], writes=[rpv], sig=(sq == 15 and lo == 64))
        b.op("dve", lambda e, g=g: e.tensor_tensor(
            rden[:, 0:256].rearrange("p (s a t) -> p s a t", s=16, a=4), ptd[:, 0:256].rearrange("p (s a t) -> p s a t", s=16, a=4),
            esg[:].unsqueeze(1).unsqueeze(3).broadcast_to([128, 16, 4, 4]), ALU.add),
            reads=[rpd, r_bias], writes=[r_rden])
        b.op("act", lambda e: e.activation(rden[:, 0:256], rden[:, 0:256], AF.Ln), reads=[r_rden], writes=[r_rden])
        b.op("act", lambda e: e.activation(rden[:, 0:256], rden[:, 0:256], AF.Exp, scale=-1.0), reads=[r_rden], writes=[r_rden])
        for hq in range(4):
            lo = 0 if hq % 2 == 0 else 64
            ch = (2 * g + hq // 2)
            b.op("dve", lambda e, hq=hq, lo=lo, ch=ch: e.tensor_tensor(
                OT[lo:lo + 64, ch, NP:NQC].rearrange("p (s t) -> p s t", t=4),
                ptv[lo:lo + 64, 0:256].rearrange("p (s a t) -> p s a t", s=16, a=4)[:, :, hq, :],
                rden[lo:lo + 64, 0:256].rearrange("p (s a t) -> p s a t", s=16, a=4)[:, :, hq, :], ALU.mult),
                reads=[rpv, r_rden, r_x], writes=[r_OT])

        yield

    def drain(gen):
        for _ in gen:
            pass

    drain(proj(0))
    for g in range(NKV):
        crit = attn(g)
        fill = proj(g + 1) if g + 1 < NKV else iter(())
        done_c = done_f = False
        while not (done_c and done_f):
            if not done_c:
                try:
                    next(crit)
                except StopIteration:
                    done_c = True
            for _ in range(2):
                if not done_f:
                    try:
                        next(fill)
                    except StopIteration:
                        done_f = True
    xsem = [d[11], d[12]]
    osem = [d[13], d[14]]
    outs = []
    for i in range(KC):
        wn = NKV * 4 + i; w = wring[wn % NWB]; rw = r_wr[wn % NWB]
        xi = xr[i % 2]; rxi = r_xr[i % 2]
        b.dma("sp", xi, xT[:, i, NH:NCOL], xsem[i % 2], writes=[rxi])
        for (s, n) in ntiles(NQC):
            pt, rp = pr.next()
            for k in range(KC):
                b.op("pe", lambda e, pt=pt, w=w, k=k, s=s, n=n: e.matmul(
                    pt[:, 0:n], w[:, k, :], OT[:, k, s:s + n], start=(k == 0), stop=(k == KC - 1)),
                    reads=[rw, r_OT], writes=[rp], sig=(k == KC - 1))
            if s + n <= NP:
                b.op("dve", lambda e, pt=pt, i=i, s=s, n=n, xi=xi: e.scalar_tensor_tensor(
                    xi[:, s:s + n], pt[:, 0:n], vec[:, 3, i:i + 1], xi[:, s:s + n], ALU.mult, ALU.add),
                    reads=[rp, r_const, rxi], writes=[rxi])
            else:
                b.op("dve", lambda e, pt=pt, i=i: e.tensor_tensor(
                    tmps[:].rearrange("p (s t) -> p s t", t=4), pt[:, 0:NS].rearrange("p (s t) -> p s t", t=4),
                    mods[:, 2, i, :].unsqueeze(2).broadcast_to([128, 16, 4]), ALU.mult),
                    reads=[rp, r_const], writes=[r_tmps])
                b.op("dve", lambda e, xi=xi: e.tensor_tensor(xi[:, NP:NQC], xi[:, NP:NQC], tmps[:], ALU.add),
                     reads=[r_tmps, rxi], writes=[rxi])
        outs.append(b.dma("sp", xo[:, i, :], xi, osem[i % 2], reads=[rxi]))
        load_w(wn + 4)
    outs.append(b.dma("sp", kout, koutS[:], d[15], reads=[r_ko]))
    outs.append(b.dma("sp", vout, voutS[:], d[7], reads=[r_vo]))
    b.wait_all("sp", outs)
    b.emit()
    b.close()
    return nc


def build_adaln():
    nc = bass.Bass("TRN2", target_bir_lowering=False)
    NSEQ = 129
    NCH = 24
    dram = lambda name, shape, kind="ExternalInput": nc.dram_tensor(name, list(shape), F32, kind=kind).ap()
    cT = dram("cT", [128, KC, NSEQ])
    wada = dram("wada", [NCH, 128, KC, 128])
    bada = dram("bada", [128, NCH])
    modT = dram("modT", [128, NCH, NSEQ], "ExternalOutput")
    b = Builder(nc, n_dsem=8)
    d = b.dsems
    pr = PsumRot(b)
    cs = b.sb("cs", [128, KC, NSEQ], F32)
    sc = b.sb("sc", [128, KC, NSEQ], BF16)
    bs = b.sb("bs", [128, NCH], F32)
    outT = b.sb("outT", [128, NCH, NSEQ], F32)
    NWB = 4
    wr = [b.sb("wr%d" % i, [128, KC, 128], BF16) for i in range(NWB)]
    r_c, r_sc, r_b, r_out = Res(), Res(), Res(), Res()
    r_wr = [Res() for _ in range(NWB)]
    b.dma("sp", cs[:], cT, d[0], writes=[r_c])
    b.dma("sp", bs[:], bada, d[1], writes=[r_b])

    def load_w(n):
        if n < NCH:
            b.dma("pool", wr[n % NWB][:], wada[n], d[2 + n % NWB], writes=[r_wr[n % NWB]])
    for n in range(NWB - 1):
        load_w(n)
    b.op("act", lambda e: e.activation(sc[:], cs[:], AF.Silu), reads=[r_c], writes=[r_sc])
    for n in range(NCH):
        load_w(n + NWB - 1)
        w, rw = wr[n % NWB], r_wr[n % NWB]
        pt, rp = pr.next()
        for k in range(KC):
            b.op("pe", lambda e, pt=pt, w=w, k=k: e.matmul(pt[:, 0:NSEQ], w[:, k, :], sc[:, k, :], start=(k == 0), stop=(k == KC - 1)),
                 reads=[rw, r_sc], writes=[rp], sig=(k == KC - 1))
        b.op("act", lambda e, pt=pt, n=n: e.activation(outT[:, n, :], pt[:, 0:NSEQ], AF.Identity, bias=bs[:, n:n + 1], scale=1.0),
             reads=[rp, r_b], writes=[r_out])
    t = b.dma("sp", modT, outT[:], d[6], reads=[r_out])
    b.wait_all("sp", [t])
    b.emit(); b.close()
    return nc


NHH = 16
CH = 32
NRK = 7


def cumsum_chunks(b, engs, bufA, rA, bufB, rB, ncol0, ncols, clen):
    src, rs, dst, rd = bufA, rA, bufB, rB
    s = 1
    i = 0
    while s < clen:
        sv = src[:, ncol0:ncol0 + ncols].rearrange("p (c t) -> p c t", t=clen)
        dv = dst[:, ncol0:ncol0 + ncols].rearrange("p (c t) -> p c t", t=clen)
        eng = engs[i % len(engs)]
        b.op(eng, lambda e, dv=dv, sv=sv, s=s: e.tensor_tensor(dv[:, :, s:clen], sv[:, :, s:clen], sv[:, :, 0:clen - s], ALU.add),
             reads=[rs], writes=[rd])
        b.op(eng, lambda e, dv=dv, sv=sv, s=s: e.tensor_copy(dv[:, :, 0:s], sv[:, :, 0:s]), reads=[rs], writes=[rd])
        src, rs, dst, rd = dst, rd, src, rs
        s *= 2
        i += 1
    return src, rs


def build_hgrn(pass1, modeA=False):
    nc = bass.Bass("TRN2", target_bir_lowering=False)
    NT = NP if pass1 else NP + NS
    NBLK = NP // 128
    NCK = NP // CH
    dram = lambda name, shape, kind="ExternalInput": nc.dram_tensor(name, list(shape), F32, kind=kind).ap()
    xT = dram("xT", [128, KC, NT])
    vecd = dram("vec", [128, 4, KC])
    modsd = dram("mods", [128, 3, KC, 16])
    whg = dram("whg", [NHH, 128, 4, KC, 128])
    lbpd = dram("lbp", [128, 2, NHH])
    m01d = dram("m01", [128, 128]); cmd = dram("cm", [128, 4, 128])
    identd = dram("ident", [128, 128])
    if not pass1:
        if not modeA:
            wo = dram("wo", [KC, 128, KC, 128])
            ngd = dram("ng", [128, NHH])
            srd = dram("sr", [NHH, 128, NRK, 128])
            drd = dram("dr", [128, NHH, NRK])
            xo = dram("xo", [128, KC, NT], "ExternalOutput")
        else:
            oloc = dram("oloc", [NHH, 128, NT], "ExternalOutput")
            qbo = dram("qbo", [NHH, 128, NT], "ExternalOutput")
            sgo = dram("sgo", [NHH, 128, NT], "ExternalOutput")
        s0d = dram("s0", [NHH, 128, 16, 128])
        msd = dram("ms", [64, 64]); cmsd = dram("cms", [128, 16, 64])
        snew = dram("snew", [NHH, 128, 16, 128], "ExternalOutput")
    send = dram("send", [128, NHH, 128], "ExternalOutput")
    dout = dram("dout", [128, NHH], "ExternalOutput")

    b = Builder(nc, n_dsem=18)
    d = b.dsems
    pr = PsumRot(b)
    xbuf = b.sb("xbuf", [128, KC, NT], F32)
    x = xbuf
    O2T = xbuf[:].rearrange("p k n -> p (k n)").bitcast(BF16)[:, 0:KC * NT].rearrange("p (k n) -> p k n", k=KC)
    h = b.sb("h", [128, KC, NT], BF16)
    rstd = b.sb("rstd", [128, NT], F32)
    xsq0 = b.sb("xsq0", [128, NT], BF16)
    t32a = b.sb("t32a", [128, NT], F32)
    NWB = 5
    wring = [b.sb("wr%d" % i, [128, KC, 128], BF16) for i in range(NWB)]
    qf = b.sb("qf", [128, NT], F32); kf = b.sb("kf", [128, NT], F32)
    bA = b.sb("bA", [128, NT], F32); bB = b.sb("bB", [128, NT], F32)
    sg = b.sb("sg", [128, NT], F32); Oraw = b.sb("Oraw", [128, NT], F32)
    qt = b.sb("qt", [128, NT], BF16); kt = b.sb("kt", [128, NT], BF16); kh = b.sb("kh", [128, NT], BF16)
    dec = b.sb("dec", [128, NCK + 16], F32)
    Vt = b.sb("Vt", [128, NBLK + 1, 128], BF16)
    At = b.sb("At", [128, 128], BF16)
    Khm = b.sb("Khm", [128, 4, 128], BF16)
    KhmT = b.sb("KhmT", [128, 4, 128], BF16)
    S = b.sb("S", [128, 128], F32); Sbf = b.sb("Sbf", [128, 128], BF16)
    Dall = b.sb("Dall", [128, NHH], F32)
    btot = b.sb("btot", [128, 1], F32)
    vec = b.sb("vecs", [128, 4, KC], F32)
    mods = b.sb("modss", [128, 3, KC, 16], F32)
    lbp = b.sb("lbps", [128, 2, NHH], F32)
    oml = b.sb("oml", [128, NHH], F32)
    m01 = b.sb("m01s", [128, 128], F32); cm = b.sb("cms_", [128, 4, 128], F32)
    ident = b.sb("idents", [128, 128], BF16); identf = b.sb("identf", [128, 128], F32)
    ones = b.sb("ones", [128, 128], BF16)
    gm = b.sb("gm", [128, KC], F32); gms = b.sb("gms", [128, KC, 16], F32)
    if not pass1:
        if not modeA:
            ng = b.sb("ngs", [128, NHH], F32)
            sr = b.sb("srs", [128, NRK, 128], F32)
            dr = b.sb("drs", [128, NHH, NRK], F32)
        else:
            QBf = b.sb("QBf", [128, NT], F32)
            pbA = b.sb("pbA", [128, NCK], F32); pbB = b.sb("pbB", [128, NCK], F32)
            r_QBf, r_pbA, r_pbB = Res("QBf"), Res("pbA"), Res("pbB")
        S0 = b.sb("S0", [128, 16, 128], F32); S0b = b.sb("S0b", [128, 16, 128], BF16)
        ms = b.sb("mss", [64, 64], F32); cms = b.sb("cmss", [128, 16, 64], F32)
        Ats = b.sb("Ats", [64, 64], BF16)
        Khms = b.sb("Khms", [128, 16, 64], BF16)
        KhmTs = b.sb("KhmTs", [64, 16, 128], BF16)
        tmps = b.sb("tmps", [128, NS], F32)
    R = Res
    r_x, r_h, r_rstd, r_const, r_gm, r_gms = [R(n) for n in "x h rstd const gm gms".split()]
    r_xsq0, r_t32a = R("xsq"), R("t32")
    r_wr = [R("wr%d" % i) for i in range(NWB)]
    r_qf, r_kf, r_bA, r_bB, r_sg, r_Oraw, r_qt, r_kt, r_kh, r_dec, r_Vt, r_At, r_Khm, r_KhmT, r_S, r_Sbf, r_Sall, r_bt = [
        R(n) for n in "qf kf bA bB sg Oraw qt kt kh dec Vt At Khm KhmT S Sbf Sall bt".split()]
    r_sr, r_S0, r_S0b, r_Ats, r_Khms, r_KhmTs, r_tmps = [R(n) for n in "sr S0 S0b Ats Khms KhmTs tmps".split()]
    r_O2T = r_x

    b.dma("sp", x[:], xT, d[0], writes=[r_x])
    cl = [(vec, vecd), (mods, modsd), (lbp, lbpd), (m01, m01d), (cm, cmd), (identf, identd)]
    if not pass1:
        cl += [(ms, msd), (cms, cmsd)] + ([] if modeA else [(ng, ngd), (dr, drd)])
    for dst, src in cl:
        b.dma("sp", dst[:], src, d[1], writes=[r_const])
    b.op("pool", lambda e: e.memset(ones[:], 1.0), writes=[r_const])
    b.op("act", lambda e: e.activation(ident[:], identf[:], AF.Copy), reads=[r_const], writes=[r_const])
    b.op("dve", lambda e: e.tensor_tensor(oml[:], lbp[:, 0, :], lbp[:, 1, :], ALU.subtract), reads=[r_const], writes=[r_const])
    b.op("act", lambda e: e.activation(oml[:], oml[:], AF.Sigmoid), reads=[r_const], writes=[r_const])

    wsem = [d[2], d[3], d[4], d[5], d[6]]
    NWT = NHH * 4 + (0 if (pass1 or modeA) else KC)
    used = [1, 2] if pass1 else [0, 1, 2, 3]
    wlist = [(hh, wh) for hh in range(NHH) for wh in used] + ([("o", i) for i in range(KC)] if not (pass1 or modeA) else [])

    def load_w(n):
        if n >= len(wlist):
            return
        a, c = wlist[n]
        src = wo[c] if a == "o" else whg[a][:, c, :, :]
        b.dma("pool", wring[n % NWB][:], src, wsem[n % NWB], writes=[r_wr[n % NWB]])
    for n in range(4):
        load_w(n)
    wcount = [0]

    def next_w():
        n = wcount[0]
        wcount[0] += 1
        return n, wring[n % NWB], r_wr[n % NWB]

    emit_norm_mod(b, pr, x, r_x, h, r_h, NT, NP, vec, 0, 1, 2, mods, 0, 1, ones, r_const,
                  ([xsq0, xsq0], [r_xsq0, r_xsq0], rstd, r_rstd, gm, r_gm, gms, r_gms, [t32a, t32a], [r_t32a, r_t32a]))

    def proj_fm(dst_fn):
        n_, w, rw = next_w()
        for (s, n) in ntiles(NT):
            pt, rp = pr.next()
            for k in range(KC):
                b.op("pe", lambda e, pt=pt, w=w, k=k, s=s, n=n: e.matmul(
                    pt[:, 0:n], w[:, k, :], h[:, k, s:s + n], start=(k == 0), stop=(k == KC - 1)),
                    reads=[rw, r_h], writes=[rp], sig=(k == KC - 1))
            dst_fn(pt, rp, s, n)
        load_w(n_ + 4)

    outs = []
    for hh in range(NHH):
        if not pass1:
            if not modeA:
                b.dma("sp", sr[:], srd[hh], d[7], writes=[r_sr])
            b.dma("sp", S0[:], s0d[hh], d[8], writes=[r_S0])
            b.dma("pool", S0b[:], s0d[hh], d[9], writes=[r_S0b])
            proj_fm(lambda pt, rp, s, n: b.op("act", lambda e: e.activation(qf[:, s:s + n], pt[:, 0:n], AF.Silu),
                                              reads=[rp], writes=[r_qf]))
        proj_fm(lambda pt, rp, s, n: b.op("act", lambda e: e.activation(kf[:, s:s + n], pt[:, 0:n], AF.Sigmoid, scale=-1.0),
                                          reads=[rp], writes=[r_kf]))
        b.op("dve", lambda e, hh=hh: e.tensor_scalar(kf[:], kf[:], oml[:, hh:hh + 1], None, ALU.mult),
             reads=[r_kf, r_const], writes=[r_kf])
        b.op("act", lambda e: e.activation(bA[:], kf[:], AF.Ln, bias=1.0, scale=-1.0), reads=[r_kf], writes=[r_bA])
        n_, w, rw = next_w()
        for blk in range(NBLK + (0 if pass1 else 1)):
            m = 128 if blk < NBLK else NS
            c0 = blk * 128
            pt, rp = pr.next()
            for k in range(KC):
                b.op("pe", lambda e, pt=pt, w=w, k=k, c0=c0, m=m: e.matmul(
                    pt[0:m, 0:128], h[:, k, c0:c0 + m], w[:, k, :], start=(k == 0), stop=(k == KC - 1)),
                    reads=[rw, r_h], writes=[rp], sig=(k == KC - 1))
            b.op("act", lambda e, pt=pt, blk=blk, m=m: e.activation(Vt[0:m, blk, :], pt[0:m, 0:128], AF.Copy),
                 reads=[rp], writes=[r_Vt])
        load_w(n_ + 4)
        if not pass1:
            proj_fm(lambda pt, rp, s, n: b.op("act", lambda e: e.activation(sg[:, s:s + n], pt[:, 0:n], AF.Silu),
                                              reads=[rp], writes=[r_sg]))
        bb, rbb = cumsum_chunks(b, ["dve", "pool"], bA, r_bA, bB, r_bB, 0, NP, CH)
        other, rother = (bB, r_bB) if bb is bA else (bA, r_bA)
        if not pass1:
            if bb is not bA:
                b.op("pool", lambda e: e.tensor_copy(bB[:, NP:NT], bA[:, NP:NT]), reads=[r_bA], writes=[r_bB])
            sb_, rsb = cumsum_chunks(b, ["pool"], bb, rbb, other, rother, NP, NS, 4)
            if sb_ is not bb:
                b.op("pool", lambda e, sb_=sb_, bb=bb: e.tensor_copy(bb[:, NP:NT], sb_[:, NP:NT]), reads=[rsb], writes=[rbb])
        bv = bb[:, 0:NP].rearrange("p (c t) -> p c t", t=CH)
        ov = other[:, 0:NP].rearrange("p (c t) -> p c t", t=CH)
        b.op("act", lambda e, bv=bv: e.activation(dec[:, 0:NCK].unsqueeze(2), bv[:, :, CH - 1:CH], AF.Exp), reads=[rbb], writes=[r_dec])
        b.op("dve", lambda e, bv=bv, ov=ov: e.tensor_tensor(ov, bv[:, :, CH - 1:CH].broadcast_to([128, NCK, CH]), bv, ALU.subtract),
             reads=[rbb], writes=[rother])
        if not pass1:
            bs_ = bb[:, NP:NT].rearrange("p (c t) -> p c t", t=4)
            os_ = other[:, NP:NT].rearrange("p (c t) -> p c t", t=4)
            b.op("act", lambda e, bs_=bs_: e.activation(dec[:, NCK:NCK + 16].unsqueeze(2), bs_[:, :, 3:4], AF.Exp), reads=[rbb], writes=[r_dec])
            b.op("dve", lambda e, bs_=bs_, os_=os_: e.tensor_tensor(os_, bs_[:, :, 3:4].broadcast_to([128, 16, 4]), bs_, ALU.subtract),
                 reads=[rbb], writes=[rother])
        b.op("act", lambda e, other=other: e.activation(other[:, 0:NT], other[:, 0:NT], AF.Exp), reads=[rother], writes=[rother])
        b.op("dve", lambda e, other=other: e.tensor_tensor(kh[:], kf[:], other[:, 0:NT], ALU.mult), reads=[rother, r_kf], writes=[r_kh])
        if pass1:
            b.op("dve", lambda e, bv=bv: e.tensor_reduce(btot[:], bv[:, :, CH - 1], AX.X, ALU.add), reads=[rbb], writes=[r_bt])
            b.op("act", lambda e, hh=hh: e.activation(Dall[:, hh:hh + 1], btot[:], AF.Exp), reads=[r_bt], writes=[r_Sall])
        else:
            b.op("act", lambda e, other=other, bb=bb: e.activation(other[:, 0:NT], bb[:, 0:NT], AF.Exp), reads=[rbb, r_kh], writes=[rother])
            b.op("dve", lambda e, other=other: e.tensor_tensor(qt[:], qf[:], other[:, 0:NT], ALU.mult), reads=[rother, r_qf], writes=[r_qt])
            b.op("act", lambda e, other=other, bb=bb: e.activation(other[:, 0:NT], bb[:, 0:NT], AF.Exp, scale=-1.0), reads=[rbb, r_qt], writes=[rother])
            b.op("dve", lambda e, other=other: e.tensor_tensor(kt[:], kf[:], other[:, 0:NT], ALU.mult), reads=[rother, r_kf], writes=[r_kt])
            if modeA:
                b.op("pool", lambda e, bv=bv: e.tensor_copy(pbA[:].unsqueeze(2), bv[:, :, CH - 1:CH]), reads=[rbb], writes=[r_pbA])
                pin, rpin = cumsum_chunks(b, ["pool"], pbA, r_pbA, pbB, r_pbB, 0, NCK, NCK)
                pex, rpex = (pbB, r_pbB) if pin is pbA else (pbA, r_pbA)
                b.op("act", lambda e, hh=hh, pin=pin: e.activation(Dall[:, hh:hh + 1], pin[:, NCK - 1:NCK], AF.Exp), reads=[rpin], writes=[r_Sall])
                b.op("pool", lambda e, pin=pin, pex=pex, bv=bv: e.tensor_tensor(pex[:].unsqueeze(2), pin[:].unsqueeze(2), bv[:, :, CH - 1:CH], ALU.subtract),
                     reads=[rpin, rbb], writes=[rpex])
                b.op("act", lambda e, pex=pex: e.activation(pex[:], pex[:], AF.Exp), reads=[rpex], writes=[rpex])
                b.op("pool", lambda e, pex=pex: e.tensor_tensor(
                    QBf[:, 0:NP].rearrange("p (c t) -> p c t", t=CH), qt[:, 0:NP].rearrange("p (c t) -> p c t", t=CH),
                    pex[:].unsqueeze(2).broadcast_to([128, NCK, CH]), ALU.mult), reads=[rpex, r_qt], writes=[r_QBf])
                b.op("pool", lambda e: e.memset(QBf[:, NP:NT], 0.0), writes=[r_QBf])
                outs.append(b.dma("sp", qbo[hh], QBf[:], d[7], reads=[r_QBf]))
                outs.append(b.dma("sp", sgo[hh], sg[:], d[13], reads=[r_sg]))
        if pass1 or modeA:
            b.op("pool", lambda e: e.memset(S[:], 0.0), writes=[r_S])
            if modeA:
                b.op("pool", lambda e: e.memset(Sbf[:], 0.0), writes=[r_Sbf])
        else:
            b.op("dve", lambda e, hh=hh: e.tensor_scalar(S[:], sr[:, 0, :], 1.0, None, ALU.mult), reads=[r_sr], writes=[r_S])
            for r in range(1, NRK):
                b.op("dve", lambda e, hh=hh, r=r: e.scalar_tensor_tensor(S[:], S[:], dr[:, hh, r:r + 1], sr[:, r, :], ALU.mult, ALU.add),
                     reads=[r_S, r_sr, r_const], writes=[r_S])
            b.op("act", lambda e: e.activation(Sbf[:], S[:], AF.Copy), reads=[r_S], writes=[r_Sbf])
        for blk in range(NBLK):
            c0 = blk * 128
            b.op("dve", lambda e, c0=c0: e.tensor_tensor(Khm[:], kh[:, c0:c0 + 128].unsqueeze(1).broadcast_to([128, 4, 128]), cm[:], ALU.mult),
                 reads=[r_kh, r_const], writes=[r_Khm])
            ptT, rpT = pr.next()
            ptTb = ptT[:, :].bitcast(BF16)
            for c in range(4):
                b.op("pe", lambda e, ptTb=ptTb, c=c: e.transpose(ptTb[:, c * 128:(c + 1) * 128], Khm[:, c, :], ident[:]),
                     reads=[r_Khm, r_const], writes=[rpT], sig=(c == 3))
            b.op("act", lambda e, ptTb=ptTb: e.activation(KhmT[:].rearrange("p c k -> p (c k)"), ptTb[:, 0:512], AF.Copy),
                 reads=[rpT], writes=[r_KhmT])
            if not pass1:
                pa, rpa = pr.next()
                b.op("pe", lambda e, pa=pa, c0=c0: e.matmul(pa[:, 0:128], kt[:, c0:c0 + 128], qt[:, c0:c0 + 128], start=True, stop=True),
                     reads=[r_kt, r_qt], writes=[rpa])
                b.op("dve", lambda e, pa=pa: e.tensor_tensor(At[:], pa[:, 0:128], m01[:], ALU.mult), reads=[rpa, r_const], writes=[r_At])
                po, rpo = pr.next()
                b.op("pe", lambda e, po=po, blk=blk: e.matmul(po[:, 0:128], Vt[:, blk, :], At[:], start=True, stop=False),
                     reads=[r_Vt, r_At], writes=[rpo], sig=False)
            for c in range(4):
                ck = blk * 4 + c
                if not pass1:
                    b.op("pe", lambda e, po=po, c=c, c0=c0: e.matmul(po[:, c * CH:(c + 1) * CH], Sbf[:], qt[:, c0 + c * CH:c0 + (c + 1) * CH],
                                                                   start=False, stop=(c == 3)),
                         reads=[r_Sbf, r_qt], writes=[rpo], sig=True)
                pu, rpu = pr.next()
                b.op("pe", lambda e, pu=pu, c=c, blk=blk: e.matmul(pu[:, 0:128], KhmT[:, c, :], Vt[:, blk, :], start=True, stop=True),
                     reads=[r_KhmT, r_Vt], writes=[rpu])
                b.op("dve", lambda e, pu=pu, ck=ck: e.scalar_tensor_tensor(S[:], S[:], dec[:, ck:ck + 1], pu[:, 0:128], ALU.mult, ALU.add),
                     reads=[rpu, r_S, r_dec], writes=[r_S])
                if not pass1:
                    b.op("act", lambda e: e.activation(Sbf[:], S[:], AF.Copy), reads=[r_S], writes=[r_Sbf])
            if not pass1:
                b.op("act", lambda e, po=po, c0=c0: e.activation(Oraw[:, c0:c0 + 128], po[:, 0:128], AF.Copy), reads=[rpo], writes=[r_Oraw])
        outs.append(b.dma("sp", send[:, hh, :], S[:], d[11], reads=[r_S]))
        if pass1:
            continue
        b.op("dve", lambda e: e.tensor_tensor(Khms[:], kh[:, NP:NT].unsqueeze(1).broadcast_to([128, 16, 64]), cms[:], ALU.mult),
             reads=[r_kh, r_const], writes=[r_Khms])
        for half in range(2):
            ptT, rpT = pr.next()
            ptTb = ptT[:, :].bitcast(BF16)
            for s8 in range(8):
                sq = half * 8 + s8
                b.op("pe", lambda e, ptTb=ptTb, s8=s8, sq=sq: e.transpose(ptTb[0:64, s8 * 128:(s8 + 1) * 128], Khms[:, sq, :], ident[:]),
                     reads=[r_Khms, r_const], writes=[rpT], sig=(s8 == 7))
            b.op("act", lambda e, ptTb=ptTb, half=half: e.activation(
                KhmTs[:, half * 8:half * 8 + 8, :].rearrange("p c k -> p (c k)"), ptTb[0:64, 0:1024], AF.Copy),
                reads=[rpT], writes=[r_KhmTs])
        pa, rpa = pr.next()
        b.op("pe", lambda e, pa=pa: e.matmul(pa[0:64, 0:64], kt[:, NP:NT], qt[:, NP:NT], start=True, stop=True),
             reads=[r_kt, r_qt], writes=[rpa])
        b.op("dve", lambda e, pa=pa: e.tensor_tensor(Ats[:], pa[0:64, 0:64], ms[:], ALU.mult), reads=[rpa, r_const], writes=[r_Ats])
        po, rpo = pr.next()
        b.op("pe", lambda e, po=po: e.matmul(po[:, 0:64], Vt[0:64, NBLK, :], Ats[:], start=True, stop=False),
             reads=[r_Vt, r_Ats], writes=[rpo], sig=False)
        for sq in range(16):
            b.op("pe", lambda e, po=po, sq=sq: e.matmul(po[:, sq * 4:sq * 4 + 4], S0b[:, sq, :], qt[:, NP + sq * 4:NP + sq * 4 + 4],
                                                       start=False, stop=(sq == 15)),
                 reads=[r_S0b, r_qt], writes=[rpo], sig=(sq == 15))
        b.op("act", lambda e, po=po: e.activation(Oraw[:, NP:NT], po[:, 0:64], AF.Copy), reads=[rpo], writes=[r_Oraw])
        for q4 in range(4):
            pu, rpu = pr.next()
            for s4 in range(4):
                sq = q4 * 4 + s4
                b.op("pe", lambda e, pu=pu, s4=s4, sq=sq: e.matmul(pu[:, s4 * 128:(s4 + 1) * 128], KhmTs[:, sq, :], Vt[0:64, NBLK, :],
                                                                 start=True, stop=True),
                     reads=[r_KhmTs, r_Vt], writes=[rpu], sig=(s4 == 3))
            for s4 in range(4):
                sq = q4 * 4 + s4
                b.op("dve", lambda e, pu=pu, s4=s4, sq=sq: e.scalar_tensor_tensor(
                    S0[:, sq, :], S0[:, sq, :], dec[:, NCK + sq:NCK + sq + 1], pu[:, s4 * 128:(s4 + 1) * 128], ALU.mult, ALU.add),
                    reads=[rpu, r_S0, r_dec], writes=[r_S0])
        outs.append(b.dma("sp", snew[hh], S0[:], d[10], reads=[r_S0]))
        if modeA:
            outs.append(b.dma("sp", oloc[hh], Oraw[:], d[14], reads=[r_Oraw]))
            continue
        b.op("act", lambda e: e.activation(xsq0[:], Oraw[:], AF.Square), reads=[r_Oraw], writes=[r_xsq0])
        tl = ntiles(NT)
        bk = [pr.next() for _ in tl]
        for ti, (s, n) in enumerate(tl):
            pt, rp = bk[ti]
            b.op("pe", lambda e, pt=pt, s=s, n=n: e.matmul(pt[:, 0:n], ones[:], xsq0[:, s:s + n], start=True, stop=True),
                 reads=[r_xsq0, r_const], writes=[rp])
            b.op("act", lambda e, pt=pt, s=s, n=n: e.activation(rstd[:, s:s + n], pt[:, 0:n], AF.Sqrt, bias=EPS, scale=1.0 / 128),
                 reads=[rp], writes=[r_rstd])
        b.op("dve", lambda e: e.reciprocal(rstd[:], rstd[:]), reads=[r_rstd], writes=[r_rstd])
        b.op("dve", lambda e, hh=hh: e.scalar_tensor_tensor(Oraw[:], Oraw[:], ng[:, hh:hh + 1], rstd[:], ALU.mult, ALU.mult),
             reads=[r_Oraw, r_rstd, r_const], writes=[r_Oraw])
        b.op("dve", lambda e, hh=hh: e.tensor_tensor(O2T[:, hh, :], Oraw[:], sg[:], ALU.mult), reads=[r_Oraw, r_sg, r_x], writes=[r_O2T])

    if pass1 or modeA:
        outs.append(b.dma("sp", dout, Dall[:], d[12], reads=[r_Sall]))
    else:
        b.op("pool", lambda e: e.memset(Dall[:], 0.0), writes=[r_Sall])
        outs.append(b.dma("sp", dout, Dall[:], d[12], reads=[r_Sall]))
        xr = [t32a, rstd]; r_xr = [r_t32a, r_rstd]
        xsem = [d[13], d[14]]; osem = [d[15], d[16]]
        for i in range(KC):
            n_, w, rw = next_w()
            xi = xr[i % 2]; rxi = r_xr[i % 2]
            b.dma("sp", xi[:], xT[:, i, :], xsem[i % 2], writes=[rxi])
            for (s, n) in ntiles(NT):
                pt, rp = pr.next()
                for k in range(KC):
                    b.op("pe", lambda e, pt=pt, w=w, k=k, s=s, n=n: e.matmul(
                        pt[:, 0:n], w[:, k, :], O2T[:, k, s:s + n], start=(k == 0), stop=(k == KC - 1)),
                        reads=[rw, r_O2T], writes=[rp], sig=(k == KC - 1))
                if s + n <= NP:
                    b.op("dve", lambda e, pt=pt, i=i, s=s, n=n, xi=xi: e.scalar_tensor_tensor(
                        xi[:, s:s + n], pt[:, 0:n], vec[:, 3, i:i + 1], xi[:, s:s + n], ALU.mult, ALU.add),
                        reads=[rp, r_const, rxi], writes=[rxi])
                else:
                    b.op("dve", lambda e, pt=pt, i=i: e.tensor_tensor(
                        tmps[:].rearrange("p (s t) -> p s t", t=4), pt[:, 0:NS].rearrange("p (s t) -> p s t", t=4),
                        mods[:, 2, i, :].unsqueeze(2).broadcast_to([128, 16, 4]), ALU.mult),
                        reads=[rp, r_const], writes=[r_tmps])
                    b.op("dve", lambda e, xi=xi: e.tensor_tensor(xi[:, NP:NT], xi[:, NP:NT], tmps[:], ALU.add),
                         reads=[r_tmps, rxi], writes=[rxi])
            outs.append(b.dma("sp", xo[:, i, :], xi[:], osem[i % 2], reads=[rxi]))
            load_w(n_ + 4)
    b.wait_all("sp", outs)
    b.emit(); b.close()
    return nc


def build_hgrnb():
    nc = bass.Bass("TRN2", target_bir_lowering=False)
    NT = NP + NS
    dram = lambda name, shape, kind="ExternalInput": nc.dram_tensor(name, list(shape), F32, kind=kind).ap()
    xT = dram("xT", [128, KC, NT])
    vecd = dram("vec", [128, 4, KC])
    modsd = dram("mods", [128, 3, KC, 16])
    olocd = dram("oloc", [NHH, 128, NT]); qbd = dram("qb", [NHH, 128, NT]); sgd = dram("sg", [NHH, 128, NT])
    srd = dram("sr", [NHH, 128, NRK, 128]); drd = dram("dr", [128, NHH, NRK])
    slocd = dram("sloc", [128, NHH, 128]); dld = dram("dl", [128, NHH])
    ngd = dram("ng", [128, NHH])
    wo = dram("wo", [KC, 128, KC, 128])
    xo = dram("xo", [128, KC, NT], "ExternalOutput")
    send = dram("send", [128, NHH, 128], "ExternalOutput")

    b = Builder(nc, n_dsem=18)
    d = b.dsems
    pr = PsumRot(b)
    O2T = b.sb("O2T", [128, KC, NT], BF16)
    ol = [b.sb("ol%d" % i, [128, NT], F32) for i in range(2)]
    qbb = [b.sb("qbb%d" % i, [128, NT], BF16) for i in range(2)]
    sgl = [b.sb("sgl%d" % i, [128, NT], F32) for i in range(2)]
    srall = b.sb("srall", [128, NHH, NRK, 128], F32)
    Sall = b.sb("Sall", [128, NHH, 128], F32)
    Oraws = [b.sb("Oraw%d" % i, [128, NT], F32) for i in range(2)]
    xsqs = [b.sb("xsq%d" % i, [128, NT], BF16) for i in range(2)]
    rstds = [b.sb("rstd%d" % i, [128, NT], F32) for i in range(2)]
    Ss = [b.sb("S%d" % i, [128, 128], F32) for i in range(2)]; Sbfs = [b.sb("Sbf%d" % i, [128, 128], BF16) for i in range(2)]
    Se = b.sb("Se", [128, NHH, 128], F32)
    sloc = b.sb("slocs", [128, NHH, 128], F32)
    dr = b.sb("drs", [128, NHH, NRK], F32); dl = b.sb("dls", [128, NHH], F32); ng = b.sb("ngs", [128, NHH], F32)
    vec = b.sb("vecs", [128, 4, KC], F32); mods = b.sb("modss", [128, 3, KC, 16], F32)
    ones = b.sb("ones", [128, 128], BF16)
    NWB = 4
    wring = [b.sb("wr%d" % i, [128, KC, 128], BF16) for i in range(NWB)]
    xr = [b.sb("xr%d" % i, [128, NT], F32) for i in range(2)]
    tmps = b.sb("tmps", [128, NS], F32)
    R = Res
    r_O2T, r_Se, r_const, r_tmps = [R(n) for n in "O2T Se const tmps".split()]
    r_Oraws = [R("a"), R("b")]; r_xsqs = [R("a"), R("b")]; r_rstds = [R("a"), R("b")]; r_Ss = [R("a"), R("b")]; r_Sbfs = [R("a"), R("b")]
    r_ol = [R("a"), R("b")]; r_qbb = [R("a"), R("b")]; r_sgl = [R("a"), R("b")]; r_srall = R("srall"); r_Sall = R("Sall")
    r_wr = [R("w%d" % i) for i in range(NWB)]; r_xr = [R("a"), R("b")]
    for dst, src in [(vec, vecd), (mods, modsd), (sloc, slocd), (dr, drd), (dl, dld), (ng, ngd)]:
        b.dma("sp", dst[:], src, d[0], writes=[r_const])
    b.op("pool", lambda e: e.memset(ones[:], 1.0), writes=[r_const])

    def load_w(i):
        if i < KC:
            b.dma("pool", wring[i % NWB][:], wo[i], d[1 + i % NWB], writes=[r_wr[i % NWB]])

    def load_head(hh):
        if hh >= NHH:
            return
        i = hh % 2
        b.dma("sp", ol[i][:], olocd[hh], d[5 + i], writes=[r_ol[i]])
        b.dma("pool", qbb[i][:], qbd[hh], d[7 + i], writes=[r_qbb[i]])
        b.dma("sp", sgl[i][:], sgd[hh], d[9 + i], writes=[r_sgl[i]])
    b.dma("sp", srall[:], srd.rearrange("h p r v -> p h r v"), d[11], writes=[r_srall])
    load_head(0)
    b.op("dve", lambda e: e.tensor_copy(Sall[:], srall[:, :, 0, :]), reads=[r_srall], writes=[r_Sall])
    for r in range(1, NRK):
        b.op("dve", lambda e, r=r: e.tensor_tensor(Sall[:], Sall[:], dr[:, :, r:r + 1].broadcast_to([128, NHH, 128]), ALU.mult),
             reads=[r_Sall, r_const], writes=[r_Sall])
        b.op("pool", lambda e, r=r: e.tensor_tensor(Sall[:], Sall[:], srall[:, :, r, :], ALU.add),
             reads=[r_Sall, r_srall], writes=[r_Sall])
    for i in range(NWB - 1):
        load_w(i)
    outs = []
    for hh in range(NHH):
        load_head(hh + 1)
        i2 = hh % 2
        Oraw, xsq0, rstd, S, Sbf = Oraws[i2], xsqs[i2], rstds[i2], Ss[i2], Sbfs[i2]
        r_Oraw, r_xsq0, r_rstd, r_S, r_Sbf = r_Oraws[i2], r_xsqs[i2], r_rstds[i2], r_Ss[i2], r_Sbfs[i2]
        b.op("act", lambda e, hh=hh: e.activation(Sbf[:], Sall[:, hh, :], AF.Copy), reads=[r_Sall], writes=[r_Sbf])
        b.op("dve", lambda e, hh=hh: e.scalar_tensor_tensor(Se[:, hh, :], Sall[:, hh, :], dl[:, hh:hh + 1], sloc[:, hh, :], ALU.mult, ALU.add),
             reads=[r_Sall, r_const], writes=[r_Se])
        for (s, n) in ntiles(NT):
            pt, rp = pr.next()
            b.op("pe", lambda e, pt=pt, s=s, n=n, i2=i2: e.matmul(pt[:, 0:n], Sbf[:], qbb[i2][:, s:s + n], start=True, stop=True),
                 reads=[r_Sbf, r_qbb[i2]], writes=[rp])
            b.op("dve", lambda e, pt=pt, s=s, n=n, i2=i2: e.tensor_tensor(Oraw[:, s:s + n], pt[:, 0:n], ol[i2][:, s:s + n], ALU.add),
                 reads=[rp, r_ol[i2]], writes=[r_Oraw])
        b.op("act", lambda e: e.activation(xsq0[:], Oraw[:], AF.Square), reads=[r_Oraw], writes=[r_xsq0])
        for (s, n) in ntiles(NT):
            pt, rp = pr.next()
            b.op("pe", lambda e, pt=pt, s=s, n=n: e.matmul(pt[:, 0:n], ones[:], xsq0[:, s:s + n], start=True, stop=True),
                 reads=[r_xsq0, r_const], writes=[rp])
            b.op("act", lambda e, pt=pt, s=s, n=n: e.activation(rstd[:, s:s + n], pt[:, 0:n], AF.Ln, bias=EPS, scale=1.0 / 128),
                 reads=[rp], writes=[r_rstd])
        b.op("act", lambda e: e.activation(rstd[:], rstd[:], AF.Exp, scale=-0.5), reads=[r_rstd], writes=[r_rstd])
        b.op("dve", lambda e, hh=hh: e.scalar_tensor_tensor(Oraw[:], Oraw[:], ng[:, hh:hh + 1], rstd[:], ALU.mult, ALU.mult),
             reads=[r_Oraw, r_rstd, r_const], writes=[r_Oraw])
        b.op("pool", lambda e, hh=hh, i2=i2: e.tensor_tensor(O2T[:, hh, :], Oraw[:], sgl[i2][:], ALU.mult),
             reads=[r_Oraw, r_sgl[i2]], writes=[r_O2T])
    outs.append(b.dma("sp", send, Se[:], d[13], reads=[r_Se]))
    for i in range(KC):
        load_w(i + NWB - 1)
        w, rw = wring[i % NWB], r_wr[i % NWB]
        xi, rxi = xr[i % 2], r_xr[i % 2]
        b.dma("sp", xi[:], xT[:, i, :], d[14 + i % 2], writes=[rxi])
        for (s, n) in ntiles(NT):
            pt, rp = pr.next()
            for k in range(KC):
                b.op("pe", lambda e, pt=pt, w=w, k=k, s=s, n=n: e.matmul(
                    pt[:, 0:n], w[:, k, :], O2T[:, k, s:s + n], start=(k == 0), stop=(k == KC - 1)),
                    reads=[rw, r_O2T], writes=[rp], sig=(k == KC - 1))
            if s + n <= NP:
                b.op("dve", lambda e, pt=pt, i=i, s=s, n=n, xi=xi: e.scalar_tensor_tensor(
                    xi[:, s:s + n], pt[:, 0:n], vec[:, 3, i:i + 1], xi[:, s:s + n], ALU.mult, ALU.add),
                    reads=[rp, r_const, rxi], writes=[rxi])
            else:
                b.op("dve", lambda e, pt=pt, i=i: e.tensor_tensor(
                    tmps[:].rearrange("p (s t) -> p s t", t=4), pt[:, 0:NS].rearrange("p (s t) -> p s t", t=4),
                    mods[:, 2, i, :].unsqueeze(2).broadcast_to([128, 16, 4]), ALU.mult),
                    reads=[rp, r_const], writes=[r_tmps])
                b.op("dve", lambda e, xi=xi: e.tensor_tensor(xi[:, NP:NT], xi[:, NP:NT], tmps[:], ALU.add),
                     reads=[r_tmps, rxi], writes=[rxi])
        outs.append(b.dma("sp", xo[:, i, :], xi[:], d[16 + i % 2], reads=[rxi]))
    b.wait_all("sp", outs)
    b.emit(); b.close()
    return nc


class _HSet:
    pass


FILL_RATIO = 2
CRIT_RATIO = 3


def build_hgrna():
    nc = bass.Bass("TRN2", target_bir_lowering=False)
    NT = NP + NS
    NBLK = NP // 128
    NCK = NP // CH
    dram = lambda name, shape, kind="ExternalInput": nc.dram_tensor(name, list(shape), F32, kind=kind).ap()
    xT = dram("xT", [128, KC, NT])
    vecd = dram("vec", [128, 4, KC]); modsd = dram("mods", [128, 3, KC, 16])
    whg = dram("whg", [NHH, 128, 4, KC, 128])
    lbpd = dram("lbp", [128, 2, NHH])
    m01d = dram("m01", [128, 128]); cmd = dram("cm", [128, 4, 128]); identd = dram("ident", [128, 128])
    s0d = dram("s0", [NHH, 128, 16, 128]); msd = dram("ms", [64, 64]); cmsd = dram("cms", [128, 16, 64])
    smd = dram("smask", [128, NT])
    oloc = dram("oloc", [NHH, 128, NT], "ExternalOutput")
    qbo = dram("qbo", [NHH, 128, NT], "ExternalOutput")
    sgo = dram("sgo", [NHH, 128, NT], "ExternalOutput")
    snew = dram("snew", [NHH, 128, 16, 128], "ExternalOutput")
    send = dram("send", [128, NHH, 128], "ExternalOutput")
    dout = dram("dout", [128, NHH], "ExternalOutput")

    b = Builder(nc, n_dsem=22)
    d = b.dsems
    pr = PsumRot(b, 4)
    pr_loop = PsumRot.__new__(PsumRot); pr_loop.tiles = pr.tiles[0:2]; pr_loop.res = pr.res[0:2]; pr_loop.i = 0
    pr_prep = PsumRot.__new__(PsumRot); pr_prep.tiles = pr.tiles[2:4]; pr_prep.res = pr.res[2:4]; pr_prep.i = 0
    po_banks = [(b.ps("pob%d" % i, [128, 512]), Res("pob%d" % i)) for i in range(2)]
    pu_banks = [(b.ps("pub%d" % i, [128, 512]), Res("pub%d" % i)) for i in range(2)]
    xbuf = b.sb("xbuf", [128, KC * NT], F32)
    x = xbuf[:].rearrange("p (k n) -> p k n", k=KC)
    h = b.sb("h", [128, KC, NT], BF16)
    rstd = b.sb("rstd", [128, NT], F32)
    xsq0 = b.sb("xsq0", [128, NT], BF16)
    t32a = b.sb("t32a", [128, NT], F32)
    NWB = 5
    wring = [b.sb("wr%d" % i, [128, KC, 128], BF16) for i in range(NWB)]
    Dall = b.sb("Dall", [128, NHH], F32)
    vec = b.sb("vecs", [128, 4, KC], F32); mods = b.sb("modss", [128, 3, KC, 16], F32)
    lbp = b.sb("lbps", [128, 2, NHH], F32); oml = b.sb("oml", [128, NHH], F32)
    m01 = b.sb("m01s", [128, 128], F32); cm = b.sb("cms_", [128, 4, 128], F32)
    ident = b.sb("idents", [128, 128], BF16); identf = b.sb("identf", [128, 128], F32)
    ones = b.sb("ones", [128, 128], BF16)
    gm = b.sb("gm", [128, KC], F32); gms = b.sb("gms", [128, KC, 16], F32)
    ms = b.sb("mss", [64, 64], F32); cms = b.sb("cmss", [128, 16, 64], F32)
    smask = b.sb("smasks", [128, NT], F32); onesf = b.sb("onesf", [128, NCK], F32)
    R = Res
    r_x, r_h, r_rstd, r_const, r_gm, r_gms, r_xsq0, r_t32a, r_D = [R(n) for n in "x h rstd const gm gms xsq t32 D".split()]
    r_wr = [R("wr%d" % i) for i in range(NWB)]

    f32_names = ["qf", "kf", "bA", "bB", "sg", "Oraw", "QBf"]
    bf_names = ["qt", "kt", "kh"]
    sets = []
    for si in range(2):
        B = _HSet()
        if si == 0:
            for nm in f32_names:
                if nm == "QBf":
                    B.QBf = t32a[:]
                elif nm == "Oraw":
                    B.Oraw = rstd[:]
                else:
                    setattr(B, nm, b.sb(nm + "0", [128, NT], F32)[:])
            for nm in bf_names:
                setattr(B, nm, b.sb(nm + "0", [128, NT], BF16)[:])
            B.Vt = b.sb("Vt0", [128, NBLK + 1, 128], BF16)[:]
            B.S0 = b.sb("S00", [128, 16, 128], F32)[:]; B.S0b = b.sb("S0b0", [128, 16, 128], BF16)[:]
            B.Khms = b.sb("Khms0", [128, 16, 64], BF16)[:]; B.KhmTs = b.sb("KhmTs0", [64, 16, 128], BF16)[:]
            B.At = b.sb("At0", [128, 128], BF16)[:]; B.Khm = b.sb("Khm0", [128, 4, 128], BF16)[:]
            B.KhmT = b.sb("KhmT0", [128, 4, 128], BF16)[:]
            B.At_b = b.sb("At0b", [128, 128], BF16)[:]; B.Khm_b = b.sb("Khm0b", [128, 4, 128], BF16)[:]
            B.KhmT_b = b.sb("KhmT0b", [128, 4, 128], BF16)[:]
            B.S = b.sb("S_0", [128, 128], F32)[:]; B.Sbf = b.sb("Sbf0", [128, 128], BF16)[:]
            B.S2 = b.sb("S2_0", [128, 128], F32)[:]; B.Sbf2 = b.sb("Sbf2_0", [128, 128], BF16)[:]
            B.dec = b.sb("dec0", [128, NCK + 16], F32)[:]
            B.pbA = b.sb("pbA0", [128, NCK], F32)[:]; B.pbB = b.sb("pbB0", [128, NCK], F32)[:]
            B.Ats = b.sb("Ats0", [64, 64], BF16)[:]
        else:
            off = [0]

            def carve(n_f32, dt, shape):
                v = xbuf[:, off[0]:off[0] + n_f32]
                off[0] += n_f32
                if dt is BF16:
                    v = v.bitcast(BF16)
                if len(shape) == 2:
                    return v[:, 0:shape[1]] if shape[0] == 128 else v[0:shape[0], 0:shape[1]]
                if len(shape) == 3:
                    vv = v[:, 0:shape[1] * shape[2]].rearrange("p (a c) -> p a c", a=shape[1])
                    return vv if shape[0] == 128 else vv[0:shape[0]]
            for nm in f32_names:
                setattr(B, nm, carve(NT, F32, [128, NT]))
            for nm in bf_names:
                setattr(B, nm, carve(NT // 2, BF16, [128, NT]))
            B.Vt = carve((NBLK + 1) * 64, BF16, [128, NBLK + 1, 128])
            B.S0 = carve(2048, F32, [128, 16, 128]); B.S0b = carve(1024, BF16, [128, 16, 128])
            B.Khms = carve(512, BF16, [128, 16, 64]); B.KhmTs = carve(1024, BF16, [64, 16, 128])
            B.At = carve(64, BF16, [128, 128]); B.Khm = carve(256, BF16, [128, 4, 128]); B.KhmT = carve(256, BF16, [128, 4, 128])
            B.At_b = carve(64, BF16, [128, 128]); B.Khm_b = carve(256, BF16, [128, 4, 128]); B.KhmT_b = carve(256, BF16, [128, 4, 128])
            B.S = carve(128, F32, [128, 128]); B.Sbf = carve(64, BF16, [128, 128])
            B.S2 = carve(128, F32, [128, 128]); B.Sbf2 = carve(64, BF16, [128, 128])
            B.dec = carve(NCK + 16, F32, [128, NCK + 16])
            B.pbA = carve(NCK, F32, [128, NCK]); B.pbB = carve(NCK, F32, [128, NCK])
            B.Ats = carve(32, BF16, [64, 64])
            assert off[0] <= KC * NT, off[0]
        for nm in f32_names + bf_names + ["Vt", "S0", "S0b", "Khms", "KhmTs", "At", "Khm", "KhmT", "At_b", "Khm_b", "KhmT_b", "S", "Sbf", "S2", "Sbf2", "dec", "pbA", "pbB", "Ats"]:
            setattr(B, "r_" + nm, Res(nm + str(si)))
        if si == 0:
            B.r_QBf = r_t32a
            B.r_Oraw = r_rstd
        sets.append(B)

    b.dma("sp", xbuf[:], xT.rearrange("p k n -> p (k n)"), d[0], writes=[r_x])
    for dst, src in [(vec, vecd), (mods, modsd), (lbp, lbpd), (m01, m01d), (cm, cmd), (identf, identd), (ms, msd), (cms, cmsd), (smask, smd)]:
        b.dma("sp", dst[:], src, d[1], writes=[r_const])
    b.op("pool", lambda e: e.memset(ones[:], 1.0), writes=[r_const])
    b.op("pool", lambda e: e.memset(Dall[:], 0.0), writes=[r_D])
    b.op("pool", lambda e: e.memset(onesf[:], 1.0), writes=[r_const])
    b.op("act", lambda e: e.activation(ident[:], identf[:], AF.Copy), reads=[r_const], writes=[r_const])
    b.op("dve", lambda e: e.tensor_tensor(oml[:], lbp[:, 0, :], lbp[:, 1, :], ALU.subtract), reads=[r_const], writes=[r_const])
    b.op("act", lambda e: e.activation(oml[:], oml[:], AF.Sigmoid), reads=[r_const], writes=[r_const])

    wsem = [d[2], d[3], d[4], d[5], d[6]]
    wlist = [(hh, wh) for hh in range(NHH) for wh in range(4)]

    def load_w(n):
        if n >= len(wlist):
            return
        a, c = wlist[n]
        b.dma("pool", wring[n % NWB][:], whg[a][:, c, :, :], wsem[n % NWB], writes=[r_wr[n % NWB]])
    for n in range(4):
        load_w(n)
    wcount = [0]

    def next_w():
        n = wcount[0]
        wcount[0] += 1
        return n, wring[n % NWB], r_wr[n % NWB]

    emit_norm_mod(b, pr, x, r_x, h, r_h, NT, NP, vec, 0, 1, 2, mods, 0, 1, ones, r_const,
                  ([xsq0[:], xsq0[:]], [r_xsq0, r_xsq0], rstd[:], r_rstd, gm, r_gm, gms, r_gms, [t32a[:], t32a[:]], [r_t32a, r_t32a]))
    B1 = sets[1]
    for nm in f32_names + bf_names + ["Vt", "S0", "S0b", "Khms", "KhmTs", "At", "Khm", "KhmT", "At_b", "Khm_b", "KhmT_b", "S", "Sbf", "S2", "Sbf2", "dec", "pbA", "pbB", "Ats"]:
        rr = getattr(B1, "r_" + nm)
        rr.r = list(r_x.r)
        rr.w = r_x.w

    outs = []
    dsem_set = [dict(s0=d[7], s0b=d[8], qb=d[9], sg=d[10], ol=d[11], sn=d[12], se=d[13]),
                dict(s0=d[14], s0b=d[15], qb=d[16], sg=d[17], ol=d[18], sn=d[19], se=d[20])]

    def proj_fm(dst_fn):
        n_, w, rw = next_w()
        for (s, n) in ntiles(NT):
            pt, rp = pr_prep.next()
            for k in range(KC):
                b.op("pe", lambda e, pt=pt, w=w, k=k, s=s, n=n: e.matmul(
                    pt[:, 0:n], w[:, k, :], h[:, k, s:s + n], start=(k == 0), stop=(k == KC - 1)),
                    reads=[rw, r_h], writes=[rp], sig=(k == KC - 1))
                if k % 4 == 3 and k != KC - 1 and n > 128:
                    yield
            dst_fn(pt, rp, s, n)
            yield
        load_w(n_ + 4)

    def prep(hh):
        B = sets[hh % 2]; ds = dsem_set[hh % 2]
        b.dma("sp", B.S0, s0d[hh], ds["s0"], writes=[B.r_S0])
        b.dma("pool", B.S0b, s0d[hh], ds["s0b"], writes=[B.r_S0b])
        yield from proj_fm(lambda pt, rp, s, n: b.op("act", lambda e: e.activation(B.qf[:, s:s + n], pt[:, 0:n], AF.Silu),
                                                     reads=[rp], writes=[B.r_qf]))
        yield from proj_fm(lambda pt, rp, s, n: b.op("act", lambda e: e.activation(B.kf[:, s:s + n], pt[:, 0:n], AF.Sigmoid, scale=-1.0),
                                                     reads=[rp], writes=[B.r_kf]))
        b.op("dve", lambda e: e.tensor_scalar(B.kf, B.kf, oml[:, hh:hh + 1], None, ALU.mult), reads=[B.r_kf, r_const], writes=[B.r_kf])
        b.op("act", lambda e: e.activation(B.bA, B.kf, AF.Ln, bias=1.0, scale=-1.0), reads=[B.r_kf], writes=[B.r_bA])
        yield
        n_, w, rw = next_w()
        for blk in range(NBLK + 1):
            m = 128 if blk < NBLK else NS
            c0 = blk * 128
            pt, rp = pr_prep.next()
            for k in range(KC):
                b.op("pe", lambda e, pt=pt, w=w, k=k, c0=c0, m=m: e.matmul(
                    pt[0:m, 0:128], h[:, k, c0:c0 + m], w[:, k, :], start=(k == 0), stop=(k == KC - 1)),
                    reads=[rw, r_h], writes=[rp], sig=(k == KC - 1))
            b.op("act", lambda e, pt=pt, blk=blk, m=m: e.activation(B.Vt[0:m, blk, :], pt[0:m, 0:128], AF.Copy),
                 reads=[rp], writes=[B.r_Vt])
            yield
        load_w(n_ + 4)
        yield from proj_fm(lambda pt, rp, s, n: b.op("act", lambda e: e.activation(B.sg[:, s:s + n], pt[:, 0:n], AF.Silu),
                                                     reads=[rp], writes=[B.r_sg]))
        outs.append(b.dma("sp", sgo[hh], B.sg, ds["sg"], reads=[B.r_sg]))
        b.op("dve", lambda e: e.tensor_tensor_scan(B.bB, smask[:], B.bA, 0.0, ALU.mult, ALU.add),
             reads=[B.r_bA, r_const], writes=[B.r_bB])
        bb, rbb = B.bB, B.r_bB
        other, rother = B.bA, B.r_bA
        yield
        bv = bb[:, 0:NP].rearrange("p (c t) -> p c t", t=CH)
        ov = other[:, 0:NP].rearrange("p (c t) -> p c t", t=CH)
        b.op("act", lambda e: e.activation(B.dec[:, 0:NCK].unsqueeze(2), bv[:, :, CH - 1:CH], AF.Exp), reads=[rbb], writes=[B.r_dec])
        b.op("dve", lambda e: e.tensor_tensor(ov, bv[:, :, CH - 1:CH].broadcast_to([128, NCK, CH]), bv, ALU.subtract),
             reads=[rbb], writes=[rother])
        bs_ = bb[:, NP:NT].rearrange("p (c t) -> p c t", t=4)
        os_ = other[:, NP:NT].rearrange("p (c t) -> p c t", t=4)
        b.op("act", lambda e: e.activation(B.dec[:, NCK:NCK + 16].unsqueeze(2), bs_[:, :, 3:4], AF.Exp), reads=[rbb], writes=[B.r_dec])
        b.op("dve", lambda e: e.tensor_tensor(os_, bs_[:, :, 3:4].broadcast_to([128, 16, 4]), bs_, ALU.subtract),
             reads=[rbb], writes=[rother])
        b.op("act", lambda e: e.activation(other[:, 0:NT], other[:, 0:NT], AF.Exp), reads=[rother], writes=[rother])
        b.op("dve", lambda e: e.tensor_tensor(B.kh, B.kf, other[:, 0:NT], ALU.mult), reads=[rother, B.r_kf], writes=[B.r_kh])
        yield
        b.op("act", lambda e: e.activation(other[:, 0:NT], bb[:, 0:NT], AF.Exp), reads=[rbb, B.r_kh], writes=[rother])
        b.op("dve", lambda e: e.tensor_tensor(B.qt, B.qf, other[:, 0:NT], ALU.mult), reads=[rother, B.r_qf], writes=[B.r_qt])
        b.op("act", lambda e: e.activation(other[:, 0:NT], bb[:, 0:NT], AF.Exp, scale=-1.0), reads=[rbb, B.r_qt], writes=[rother])
        b.op("dve", lambda e: e.tensor_tensor(B.kt, B.kf, other[:, 0:NT], ALU.mult), reads=[rother, B.r_kf], writes=[B.r_kt])
        yield
        b.op("pool", lambda e: e.tensor_copy(B.pbA.unsqueeze(2), bv[:, :, CH - 1:CH]), reads=[rbb], writes=[B.r_pbA])
        b.op("dve", lambda e: e.tensor_tensor_scan(B.pbB, onesf[:], B.pbA, 0.0, ALU.mult, ALU.add),
             reads=[B.r_pbA, r_const], writes=[B.r_pbB])
        pin, rpin = B.pbB, B.r_pbB
        pex, rpex = B.pbA, B.r_pbA
        b.op("act", lambda e: e.activation(Dall[:, hh:hh + 1], pin[:, NCK - 1:NCK], AF.Exp), reads=[rpin], writes=[r_D])
        b.op("pool", lambda e: e.tensor_tensor(pex.unsqueeze(2), pin.unsqueeze(2), bv[:, :, CH - 1:CH], ALU.subtract),
             reads=[rpin, rbb], writes=[rpex])
        b.op("act", lambda e: e.activation(pex, pex, AF.Exp), reads=[rpex], writes=[rpex])
        b.op("pool", lambda e: e.tensor_tensor(
            B.QBf[:, 0:NP].rearrange("p (c t) -> p c t", t=CH), B.qt[:, 0:NP].rearrange("p (c t) -> p c t", t=CH),
            pex.unsqueeze(2).broadcast_to([128, NCK, CH]), ALU.mult), reads=[rpex, B.r_qt], writes=[B.r_QBf])
        b.op("pool", lambda e: e.memset(B.QBf[:, NP:NT], 0.0), writes=[B.r_QBf])
        outs.append(b.dma("sp", qbo[hh], B.QBf, ds["qb"], reads=[B.r_QBf]))
        b.op("pool", lambda e: e.memset(B.S, 0.0), writes=[B.r_S])
        b.op("pool", lambda e: e.memset(B.Sbf, 0.0), writes=[B.r_Sbf])
        yield

    def loop(hh):
        B = sets[hh % 2]; ds = dsem_set[hh % 2]
        def front(blk):
            c0 = blk * 128
            Khm, rKhm, KhmT, rKhmT, At, rAt = ((B.Khm, B.r_Khm, B.KhmT, B.r_KhmT, B.At, B.r_At) if blk % 2 == 0 else
                                               (B.Khm_b, B.r_Khm_b, B.KhmT_b, B.r_KhmT_b, B.At_b, B.r_At_b))
            b.op("dve", lambda e: e.tensor_tensor(Khm, B.kh[:, c0:c0 + 128].unsqueeze(1).broadcast_to([128, 4, 128]), cm[:], ALU.mult),
                 reads=[B.r_kh, r_const], writes=[rKhm])
            ptT, rpT = pr_loop.next()
            ptTb = ptT[:, :].bitcast(BF16)
            for c in range(4):
                b.op("pe", lambda e, c=c: e.transpose(ptTb[:, c * 128:(c + 1) * 128], Khm[:, c, :], ident[:]),
                     reads=[rKhm, r_const], writes=[rpT], sig=(c == 3))
            b.op("act", lambda e: e.activation(KhmT.rearrange("p c k -> p (c k)"), ptTb[:, 0:512], AF.Copy),
                 reads=[rpT], writes=[rKhmT])
            yield
            pa, rpa = pr_loop.next()
            b.op("pe", lambda e: e.matmul(pa[:, 0:128], B.kt[:, c0:c0 + 128], B.qt[:, c0:c0 + 128], start=True, stop=True),
                 reads=[B.r_kt, B.r_qt], writes=[rpa])
            b.op("dve", lambda e: e.tensor_tensor(At, pa[:, 0:128], m01[:], ALU.mult), reads=[rpa, r_const], writes=[rAt])
            yield
            po, rpo = po_banks[blk % 2]
            b.op("pe", lambda e: e.matmul(po[:, 0:128], B.Vt[:, blk, :], At, start=True, stop=False),
                 reads=[B.r_Vt, rAt], writes=[rpo], sig=False)
            pu, rpu = pu_banks[blk % 2]
            for c in range(4):
                b.op("pe", lambda e, c=c: e.matmul(pu[:, c * 128:(c + 1) * 128], KhmT[:, c, :], B.Vt[:, blk, :], start=True, stop=True),
                     reads=[rKhmT, B.r_Vt], writes=[rpu], sig=(c == 3))

        for _ in front(0):
            pass
        for blk in range(NBLK):
            c0 = blk * 128
            nxt = front(blk + 1) if blk + 1 < NBLK else iter(())
            yield
            po, rpo = po_banks[blk % 2]
            pu, rpu = pu_banks[blk % 2]
            for c in range(4):
                ck = blk * 4 + c
                Sc, rSc, Sn, rSn = (B.S, B.r_S, B.S2, B.r_S2) if ck % 2 == 0 else (B.S2, B.r_S2, B.S, B.r_S)
                Sbc, rSbc, Sbn, rSbn = (B.Sbf, B.r_Sbf, B.Sbf2, B.r_Sbf2) if ck % 2 == 0 else (B.Sbf2, B.r_Sbf2, B.Sbf, B.r_Sbf)
                b.op("pe", lambda e, c=c: e.matmul(po[:, c * CH:(c + 1) * CH], Sbc, B.qt[:, c0 + c * CH:c0 + (c + 1) * CH],
                                                   start=False, stop=(c == 3)),
                     reads=[rSbc, B.r_qt], writes=[rpo], sig=True)
                b.op("dve", lambda e, c=c, ck=ck: e.scalar_tensor_tensor(Sn, Sc, B.dec[:, ck:ck + 1], pu[:, c * 128:(c + 1) * 128], ALU.mult, ALU.add),
                     reads=[rpu, rSc, B.r_dec], writes=[rSn])
                b.op("pool", lambda e: e.tensor_copy(Sbn, Sn), reads=[rSn], writes=[rSbn])
                next(nxt, None)
                yield
            for _ in nxt:
                pass
            b.op("act", lambda e: e.activation(B.Oraw[:, c0:c0 + 128], po[:, 0:128], AF.Copy), reads=[rpo], writes=[B.r_Oraw])
        outs.append(b.dma("sp", send[:, hh, :], B.S, ds["se"], reads=[B.r_S]))
        b.op("dve", lambda e: e.tensor_tensor(B.Khms, B.kh[:, NP:NT].unsqueeze(1).broadcast_to([128, 16, 64]), cms[:], ALU.mult),
             reads=[B.r_kh, r_const], writes=[B.r_Khms])
        for half in range(2):
            ptT, rpT = pr_loop.next()
            ptTb = ptT[:, :].bitcast(BF16)
            for s8 in range(8):
                sq = half * 8 + s8
                b.op("pe", lambda e, ptTb=ptTb, s8=s8, sq=sq: e.transpose(ptTb[0:64, s8 * 128:(s8 + 1) * 128], B.Khms[:, sq, :], ident[:]),
                     reads=[B.r_Khms, r_const], writes=[rpT], sig=(s8 == 7))
            b.op("act", lambda e, ptTb=ptTb, half=half: e.activation(
                B.KhmTs[:, half * 8:half * 8 + 8, :].rearrange("p c k -> p (c k)"), ptTb[0:64, 0:1024], AF.Copy),
                reads=[rpT], writes=[B.r_KhmTs])
            yield
        pa, rpa = pr_loop.next()
        b.op("pe", lambda e, pa=pa: e.matmul(pa[0:64, 0:64], B.kt[:, NP:NT], B.qt[:, NP:NT], start=True, stop=True),
             reads=[B.r_kt, B.r_qt], writes=[rpa])
        b.op("dve", lambda e, pa=pa: e.tensor_tensor(B.Ats, pa[0:64, 0:64], ms[:], ALU.mult), reads=[rpa, r_const], writes=[B.r_Ats])
        po, rpo = pr_loop.next()
        b.op("pe", lambda e, po=po: e.matmul(po[:, 0:64], B.Vt[0:64, NBLK, :], B.Ats, start=True, stop=False),
             reads=[B.r_Vt, B.r_Ats], writes=[rpo], sig=False)
        for sq in range(16):
            b.op("pe", lambda e, po=po, sq=sq: e.matmul(po[:, sq * 4:sq * 4 + 4], B.S0b[:, sq, :], B.qt[:, NP + sq * 4:NP + sq * 4 + 4],
                                                       start=False, stop=(sq == 15)),
                 reads=[B.r_S0b, B.r_qt], writes=[rpo], sig=(sq == 15))
        b.op("act", lambda e, po=po: e.activation(B.Oraw[:, NP:NT], po[:, 0:64], AF.Copy), reads=[rpo], writes=[B.r_Oraw])
        outs.append(b.dma("sp", oloc[hh], B.Oraw, ds["ol"], reads=[B.r_Oraw]))
        yield
        for q4 in range(4):
            pu, rpu = pr_loop.next()
            for s4 in range(4):
                sq = q4 * 4 + s4
                b.op("pe", lambda e, pu=pu, s4=s4, sq=sq: e.matmul(pu[:, s4 * 128:(s4 + 1) * 128], B.KhmTs[:, sq, :], B.Vt[0:64, NBLK, :],
                                                                 start=True, stop=True),
                     reads=[B.r_KhmTs, B.r_Vt], writes=[rpu], sig=(s4 == 3))
            for s4 in range(4):
                sq = q4 * 4 + s4
                b.op("dve", lambda e, pu=pu, s4=s4, sq=sq: e.scalar_tensor_tensor(
                    B.S0[:, sq, :], B.S0[:, sq, :], B.dec[:, NCK + sq:NCK + sq + 1], pu[:, s4 * 128:(s4 + 1) * 128], ALU.mult, ALU.add),
                    reads=[rpu, B.r_S0, B.r_dec], writes=[B.r_S0])
            yield
        outs.append(b.dma("sp", snew[hh], B.S0, ds["sn"], reads=[B.r_S0]))

    def drain(g):
        for _ in g:
            pass

    drain(prep(0))
    for hh in range(NHH):
        crit = loop(hh)
        fill = prep(hh + 1) if hh + 1 < NHH else iter(())
        done_c = done_f = False
        while not (done_c and done_f):
            for _ in range(CRIT_RATIO):
                if not done_c:
                    try:
                        next(crit)
                    except StopIteration:
                        done_c = True
            for _ in range(FILL_RATIO):
                if not done_f:
                    try:
                        next(fill)
                    except StopIteration:
                        done_f = True
    outs.append(b.dma("sp", dout, Dall[:], d[21], reads=[r_D]))
    b.wait_all("sp", outs)
    b.emit(); b.close()
    return nc


def fm(a):
    n = a.shape[0]
    return np.ascontiguousarray(a.T.reshape(16, 128, n).transpose(1, 0, 2))
def unfm(t):
    return np.ascontiguousarray(t.transpose(2, 1, 0).reshape(t.shape[2], D))
def vfm(v):
    return v.reshape(-1, 128).T
def mods_fm(m3):
    return np.ascontiguousarray(m3.reshape(3, 16, 16, 128).transpose(3, 0, 2, 1))
def tile_w_in(w):
    return np.ascontiguousarray(w.reshape(16, 128, 2, 44, 128).transpose(3, 1, 2, 0, 4))
def tile_w_out(w):
    return np.ascontiguousarray(w.reshape(4, 11, 128, 16, 128).transpose(0, 3, 2, 1, 4))
def tile_cols(w):
    return w.reshape(16, 128, w.shape[1]).transpose(1, 0, 2)
def tile_wqkv(w):
    out = np.empty((8, 128, 4, 16, 128), np.float32)
    for g in range(8):
        out[g, :, 0] = tile_cols(w[:, 256 * g:256 * g + 128])
        out[g, :, 1] = tile_cols(w[:, 256 * g + 128:256 * g + 256])
        kk = w[:, 2048 + 64 * g:2048 + 64 * g + 64]; vv = w[:, 2560 + 64 * g:2560 + 64 * g + 64]
        out[g, :, 2] = tile_cols(np.concatenate([kk, kk], 1))
        out[g, :, 3] = tile_cols(np.concatenate([vv, vv], 1))
    return out
def tile_sq(w):
    return np.ascontiguousarray(w.reshape(16, 128, 16, 128).transpose(2, 1, 0, 3))
NEG = -30000.0
def attn_consts():
    s = np.arange(128)[:, None]; q = np.arange(128)[None, :]
    nd = np.zeros((128, 2, 128), np.float32); mk = np.zeros((128, 2, 128), np.float32)
    nd[:, 0] = -(q - s + 128); mk[:, 0] = np.where(s > q, 0, NEG)
    nd[:, 1] = -(q - s); mk[:, 1] = np.where(s <= q, 0, NEG)
    nd = np.where(mk < 0, 0, nd).astype(np.float32)
    t = np.arange(4)[None, :]
    ndc = -(128 + t - s).astype(np.float32); mkc = np.where(s > t, 0, NEG).astype(np.float32)
    ndc = np.where(mkc < 0, 0, ndc).astype(np.float32)
    a = np.arange(64)
    same = (a[:, None] // 4) == (a[None, :] // 4)
    tp = a[:, None] % 4; tq = a[None, :] % 4
    ok = same & (tp <= tq)
    ndn = np.where(ok, -(tq - tp), 0).astype(np.float32); mkn = np.where(ok, 0, NEG).astype(np.float32)
    return dict(nd=nd, mk=mk, ndc=ndc, mkc=mkc, ndn=ndn, mkn=mkn)
def tile_whg(w):
    out = np.empty((16, 128, 4, 16, 128), np.float32)
    for hh in range(16):
        for wh in range(4):
            out[hh, :, wh] = tile_cols(w[:, wh * 2048 + hh * 128: wh * 2048 + hh * 128 + 128])
    return out
def hgrn_consts():
    a = np.arange(128)
    m01 = ((a[:, None] // 32 == a[None, :] // 32) & (a[:, None] <= a[None, :])).astype(np.float32)
    cm = np.broadcast_to((a[None, :] // 32 == np.arange(4)[:, None]).astype(np.float32)[None], (128, 4, 128)).copy()
    s = np.arange(64)
    ms = ((s[:, None] // 4 == s[None, :] // 4) & (s[:, None] <= s[None, :])).astype(np.float32)
    cms = np.broadcast_to((s[None, :] // 4 == np.arange(16)[:, None]).astype(np.float32)[None], (128, 16, 64)).copy()
    t = np.arange(1088)
    sm = np.where(t < 1024, (t % 32) != 0, ((t - 1024) % 4) != 0).astype(np.float32)
    smask = np.ascontiguousarray(np.broadcast_to(sm[None], (128, 1088)))
    return dict(m01=m01, cm=cm, ms=ms, cms=cms, ident=np.eye(128, dtype=np.float32), smask=smask)

NCORE = 8
_PROGS = {}


def _prog(name, fn):
    if name not in _PROGS:
        _PROGS[name] = fn()
    return _PROGS[name]


def _run(name, fn, in_maps):
    nc = _prog(name, fn)
    res = run_bass_kernel_spmd(nc, in_maps, core_ids=list(range(NCORE)))
    return res.results


def _f32(a):
    return np.ascontiguousarray(np.asarray(a, dtype=np.float32))


def kernel(x_prompt, x_sample, cache_swa_k, cache_swa_v, state_hgrn, state_ffn_conv, c_prompt, c_sample,
           norm1_g, norm2_g, w_ada, b_ada, attn_w_qkv, attn_w_o, attn_sinks,
           hgrn_w_in, hgrn_lower_bounds, hgrn_norm_g, hgrn_w_o,
           ffn_w_in, ffn_conv_w, ffn_conv_b, ffn_w_out, final_norm_g):
    (x_prompt, x_sample, cache_swa_k, cache_swa_v, state_hgrn, state_ffn_conv, c_prompt, c_sample,
     norm1_g, norm2_g, w_ada, b_ada, attn_w_qkv, attn_w_o, attn_sinks,
     hgrn_w_in, hgrn_lower_bounds, hgrn_norm_g, hgrn_w_o,
     ffn_w_in, ffn_conv_w, ffn_conv_b, ffn_w_out, final_norm_g) = [_f32(a) for a in (
        x_prompt, x_sample, cache_swa_k, cache_swa_v, state_hgrn, state_ffn_conv, c_prompt, c_sample,
        norm1_g, norm2_g, w_ada, b_ada, attn_w_qkv, attn_w_o, attn_sinks,
        hgrn_w_in, hgrn_lower_bounds, hgrn_norm_g, hgrn_w_o,
        ffn_w_in, ffn_conv_w, ffn_conv_b, ffn_w_out, final_norm_g)]
    Dm = 2048
    xp = x_prompt[0]
    xs = x_sample.reshape(128 * 4, Dm)

    c_all = np.concatenate([c_prompt, c_sample], 0)
    cT = fm(c_all)
    maps = []
    for c in range(NCORE):
        wt = np.empty((24, 128, 16, 128), np.float32)
        bt = np.empty((128, 24), np.float32)
        for n in range(24):
            l, ch = n // 12, 12 * c + n % 12
            wt[n] = tile_cols(w_ada[l][:, ch * 128:(ch + 1) * 128])
            bt[:, n] = b_ada[l][ch * 128:(ch + 1) * 128]
        maps.append({"cT": cT, "wada": wt, "bada": bt})
    res = _run("adaln", build_adaln, maps)
    mod = np.empty((2, 129, 6 * Dm), np.float32)
    for c in range(NCORE):
        mt = res[c]["modT"]
        for n in range(24):
            l, ch = n // 12, 12 * c + n % 12
            mod[l][:, ch * 128:(ch + 1) * 128] = mt[:, n, :].T
    mod = mod.reshape(2, 129, 6, Dm)

    def vec_for(l, g_vec, i0, extra=None):
        rows = [vfm(g_vec), vfm(mod[l, 0, i0]), vfm(mod[l, 0, i0 + 1]), vfm(mod[l, 0, i0 + 2])]
        if extra is not None:
            rows.append(vfm(extra))
        return np.ascontiguousarray(np.stack(rows, 1))

    def mods_for(l, c, i0):
        sl = slice(1 + 16 * c, 1 + 16 * c + 16)
        return mods_fm(np.stack([mod[l, sl, i0], mod[l, sl, i0 + 1], mod[l, sl, i0 + 2]], 0))

    aconst = attn_consts()
    wq_t = tile_wqkv(attn_w_qkv)
    wo_t = tile_sq(attn_w_o)
    sinks_b = np.ascontiguousarray(np.broadcast_to(attn_sinks[None], (128, 32)))
    vec0 = vec_for(0, norm1_g[0], 0)
    maps = []
    for c in range(NCORE):
        halo = xp[1024 * c - 128:1024 * c] if c > 0 else np.zeros((128, Dm), np.float32)
        xc = np.concatenate([halo, xp[1024 * c:1024 * (c + 1)], xs[64 * c:64 * (c + 1)]], 0)
        m = {"xT": fm(xc), "vec": vec0, "mods": mods_for(0, c, 0), "wqkv": wq_t, "wo": wo_t,
             "kcT": np.ascontiguousarray(cache_swa_k[16 * c:16 * c + 16].transpose(2, 3, 0, 1)),
             "vc": np.ascontiguousarray(cache_swa_v[16 * c:16 * c + 16].transpose(2, 1, 0, 3)),
             "sinks": sinks_b, "hb": np.full((128, 1), NEG if c == 0 else 0.0, np.float32)}
        m.update(aconst)
        maps.append(m)
    res = _run("attn", build_attn2, maps)
    x1 = [unfm(res[c]["xo"]) for c in range(NCORE)]
    ko = res[NCORE - 1]["kout"].reshape(2, 64, 4, 192).transpose(1, 2, 0, 3).reshape(64, 8, 192)
    swa_k_prompt = np.ascontiguousarray(ko[:, :, :128].transpose(2, 1, 0))[None]
    swa_v_prompt = np.ascontiguousarray(res[NCORE - 1]["vout"][:, 0])[None]
    swa_k_sample = np.empty((128, 4, 8, 64), np.float32)
    swa_v_sample = np.empty((128, 4, 8, 64), np.float32)
    for c in range(NCORE):
        ko = res[c]["kout"].reshape(2, 64, 4, 192).transpose(1, 2, 0, 3).reshape(64, 8, 192)
        swa_k_sample[16 * c:16 * c + 16] = ko[:, :, 128:].transpose(2, 1, 0).reshape(16, 4, 8, 64)
        swa_v_sample[16 * c:16 * c + 16] = res[c]["vout"][:64, 1].reshape(16, 4, 8, 64)

    def run_ffn(l, xin, last):
        w_in_t = tile_w_in(ffn_w_in[l])
        w_out_t = tile_w_out(ffn_w_out[l])
        convw = np.ascontiguousarray(np.concatenate([ffn_conv_w[l], ffn_conv_b[l][None]], 0).reshape(4, 44, 128).transpose(2, 1, 0))
        vec = vec_for(l, norm2_g[l], 3, extra=final_norm_g)
        maps = []
        for c in range(NCORE):
            halo = xin[c - 1][1022:1024] if c > 0 else np.zeros((2, Dm), np.float32)
            xc = np.concatenate([halo, xin[c]], 0)
            maps.append({"xT": fm(xc), "vec": vec, "mods": mods_for(l, c, 3), "w_in": w_in_t, "w_out": w_out_t,
                         "convw": convw,
                         "cstate": np.ascontiguousarray(state_ffn_conv[l, 16 * c:16 * c + 16].reshape(16, 2, 44, 128).transpose(3, 2, 0, 1)),
                         "flag": np.full((128, 1), 0.0 if c == 0 else 1.0, np.float32)})
        res = _run("ffn_last" if last else "ffn", (lambda: build_ffn(True)) if last else (lambda: build_ffn(False)), maps)
        xout = [unfm(res[c]["xo"]) for c in range(NCORE)]
        cbp = np.ascontiguousarray(res[NCORE - 1]["cbp"].transpose(2, 1, 0).reshape(2, 5632))[None]
        cbs = np.concatenate([res[c]["cbs"].transpose(2, 3, 1, 0).reshape(16, 2, 5632) for c in range(NCORE)], 0)
        return xout, cbp, cbs

    x2, cbp0, cbs0 = run_ffn(0, x1, False)

    hconst = hgrn_consts()
    whg_t = tile_whg(hgrn_w_in)
    lbp = np.ascontiguousarray(hgrn_lower_bounds.reshape(2, 16, 128).transpose(2, 0, 1))
    vec1 = vec_for(1, norm1_g[1], 0)
    maps = []
    for c in range(NCORE):
        m = {"xT": fm(x2[c]), "vec": vec1, "mods": mods_for(1, c, 0), "whg": whg_t, "lbp": lbp,
             "s0": np.ascontiguousarray(state_hgrn[16 * c:16 * c + 16].transpose(1, 2, 0, 3))}
        m.update(hconst)
        maps.append(m)
    resA = _run("hgrnA", build_hgrna, maps)
    s_loc = [resA[c]["send"] for c in range(NCORE)]
    d_loc = [resA[c]["dout"] for c in range(NCORE)]
    hgrn_state_sample = np.concatenate([resA[c]["snew"].transpose(2, 0, 1, 3) for c in range(NCORE)], 0)

    wo2_t = tile_sq(hgrn_w_o)
    ng = np.ascontiguousarray(hgrn_norm_g.reshape(16, 128).T)
    maps = []
    for c in range(NCORE):
        sr = np.zeros((16, 128, NRK, 128), np.float32)
        dr = np.ones((128, 16, NRK), np.float32)
        for r in range(c):
            sr[:, :, r, :] = s_loc[r].transpose(1, 0, 2)
            dr[:, :, r] = d_loc[r]
        maps.append({"xT": fm(x2[c]), "vec": vec1, "mods": mods_for(1, c, 0), "oloc": resA[c]["oloc"], "qb": resA[c]["qbo"],
                     "sg": resA[c]["sgo"], "sr": sr, "dr": dr, "sloc": s_loc[c], "dl": d_loc[c], "ng": ng, "wo": wo2_t})
    res = _run("hgrnB", build_hgrnb, maps)
    x3 = [unfm(res[c]["xo"]) for c in range(NCORE)]
    hgrn_state_prompt = np.ascontiguousarray(res[NCORE - 1]["send"].transpose(1, 0, 2))[None]

    y, cbp1, cbs1 = run_ffn(1, x3, True)
    y_prompt = np.concatenate([y[c][:1024] for c in range(NCORE)], 0)[None]
    y_sample = np.concatenate([y[c][1024:] for c in range(NCORE)], 0).reshape(128, 4, Dm)
    ffn_conv_prompt = np.stack([cbp0, cbp1], 0)
    ffn_conv_sample = np.stack([cbs0, cbs1], 0)
    outs = (y_prompt, y_sample, swa_k_prompt, swa_v_prompt, swa_k_sample, swa_v_sample,
            hgrn_state_prompt, hgrn_state_sample, ffn_conv_prompt, ffn_conv_sample)
    return tuple(np.ascontiguousarray(o, dtype=np.float32) for o in outs)
```
